# Optimizing a Trainium2 kernel written in Bass

```python
import math
import jax, jax.numpy as jnp
from jax import lax
import numpy as np

D_MODEL = 1024
BATCH = 4
SEQ = 4096
DEPTH = 2

CHUNK = 64
NORM_EPS = 1e-6
ATT_HEADS = 8
ATT_HEAD_DIM = 64
ATT_WIDTH = ATT_HEADS * ATT_HEAD_DIM
LEFT_CHUNKS = 8
BAND_CHUNKS = LEFT_CHUNKS + 1
MAX_REL_DIST = 256
N_REL = 2 * MAX_REL_DIST + 1
MLSTM_HEADS = 4
MLSTM_HEAD_DIM = 128
MLSTM_WIDTH = MLSTM_HEADS * MLSTM_HEAD_DIM
CONV_WIDTH = 4
D_FF = 4 * D_MODEL
IN_SIZES = (ATT_WIDTH, ATT_WIDTH, ATT_WIDTH,
            MLSTM_WIDTH, MLSTM_WIDTH, MLSTM_WIDTH, MLSTM_WIDTH,
            MLSTM_HEADS, MLSTM_HEADS,
            D_MODEL, D_MODEL)
D_IN = 3 * ATT_WIDTH + 4 * MLSTM_WIDTH + 2 * MLSTM_HEADS + 2 * D_MODEL

kernel_name = "hybrid_gated_chunkattn_mlstm_block"


def rms_norm(x, g):
    xf = x.astype(jnp.float32)
    y = xf * lax.rsqrt(jnp.mean(xf * xf, axis=-1, keepdims=True) + NORM_EPS)
    return (y * g.astype(jnp.float32)).astype(x.dtype)


def causal_depthwise_conv(x, w, bias):
    c = x.shape[-1]
    out = lax.conv_general_dilated(
        x, w[:, None, :].astype(x.dtype), window_strides=(1,),
        padding=[(CONV_WIDTH - 1, 0)], dimension_numbers=('NWC', 'WIO', 'NWC'),
        feature_group_count=c)
    return out + bias.astype(x.dtype)


def chunk_band_attention(q, k, v, rel_bias):
    b, s, h, d = q.shape
    nc = s // CHUNK
    f32 = jnp.float32
    qc = q.astype(f32).reshape(b, nc, CHUNK, h, d)
    pad = ((0, 0), (LEFT_CHUNKS, 0), (0, 0), (0, 0), (0, 0))
    kp = jnp.pad(k.astype(f32).reshape(b, nc, CHUNK, h, d), pad)
    vp = jnp.pad(v.astype(f32).reshape(b, nc, CHUNK, h, d), pad)
    kband = jnp.concatenate([kp[:, j:j + nc] for j in range(BAND_CHUNKS)], axis=2)
    vband = jnp.concatenate([vp[:, j:j + nc] for j in range(BAND_CHUNKS)], axis=2)
    scores = jnp.einsum('bclhd,bckhd->bchlk', qc, kband) * (1.0 / math.sqrt(d))
    t_idx = jnp.arange(CHUNK)[:, None]
    kb_idx = jnp.arange(BAND_CHUNKS * CHUNK)[None, :]
    rel = t_idx - kb_idx + LEFT_CHUNKS * CHUNK
    rel_idx = jnp.clip(rel, -MAX_REL_DIST, MAX_REL_DIST) + MAX_REL_DIST
    bias = rel_bias.astype(f32)[:, rel_idx]
    key_chunk = jnp.arange(nc)[:, None] - LEFT_CHUNKS + kb_idx // CHUNK
    valid = key_chunk >= 0
    scores = scores + bias[None, None]
    scores = jnp.where(valid[None, :, None, None, :], scores, -1e30)
    p = jax.nn.softmax(scores, axis=-1)
    out = jnp.einsum('bchlk,bckhd->bclhd', p, vband)
    return out.reshape(b, s, h * d).astype(q.dtype)


def mlstm_chunkwise(q, k, v, i_pre, f_pre):
    b, s, h, d = q.shape
    nc = s // CHUNK
    L = CHUNK
    f32 = jnp.float32
    q = q.astype(f32).reshape(b, nc, L, h, d)
    k = k.astype(f32).reshape(b, nc, L, h, d) * (1.0 / math.sqrt(d))
    v = v.astype(f32).reshape(b, nc, L, h, d)
    ig = i_pre.astype(f32).reshape(b, nc, L, h)
    logf = jax.nn.log_sigmoid(f_pre.astype(f32)).reshape(b, nc, L, h)
    bcum = jnp.cumsum(logf, axis=2)
    b_last = bcum[:, :, -1]
    a = b_last[:, :, None] - bcum + ig
    m_loc = jnp.max(a, axis=2)
    w = jnp.exp(a - m_loc[:, :, None])
    c_loc = jnp.einsum('bclh,bclhv,bclhk->bchvk', w, v, k)
    n_loc = jnp.einsum('bclh,bclhk->bchk', w, k)

    def step(carry, xs):
        c_prev, n_prev, m_prev = carry
        c_l, n_l, m_l, bl = xs
        m_new = jnp.maximum(bl + m_prev, m_l)
        s_prev = jnp.exp(bl + m_prev - m_new)
        s_loc = jnp.exp(m_l - m_new)
        c_new = s_prev[..., None, None] * c_prev + s_loc[..., None, None] * c_l
        n_new = s_prev[..., None] * n_prev + s_loc[..., None] * n_l
        return (c_new, n_new, m_new), (c_prev, n_prev, m_prev)

    init = (jnp.zeros((b, h, d, d), f32), jnp.zeros((b, h, d), f32), jnp.zeros((b, h), f32))
    xs = (jnp.moveaxis(c_loc, 1, 0), jnp.moveaxis(n_loc, 1, 0),
          jnp.moveaxis(m_loc, 1, 0), jnp.moveaxis(b_last, 1, 0))
    _, (c_st, n_st, m_st) = lax.scan(step, init, xs)
    c_st = jnp.moveaxis(c_st, 0, 1)
    n_st = jnp.moveaxis(n_st, 0, 1)
    m_st = jnp.moveaxis(m_st, 0, 1)

    g = bcum + m_st[:, :, None]
    dmat = bcum[:, :, :, None, :] - bcum[:, :, None, :, :] + ig[:, :, None, :, :]
    causal = jnp.tril(jnp.ones((L, L), dtype=bool))
    dmat = jnp.where(causal[None, None, :, :, None], dmat, -jnp.inf)
    m_t = jnp.maximum(g, jnp.max(dmat, axis=3))
    dw = jnp.exp(dmat - m_t[:, :, :, None, :])
    qk = jnp.einsum('bcthd,bcshd->bctsh', q, k) * dw
    inter = jnp.exp(g - m_t)
    num = (inter[..., None] * jnp.einsum('bchvk,bcthk->bcthv', c_st, q)
           + jnp.einsum('bctsh,bcshv->bcthv', qk, v))
    den = inter * jnp.einsum('bchk,bcthk->bcth', n_st, q) + jnp.sum(qk, axis=3)
    hout = num / jnp.maximum(jnp.abs(den), jnp.exp(-m_t))[..., None]
    return hout.reshape(b, s, h * d)


def hybrid_layer(x, mix_norm_g, w_in, conv_w, conv_b, b_igate, b_fgate, rel_bias,
                 mh_norm_g, w_att_proj, w_mlstm_proj, w_out, ffn_norm_g, w_up, w_down):
    bsz, s, _ = x.shape
    xn = rms_norm(x, mix_norm_g)
    proj = xn @ w_in
    points = []
    acc = 0
    for size in IN_SIZES[:-1]:
        acc += size
        points.append(acc)
    aq, ak, av, mq, mk, mv, mo, ipre, fpre, ga, gm = jnp.split(proj, points, axis=-1)

    att = chunk_band_attention(
        aq.reshape(bsz, s, ATT_HEADS, ATT_HEAD_DIM),
        ak.reshape(bsz, s, ATT_HEADS, ATT_HEAD_DIM),
        av.reshape(bsz, s, ATT_HEADS, ATT_HEAD_DIM), rel_bias)

    mqk = jax.nn.silu(causal_depthwise_conv(jnp.concatenate([mq, mk], axis=-1), conv_w, conv_b))
    mq_c, mk_c = jnp.split(mqk, [MLSTM_WIDTH], axis=-1)
    hm = mlstm_chunkwise(
        mq_c.reshape(bsz, s, MLSTM_HEADS, MLSTM_HEAD_DIM),
        mk_c.reshape(bsz, s, MLSTM_HEADS, MLSTM_HEAD_DIM),
        mv.reshape(bsz, s, MLSTM_HEADS, MLSTM_HEAD_DIM),
        ipre + b_igate, fpre + b_fgate)
    hm = rms_norm(hm.reshape(bsz, s, MLSTM_HEADS, MLSTM_HEAD_DIM),
                  mh_norm_g.reshape(MLSTM_HEADS, MLSTM_HEAD_DIM)).reshape(bsz, s, MLSTM_WIDTH)
    mlstm_out = (jax.nn.sigmoid(mo.astype(jnp.float32)) * hm).astype(x.dtype)

    y = jax.nn.sigmoid(ga) * (att @ w_att_proj) + jax.nn.sigmoid(gm) * (mlstm_out @ w_mlstm_proj)
    x = x + y @ w_out

    hn = rms_norm(x, ffn_norm_g)
    x = x + jnp.square(jax.nn.relu(hn @ w_up)) @ w_down
    return x


def setup_inputs(seed: int = 0) -> dict:
    key = jax.random.key(seed)
    ks = jax.random.split(key, 16)
    f32 = jnp.float32
    nrm = lambda k, shape, scale: jax.random.normal(k, shape, f32) * scale
    return {
        "x": nrm(ks[0], (BATCH, SEQ, D_MODEL), 1.0),
        "mix_norm_g": 1.0 + nrm(ks[1], (DEPTH, D_MODEL), 0.02),
        "w_in": nrm(ks[2], (DEPTH, D_MODEL, D_IN), D_MODEL ** -0.5),
        "conv_w": nrm(ks[3], (DEPTH, CONV_WIDTH, 2 * MLSTM_WIDTH), CONV_WIDTH ** -0.5),
        "conv_b": nrm(ks[4], (DEPTH, 2 * MLSTM_WIDTH), 0.01),
        "b_igate": nrm(ks[5], (DEPTH, MLSTM_HEADS), 0.1),
        "b_fgate": 3.0 + 3.0 * jax.random.uniform(ks[6], (DEPTH, MLSTM_HEADS), f32),
        "rel_bias": nrm(ks[7], (DEPTH, ATT_HEADS, N_REL), 0.2),
        "mh_norm_g": 1.0 + nrm(ks[8], (DEPTH, MLSTM_WIDTH), 0.02),
        "w_att_proj": nrm(ks[9], (DEPTH, ATT_WIDTH, D_MODEL), ATT_WIDTH ** -0.5),
        "w_mlstm_proj": nrm(ks[10], (DEPTH, MLSTM_WIDTH, D_MODEL), MLSTM_WIDTH ** -0.5),
        "w_out": nrm(ks[11], (DEPTH, D_MODEL, D_MODEL), D_MODEL ** -0.5),
        "ffn_norm_g": 1.0 + nrm(ks[12], (DEPTH, D_MODEL), 0.02),
        "w_up": nrm(ks[13], (DEPTH, D_MODEL, D_FF), D_MODEL ** -0.5),
        "w_down": nrm(ks[14], (DEPTH, D_FF, D_MODEL), D_FF ** -0.5),
        "final_norm_g": 1.0 + nrm(ks[15], (D_MODEL,), 0.02),
    }


def reference(x, mix_norm_g, w_in, conv_w, conv_b, b_igate, b_fgate, rel_bias, mh_norm_g,
              w_att_proj, w_mlstm_proj, w_out, ffn_norm_g, w_up, w_down, final_norm_g):
    for l in range(DEPTH):
        x = hybrid_layer(x, mix_norm_g[l], w_in[l], conv_w[l], conv_b[l], b_igate[l], b_fgate[l],
                         rel_bias[l], mh_norm_g[l], w_att_proj[l], w_mlstm_proj[l], w_out[l],
                         ffn_norm_g[l], w_up[l], w_down[l])
    return rms_norm(x, final_norm_g)
```

```python
import math
from contextlib import ExitStack

import numpy as np
import concourse.bass as bass
import concourse.mybir as mybir
from concourse.bass_utils import run_bass_kernel_spmd

F32 = mybir.dt.float32
BF16 = mybir.dt.bfloat16
AF = mybir.ActivationFunctionType
ALU = mybir.AluOpType

D_MODEL = 1024
BATCH = 4
SEQ = 4096
DEPTH = 2
TB = 512
NW_TILES = 31
NWS = 4
NTF = 8
NTB = 4
EPS = 1e-6
NEG = -30000.0
GA0 = 3592
GM0 = 4616
EPOCH = 8000
SAME_ENGINE_SYNC = "raw"
LIST_SCHED = True
ATT_MUL_ENG = "vector"
NH = 4
XENG_LAT = 200.0
import os as _os
KNOP = set(int(v) for v in _os.environ.get('KNOP', '').split(',') if v)


def _is_cc(key):
    return isinstance(key, tuple) and key[0] == "cc"


class _Op:
    __slots__ = ("eng", "fn", "deps", "dma_sem", "token", "needs_inc", "idx", "raw", "cost", "alldeps")


class Sched:
    def __init__(self):
        self.ops = []
        self.lastw = {}
        self.readers = {}

    def op(self, eng, fn, reads=(), writes=(), dma_sem=None, cost=300.0):
        o = _Op()
        o.cost = cost
        o.eng = eng
        o.fn = fn
        o.dma_sem = dma_sem
        o.idx = len(self.ops)
        o.needs_inc = dma_sem is not None
        o.token = None
        deps = set()
        raw = set()
        for r in reads:
            if r in self.lastw:
                deps.add(self.lastw[r])
                raw.add(self.lastw[r])
        o.raw = raw
        for w in writes:
            if w in self.lastw:
                deps.add(self.lastw[w])
            for rd in self.readers.get(w, ()):
                deps.add(rd)
        deps.discard(o.idx)
        o.deps = deps
        for r in reads:
            self.readers.setdefault(r, set()).add(o.idx)
        for w in writes:
            self.lastw[w] = o.idx
            self.readers[w] = set()
        self.ops.append(o)
        return o

    def finalize(self):
        import os
        if os.environ.get("KTRUNC"):
            self.ops = self.ops[:int(os.environ["KTRUNC"])]
        if os.environ.get("KSKIP"):
            a, b = [int(v) for v in os.environ["KSKIP"].split(",")]
            kept = [o for o in self.ops if not (a <= o.idx < b)]
            remap = {o.idx: i for i, o in enumerate(kept)}
            for o in kept:
                o.deps = set(remap[d] for d in o.deps if d in remap)
                o.raw = set(remap[d] for d in o.raw if d in remap)
                o.idx = remap[o.idx]
            self.ops = kept
        ops = self.ops
        for o in ops:
            o.alldeps = set(o.deps)
        self.order = self.list_schedule() if LIST_SCHED else None
        for o in ops:
            keep = set()
            for d in o.deps:
                do = ops[d]
                if do.eng == o.eng and do.dma_sem is None and o.dma_sem is None:
                    if o.eng == "tensor" or not SAME_ENGINE_SYNC:
                        continue
                    if SAME_ENGINE_SYNC == "raw" and d not in o.raw:
                        continue
                keep.add(d)
            o.deps = keep
            for d in keep:
                ops[d].needs_inc = True

    def list_schedule(self):
        import heapq
        ops = self.ops
        n = len(ops)
        ndep = [len(o.alldeps) for o in ops]
        users = [[] for _ in range(n)]
        for o in ops:
            for d in o.alldeps:
                users[d].append(o.idx)
        finish = [0.0] * n
        ready_t = [0.0] * n
        engs = ["tensor", "vector", "scalar", "gpsimd", "sync"]
        pend = {e: [] for e in engs}
        avail = {e: [] for e in engs}
        t_free = {e: 0.0 for e in engs}
        for o in ops:
            if ndep[o.idx] == 0:
                heapq.heappush(pend[o.eng], (0.0, o.idx))
        order = []
        done = 0
        while done < n:
            best = None
            for e in engs:
                while pend[e] and pend[e][0][0] <= t_free[e]:
                    rt, i = heapq.heappop(pend[e])
                    heapq.heappush(avail[e], i)
                if avail[e]:
                    cand = (t_free[e], avail[e][0], e, True)
                elif pend[e]:
                    cand = (pend[e][0][0], pend[e][0][1], e, False)
                else:
                    continue
                if best is None or cand[:2] < best[:2]:
                    best = cand
            st, i, e, from_avail = best
            if from_avail:
                heapq.heappop(avail[e])
            else:
                heapq.heappop(pend[e])
            o = ops[i]
            fin = st + o.cost
            t_free[e] = fin
            finish[i] = fin
            order.append(i)
            done += 1
            for u in users[i]:
                lat = 0.0 if ops[u].eng == e else XENG_LAT
                if fin + lat > ready_t[u]:
                    ready_t[u] = fin + lat
                ndep[u] -= 1
                if ndep[u] == 0:
                    heapq.heappush(pend[ops[u].eng], (ready_t[u], u))
        self.est_total = max(finish) if finish else 0.0
        return order

    def emit(self, nc, es, dma_sems, final_waits):
        ops = self.ops
        seq = [ops[i] for i in self.order] if self.order is not None else list(ops)
        eng_sems = {}
        counters = {}
        dma_counts = {}
        for o in seq:
            if o.dma_sem is not None:
                c = dma_counts.get(o.dma_sem, 0) + (1 if _is_cc(o.dma_sem) else 16)
                dma_counts[o.dma_sem] = c
                o.token = (dma_sems[o.dma_sem], c)
            elif o.needs_inc:
                ep, c = counters.get(o.eng, (0, 0))
                if c >= EPOCH:
                    ep, c = ep + 1, 0
                c += 1
                counters[o.eng] = (ep, c)
                key = (o.eng, ep)
                if key not in eng_sems:
                    eng_sems[key] = es.enter_context(nc.semaphore("p_%s_%d" % (o.eng, ep)))
                o.token = (eng_sems[key], c)
        for o in ops:
            if o.dma_sem in ("setup", "setup_sw"):
                o.token = (dma_sems[o.dma_sem], dma_counts[o.dma_sem])
        by_eng = {}
        for o in seq:
            by_eng.setdefault(o.eng, []).append(o)

        def run(eng_name, e):
            waited = {}
            for o in by_eng.get(eng_name, []):
                need = {}
                for d in o.deps:
                    s, v = ops[d].token
                    k = id(s)
                    if waited.get(k, 0) >= v:
                        continue
                    if k not in need or need[k][1] < v:
                        need[k] = (s, v)
                for k, (s, v) in need.items():
                    e.wait_ge(s, v)
                    waited[k] = v
                if o.idx in KNOP:
                    ins = e.nop()
                else:
                    ins = o.fn(e)
                if o.dma_sem is not None:
                    if _is_cc(o.dma_sem):
                        ins.then_inc(o.token[0])
                    else:
                        ins.then_inc(o.token[0], 16)
                elif o.needs_inc:
                    ins.then_inc(o.token[0], 1)
            mine = []
            for o in by_eng.get(eng_name, []):
                if o.dma_sem is not None and o.dma_sem not in mine:
                    mine.append(o.dma_sem)
            for key in mine:
                e.wait_ge(dma_sems[key], dma_counts[key])

        with nc.Block() as block:
            @block.sync
            def _(e):
                run("sync", e)

            @block.gpsimd
            def _(e):
                run("gpsimd", e)

            @block.tensor
            def _(e):
                run("tensor", e)

            @block.vector
            def _(e):
                run("vector", e)

            @block.scalar
            def _(e):
                run("scalar", e)


def build_program(nblk, depth=DEPTH, pipe=False):
    T = nblk * TB
    nc = bass.Bass("TRN2", target_bir_lowering=False)
    L = 1 if pipe else depth
    nstep = nblk + 1 if pipe else nblk
    x_d = nc.dram_tensor("x", [T, D_MODEL], F32, kind="ExternalInput").ap()
    w_d = nc.dram_tensor("wstream", [L * NW_TILES, 128, 4096], F32, kind="ExternalInput").ap()
    wif_d = nc.dram_tensor("wif", [128, L * 64], F32, kind="ExternalInput").ap()
    bias_d = nc.dram_tensor("biasg", [128, L * 8 * 640], F32, kind="ExternalInput").ap()
    cst_d = nc.dram_tensor("cst", [128, 5 * 128], F32, kind="ExternalInput").ap()
    negm_d = nc.dram_tensor("negm", [128, 640], F32, kind="ExternalInput").ap()
    NPAR = L * 8 + L * 8 + 8 + L * 32 + L * 8 + L * 4 + L * 16 + L * 16 + 1 + (1 + 2 * (nblk + 1) if pipe else 0)
    par_d = nc.dram_tensor("params", [128, NPAR], F32, kind="ExternalInput").ap()
    out_d = nc.dram_tensor("out", [T, D_MODEL], F32, kind="ExternalOutput").ap()

    es = ExitStack()
    with es:
        def sb(name, shape, dt):
            return es.enter_context(nc.sbuf_tensor(name, shape, dt))

        io = sb("io", [128, 4, 1024], F32)
        if pipe:
            xin = sb("xin", [128, 4, 1024], F32)
            rbuf = sb("rbuf", [128, 8, TB], F32)
            CPP = 8 // NH
            send_d = [[nc.dram_tensor("send%d_%d" % (k, hf), [128, CPP * TB], F32, kind="Internal").ap()
                       for hf in range(NH)] for k in range(nblk)]
            recv_d = [[nc.dram_tensor("recv%d_%d" % (k, hf), [256, CPP * TB], F32, kind="Internal").ap()
                       for hf in range(NH)] for k in range(nblk)]
        else:
            xin = io
        xT = sb("xT", [128, 8, TB], F32)
        xnT = sb("xnT", [128, 8, TB], BF16)
        arena = sb("arena", [128, 32, TB], BF16)
        kh = sb("kh", [128, L * 2 * 4, TB], BF16)
        vh = sb("vh", [128, L * 2 * 4, TB], BF16)
        wb = sb("wb", [128, NWS, 4096], BF16)
        wif = sb("wifs", [128, L * 64], BF16)
        BM = sb("BM", [128, L * 8, 640], BF16)
        negm = sb("negms", [128, 640], BF16)
        S = sb("S", [128, L * 4, 256], F32)
        Sb = sb("Sb", [128, L * 4, 256], BF16)
        halo = sb("halo", [128, L * 8, 4], F32)
        cst = sb("csts", [128, 5, 128], F32)
        identB = sb("identB", [128, 128], BF16)
        cstB = sb("cstB", [128, 3, 128], BF16)
        onesB = sb("onesB", [128, 128], BF16)
        onesF = sb("onesF", [128, 128], F32)
        par = sb("pars", [128, NPAR], F32)
        tf = sb("tf", [128, NTF, 512], F32)
        tbf = sb("tbf", [128, NTB, 512], BF16)
        pre = sb("pre", [128, 2, 516], F32)
        LF = sb("LF", [128, 2, 4, 128], F32)
        gsm = sb("gsm", [128, 8, 16], F32)
        wsm = sb("wsm", [128, 8, 2], F32)
        ps = [es.enter_context(nc.psum_tensor("ps%d" % b, [128, 512], F32)) for b in range(8)]

        identF = cst[:, 0, :]
        onesMD = cst[:, 1, :]
        onesMH = cst[:, 2, :]
        Umat = cst[:, 3, :]
        NEGmat = cst[:, 4, :]
        onesMDb = cstB[:, 0, :]
        onesMHb = cstB[:, 1, :]
        NEGb = cstB[:, 2, :]

        o_g1 = 0
        o_g2 = o_g1 + L * 8
        o_gf = o_g2 + L * 8
        o_cw = o_gf + 8
        o_cb = o_cw + L * 32
        o_mh = o_cb + L * 8
        o_bi = o_mh + L * 4
        o_bf = o_bi + L * 16
        o_eps = o_bf + L * 16
        o_fb = o_eps + 1
        o_sm = o_fb + 1
        o_vd = o_sm + nstep

        def pcol(off):
            return par[:, off:off + 1]

        dma_sems = {}
        extra_sems = ([("snd", k) for k in range(NH)] + [("rcv", k) for k in range(NH)]
                      + [("cc", k) for k in range(NH * nblk)]) if pipe else []
        for k in ["setup", "setup_sw", "io_in", "io_out"] + extra_sems + [("w", s) for s in range(NWS)]:
            dma_sems[k] = es.enter_context(nc.semaphore("d_%s" % (str(k).replace(" ", ""))))

        sch = Sched()
        state = {"ps": 0, "tf": 0, "tb": 0, "w": 0, "alt": 0, "pre": 0, "lf": 0, "wsm": 0}

        def newps(exclude=()):
            b = state["ps"]
            while b in exclude:
                b = (b + 1) % 8
            state["ps"] = (b + 1) % 8
            return b

        def newtf():
            k = state["tf"]
            state["tf"] = (k + 1) % NTF
            return k

        def newtb():
            k = state["tb"]
            state["tb"] = (k + 1) % NTB
            return k

        def fsz(ap):
            n = 1
            for d in ap.shape[1:]:
                n *= int(d)
            return n

        def mm(out, lhsT, rhs, start, stop, r, w):
            c = max(fsz(rhs), 64) / 1.95 + 15.0
            if lhsT.dtype == F32:
                c *= 4.0
            sch.op("tensor", lambda e: e.matmul(out, lhsT, rhs, start=start, stop=stop), r, w, cost=c)

        def tr(out, in_, r, w):
            sch.op("tensor", lambda e: e.transpose(out, in_, identF), list(r) + ["cst"], w, cost=280.0)

        def act(out, in_, func, r, w, bias=None, scale=None):
            kw = {}
            if bias is not None:
                kw["bias"] = bias
            if scale is not None:
                kw["scale"] = scale
            c = fsz(out) / 1.4 + 220.0 + (90.0 if bias is not None and not isinstance(bias, float) else 0.0)
            sch.op("scalar", lambda e: e.activation(out, in_, func, **kw), r, w, cost=c)

        def dcost(ap, mult=1.0):
            return max(fsz(ap), 64) * mult / 0.96 + 90.0

        def tt(out, in0, in1, op, r, w, eng="vector"):
            sch.op(eng, lambda e: e.tensor_tensor(out, in0, in1, op), r, w, cost=dcost(out))

        def ts(out, in0, s1, s2, op0, op1, r, w, eng="vector"):
            if op1 is None:
                sch.op(eng, lambda e: e.tensor_scalar(out, in0, s1, None, op0), r, w, cost=dcost(out))
            else:
                sch.op(eng, lambda e: e.tensor_scalar(out, in0, s1, s2, op0, op1), r, w, cost=dcost(out))

        def stt(out, in0, scalar, in1, op0, op1, r, w):
            sch.op("vector", lambda e: e.scalar_tensor_tensor(out, in0, scalar, in1, op0, op1), r, w,
                   cost=dcost(out))

        def recip(out, in_, r, w):
            sch.op("vector", lambda e: e.reciprocal(out, in_), r, w, cost=dcost(out, 8.0))

        def copy_any(out, in_, r, w):
            state["alt"] ^= 1
            if state["alt"]:
                sch.op("scalar", lambda e: e.copy(out, in_), r, w, cost=fsz(out) / 1.4 + 220.0)
            else:
                sch.op("vector", lambda e: e.tensor_copy(out, in_), r, w, cost=dcost(out))

        def dsetup(eng, out, in_, w):
            sch.op(eng, lambda e: e.dma_start(out=out, in_=in_), [], w,
                   dma_sem="setup_sw" if eng == "gpsimd" else "setup")

        dsetup("sync", cst[:, :, :], cst_d.rearrange("p (k n) -> p k n", k=5), ["cst"])
        dsetup("sync", par[:, :], par_d, ["par"])
        dsetup("gpsimd", wif[:, :], wif_d, ["wif"])
        dsetup("gpsimd", negm[:, :], negm_d, ["negm"])
        for l in range(L):
            for h in range(8):
                dsetup("gpsimd", BM[:, l * 8 + h, :],
                       bias_d[:, (l * 8 + h) * 640:(l * 8 + h + 1) * 640], [("BM", l * 8 + h)])
        sch.op("vector", lambda e: e.memset(S[:, :, :], 0.0), [], ["S%d" % k for k in range(L * 4)])
        sch.op("vector", lambda e: e.memset(Sb[:, :, :], 0.0), [], ["Sb%d" % k for k in range(L * 4)])
        sch.op("vector", lambda e: e.memset(halo[:, :, :], 0.0), [], [("halo", k) for k in range(L * 8)])
        sch.op("vector", lambda e: e.memset(kh[:, :, :], 0.0), [], [("kh", k) for k in range(L * 8)])
        sch.op("vector", lambda e: e.memset(vh[:, :, :], 0.0), [], [("vh", k) for k in range(L * 8)])
        sch.op("vector", lambda e: e.memset(onesB[:, :], 1.0), [], ["onesB"])
        sch.op("vector", lambda e: e.memset(onesF[:, :], 1.0), [], ["onesF"])
        sch.op("vector", lambda e: e.tensor_copy(identB[:, :], identF), ["cst"], ["identB"])
        sch.op("vector", lambda e: e.tensor_copy(cstB[:, 0:2, :], cst[:, 1:3, :]), ["cst"], ["cstB"])
        sch.op("vector", lambda e: e.tensor_copy(cstB[:, 2, :], cst[:, 4, :]), ["cst", "cstB"], ["cstB"])
        for l in range(L):
            for h in range(8):
                k = l * 8 + h
                tt(BM[:, k, :], BM[:, k, :], negm[:, :], ALU.add, [("BM", k), "negm"], [("BM", k)])
                if ATT_MUL_ENG:
                    act(BM[:, k, :], BM[:, k, :], AF.Exp, [("BM", k)], [("BM", k)])

        def next_w(l, k_expected):
            n = state["w"]
            state["w"] = n + 1
            assert n % NW_TILES == k_expected and n // NW_TILES % L == l, (n, l, k_expected)
            slot = n % NWS
            src = w_d[(n % (L * NW_TILES)), :, :]
            sch.op("gpsimd", lambda e: e.dma_start(out=wb[:, slot, :], in_=src), [], [("w", slot)],
                   dma_sem=("w", slot), cost=9000.0)
            return slot

        def wview(slot, kc, c0, n, kcn=8):
            width = 4096 // kcn
            return wb[:, slot, kc * width + c0: kc * width + c0 + n]

        def rstd_from(P_ap, kr, rd):
            act(tf[:, kr, :], P_ap, AF.Ln, rd + ["par"], [("tf", kr)], bias=pcol(o_eps))
            act(tf[:, kr, :], tf[:, kr, :], AF.Exp, [("tf", kr)], [("tf", kr)], scale=-0.5)

        def norm_stats():
            P = newps()
            for c in range(8):
                k = newtb()
                act(tbf[:, k, :], xT[:, c, :], AF.Square, [("xT", c)], [("tb", k)])
                mm(ps[P][:, :], onesMDb, tbf[:, k, :], c == 0, c == 7, [("tb", k), "cstB"], [("ps", P)])
            kr = newtf()
            rstd_from(ps[P][:, :], kr, [("ps", P)])
            return kr

        def rmsnorm(gcol0, dst_bf16):
            kr = norm_stats()
            for c in range(8):
                if dst_bf16:
                    stt(xnT[:, c, :], xT[:, c, :], pcol(gcol0 + c), tf[:, kr, :], ALU.mult, ALU.mult,
                        [("xT", c), ("tf", kr), "par"], [("xn", c)])
                else:
                    stt(xT[:, c, :], xT[:, c, :], pcol(gcol0 + c), tf[:, kr, :], ALU.mult, ALU.mult,
                        [("xT", c), ("tf", kr), "par"], [("xT", c)])

        def pg(n):
            return ("pg", n)

        SC_M = 1.0 / math.sqrt(128.0)

        def layer(l, j):
            parity = j % 2
            kcur = lambda p: (l * 2 + parity) * 4 + p
            kprev = lambda p: (l * 2 + 1 - parity) * 4 + p

            rmsnorm(o_g1 + l * 8, True)
            xn_all = [("xn", c) for c in range(8)]

            for g in range(2):
                slot = next_w(l, g)
                for m in range(4):
                    P = newps()
                    for kc in range(8):
                        mm(ps[P][:, :], wview(slot, kc, m * 128, 128), xnT[:, kc, :], kc == 0, kc == 7,
                           [("w", slot), ("xn", kc)], [("ps", P)])
                    if g == 0:
                        copy_any(arena[:, m, :], ps[P][:, :], [("ps", P)], [pg(m)])
                    else:
                        copy_any(kh[:, kcur(m), :], ps[P][:, :], [("ps", P)], [("kh", kcur(m))])
            slot = next_w(l, 2)
            for i in range(4):
                P = newps()
                for kc in range(8):
                    mm(ps[P][:, :], xnT[:, kc, i * 128:(i + 1) * 128], wview(slot, kc, 0, 512), kc == 0, kc == 7,
                       [("w", slot), ("xn", kc)], [("ps", P)])
                copy_any(vh[:, kcur(i), :], ps[P][:, :], [("ps", P)], [("vh", kcur(i))])

            for p in range(4):
                O = newps()
                Dn = newps()
                for e_ in range(2):
                    h = 2 * p + e_
                    rows = slice(e_ * 64, (e_ + 1) * 64)
                    rlist = [4] + [r for r in range(8) if (j > 0 or r >= 4 or pipe) and r != 4]
                    for idx, r in enumerate(rlist):
                        kidx = kprev(p) if r < 4 else kcur(p)
                        vidx = kprev(r % 4) if r < 4 else kcur(r % 4)
                        qlo = max(8, 2 * r) * 64 - 512
                        qhi = (min(15, 2 * r + 9) + 1) * 64 - 512
                        n = qhi - qlo
                        u0 = 512 + qlo - 128 * r
                        sc = newps(exclude=(O, Dn))
                        mm(ps[sc][:, 0:n], kh[rows, kidx, (r % 4) * 128:(r % 4 + 1) * 128], arena[rows, p, qlo:qhi],
                           True, True, [("kh", kidx), pg(p)], [("ps", sc)])
                        k1 = newtf()
                        k2 = newtb()
                        if ATT_MUL_ENG:
                            if pipe and r < 4:
                                act(tbf[:, k2, 0:n], ps[sc][:, 0:n], AF.Exp, [("ps", sc), "par"], [("tb", k2)],
                                    bias=pcol(o_sm + j), scale=0.125)
                            else:
                                act(tbf[:, k2, 0:n], ps[sc][:, 0:n], AF.Exp, [("ps", sc)], [("tb", k2)], scale=0.125)
                            sch.op(ATT_MUL_ENG, lambda e, k2=k2, n=n, u0=u0, lh=l * 8 + h: e.tensor_tensor(
                                tbf[:, k2, 0:n], tbf[:, k2, 0:n], BM[:, lh, u0:u0 + n], ALU.mult),
                                [("tb", k2), ("BM", l * 8 + h)], [("tb", k2)],
                                cost=(n / 0.5 + 500.0) if ATT_MUL_ENG == "gpsimd" else (n / 1.9 + 90.0))
                        else:
                            stt(tf[:, k1, 0:n], ps[sc][:, 0:n], 0.125, BM[:, l * 8 + h, u0:u0 + n], ALU.mult, ALU.add,
                                [("ps", sc), ("BM", l * 8 + h)], [("tf", k1)])
                            if pipe and r < 4:
                                act(tbf[:, k2, 0:n], tf[:, k1, 0:n], AF.Exp, [("tf", k1), "par"], [("tb", k2)],
                                    bias=pcol(o_sm + j))
                            else:
                                act(tbf[:, k2, 0:n], tf[:, k1, 0:n], AF.Exp, [("tf", k1)], [("tb", k2)])
                        first = idx == 0
                        last = idx == len(rlist) - 1
                        mm(ps[O][rows, qlo:qhi], vh[:, vidx, h * 64:(h + 1) * 64], tbf[:, k2, 0:n], first, last,
                           [("vh", vidx), ("tb", k2)], [("ps", O)])
                        mm(ps[Dn][rows, qlo:qhi], onesB[:, 0:64], tbf[:, k2, 0:n], first, last,
                           ["onesB", ("tb", k2)], [("ps", Dn)])
                kr = newtf()
                act(tf[:, kr, :], ps[Dn][:, :], AF.Ln, [("ps", Dn)], [("tf", kr)])
                act(tf[:, kr, :], tf[:, kr, :], AF.Exp, [("tf", kr)], [("tf", kr)], scale=-1.0)
                tt(arena[:, 24 + p, :], ps[O][:, :], tf[:, kr, :], ALU.mult, [("ps", O), ("tf", kr)], [pg(24 + p)])

            for g in (3, 4):
                slot = next_w(l, g)
                for m in range(4):
                    ch = (g - 3) * 4 + m
                    P = newps()
                    for kc in range(8):
                        mm(ps[P][:, :], wview(slot, kc, m * 128, 128), xnT[:, kc, :], kc == 0, kc == 7,
                           [("w", slot), ("xn", kc)], [("ps", P)])
                    pk = state["pre"]
                    state["pre"] ^= 1
                    hk = l * 8 + ch
                    copy_any(pre[:, pk, 3:515], ps[P][:, :], [("ps", P)], [("pre", pk)])
                    sch.op("vector", lambda e, pk=pk, hk=hk: e.tensor_copy(pre[:, pk, 0:3], halo[:, hk, 0:3]),
                           [("halo", hk)], [("pre", pk)])
                    sch.op("vector", lambda e, pk=pk, hk=hk: e.tensor_copy(halo[:, hk, 0:3], pre[:, pk, 512:515]),
                           [("pre", pk)], [("halo", hk)])
                    ka = newtf()
                    cw = lambda tap: pcol(o_cw + l * 32 + tap * 8 + ch)
                    ts(tf[:, ka, :], pre[:, pk, 0:512], cw(0), None, ALU.mult, None,
                       [("pre", pk), "par"], [("tf", ka)])
                    for tap in (1, 2, 3):
                        stt(tf[:, ka, :], pre[:, pk, tap:tap + 512], cw(tap), tf[:, ka, :], ALU.mult, ALU.add,
                            [("pre", pk), "par", ("tf", ka)], [("tf", ka)])
                    act(arena[:, 4 + ch, :], tf[:, ka, :], AF.Silu, [("tf", ka), "par"], [pg(4 + ch)],
                        bias=pcol(o_cb + l * 8 + ch))
            slot = next_w(l, 5)
            for i in range(4):
                P = newps()
                for kc in range(8):
                    mm(ps[P][:, :], xnT[:, kc, i * 128:(i + 1) * 128], wview(slot, kc, 0, 512), kc == 0, kc == 7,
                       [("w", slot), ("xn", kc)], [("ps", P)])
                copy_any(arena[:, 12 + i, :], ps[P][:, :], [("ps", P)], [pg(12 + i)])
            slot = next_w(l, 6)
            for m in range(4):
                P = newps()
                for kc in range(8):
                    mm(ps[P][:, :], wview(slot, kc, m * 128, 128), xnT[:, kc, :], kc == 0, kc == 7,
                       [("w", slot), ("xn", kc)], [("ps", P)])
                act(arena[:, 20 + m, :], ps[P][:, :], AF.Sigmoid, [("ps", P)], [pg(20 + m)])
            G = newps()
            for i in range(4):
                for kc in range(8):
                    mm(ps[G][:, i * 8:(i + 1) * 8], xnT[:, kc, i * 128:(i + 1) * 128],
                       wif[:, l * 64 + kc * 8: l * 64 + kc * 8 + 8], kc == 0, kc == 7,
                       ["wif", ("xn", kc)], [("ps", G)])
            G3 = ps[G][:, 0:32].rearrange("p (i c) -> p i c", c=8)
            ig = gsm[:, 0, :]
            zf = gsm[:, 1, :]
            ef = gsm[:, 2, :]
            spv = gsm[:, 3, :]
            logf = gsm[:, 4, :]
            biasv = gsm[:, 5, :]
            v3 = lambda ap: ap.rearrange("p (i c) -> p i c", c=4)
            tt(v3(ig), G3[:, :, 0:4], v3(par[:, o_bi + l * 16:o_bi + l * 16 + 16]), ALU.add,
               [("ps", G), "par"], ["g_ig"])
            tt(v3(zf), G3[:, :, 4:8], v3(par[:, o_bf + l * 16:o_bf + l * 16 + 16]), ALU.add,
               [("ps", G), "par"], ["g_zf"])
            act(ef, zf, AF.Exp, ["g_zf"], ["g_ef"], scale=-1.0)
            act(spv, ef, AF.Ln, ["g_ef"], ["g_sp"], bias=1.0)
            ts(logf, spv, -1.0, None, ALU.mult, None, ["g_sp"], ["g_lf"])
            Cm = newps()
            mm(ps[Cm][:, 0:16], Umat, logf, True, True, ["cst", "g_lf"], [("ps", Cm)])
            tt(biasv, ig, ps[Cm][:, 0:16], ALU.subtract, ["g_ig", ("ps", Cm)], ["g_bv"])

            for i in range(4):
                lfk = state["lf"]
                state["lf"] ^= 1
                lsrc = logf[:, i * 4:(i + 1) * 4].rearrange("p (h o) -> p h o", o=1).broadcast_to([128, 4, 128])
                sch.op("vector", lambda e, lfk=lfk, lsrc=lsrc: e.tensor_copy(LF[:, lfk, :, :], lsrc),
                       ["g_lf"], [("LF", lfk, h) for h in range(4)], cost=630.0)
                tsl = slice(i * 128, (i + 1) * 128)
                for h in range(4):
                    col = i * 4 + h
                    sk = l * 4 + h
                    kT = arena[:, 8 + h, tsl]
                    qT_ = arena[:, 4 + h, tsl]
                    vt = arena[:, 12 + i, h * 128:(h + 1) * 128]
                    A = newps()
                    mm(ps[A][:, 0:128], kT, qT_, True, True, [pg(8 + h), pg(4 + h)], [("ps", A)])
                    mm(ps[A][:, 128:256], LF[:, lfk, h, :], Umat, True, True, [("LF", lfk, h), "cst"], [("ps", A)])
                    mm(ps[A][:, 256:384], LF[:, lfk, h, :], Umat, True, False, [("LF", lfk, h), "cst"], [("ps", A)])
                    mm(ps[A][:, 256:384], identB[:, :], NEGb, False, True, ["identB", "cstB"], [("ps", A)])
                    ka = newtf()
                    act(tf[:, ka, 0:128], ps[A][:, 256:384], AF.Exp, [("ps", A), "g_bv"], [("tf", ka)],
                        bias=biasv[:, col:col + 1])
                    act(tf[:, ka, 128:256], ps[A][:, 128:256], AF.Exp, [("ps", A)], [("tf", ka)])
                    wk = state["wsm"]
                    state["wsm"] = (wk + 1) % 8
                    if pipe:
                        ts(wsm[:, wk, 0:1], tf[:, ka, 127:128], pcol(o_vd + j), None, ALU.mult, None,
                           [("tf", ka), "par"], [("wsm", wk)])
                    else:
                        sch.op("vector", lambda e, wk=wk, ka=ka: e.tensor_copy(wsm[:, wk, 0:1], tf[:, ka, 127:128]),
                               [("tf", ka)], [("wsm", wk)])
                    sch.op("vector", lambda e, wk=wk, ka=ka: e.tensor_copy(wsm[:, wk, 1:2], tf[:, ka, 255:256]),
                           [("tf", ka)], [("wsm", wk)])
                    kb = newtb()
                    stt(tbf[:, kb, 0:128], ps[A][:, 0:128], SC_M, tf[:, ka, 0:128], ALU.mult, ALU.mult,
                        [("ps", A), ("tf", ka)], [("tb", kb)])
                    stt(tbf[:, kb, 128:256], qT_, SC_M, tf[:, ka, 128:256], ALU.mult, ALU.mult,
                        [pg(4 + h), ("tf", ka)], [("tb", kb)])
                    AT = tbf[:, kb, 0:128]
                    qsT = tbf[:, kb, 128:256]
                    B = newps()
                    mm(ps[B][:, 0:128], vt, AT, True, False, [pg(12 + i), ("tb", kb)], [("ps", B)])
                    mm(ps[B][:, 0:128], Sb[:, sk, 0:128], qsT, False, True, ["Sb%d" % sk, ("tb", kb)], [("ps", B)])
                    mm(ps[B][:, 128:256], onesB[:, :], AT, True, False, ["onesB", ("tb", kb)], [("ps", B)])
                    mm(ps[B][:, 128:256], Sb[:, sk, 128:256], qsT, False, True, ["Sb%d" % sk, ("tb", kb)],
                       [("ps", B)])
                    kc_ = newtf()
                    act(tf[:, kc_, 0:128], ps[B][:, 128:256], AF.Abs, [("ps", B)], [("tf", kc_)])
                    ts(tf[:, kc_, 0:128], tf[:, kc_, 0:128], 1.0, None, ALU.max, None, [("tf", kc_)], [("tf", kc_)])
                    act(tf[:, kc_, 0:128], tf[:, kc_, 0:128], AF.Ln, [("tf", kc_)], [("tf", kc_)])
                    act(tf[:, kc_, 0:128], tf[:, kc_, 0:128], AF.Exp, [("tf", kc_)], [("tf", kc_)], scale=-1.0)
                    tt(tf[:, kc_, 128:256], ps[B][:, 0:128], tf[:, kc_, 0:128], ALU.mult,
                       [("ps", B), ("tf", kc_)], [("tf", kc_)])
                    act(tbf[:, kb, 384:512], tf[:, kc_, 128:256], AF.Square, [("tf", kc_)], [("tb", kb)])
                    mm(ps[B][:, 256:384], onesMHb, tbf[:, kb, 384:512], True, True, ["cstB", ("tb", kb)], [("ps", B)])
                    act(tf[:, kc_, 384:512], ps[B][:, 256:384], AF.Ln, [("ps", B), "par"], [("tf", kc_)],
                        bias=pcol(o_eps))
                    act(tf[:, kc_, 384:512], tf[:, kc_, 384:512], AF.Exp, [("tf", kc_)], [("tf", kc_)], scale=-0.5)
                    stt(tf[:, kc_, 0:128], tf[:, kc_, 128:256], pcol(o_mh + l * 4 + h), tf[:, kc_, 384:512],
                        ALU.mult, ALU.mult, [("tf", kc_), "par"], [("tf", kc_)])
                    tt(arena[:, 28 + h, tsl], tf[:, kc_, 0:128], arena[:, 20 + h, tsl], ALU.mult,
                       [("tf", kc_), pg(20 + h)], [pg(28 + h)])
                    C = newps()
                    mm(ps[C][:, 0:128], kT, identB[:, :], True, True, [pg(8 + h), "identB"], [("ps", C)])
                    act(tbf[:, kb, 256:384], ps[C][:, 0:128], AF.Copy, [("ps", C), ("wsm", wk)], [("tb", kb)],
                        scale=wsm[:, wk, 0:1])
                    kw_ = tbf[:, kb, 256:384]
                    mm(ps[C][:, 128:256], kw_, vt, True, True, [("tb", kb), pg(12 + i)], [("ps", C)])
                    mm(ps[C][:, 256:384], kw_, onesB[:, :], True, True, [("tb", kb), "onesB"], [("ps", C)])
                    stt(S[:, sk, :], S[:, sk, :], wsm[:, wk, 1:2], ps[C][:, 128:384], ALU.mult, ALU.add,
                        ["S%d" % sk, ("wsm", wk), ("ps", C)], ["S%d" % sk])
                    sch.op("scalar", lambda e, sk=sk: e.copy(Sb[:, sk, :], S[:, sk, :]), ["S%d" % sk], ["Sb%d" % sk])

            gidx = {0: 8, 1: 9, 2: 11, 3: 12}
            spp = None
            for t in range(4):
                if t % 2 == 0:
                    spp = next_w(l, 7 if t == 0 else 10)
                sg = next_w(l, gidx[t])
                for cc in range(2):
                    c = 2 * t + cc
                    GA = newps()
                    for kc in range(8):
                        mm(ps[GA][:, :], wview(sg, kc, cc * 128, 128), xnT[:, kc, :], kc == 0, kc == 7,
                           [("w", sg), ("xn", kc)], [("ps", GA)])
                    GM = newps()
                    for kc in range(8):
                        mm(ps[GM][:, :], wview(sg, kc, 256 + cc * 128, 128), xnT[:, kc, :], kc == 0, kc == 7,
                           [("w", sg), ("xn", kc)], [("ps", GM)])
                    pbase = (t % 2) * 2048 + cc * 128
                    PA = newps()
                    for kc in range(4):
                        o_ = pbase + kc * 256
                        mm(ps[PA][:, :], wb[:, spp, o_:o_ + 128], arena[:, 24 + kc, :], kc == 0, kc == 3,
                           [("w", spp), pg(24 + kc)], [("ps", PA)])
                    PM = newps()
                    for kc in range(4):
                        o_ = pbase + 1024 + kc * 256
                        mm(ps[PM][:, :], wb[:, spp, o_:o_ + 128], arena[:, 28 + kc, :], kc == 0, kc == 3,
                           [("w", spp), pg(28 + kc)], [("ps", PM)])
                    k1 = newtf()
                    k2 = newtf()
                    act(tf[:, k1, :], ps[GA][:, :], AF.Sigmoid, [("ps", GA)], [("tf", k1)])
                    act(tf[:, k2, :], ps[GM][:, :], AF.Sigmoid, [("ps", GM)], [("tf", k2)])
                    tt(tf[:, k1, :], ps[PA][:, :], tf[:, k1, :], ALU.mult, [("ps", PA), ("tf", k1)], [("tf", k1)])
                    tt(tf[:, k2, :], ps[PM][:, :], tf[:, k2, :], ALU.mult, [("ps", PM), ("tf", k2)], [("tf", k2)])
                    tt(arena[:, 4 + c, :], tf[:, k1, :], tf[:, k2, :], ALU.add, [("tf", k1), ("tf", k2)], [pg(4 + c)])
            for t in range(2):
                so = next_w(l, 13 + t)
                for m in range(4):
                    c = 4 * t + m
                    P = newps()
                    for kc in range(8):
                        mm(ps[P][:, :], wview(so, kc, m * 128, 128), arena[:, 4 + kc, :], kc == 0, kc == 7,
                           [("w", so), pg(4 + kc)], [("ps", P)])
                    tt(xT[:, c, :], xT[:, c, :], ps[P][:, :], ALU.add, [("xT", c), ("ps", P)], [("xT", c)])
            rmsnorm(o_g2 + l * 8, True)
            for t in range(8):
                su = next_w(l, 15 + t)
                for m in range(4):
                    f = 4 * t + m
                    P = newps()
                    for kc in range(8):
                        mm(ps[P][:, :], wview(su, kc, m * 128, 128), xnT[:, kc, :], kc == 0, kc == 7,
                           [("w", su), ("xn", kc)], [("ps", P)])
                    k1 = newtf()
                    act(tf[:, k1, :], ps[P][:, :], AF.Relu, [("ps", P)], [("tf", k1)])
                    tt(arena[:, f, :], ps[P][:, :], tf[:, k1, :], ALU.mult, [("ps", P), ("tf", k1)], [pg(f)])
            for c in range(8):
                sd = next_w(l, 23 + c)
                P = newps()
                for f in range(32):
                    mm(ps[P][:, :], wview(sd, f, 0, 128, kcn=32), arena[:, f, :], f == 0, f == 31,
                       [("w", sd), pg(f)], [("ps", P)])
                tt(xT[:, c, :], xT[:, c, :], ps[P][:, :], ALU.add, [("xT", c), ("ps", P)], [("xT", c)])

        def load_x(jb):
            src = x_d[jb * TB:(jb + 1) * TB, :].rearrange("(i p) d -> p i d", p=128)
            sch.op("sync", lambda e, src=src: e.dma_start(out=xin[:, :, :], in_=src), [],
                   [("xin", i) for i in range(4)], dma_sem="io_in", cost=8000.0)
            for c in range(8):
                P = newps()
                for i in range(4):
                    tr(ps[P][:, i * 128:(i + 1) * 128], xin[:, i, c * 128:(c + 1) * 128], [("xin", i)], [("ps", P)])
                copy_any(xT[:, c, :], ps[P][:, :], [("ps", P)], [("xT", c)])

        def store_out(jb, srcT, srcname):
            for i in range(4):
                for half in range(2):
                    P = newps()
                    for cc in range(4):
                        c = half * 4 + cc
                        tr(ps[P][:, cc * 128:(cc + 1) * 128], srcT[:, c, i * 128:(i + 1) * 128], [(srcname, c)],
                           [("ps", P)])
                    copy_any(io[:, i, half * 512:(half + 1) * 512], ps[P][:, :], [("ps", P)], [("io", i)])
            dst = out_d[jb * TB:(jb + 1) * TB, :].rearrange("(i p) d -> p i d", p=128)
            sch.op("sync", lambda e, dst=dst: e.dma_start(out=dst, in_=io[:, :, :]),
                   [("io", i) for i in range(4)], [], dma_sem="io_out", cost=8000.0)

        if not pipe:
            for j in range(nblk):
                load_x(j)
                for l in range(L):
                    layer(l, j)
                rmsnorm(o_gf, False)
                store_out(j, xT, "xT")
        else:
            for st in range(nstep):
                load_x(min(st, nblk - 1))
                if st >= 1:
                    for hf in range(NH):
                        rsrc = recv_d[st - 1][hf][0:128, :].rearrange("p (c t) -> p c t", c=CPP)
                        sch.op("sync", lambda e, rsrc=rsrc, hf=hf: e.dma_start(out=rbuf[:, CPP * hf:CPP * hf + CPP, :], in_=rsrc),
                               [("recv", st - 1, hf)], [("rb", c) for c in range(CPP * hf, CPP * hf + CPP)], dma_sem=("rcv", hf),
                               cost=2000.0 + 750.0 * CPP)
                        for c in range(CPP * hf, CPP * hf + CPP):
                            stt(xT[:, c, :], rbuf[:, c, :], pcol(o_fb), xT[:, c, :], ALU.mult, ALU.add,
                                [("rb", c), ("xT", c), "par"], [("xT", c)])
                layer(0, st)
                if st < nblk:
                    for hf in range(NH):
                        sdst = send_d[st][hf].rearrange("p (c t) -> p c t", c=CPP)
                        sch.op("sync", lambda e, sdst=sdst, hf=hf: e.dma_start(out=sdst, in_=xT[:, CPP * hf:CPP * hf + CPP, :]),
                               [("xT", c) for c in range(CPP * hf, CPP * hf + CPP)], [("send", st, hf)], dma_sem=("snd", hf),
                               cost=2000.0 + 750.0 * CPP)
                        sch.op("gpsimd", lambda e, st=st, hf=hf: e.collective_compute(
                            "AllGather", ALU.bypass, replica_groups=[[0, 1], [2, 3], [4, 5], [6, 7]],
                            ins=[send_d[st][hf]], outs=[recv_d[st][hf]]),
                            [("send", st, hf)], [("recv", st, hf), "cc_serial"], dma_sem=("cc", NH * st + hf),
                            cost=3000.0)
                if st >= 1:
                    kr = norm_stats()
                    for c in range(8):
                        stt(rbuf[:, c, :], xT[:, c, :], pcol(o_gf + c), tf[:, kr, :], ALU.mult, ALU.mult,
                            [("xT", c), ("tf", kr), "par"], [("rb", c)])
                    store_out(st - 1, rbuf, "rb")

        sch.finalize()
        sch.emit(nc, es, dma_sems, ["io_out"])
    return nc


def _fm(W, cols):
    sub = W[:, cols]
    n = sub.shape[1]
    return np.ascontiguousarray(sub.reshape(8, 128, n).transpose(1, 0, 2).reshape(128, 8 * n))


def host_constants():
    s = np.arange(128)
    ident = np.eye(128, dtype=np.float32)
    onesMD = np.full((128, 128), 1.0 / 1024.0, np.float32)
    onesMH = np.full((128, 128), 1.0 / 128.0, np.float32)
    U = (s[:, None] <= s[None, :]).astype(np.float32)
    NEGm = np.where(s[:, None] <= s[None, :], 0.0, NEG).astype(np.float32)
    cst = np.concatenate([ident, onesMD, onesMH, U, NEGm], axis=1)
    kp = np.arange(128)[:, None]
    u = np.arange(640)[None, :]
    dq = u // 64 - (kp >= 64)
    negm = np.where((dq >= 0) & (dq <= 8), 0.0, NEG).astype(np.float32)
    return np.ascontiguousarray(cst), np.ascontiguousarray(negm)


def host_prep(inp, depth=DEPTH, layers=None, pipe_role=None, nblk=SEQ // TB):
    if layers is None:
        layers = list(range(depth))
    L = len(layers)
    sel = np.asarray(layers)
    inp = dict(inp)
    for k_ in ("mix_norm_g", "w_in", "conv_w", "conv_b", "b_igate", "b_fgate", "rel_bias", "mh_norm_g",
               "w_att_proj", "w_mlstm_proj", "w_out", "ffn_norm_g", "w_up", "w_down"):
        inp[k_] = np.asarray(inp[k_], np.float32)[sel]
    w_in = inp["w_in"]
    tiles = []
    wifs = []
    for l in range(L):
        for g in range(7):
            tiles.append(_fm(w_in[l], np.arange(512 * g, 512 * g + 512)))
        w_att = np.asarray(inp["w_att_proj"][l], np.float32)
        w_ml = np.asarray(inp["w_mlstm_proj"][l], np.float32)

        def pp_tile(u):
            parts = []
            for tt_ in range(2):
                t = 2 * u + tt_
                for w in (w_att, w_ml):
                    sub = w[:, t * 256:(t + 1) * 256]
                    parts.append(sub.reshape(4, 128, 256).transpose(1, 0, 2).reshape(128, 1024))
            return np.ascontiguousarray(np.concatenate(parts, axis=1))

        def gagm_tile(t):
            cols = np.concatenate([GA0 + 256 * t + np.arange(256), GM0 + 256 * t + np.arange(256)])
            return _fm(w_in[l], cols)

        tiles.append(pp_tile(0))
        tiles.append(gagm_tile(0))
        tiles.append(gagm_tile(1))
        tiles.append(pp_tile(1))
        tiles.append(gagm_tile(2))
        tiles.append(gagm_tile(3))
        w_out = np.asarray(inp["w_out"][l], np.float32)
        for t in range(2):
            tiles.append(_fm(w_out, np.arange(512 * t, 512 * t + 512)))
        w_up = np.asarray(inp["w_up"][l], np.float32)
        for t in range(8):
            tiles.append(_fm(w_up, np.arange(512 * t, 512 * t + 512)))
        w_down = np.asarray(inp["w_down"][l], np.float32)
        for c in range(8):
            sub = w_down[:, c * 128:(c + 1) * 128]
            tiles.append(np.ascontiguousarray(sub.reshape(32, 128, 128).transpose(1, 0, 2).reshape(128, 4096)))
        wifs.append(_fm(w_in[l], np.arange(3584, 3592)))
    wstream = np.stack(tiles, axis=0)
    assert wstream.shape == (L * NW_TILES, 128, 4096)
    wif = np.concatenate(wifs, axis=1)

    kp = np.arange(128)[:, None]
    u = np.arange(640)[None, :]
    rel_idx = np.clip(u - kp, -256, 256) + 256
    rb = np.asarray(inp["rel_bias"], np.float32)[:L]
    bg = rb[:, :, rel_idx]
    biasg = np.ascontiguousarray(bg.transpose(2, 0, 1, 3).reshape(128, L * 8 * 640))

    def fmv(v):
        v = np.asarray(v, np.float32)
        k = v.shape[-1] // 128
        return np.moveaxis(v.reshape(v.shape[:-1] + (k, 128)), -1, 0)

    g1 = fmv(inp["mix_norm_g"][:L]).reshape(128, L * 8)
    g2 = fmv(inp["ffn_norm_g"][:L]).reshape(128, L * 8)
    gf = fmv(inp["final_norm_g"]).reshape(128, 8)
    cw = fmv(inp["conv_w"][:L]).reshape(128, L * 32)
    cb = fmv(inp["conv_b"][:L]).reshape(128, L * 8)
    mh = fmv(inp["mh_norm_g"][:L]).reshape(128, L * 4)
    bi = np.broadcast_to(np.asarray(inp["b_igate"], np.float32)[:L][None, :, None, :], (128, L, 4, 4)).reshape(128, L * 16)
    bf = np.broadcast_to(np.asarray(inp["b_fgate"], np.float32)[:L][None, :, None, :], (128, L, 4, 4)).reshape(128, L * 16)
    epsc = np.full((128, 1), EPS, np.float32)
    plist = [g1, g2, gf, cw, cb, mh, bi, bf, epsc]
    if pipe_role is not None:
        nstep = nblk + 1
        fb = np.full((128, 1), float(pipe_role), np.float32)
        smask = np.zeros((128, nstep), np.float32)
        smask[:, :pipe_role + 1] = NEG
        valid = np.ones((128, nstep), np.float32)
        valid[:, :pipe_role] = 0.0
        plist += [fb, smask, valid]
    params = np.ascontiguousarray(np.concatenate(plist, axis=1).astype(np.float32))
    cst, negm = host_constants()
    return {"wstream": wstream, "wif": np.ascontiguousarray(wif), "biasg": biasg, "cst": cst, "negm": negm,
            "params": params}


_NC_CACHE = {}


PIPE = True


def kernel(**inputs):
    x = np.asarray(inputs["x"], np.float32)
    B, S_, D = x.shape
    nblk = S_ // TB
    n_cores = 8
    key = (nblk, DEPTH, PIPE)
    if key not in _NC_CACHE:
        _NC_CACHE[key] = build_program(nblk, DEPTH, pipe=PIPE)
    nc = _NC_CACHE[key]
    in_maps = []
    if PIPE:
        assert DEPTH == 2 and B * 2 == n_cores
        roles = [host_prep(inputs, DEPTH, layers=[r], pipe_role=r, nblk=nblk) for r in range(2)]
        zeros = np.zeros((S_, D), np.float32)
        for c in range(n_cores):
            m = dict(roles[c % 2])
            m["x"] = np.ascontiguousarray(x[c // 2]) if c % 2 == 0 else zeros
            in_maps.append(m)
        res = run_bass_kernel_spmd(nc, in_maps, core_ids=list(range(n_cores)))
        out = np.stack([np.asarray(res.results[2 * b + 1]["out"], np.float32) for b in range(B)], axis=0)
        return out
    shared = host_prep(inputs, DEPTH)
    for c in range(n_cores):
        m = dict(shared)
        m["x"] = np.ascontiguousarray(x[c % B])
        in_maps.append(m)
    res = run_bass_kernel_spmd(nc, in_maps, core_ids=list(range(n_cores)))
    out = np.stack([np.asarray(res.results[b]["out"], np.float32) for b in range(B)], axis=0)
    return out
```

```python
import math
from contextlib import ExitStack

import numpy as np
import concourse.bass as bass
import concourse.mybir as mybir
from concourse.bass_utils import run_bass_kernel_spmd

F32 = mybir.dt.float32
BF16 = mybir.dt.bfloat16
AF = mybir.ActivationFunctionType
ALU = mybir.AluOpType

D_MODEL = 1024
BATCH = 4
SEQ = 4096
DEPTH = 2
TB = 512
NW_TILES = 31
NWS = 4
NTF = 8
NTB = 4
EPS = 1e-6
NEG = -30000.0
GA0 = 3592
GM0 = 4616
EPOCH = 8000
SAME_ENGINE_SYNC = "raw"
LIST_SCHED = True
ATT_MUL_ENG = "vector"
NH = 4
XENG_LAT = 200.0
import os as _os
KNOP = set(int(v) for v in _os.environ.get('KNOP', '').split(',') if v)


def _is_cc(key):
    return isinstance(key, tuple) and key[0] == "cc"


class _Op:
    __slots__ = ("eng", "fn", "deps", "dma_sem", "token", "needs_inc", "idx", "raw", "cost", "alldeps")


class Sched:
    def __init__(self):
        self.ops = []
        self.lastw = {}
        self.readers = {}

    def op(self, eng, fn, reads=(), writes=(), dma_sem=None, cost=300.0):
        o = _Op()
        o.cost = cost
        o.eng = eng
        o.fn = fn
        o.dma_sem = dma_sem
        o.idx = len(self.ops)
        o.needs_inc = dma_sem is not None
        o.token = None
        deps = set()
        raw = set()
        for r in reads:
            if r in self.lastw:
                deps.add(self.lastw[r])
                raw.add(self.lastw[r])
        o.raw = raw
        for w in writes:
            if w in self.lastw:
                deps.add(self.lastw[w])
            for rd in self.readers.get(w, ()):
                deps.add(rd)
        deps.discard(o.idx)
        o.deps = deps
        for r in reads:
            self.readers.setdefault(r, set()).add(o.idx)
        for w in writes:
            self.lastw[w] = o.idx
            self.readers[w] = set()
        self.ops.append(o)
        return o

    def finalize(self):
        import os
        if os.environ.get("KTRUNC"):
            self.ops = self.ops[:int(os.environ["KTRUNC"])]
        if os.environ.get("KSKIP"):
            a, b = [int(v) for v in os.environ["KSKIP"].split(",")]
            kept = [o for o in self.ops if not (a <= o.idx < b)]
            remap = {o.idx: i for i, o in enumerate(kept)}
            for o in kept:
                o.deps = set(remap[d] for d in o.deps if d in remap)
                o.raw = set(remap[d] for d in o.raw if d in remap)
                o.idx = remap[o.idx]
            self.ops = kept
        ops = self.ops
        for o in ops:
            o.alldeps = set(o.deps)
        self.order = self.list_schedule() if LIST_SCHED else None
        for o in ops:
            keep = set()
            for d in o.deps:
                do = ops[d]
                if do.eng == o.eng and do.dma_sem is None and o.dma_sem is None:
                    if o.eng == "tensor" or not SAME_ENGINE_SYNC:
                        continue
                    if SAME_ENGINE_SYNC == "raw" and d not in o.raw:
                        continue
                keep.add(d)
            o.deps = keep
            for d in keep:
                ops[d].needs_inc = True

    def list_schedule(self):
        import heapq
        ops = self.ops
        n = len(ops)
        ndep = [len(o.alldeps) for o in ops]
        users = [[] for _ in range(n)]
        for o in ops:
            for d in o.alldeps:
                users[d].append(o.idx)
        finish = [0.0] * n
        ready_t = [0.0] * n
        engs = ["tensor", "vector", "scalar", "gpsimd", "sync"]
        pend = {e: [] for e in engs}
        avail = {e: [] for e in engs}
        t_free = {e: 0.0 for e in engs}
        for o in ops:
            if ndep[o.idx] == 0:
                heapq.heappush(pend[o.eng], (0.0, o.idx))
        order = []
        done = 0
        while done < n:
            best = None
            for e in engs:
                while pend[e] and pend[e][0][0] <= t_free[e]:
                    rt, i = heapq.heappop(pend[e])
                    heapq.heappush(avail[e], i)
                if avail[e]:
                    cand = (t_free[e], avail[e][0], e, True)
                elif pend[e]:
                    cand = (pend[e][0][0], pend[e][0][1], e, False)
                else:
                    continue
                if best is None or cand[:2] < best[:2]:
                    best = cand
            st, i, e, from_avail = best
            if from_avail:
                heapq.heappop(avail[e])
            else:
                heapq.heappop(pend[e])
            o = ops[i]
            fin = st + o.cost
            t_free[e] = fin
            finish[i] = fin
            order.append(i)
            done += 1
            for u in users[i]:
                lat = 0.0 if ops[u].eng == e else XENG_LAT
                if fin + lat > ready_t[u]:
                    ready_t[u] = fin + lat
                ndep[u] -= 1
                if ndep[u] == 0:
                    heapq.heappush(pend[ops[u].eng], (ready_t[u], u))
        self.est_total = max(finish) if finish else 0.0
        return order

    def emit(self, nc, es, dma_sems, final_waits):
        ops = self.ops
        seq = [ops[i] for i in self.order] if self.order is not None else list(ops)
        eng_sems = {}
        counters = {}
        dma_counts = {}
        for o in seq:
            if o.dma_sem is not None:
                c = dma_counts.get(o.dma_sem, 0) + (1 if _is_cc(o.dma_sem) else 16)
                dma_counts[o.dma_sem] = c
                o.token = (dma_sems[o.dma_sem], c)
            elif o.needs_inc:
                ep, c = counters.get(o.eng, (0, 0))
                if c >= EPOCH:
                    ep, c = ep + 1, 0
                c += 1
                counters[o.eng] = (ep, c)
                key = (o.eng, ep)
                if key not in eng_sems:
                    eng_sems[key] = es.enter_context(nc.semaphore("p_%s_%d" % (o.eng, ep)))
                o.token = (eng_sems[key], c)
        for o in ops:
            if o.dma_sem in ("setup", "setup_sw"):
                o.token = (dma_sems[o.dma_sem], dma_counts[o.dma_sem])
        by_eng = {}
        for o in seq:
            by_eng.setdefault(o.eng, []).append(o)

        def run(eng_name, e):
            waited = {}
            for o in by_eng.get(eng_name, []):
                need = {}
                for d in o.deps:
                    s, v = ops[d].token
                    k = id(s)
                    if waited.get(k, 0) >= v:
                        continue
                    if k not in need or need[k][1] < v:
                        need[k] = (s, v)
                for k, (s, v) in need.items():
                    e.wait_ge(s, v)
                    waited[k] = v
                if o.idx in KNOP:
                    ins = e.nop()
                else:
                    ins = o.fn(e)
                if o.dma_sem is not None:
                    if _is_cc(o.dma_sem):
                        ins.then_inc(o.token[0])
                    else:
                        ins.then_inc(o.token[0], 16)
                elif o.needs_inc:
                    ins.then_inc(o.token[0], 1)
            mine = []
            for o in by_eng.get(eng_name, []):
                if o.dma_sem is not None and o.dma_sem not in mine:
                    mine.append(o.dma_sem)
            for key in mine:
                e.wait_ge(dma_sems[key], dma_counts[key])

        with nc.Block() as block:
            @block.sync
            def _(e):
                run("sync", e)

            @block.gpsimd
            def _(e):
                run("gpsimd", e)

            @block.tensor
            def _(e):
                run("tensor", e)

            @block.vector
            def _(e):
                run("vector", e)

            @block.scalar
            def _(e):
                run("scalar", e)


def build_program(nblk, depth=DEPTH, pipe=False):
    T = nblk * TB
    nc = bass.Bass("TRN2", target_bir_lowering=False)
    L = 1 if pipe else depth
    nstep = nblk + 1 if pipe else nblk
    if pipe:
        x_d = nc.dram_tensor("x", [D_MODEL, T], F32, kind="ExternalInput").ap()
    else:
        x_d = nc.dram_tensor("x", [T, D_MODEL], F32, kind="ExternalInput").ap()
    w_d = nc.dram_tensor("wstream", [L * NW_TILES, 128, 4096], F32, kind="ExternalInput").ap()
    wif_d = nc.dram_tensor("wif", [128, L * 64], F32, kind="ExternalInput").ap()
    bias_d = nc.dram_tensor("biasg", [128, L * 8 * 640], F32, kind="ExternalInput").ap()
    cst_d = nc.dram_tensor("cst", [128, 5 * 128], F32, kind="ExternalInput").ap()
    negm_d = nc.dram_tensor("negm", [128, 640], F32, kind="ExternalInput").ap()
    NPAR = L * 8 + L * 8 + 8 + L * 32 + L * 8 + L * 4 + L * 16 + L * 16 + 1 + (1 + 2 * (nblk + 1) if pipe else 0)
    par_d = nc.dram_tensor("params", [128, NPAR], F32, kind="ExternalInput").ap()
    if pipe:
        out_d = nc.dram_tensor("out", [D_MODEL, T], F32, kind="ExternalOutput").ap()
    else:
        out_d = nc.dram_tensor("out", [T, D_MODEL], F32, kind="ExternalOutput").ap()

    es = ExitStack()
    with es:
        def sb(name, shape, dt):
            return es.enter_context(nc.sbuf_tensor(name, shape, dt))

        if not pipe:
            io = sb("io", [128, 4, 1024], F32)
        if pipe:
            xl = sb("xl", [128, 8, TB], F32)
            rbuf = sb("rbuf", [128, 8, TB], F32)
            CPP = 8 // NH
            send_d = [[nc.dram_tensor("send%d_%d" % (k, hf), [128, CPP * TB], F32, kind="Internal").ap()
                       for hf in range(NH)] for k in range(nblk)]
            recv_d = [[nc.dram_tensor("recv%d_%d" % (k, hf), [256, CPP * TB], F32, kind="Internal").ap()
                       for hf in range(NH)] for k in range(nblk)]
        else:
            xin = io
        xT = sb("xT", [128, 8, TB], F32)
        xnT = sb("xnT", [128, 8, TB], BF16)
        arena = sb("arena", [128, 32, TB], BF16)
        kh = sb("kh", [128, L * 2 * 4, TB], BF16)
        vh = sb("vh", [128, L * 2 * 4, TB], BF16)
        wb = sb("wb", [128, NWS, 4096], BF16)
        wif = sb("wifs", [128, L * 64], BF16)
        BM = sb("BM", [128, L * 8, 640], BF16)
        negm = sb("negms", [128, 640], BF16)
        S = sb("S", [128, L * 4, 256], F32)
        Sb = sb("Sb", [128, L * 4, 256], BF16)
        halo = sb("halo", [128, L * 8, 4], F32)
        cst = sb("csts", [128, 5, 128], F32)
        identB = sb("identB", [128, 128], BF16)
        cstB = sb("cstB", [128, 3, 128], BF16)
        onesB = sb("onesB", [128, 128], BF16)
        onesF = sb("onesF", [128, 128], F32)
        par = sb("pars", [128, NPAR], F32)
        tf = sb("tf", [128, NTF, 512], F32)
        tbf = sb("tbf", [128, NTB, 512], BF16)
        pre = sb("pre", [128, 2, 516], F32)
        LF = sb("LF", [128, 2, 4, 128], F32)
        gsm = sb("gsm", [128, 8, 16], F32)
        wsm = sb("wsm", [128, 8, 2], F32)
        ps = [es.enter_context(nc.psum_tensor("ps%d" % b, [128, 512], F32)) for b in range(8)]

        identF = cst[:, 0, :]
        onesMD = cst[:, 1, :]
        onesMH = cst[:, 2, :]
        Umat = cst[:, 3, :]
        NEGmat = cst[:, 4, :]
        onesMDb = cstB[:, 0, :]
        onesMHb = cstB[:, 1, :]
        NEGb = cstB[:, 2, :]

        o_g1 = 0
        o_g2 = o_g1 + L * 8
        o_gf = o_g2 + L * 8
        o_cw = o_gf + 8
        o_cb = o_cw + L * 32
        o_mh = o_cb + L * 8
        o_bi = o_mh + L * 4
        o_bf = o_bi + L * 16
        o_eps = o_bf + L * 16
        o_fb = o_eps + 1
        o_sm = o_fb + 1
        o_vd = o_sm + nstep

        def pcol(off):
            return par[:, off:off + 1]

        dma_sems = {}
        extra_sems = ([("snd", k) for k in range(NH)] + [("rcv", k) for k in range(NH)]
                      + [("cc", k) for k in range(NH * nblk)]) if pipe else []
        for k in ["setup", "setup_sw", "io_in", "io_out"] + extra_sems + [("w", s) for s in range(NWS)]:
            dma_sems[k] = es.enter_context(nc.semaphore("d_%s" % (str(k).replace(" ", ""))))

        sch = Sched()
        state = {"ps": 0, "tf": 0, "tb": 0, "w": 0, "alt": 0, "pre": 0, "lf": 0, "wsm": 0}

        def newps(exclude=()):
            b = state["ps"]
            while b in exclude:
                b = (b + 1) % 8
            state["ps"] = (b + 1) % 8
            return b

        def newtf():
            k = state["tf"]
            state["tf"] = (k + 1) % NTF
            return k

        def newtb():
            k = state["tb"]
            state["tb"] = (k + 1) % NTB
            return k

        def fsz(ap):
            n = 1
            for d in ap.shape[1:]:
                n *= int(d)
            return n

        def mm(out, lhsT, rhs, start, stop, r, w):
            c = max(fsz(rhs), 64) / 1.95 + 15.0
            if lhsT.dtype == F32:
                c *= 4.0
            sch.op("tensor", lambda e: e.matmul(out, lhsT, rhs, start=start, stop=stop), r, w, cost=c)

        def tr(out, in_, r, w):
            sch.op("tensor", lambda e: e.transpose(out, in_, identF), list(r) + ["cst"], w, cost=280.0)

        def act(out, in_, func, r, w, bias=None, scale=None):
            kw = {}
            if bias is not None:
                kw["bias"] = bias
            if scale is not None:
                kw["scale"] = scale
            c = fsz(out) / 1.4 + 220.0 + (90.0 if bias is not None and not isinstance(bias, float) else 0.0)
            sch.op("scalar", lambda e: e.activation(out, in_, func, **kw), r, w, cost=c)

        def dcost(ap, mult=1.0):
            return max(fsz(ap), 64) * mult / 0.96 + 90.0

        def tt(out, in0, in1, op, r, w, eng="vector"):
            sch.op(eng, lambda e: e.tensor_tensor(out, in0, in1, op), r, w, cost=dcost(out))

        def ts(out, in0, s1, s2, op0, op1, r, w, eng="vector"):
            if op1 is None:
                sch.op(eng, lambda e: e.tensor_scalar(out, in0, s1, None, op0), r, w, cost=dcost(out))
            else:
                sch.op(eng, lambda e: e.tensor_scalar(out, in0, s1, s2, op0, op1), r, w, cost=dcost(out))

        def stt(out, in0, scalar, in1, op0, op1, r, w):
            sch.op("vector", lambda e: e.scalar_tensor_tensor(out, in0, scalar, in1, op0, op1), r, w,
                   cost=dcost(out))

        def recip(out, in_, r, w):
            sch.op("vector", lambda e: e.reciprocal(out, in_), r, w, cost=dcost(out, 8.0))

        def copy_any(out, in_, r, w):
            state["alt"] ^= 1
            if state["alt"]:
                sch.op("scalar", lambda e: e.copy(out, in_), r, w, cost=fsz(out) / 1.4 + 220.0)
            else:
                sch.op("vector", lambda e: e.tensor_copy(out, in_), r, w, cost=dcost(out))

        def dsetup(eng, out, in_, w):
            sch.op(eng, lambda e: e.dma_start(out=out, in_=in_), [], w,
                   dma_sem="setup_sw" if eng == "gpsimd" else "setup")

        dsetup("sync", cst[:, :, :], cst_d.rearrange("p (k n) -> p k n", k=5), ["cst"])
        dsetup("sync", par[:, :], par_d, ["par"])
        dsetup("gpsimd", wif[:, :], wif_d, ["wif"])
        dsetup("gpsimd", negm[:, :], negm_d, ["negm"])
        for l in range(L):
            for h in range(8):
                dsetup("gpsimd", BM[:, l * 8 + h, :],
                       bias_d[:, (l * 8 + h) * 640:(l * 8 + h + 1) * 640], [("BM", l * 8 + h)])
        sch.op("vector", lambda e: e.memset(S[:, :, :], 0.0), [], ["S%d" % k for k in range(L * 4)])
        sch.op("vector", lambda e: e.memset(Sb[:, :, :], 0.0), [], ["Sb%d" % k for k in range(L * 4)])
        sch.op("vector", lambda e: e.memset(halo[:, :, :], 0.0), [], [("halo", k) for k in range(L * 8)])
        sch.op("vector", lambda e: e.memset(kh[:, :, :], 0.0), [], [("kh", k) for k in range(L * 8)])
        sch.op("vector", lambda e: e.memset(vh[:, :, :], 0.0), [], [("vh", k) for k in range(L * 8)])
        sch.op("vector", lambda e: e.memset(onesB[:, :], 1.0), [], ["onesB"])
        sch.op("vector", lambda e: e.memset(onesF[:, :], 1.0), [], ["onesF"])
        sch.op("vector", lambda e: e.tensor_copy(identB[:, :], identF), ["cst"], ["identB"])
        sch.op("vector", lambda e: e.tensor_copy(cstB[:, 0:2, :], cst[:, 1:3, :]), ["cst"], ["cstB"])
        sch.op("vector", lambda e: e.tensor_copy(cstB[:, 2, :], cst[:, 4, :]), ["cst", "cstB"], ["cstB"])
        for l in range(L):
            for h in range(8):
                k = l * 8 + h
                tt(BM[:, k, :], BM[:, k, :], negm[:, :], ALU.add, [("BM", k), "negm"], [("BM", k)])
                if ATT_MUL_ENG:
                    act(BM[:, k, :], BM[:, k, :], AF.Exp, [("BM", k)], [("BM", k)])

        def next_w(l, k_expected):
            n = state["w"]
            state["w"] = n + 1
            assert n % NW_TILES == k_expected and n // NW_TILES % L == l, (n, l, k_expected)
            slot = n % NWS
            src = w_d[(n % (L * NW_TILES)), :, :]
            sch.op("gpsimd", lambda e: e.dma_start(out=wb[:, slot, :], in_=src), [], [("w", slot)],
                   dma_sem=("w", slot), cost=9000.0)
            return slot

        def wview(slot, kc, c0, n, kcn=8):
            width = 4096 // kcn
            return wb[:, slot, kc * width + c0: kc * width + c0 + n]

        def rstd_from(P_ap, kr, rd):
            act(tf[:, kr, :], P_ap, AF.Ln, rd + ["par"], [("tf", kr)], bias=pcol(o_eps))
            act(tf[:, kr, :], tf[:, kr, :], AF.Exp, [("tf", kr)], [("tf", kr)], scale=-0.5)

        def norm_stats():
            P = newps()
            for c in range(8):
                k = newtb()
                act(tbf[:, k, :], xT[:, c, :], AF.Square, [("xT", c)], [("tb", k)])
                mm(ps[P][:, :], onesMDb, tbf[:, k, :], c == 0, c == 7, [("tb", k), "cstB"], [("ps", P)])
            kr = newtf()
            rstd_from(ps[P][:, :], kr, [("ps", P)])
            return kr

        def rmsnorm(gcol0, dst_bf16):
            kr = norm_stats()
            for c in range(8):
                if dst_bf16:
                    stt(xnT[:, c, :], xT[:, c, :], pcol(gcol0 + c), tf[:, kr, :], ALU.mult, ALU.mult,
                        [("xT", c), ("tf", kr), "par"], [("xn", c)])
                else:
                    stt(xT[:, c, :], xT[:, c, :], pcol(gcol0 + c), tf[:, kr, :], ALU.mult, ALU.mult,
                        [("xT", c), ("tf", kr), "par"], [("xT", c)])

        def pg(n):
            return ("pg", n)

        SC_M = 1.0 / math.sqrt(128.0)

        def layer(l, j):
            parity = j % 2
            kcur = lambda p: (l * 2 + parity) * 4 + p
            kprev = lambda p: (l * 2 + 1 - parity) * 4 + p

            rmsnorm(o_g1 + l * 8, True)
            xn_all = [("xn", c) for c in range(8)]

            for g in range(2):
                slot = next_w(l, g)
                for m in range(4):
                    P = newps()
                    for kc in range(8):
                        mm(ps[P][:, :], wview(slot, kc, m * 128, 128), xnT[:, kc, :], kc == 0, kc == 7,
                           [("w", slot), ("xn", kc)], [("ps", P)])
                    if g == 0:
                        copy_any(arena[:, m, :], ps[P][:, :], [("ps", P)], [pg(m)])
                    else:
                        copy_any(kh[:, kcur(m), :], ps[P][:, :], [("ps", P)], [("kh", kcur(m))])
            slot = next_w(l, 2)
            for i in range(4):
                P = newps()
                for kc in range(8):
                    mm(ps[P][:, :], xnT[:, kc, i * 128:(i + 1) * 128], wview(slot, kc, 0, 512), kc == 0, kc == 7,
                       [("w", slot), ("xn", kc)], [("ps", P)])
                copy_any(vh[:, kcur(i), :], ps[P][:, :], [("ps", P)], [("vh", kcur(i))])

            for p in range(4):
                O = newps()
                Dn = newps()
                for e_ in range(2):
                    h = 2 * p + e_
                    rows = slice(e_ * 64, (e_ + 1) * 64)
                    rlist = [4] + [r for r in range(8) if (j > 0 or r >= 4 or pipe) and r != 4]
                    for idx, r in enumerate(rlist):
                        kidx = kprev(p) if r < 4 else kcur(p)
                        vidx = kprev(r % 4) if r < 4 else kcur(r % 4)
                        qlo = max(8, 2 * r) * 64 - 512
                        qhi = (min(15, 2 * r + 9) + 1) * 64 - 512
                        n = qhi - qlo
                        u0 = 512 + qlo - 128 * r
                        sc = newps(exclude=(O, Dn))
                        mm(ps[sc][:, 0:n], kh[rows, kidx, (r % 4) * 128:(r % 4 + 1) * 128], arena[rows, p, qlo:qhi],
                           True, True, [("kh", kidx), pg(p)], [("ps", sc)])
                        k1 = newtf()
                        k2 = newtb()
                        if ATT_MUL_ENG:
                            if pipe and r < 4:
                                act(tf[:, k1, 0:n], ps[sc][:, 0:n], AF.Exp, [("ps", sc), "par"], [("tf", k1)],
                                    bias=pcol(o_sm + j), scale=0.125)
                            else:
                                act(tf[:, k1, 0:n], ps[sc][:, 0:n], AF.Exp, [("ps", sc)], [("tf", k1)], scale=0.125)
                            sch.op(ATT_MUL_ENG, lambda e, k1=k1, k2=k2, n=n, u0=u0, lh=l * 8 + h: e.tensor_tensor(
                                tbf[:, k2, 0:n], tf[:, k1, 0:n], BM[:, lh, u0:u0 + n], ALU.mult),
                                [("tf", k1), ("BM", l * 8 + h)], [("tb", k2)],
                                cost=(n / 0.5 + 500.0) if ATT_MUL_ENG == "gpsimd" else (n / 0.96 + 90.0))
                        else:
                            stt(tf[:, k1, 0:n], ps[sc][:, 0:n], 0.125, BM[:, l * 8 + h, u0:u0 + n], ALU.mult, ALU.add,
                                [("ps", sc), ("BM", l * 8 + h)], [("tf", k1)])
                            if pipe and r < 4:
                                act(tbf[:, k2, 0:n], tf[:, k1, 0:n], AF.Exp, [("tf", k1), "par"], [("tb", k2)],
                                    bias=pcol(o_sm + j))
                            else:
                                act(tbf[:, k2, 0:n], tf[:, k1, 0:n], AF.Exp, [("tf", k1)], [("tb", k2)])
                        first = idx == 0
                        last = idx == len(rlist) - 1
                        mm(ps[O][rows, qlo:qhi], vh[:, vidx, h * 64:(h + 1) * 64], tbf[:, k2, 0:n], first, last,
                           [("vh", vidx), ("tb", k2)], [("ps", O)])
                        mm(ps[Dn][rows, qlo:qhi], onesB[:, 0:64], tbf[:, k2, 0:n], first, last,
                           ["onesB", ("tb", k2)], [("ps", Dn)])
                kr = newtf()
                act(tf[:, kr, :], ps[Dn][:, :], AF.Ln, [("ps", Dn)], [("tf", kr)])
                act(tf[:, kr, :], tf[:, kr, :], AF.Exp, [("tf", kr)], [("tf", kr)], scale=-1.0)
                tt(arena[:, 24 + p, :], ps[O][:, :], tf[:, kr, :], ALU.mult, [("ps", O), ("tf", kr)], [pg(24 + p)])

            for g in (3, 4):
                slot = next_w(l, g)
                for m in range(4):
                    ch = (g - 3) * 4 + m
                    P = newps()
                    for kc in range(8):
                        mm(ps[P][:, :], wview(slot, kc, m * 128, 128), xnT[:, kc, :], kc == 0, kc == 7,
                           [("w", slot), ("xn", kc)], [("ps", P)])
                    pk = state["pre"]
                    state["pre"] ^= 1
                    hk = l * 8 + ch
                    copy_any(pre[:, pk, 3:515], ps[P][:, :], [("ps", P)], [("pre", pk)])
                    sch.op("vector", lambda e, pk=pk, hk=hk: e.tensor_copy(pre[:, pk, 0:3], halo[:, hk, 0:3]),
                           [("halo", hk)], [("pre", pk)])
                    sch.op("vector", lambda e, pk=pk, hk=hk: e.tensor_copy(halo[:, hk, 0:3], pre[:, pk, 512:515]),
                           [("pre", pk)], [("halo", hk)])
                    ka = newtf()
                    cw = lambda tap: pcol(o_cw + l * 32 + tap * 8 + ch)
                    ts(tf[:, ka, :], pre[:, pk, 0:512], cw(0), None, ALU.mult, None,
                       [("pre", pk), "par"], [("tf", ka)])
                    for tap in (1, 2, 3):
                        stt(tf[:, ka, :], pre[:, pk, tap:tap + 512], cw(tap), tf[:, ka, :], ALU.mult, ALU.add,
                            [("pre", pk), "par", ("tf", ka)], [("tf", ka)])
                    act(arena[:, 4 + ch, :], tf[:, ka, :], AF.Silu, [("tf", ka), "par"], [pg(4 + ch)],
                        bias=pcol(o_cb + l * 8 + ch))
            slot = next_w(l, 5)
            for i in range(4):
                P = newps()
                for kc in range(8):
                    mm(ps[P][:, :], xnT[:, kc, i * 128:(i + 1) * 128], wview(slot, kc, 0, 512), kc == 0, kc == 7,
                       [("w", slot), ("xn", kc)], [("ps", P)])
                copy_any(arena[:, 12 + i, :], ps[P][:, :], [("ps", P)], [pg(12 + i)])
            slot = next_w(l, 6)
            for m in range(4):
                P = newps()
                for kc in range(8):
                    mm(ps[P][:, :], wview(slot, kc, m * 128, 128), xnT[:, kc, :], kc == 0, kc == 7,
                       [("w", slot), ("xn", kc)], [("ps", P)])
                act(arena[:, 20 + m, :], ps[P][:, :], AF.Sigmoid, [("ps", P)], [pg(20 + m)])
            G = newps()
            for i in range(4):
                for kc in range(8):
                    mm(ps[G][:, i * 8:(i + 1) * 8], xnT[:, kc, i * 128:(i + 1) * 128],
                       wif[:, l * 64 + kc * 8: l * 64 + kc * 8 + 8], kc == 0, kc == 7,
                       ["wif", ("xn", kc)], [("ps", G)])
            G3 = ps[G][:, 0:32].rearrange("p (i c) -> p i c", c=8)
            ig = gsm[:, 0, :]
            zf = gsm[:, 1, :]
            ef = gsm[:, 2, :]
            spv = gsm[:, 3, :]
            logf = gsm[:, 4, :]
            biasv = gsm[:, 5, :]
            v3 = lambda ap: ap.rearrange("p (i c) -> p i c", c=4)
            tt(v3(ig), G3[:, :, 0:4], v3(par[:, o_bi + l * 16:o_bi + l * 16 + 16]), ALU.add,
               [("ps", G), "par"], ["g_ig"])
            tt(v3(zf), G3[:, :, 4:8], v3(par[:, o_bf + l * 16:o_bf + l * 16 + 16]), ALU.add,
               [("ps", G), "par"], ["g_zf"])
            act(ef, zf, AF.Exp, ["g_zf"], ["g_ef"], scale=-1.0)
            act(spv, ef, AF.Ln, ["g_ef"], ["g_sp"], bias=1.0)
            ts(logf, spv, -1.0, None, ALU.mult, None, ["g_sp"], ["g_lf"])
            Cm = newps()
            mm(ps[Cm][:, 0:16], Umat, logf, True, True, ["cst", "g_lf"], [("ps", Cm)])
            tt(biasv, ig, ps[Cm][:, 0:16], ALU.subtract, ["g_ig", ("ps", Cm)], ["g_bv"])

            for i in range(4):
                lfk = state["lf"]
                state["lf"] ^= 1
                for h in range(4):
                    col = i * 4 + h
                    ts(LF[:, lfk, h, :], onesF[:, :], logf[:, col:col + 1], None, ALU.mult, None,
                       ["onesF", "g_lf"], [("LF", lfk, h)])
                tsl = slice(i * 128, (i + 1) * 128)
                for h in range(4):
                    col = i * 4 + h
                    sk = l * 4 + h
                    kT = arena[:, 8 + h, tsl]
                    qT_ = arena[:, 4 + h, tsl]
                    vt = arena[:, 12 + i, h * 128:(h + 1) * 128]
                    A = newps()
                    mm(ps[A][:, 0:128], kT, qT_, True, True, [pg(8 + h), pg(4 + h)], [("ps", A)])
                    mm(ps[A][:, 128:256], LF[:, lfk, h, :], Umat, True, True, [("LF", lfk, h), "cst"], [("ps", A)])
                    mm(ps[A][:, 256:384], LF[:, lfk, h, :], Umat, True, False, [("LF", lfk, h), "cst"], [("ps", A)])
                    mm(ps[A][:, 256:384], identB[:, :], NEGb, False, True, ["identB", "cstB"], [("ps", A)])
                    ka = newtf()
                    act(tf[:, ka, 0:128], ps[A][:, 256:384], AF.Exp, [("ps", A), "g_bv"], [("tf", ka)],
                        bias=biasv[:, col:col + 1])
                    act(tf[:, ka, 128:256], ps[A][:, 128:256], AF.Exp, [("ps", A)], [("tf", ka)])
                    wk = state["wsm"]
                    state["wsm"] = (wk + 1) % 8
                    if pipe:
                        ts(wsm[:, wk, 0:1], tf[:, ka, 127:128], pcol(o_vd + j), None, ALU.mult, None,
                           [("tf", ka), "par"], [("wsm", wk)])
                    else:
                        sch.op("vector", lambda e, wk=wk, ka=ka: e.tensor_copy(wsm[:, wk, 0:1], tf[:, ka, 127:128]),
                               [("tf", ka)], [("wsm", wk)])
                    sch.op("vector", lambda e, wk=wk, ka=ka: e.tensor_copy(wsm[:, wk, 1:2], tf[:, ka, 255:256]),
                           [("tf", ka)], [("wsm", wk)])
                    kb = newtb()
                    stt(tbf[:, kb, 0:128], ps[A][:, 0:128], SC_M, tf[:, ka, 0:128], ALU.mult, ALU.mult,
                        [("ps", A), ("tf", ka)], [("tb", kb)])
                    stt(tbf[:, kb, 128:256], qT_, SC_M, tf[:, ka, 128:256], ALU.mult, ALU.mult,
                        [pg(4 + h), ("tf", ka)], [("tb", kb)])
                    AT = tbf[:, kb, 0:128]
                    qsT = tbf[:, kb, 128:256]
                    B = newps()
                    mm(ps[B][:, 0:128], vt, AT, True, False, [pg(12 + i), ("tb", kb)], [("ps", B)])
                    mm(ps[B][:, 0:128], Sb[:, sk, 0:128], qsT, False, True, ["Sb%d" % sk, ("tb", kb)], [("ps", B)])
                    mm(ps[B][:, 128:256], onesB[:, :], AT, True, False, ["onesB", ("tb", kb)], [("ps", B)])
                    mm(ps[B][:, 128:256], Sb[:, sk, 128:256], qsT, False, True, ["Sb%d" % sk, ("tb", kb)],
                       [("ps", B)])
                    kc_ = newtf()
                    act(tf[:, kc_, 0:128], ps[B][:, 128:256], AF.Abs, [("ps", B)], [("tf", kc_)])
                    ts(tf[:, kc_, 0:128], tf[:, kc_, 0:128], 1.0, None, ALU.max, None, [("tf", kc_)], [("tf", kc_)])
                    recip(tf[:, kc_, 0:128], tf[:, kc_, 0:128], [("tf", kc_)], [("tf", kc_)])
                    tt(tf[:, kc_, 128:256], ps[B][:, 0:128], tf[:, kc_, 0:128], ALU.mult,
                       [("ps", B), ("tf", kc_)], [("tf", kc_)])
                    act(tbf[:, kb, 384:512], tf[:, kc_, 128:256], AF.Square, [("tf", kc_)], [("tb", kb)])
                    mm(ps[B][:, 256:384], onesMHb, tbf[:, kb, 384:512], True, True, ["cstB", ("tb", kb)], [("ps", B)])
                    act(tf[:, kc_, 384:512], ps[B][:, 256:384], AF.Ln, [("ps", B), "par"], [("tf", kc_)],
                        bias=pcol(o_eps))
                    act(tf[:, kc_, 384:512], tf[:, kc_, 384:512], AF.Exp, [("tf", kc_)], [("tf", kc_)], scale=-0.5)
                    stt(tf[:, kc_, 0:128], tf[:, kc_, 128:256], pcol(o_mh + l * 4 + h), tf[:, kc_, 384:512],
                        ALU.mult, ALU.mult, [("tf", kc_), "par"], [("tf", kc_)])
                    tt(arena[:, 28 + h, tsl], tf[:, kc_, 0:128], arena[:, 20 + h, tsl], ALU.mult,
                       [("tf", kc_), pg(20 + h)], [pg(28 + h)])
                    C = newps()
                    mm(ps[C][:, 0:128], kT, identB[:, :], True, True, [pg(8 + h), "identB"], [("ps", C)])
                    act(tbf[:, kb, 256:384], ps[C][:, 0:128], AF.Copy, [("ps", C), ("wsm", wk)], [("tb", kb)],
                        scale=wsm[:, wk, 0:1])
                    kw_ = tbf[:, kb, 256:384]
                    mm(ps[C][:, 128:256], kw_, vt, True, True, [("tb", kb), pg(12 + i)], [("ps", C)])
                    mm(ps[C][:, 256:384], kw_, onesB[:, :], True, True, [("tb", kb), "onesB"], [("ps", C)])
                    stt(S[:, sk, :], S[:, sk, :], wsm[:, wk, 1:2], ps[C][:, 128:384], ALU.mult, ALU.add,
                        ["S%d" % sk, ("wsm", wk), ("ps", C)], ["S%d" % sk])
                    sch.op("scalar", lambda e, sk=sk: e.copy(Sb[:, sk, :], S[:, sk, :]), ["S%d" % sk], ["Sb%d" % sk])

            gidx = {0: 8, 1: 9, 2: 11, 3: 12}
            spp = None
            for t in range(4):
                if t % 2 == 0:
                    spp = next_w(l, 7 if t == 0 else 10)
                sg = next_w(l, gidx[t])
                for cc in range(2):
                    c = 2 * t + cc
                    GA = newps()
                    for kc in range(8):
                        mm(ps[GA][:, :], wview(sg, kc, cc * 128, 128), xnT[:, kc, :], kc == 0, kc == 7,
                           [("w", sg), ("xn", kc)], [("ps", GA)])
                    GM = newps()
                    for kc in range(8):
                        mm(ps[GM][:, :], wview(sg, kc, 256 + cc * 128, 128), xnT[:, kc, :], kc == 0, kc == 7,
                           [("w", sg), ("xn", kc)], [("ps", GM)])
                    pbase = (t % 2) * 2048 + cc * 128
                    PA = newps()
                    for kc in range(4):
                        o_ = pbase + kc * 256
                        mm(ps[PA][:, :], wb[:, spp, o_:o_ + 128], arena[:, 24 + kc, :], kc == 0, kc == 3,
                           [("w", spp), pg(24 + kc)], [("ps", PA)])
                    PM = newps()
                    for kc in range(4):
                        o_ = pbase + 1024 + kc * 256
                        mm(ps[PM][:, :], wb[:, spp, o_:o_ + 128], arena[:, 28 + kc, :], kc == 0, kc == 3,
                           [("w", spp), pg(28 + kc)], [("ps", PM)])
                    k1 = newtf()
                    k2 = newtf()
                    act(tf[:, k1, :], ps[GA][:, :], AF.Sigmoid, [("ps", GA)], [("tf", k1)])
                    act(tf[:, k2, :], ps[GM][:, :], AF.Sigmoid, [("ps", GM)], [("tf", k2)])
                    tt(tf[:, k1, :], ps[PA][:, :], tf[:, k1, :], ALU.mult, [("ps", PA), ("tf", k1)], [("tf", k1)])
                    tt(tf[:, k2, :], ps[PM][:, :], tf[:, k2, :], ALU.mult, [("ps", PM), ("tf", k2)], [("tf", k2)])
                    tt(arena[:, 4 + c, :], tf[:, k1, :], tf[:, k2, :], ALU.add, [("tf", k1), ("tf", k2)], [pg(4 + c)])
            for t in range(2):
                so = next_w(l, 13 + t)
                for m in range(4):
                    c = 4 * t + m
                    P = newps()
                    for kc in range(8):
                        mm(ps[P][:, :], wview(so, kc, m * 128, 128), arena[:, 4 + kc, :], kc == 0, kc == 7,
                           [("w", so), pg(4 + kc)], [("ps", P)])
                    tt(xT[:, c, :], xT[:, c, :], ps[P][:, :], ALU.add, [("xT", c), ("ps", P)], [("xT", c)])
            rmsnorm(o_g2 + l * 8, True)
            for t in range(8):
                su = next_w(l, 15 + t)
                for m in range(4):
                    f = 4 * t + m
                    P = newps()
                    for kc in range(8):
                        mm(ps[P][:, :], wview(su, kc, m * 128, 128), xnT[:, kc, :], kc == 0, kc == 7,
                           [("w", su), ("xn", kc)], [("ps", P)])
                    k1 = newtf()
                    act(tf[:, k1, :], ps[P][:, :], AF.Relu, [("ps", P)], [("tf", k1)])
                    tt(arena[:, f, :], ps[P][:, :], tf[:, k1, :], ALU.mult, [("ps", P), ("tf", k1)], [pg(f)])
            for c in range(8):
                sd = next_w(l, 23 + c)
                P = newps()
                for f in range(32):
                    mm(ps[P][:, :], wview(sd, f, 0, 128, kcn=32), arena[:, f, :], f == 0, f == 31,
                       [("w", sd), pg(f)], [("ps", P)])
                tt(xT[:, c, :], xT[:, c, :], ps[P][:, :], ALU.add, [("xT", c), ("ps", P)], [("xT", c)])

        def load_x(jb):
            src = x_d[jb * TB:(jb + 1) * TB, :].rearrange("(i p) d -> p i d", p=128)
            sch.op("sync", lambda e, src=src: e.dma_start(out=xin[:, :, :], in_=src), [],
                   [("xin", i) for i in range(4)], dma_sem="io_in", cost=8000.0)
            for c in range(8):
                P = newps()
                for i in range(4):
                    tr(ps[P][:, i * 128:(i + 1) * 128], xin[:, i, c * 128:(c + 1) * 128], [("xin", i)], [("ps", P)])
                copy_any(xT[:, c, :], ps[P][:, :], [("ps", P)], [("xT", c)])

        def store_out(jb, srcT, srcname):
            for i in range(4):
                for half in range(2):
                    P = newps()
                    for cc in range(4):
                        c = half * 4 + cc
                        tr(ps[P][:, cc * 128:(cc + 1) * 128], srcT[:, c, i * 128:(i + 1) * 128], [(srcname, c)],
                           [("ps", P)])
                    copy_any(io[:, i, half * 512:(half + 1) * 512], ps[P][:, :], [("ps", P)], [("io", i)])
            dst = out_d[jb * TB:(jb + 1) * TB, :].rearrange("(i p) d -> p i d", p=128)
            sch.op("sync", lambda e, dst=dst: e.dma_start(out=dst, in_=io[:, :, :]),
                   [("io", i) for i in range(4)], [], dma_sem="io_out", cost=8000.0)

        if not pipe:
            for j in range(nblk):
                load_x(j)
                for l in range(L):
                    layer(l, j)
                rmsnorm(o_gf, False)
                store_out(j, xT, "xT")
        else:
            for st in range(nstep):
                jb = min(st, nblk - 1)
                xsrc = x_d[:, jb * TB:(jb + 1) * TB].rearrange("(c p) t -> p c t", p=128)
                sch.op("sync", lambda e, xsrc=xsrc: e.dma_start(out=xl[:, :, :], in_=xsrc), [],
                       [("xl", c) for c in range(8)], dma_sem="io_in", cost=8000.0)
                if st == 0:
                    for c in range(8):
                        copy_any(xT[:, c, :], xl[:, c, :], [("xl", c)], [("xT", c)])
                else:
                    for hf in range(NH):
                        rsrc = recv_d[st - 1][hf][0:128, :].rearrange("p (c t) -> p c t", c=CPP)
                        sch.op("sync", lambda e, rsrc=rsrc, hf=hf: e.dma_start(out=rbuf[:, CPP * hf:CPP * hf + CPP, :], in_=rsrc),
                               [("recv", st - 1, hf)], [("rb", c) for c in range(CPP * hf, CPP * hf + CPP)], dma_sem=("rcv", hf),
                               cost=2000.0 + 750.0 * CPP)
                        for c in range(CPP * hf, CPP * hf + CPP):
                            stt(xT[:, c, :], rbuf[:, c, :], pcol(o_fb), xl[:, c, :], ALU.mult, ALU.add,
                                [("rb", c), ("xl", c), "par"], [("xT", c)])
                layer(0, st)
                if st < nblk:
                    for hf in range(NH):
                        sdst = send_d[st][hf].rearrange("p (c t) -> p c t", c=CPP)
                        sch.op("sync", lambda e, sdst=sdst, hf=hf: e.dma_start(out=sdst, in_=xT[:, CPP * hf:CPP * hf + CPP, :]),
                               [("xT", c) for c in range(CPP * hf, CPP * hf + CPP)], [("send", st, hf)], dma_sem=("snd", hf),
                               cost=2000.0 + 750.0 * CPP)
                        sch.op("gpsimd", lambda e, st=st, hf=hf: e.collective_compute(
                            "AllGather", ALU.bypass, replica_groups=[[0, 1], [2, 3], [4, 5], [6, 7]],
                            ins=[send_d[st][hf]], outs=[recv_d[st][hf]]),
                            [("send", st, hf)], [("recv", st, hf), "cc_serial"], dma_sem=("cc", NH * st + hf),
                            cost=3000.0)
                if st >= 1:
                    kr = norm_stats()
                    for c in range(8):
                        stt(rbuf[:, c, :], xT[:, c, :], pcol(o_gf + c), tf[:, kr, :], ALU.mult, ALU.mult,
                            [("xT", c), ("tf", kr), "par"], [("rb", c)])
                    odst = out_d[:, (st - 1) * TB:st * TB].rearrange("(c p) t -> p c t", p=128)
                    sch.op("sync", lambda e, odst=odst: e.dma_start(out=odst, in_=rbuf[:, :, :]),
                           [("rb", c) for c in range(8)], [], dma_sem="io_out", cost=8000.0)

        sch.finalize()
        sch.emit(nc, es, dma_sems, ["io_out"])
    return nc


def _fm(W, cols):
    sub = W[:, cols]
    n = sub.shape[1]
    return np.ascontiguousarray(sub.reshape(8, 128, n).transpose(1, 0, 2).reshape(128, 8 * n))


def host_constants():
    s = np.arange(128)
    ident = np.eye(128, dtype=np.float32)
    onesMD = np.full((128, 128), 1.0 / 1024.0, np.float32)
    onesMH = np.full((128, 128), 1.0 / 128.0, np.float32)
    U = (s[:, None] <= s[None, :]).astype(np.float32)
    NEGm = np.where(s[:, None] <= s[None, :], 0.0, NEG).astype(np.float32)
    cst = np.concatenate([ident, onesMD, onesMH, U, NEGm], axis=1)
    kp = np.arange(128)[:, None]
    u = np.arange(640)[None, :]
    dq = u // 64 - (kp >= 64)
    negm = np.where((dq >= 0) & (dq <= 8), 0.0, NEG).astype(np.float32)
    return np.ascontiguousarray(cst), np.ascontiguousarray(negm)


def host_prep(inp, depth=DEPTH, layers=None, pipe_role=None, nblk=SEQ // TB):
    if layers is None:
        layers = list(range(depth))
    L = len(layers)
    sel = np.asarray(layers)
    inp = dict(inp)
    for k_ in ("mix_norm_g", "w_in", "conv_w", "conv_b", "b_igate", "b_fgate", "rel_bias", "mh_norm_g",
               "w_att_proj", "w_mlstm_proj", "w_out", "ffn_norm_g", "w_up", "w_down"):
        inp[k_] = np.asarray(inp[k_], np.float32)[sel]
    w_in = inp["w_in"]
    tiles = []
    wifs = []
    for l in range(L):
        for g in range(7):
            tiles.append(_fm(w_in[l], np.arange(512 * g, 512 * g + 512)))
        w_att = np.asarray(inp["w_att_proj"][l], np.float32)
        w_ml = np.asarray(inp["w_mlstm_proj"][l], np.float32)

        def pp_tile(u):
            parts = []
            for tt_ in range(2):
                t = 2 * u + tt_
                for w in (w_att, w_ml):
                    sub = w[:, t * 256:(t + 1) * 256]
                    parts.append(sub.reshape(4, 128, 256).transpose(1, 0, 2).reshape(128, 1024))
            return np.ascontiguousarray(np.concatenate(parts, axis=1))

        def gagm_tile(t):
            cols = np.concatenate([GA0 + 256 * t + np.arange(256), GM0 + 256 * t + np.arange(256)])
            return _fm(w_in[l], cols)

        tiles.append(pp_tile(0))
        tiles.append(gagm_tile(0))
        tiles.append(gagm_tile(1))
        tiles.append(pp_tile(1))
        tiles.append(gagm_tile(2))
        tiles.append(gagm_tile(3))
        w_out = np.asarray(inp["w_out"][l], np.float32)
        for t in range(2):
            tiles.append(_fm(w_out, np.arange(512 * t, 512 * t + 512)))
        w_up = np.asarray(inp["w_up"][l], np.float32)
        for t in range(8):
            tiles.append(_fm(w_up, np.arange(512 * t, 512 * t + 512)))
        w_down = np.asarray(inp["w_down"][l], np.float32)
        for c in range(8):
            sub = w_down[:, c * 128:(c + 1) * 128]
            tiles.append(np.ascontiguousarray(sub.reshape(32, 128, 128).transpose(1, 0, 2).reshape(128, 4096)))
        wifs.append(_fm(w_in[l], np.arange(3584, 3592)))
    wstream = np.stack(tiles, axis=0)
    assert wstream.shape == (L * NW_TILES, 128, 4096)
    wif = np.concatenate(wifs, axis=1)

    kp = np.arange(128)[:, None]
    u = np.arange(640)[None, :]
    rel_idx = np.clip(u - kp, -256, 256) + 256
    rb = np.asarray(inp["rel_bias"], np.float32)[:L]
    bg = rb[:, :, rel_idx]
    biasg = np.ascontiguousarray(bg.transpose(2, 0, 1, 3).reshape(128, L * 8 * 640))

    def fmv(v):
        v = np.asarray(v, np.float32)
        k = v.shape[-1] // 128
        return np.moveaxis(v.reshape(v.shape[:-1] + (k, 128)), -1, 0)

    g1 = fmv(inp["mix_norm_g"][:L]).reshape(128, L * 8)
    g2 = fmv(inp["ffn_norm_g"][:L]).reshape(128, L * 8)
    gf = fmv(inp["final_norm_g"]).reshape(128, 8)
    cw = fmv(inp["conv_w"][:L]).reshape(128, L * 32)
    cb = fmv(inp["conv_b"][:L]).reshape(128, L * 8)
    mh = fmv(inp["mh_norm_g"][:L]).reshape(128, L * 4)
    bi = np.broadcast_to(np.asarray(inp["b_igate"], np.float32)[:L][None, :, None, :], (128, L, 4, 4)).reshape(128, L * 16)
    bf = np.broadcast_to(np.asarray(inp["b_fgate"], np.float32)[:L][None, :, None, :], (128, L, 4, 4)).reshape(128, L * 16)
    epsc = np.full((128, 1), EPS, np.float32)
    plist = [g1, g2, gf, cw, cb, mh, bi, bf, epsc]
    if pipe_role is not None:
        nstep = nblk + 1
        fb = np.full((128, 1), float(pipe_role), np.float32)
        smask = np.zeros((128, nstep), np.float32)
        smask[:, :pipe_role + 1] = NEG
        valid = np.ones((128, nstep), np.float32)
        valid[:, :pipe_role] = 0.0
        plist += [fb, smask, valid]
    params = np.ascontiguousarray(np.concatenate(plist, axis=1).astype(np.float32))
    cst, negm = host_constants()
    return {"wstream": wstream, "wif": np.ascontiguousarray(wif), "biasg": biasg, "cst": cst, "negm": negm,
            "params": params}


_NC_CACHE = {}


PIPE = True


def kernel(**inputs):
    x = np.asarray(inputs["x"], np.float32)
    B, S_, D = x.shape
    nblk = S_ // TB
    n_cores = 8
    key = (nblk, DEPTH, PIPE)
    if key not in _NC_CACHE:
        _NC_CACHE[key] = build_program(nblk, DEPTH, pipe=PIPE)
    nc = _NC_CACHE[key]
    in_maps = []
    if PIPE:
        assert DEPTH == 2 and B * 2 == n_cores
        roles = [host_prep(inputs, DEPTH, layers=[r], pipe_role=r, nblk=nblk) for r in range(2)]
        zeros = np.zeros((D, S_), np.float32)
        for c in range(n_cores):
            m = dict(roles[c % 2])
            m["x"] = np.ascontiguousarray(x[c // 2].T) if c % 2 == 0 else zeros
            in_maps.append(m)
        res = run_bass_kernel_spmd(nc, in_maps, core_ids=list(range(n_cores)))
        out = np.stack([np.ascontiguousarray(np.asarray(res.results[2 * b + 1]["out"], np.float32).T)
                        for b in range(B)], axis=0)
        return out
    shared = host_prep(inputs, DEPTH)
    for c in range(n_cores):
        m = dict(shared)
        m["x"] = np.ascontiguousarray(x[c % B])
        in_maps.append(m)
    res = run_bass_kernel_spmd(nc, in_maps, core_ids=list(range(n_cores)))
    out = np.stack([np.asarray(res.results[b]["out"], np.float32) for b in range(B)], axis=0)
    return out
```

```python
import math
from contextlib import ExitStack

import numpy as np
import concourse.bass as bass
import concourse.mybir as mybir
from concourse.bass_utils import run_bass_kernel_spmd

F32 = mybir.dt.float32
BF16 = mybir.dt.bfloat16
AF = mybir.ActivationFunctionType
ALU = mybir.AluOpType

D_MODEL = 1024
BATCH = 4
SEQ = 4096
DEPTH = 2
TB = 512
NW_TILES = 31
NWS = 6
NTF = 8
NTB = 4
EPS = 1e-6
NEG = -30000.0
GA0 = 3592
GM0 = 4616
EPOCH = 8000
SAME_ENGINE_SYNC = "raw"
LIST_SCHED = True
GATE_PREFETCH = True
ATT_MUL_ENG = "vector"
NH = 4
XENG_LAT = 200.0
import os as _os
KNOP = set(int(v) for v in _os.environ.get('KNOP', '').split(',') if v)


def _is_cc(key):
    return isinstance(key, tuple) and key[0] == "cc"


class _Op:
    __slots__ = ("eng", "fn", "deps", "dma_sem", "token", "needs_inc", "idx", "raw", "cost", "alldeps")


class Sched:
    def __init__(self):
        self.ops = []
        self.lastw = {}
        self.readers = {}

    def op(self, eng, fn, reads=(), writes=(), dma_sem=None, cost=300.0):
        o = _Op()
        o.cost = cost
        o.eng = eng
        o.fn = fn
        o.dma_sem = dma_sem
        o.idx = len(self.ops)
        o.needs_inc = dma_sem is not None
        o.token = None
        deps = set()
        raw = set()
        for r in reads:
            if r in self.lastw:
                deps.add(self.lastw[r])
                raw.add(self.lastw[r])
        o.raw = raw
        for w in writes:
            if w in self.lastw:
                deps.add(self.lastw[w])
            for rd in self.readers.get(w, ()):
                deps.add(rd)
        deps.discard(o.idx)
        o.deps = deps
        for r in reads:
            self.readers.setdefault(r, set()).add(o.idx)
        for w in writes:
            self.lastw[w] = o.idx
            self.readers[w] = set()
        self.ops.append(o)
        return o

    def finalize(self):
        import os
        if os.environ.get("KTRUNC"):
            self.ops = self.ops[:int(os.environ["KTRUNC"])]
        if os.environ.get("KSKIP"):
            a, b = [int(v) for v in os.environ["KSKIP"].split(",")]
            kept = [o for o in self.ops if not (a <= o.idx < b)]
            remap = {o.idx: i for i, o in enumerate(kept)}
            for o in kept:
                o.deps = set(remap[d] for d in o.deps if d in remap)
                o.raw = set(remap[d] for d in o.raw if d in remap)
                o.idx = remap[o.idx]
            self.ops = kept
        ops = self.ops
        for o in ops:
            o.alldeps = set(o.deps)
        self.order = self.list_schedule() if LIST_SCHED else None
        for o in ops:
            keep = set()
            for d in o.deps:
                do = ops[d]
                if do.eng == o.eng and do.dma_sem is None and o.dma_sem is None:
                    if o.eng == "tensor" or not SAME_ENGINE_SYNC:
                        continue
                    if SAME_ENGINE_SYNC == "raw" and d not in o.raw:
                        continue
                keep.add(d)
            o.deps = keep
            for d in keep:
                ops[d].needs_inc = True

    def list_schedule(self):
        import heapq
        ops = self.ops
        n = len(ops)
        ndep = [len(o.alldeps) for o in ops]
        users = [[] for _ in range(n)]
        for o in ops:
            for d in o.alldeps:
                users[d].append(o.idx)
        finish = [0.0] * n
        ready_t = [0.0] * n
        engs = ["tensor", "vector", "scalar", "gpsimd", "sync"]
        pend = {e: [] for e in engs}
        avail = {e: [] for e in engs}
        t_free = {e: 0.0 for e in engs}
        for o in ops:
            if ndep[o.idx] == 0:
                heapq.heappush(pend[o.eng], (0.0, o.idx))
        order = []
        done = 0
        while done < n:
            best = None
            for e in engs:
                while pend[e] and pend[e][0][0] <= t_free[e]:
                    rt, i = heapq.heappop(pend[e])
                    heapq.heappush(avail[e], i)
                if avail[e]:
                    cand = (t_free[e], avail[e][0], e, True)
                elif pend[e]:
                    cand = (pend[e][0][0], pend[e][0][1], e, False)
                else:
                    continue
                if best is None or cand[:2] < best[:2]:
                    best = cand
            st, i, e, from_avail = best
            if from_avail:
                heapq.heappop(avail[e])
            else:
                heapq.heappop(pend[e])
            o = ops[i]
            fin = st + o.cost
            t_free[e] = fin
            finish[i] = fin
            order.append(i)
            done += 1
            for u in users[i]:
                lat = 0.0 if ops[u].eng == e else XENG_LAT
                if fin + lat > ready_t[u]:
                    ready_t[u] = fin + lat
                ndep[u] -= 1
                if ndep[u] == 0:
                    heapq.heappush(pend[ops[u].eng], (ready_t[u], u))
        self.est_total = max(finish) if finish else 0.0
        return order

    def emit(self, nc, es, dma_sems, final_waits):
        ops = self.ops
        seq = [ops[i] for i in self.order] if self.order is not None else list(ops)
        eng_sems = {}
        counters = {}
        dma_counts = {}
        for o in seq:
            if o.dma_sem is not None:
                c = dma_counts.get(o.dma_sem, 0) + (1 if _is_cc(o.dma_sem) else 16)
                dma_counts[o.dma_sem] = c
                o.token = (dma_sems[o.dma_sem], c)
            elif o.needs_inc:
                ep, c = counters.get(o.eng, (0, 0))
                if c >= EPOCH:
                    ep, c = ep + 1, 0
                c += 1
                counters[o.eng] = (ep, c)
                key = (o.eng, ep)
                if key not in eng_sems:
                    eng_sems[key] = es.enter_context(nc.semaphore("p_%s_%d" % (o.eng, ep)))
                o.token = (eng_sems[key], c)
        for o in ops:
            if o.dma_sem in ("setup", "setup_sw"):
                o.token = (dma_sems[o.dma_sem], dma_counts[o.dma_sem])
        by_eng = {}
        for o in seq:
            by_eng.setdefault(o.eng, []).append(o)

        def run(eng_name, e):
            waited = {}
            for o in by_eng.get(eng_name, []):
                need = {}
                for d in o.deps:
                    s, v = ops[d].token
                    k = id(s)
                    if waited.get(k, 0) >= v:
                        continue
                    if k not in need or need[k][1] < v:
                        need[k] = (s, v)
                for k, (s, v) in need.items():
                    e.wait_ge(s, v)
                    waited[k] = v
                if o.idx in KNOP:
                    ins = e.nop()
                else:
                    ins = o.fn(e)
                if o.dma_sem is not None:
                    if _is_cc(o.dma_sem):
                        ins.then_inc(o.token[0])
                    else:
                        ins.then_inc(o.token[0], 16)
                elif o.needs_inc:
                    ins.then_inc(o.token[0], 1)
            mine = []
            for o in by_eng.get(eng_name, []):
                if o.dma_sem is not None and o.dma_sem not in mine:
                    mine.append(o.dma_sem)
            for key in mine:
                e.wait_ge(dma_sems[key], dma_counts[key])

        with nc.Block() as block:
            @block.sync
            def _(e):
                run("sync", e)

            @block.gpsimd
            def _(e):
                run("gpsimd", e)

            @block.tensor
            def _(e):
                run("tensor", e)

            @block.vector
            def _(e):
                run("vector", e)

            @block.scalar
            def _(e):
                run("scalar", e)


def build_program(nblk, depth=DEPTH, pipe=False):
    T = nblk * TB
    nc = bass.Bass("TRN2", target_bir_lowering=False)
    L = 1 if pipe else depth
    nstep = nblk + 1 if pipe else nblk
    if pipe:
        x_d = nc.dram_tensor("x", [D_MODEL, T], F32, kind="ExternalInput").ap()
    else:
        x_d = nc.dram_tensor("x", [T, D_MODEL], F32, kind="ExternalInput").ap()
    w_d = nc.dram_tensor("wstream", [L * NW_TILES, 128, 4096], F32, kind="ExternalInput").ap()
    wif_d = nc.dram_tensor("wif", [128, L * 64], F32, kind="ExternalInput").ap()
    bias_d = nc.dram_tensor("biasg", [128, L * 8 * 640], F32, kind="ExternalInput").ap()
    cst_d = nc.dram_tensor("cst", [128, 5 * 128], F32, kind="ExternalInput").ap()
    negm_d = nc.dram_tensor("negm", [128, 640], F32, kind="ExternalInput").ap()
    NPAR = L * 8 + L * 8 + 8 + L * 32 + L * 8 + L * 4 + L * 16 + L * 16 + 1 + (1 + 2 * (nblk + 1) if pipe else 0)
    par_d = nc.dram_tensor("params", [128, NPAR], F32, kind="ExternalInput").ap()
    if pipe:
        out_d = nc.dram_tensor("out", [D_MODEL, T], F32, kind="ExternalOutput").ap()
    else:
        out_d = nc.dram_tensor("out", [T, D_MODEL], F32, kind="ExternalOutput").ap()

    es = ExitStack()
    with es:
        def sb(name, shape, dt):
            return es.enter_context(nc.sbuf_tensor(name, shape, dt))

        if not pipe:
            io = sb("io", [128, 4, 1024], F32)
        if pipe:
            xl = sb("xl", [128, 8, TB], F32)
            rbuf = sb("rbuf", [128, 8, TB], F32)
            CPP = 8 // NH
            send_d = [[nc.dram_tensor("send%d_%d" % (k, hf), [128, CPP * TB], F32, kind="Internal").ap()
                       for hf in range(NH)] for k in range(nblk)]
            recv_d = [[nc.dram_tensor("recv%d_%d" % (k, hf), [256, CPP * TB], F32, kind="Internal").ap()
                       for hf in range(NH)] for k in range(nblk)]
        else:
            xin = io
        xT = sb("xT", [128, 8, TB], F32)
        xnT = sb("xnT", [128, 8, TB], BF16)
        arena = sb("arena", [128, 32, TB], BF16)
        kh = sb("kh", [128, L * 2 * 4, TB], BF16)
        vh = sb("vh", [128, L * 2 * 4, TB], BF16)
        wb = sb("wb", [128, NWS, 4096], BF16)
        wif = sb("wifs", [128, L * 64], BF16)
        BM = sb("BM", [128, L * 8, 640], BF16)
        negm = sb("negms", [128, 640], BF16)
        S = sb("S", [128, L * 4, 256], F32)
        Sb = sb("Sb", [128, L * 4, 256], BF16)
        halo = sb("halo", [128, L * 8, 4], F32)
        cst = sb("csts", [128, 5, 128], F32)
        identB = sb("identB", [128, 128], BF16)
        cstB = sb("cstB", [128, 3, 128], BF16)
        onesB = sb("onesB", [128, 128], BF16)
        onesF = sb("onesF", [128, 128], F32)
        par = sb("pars", [128, NPAR], F32)
        tf = sb("tf", [128, NTF, 512], F32)
        tbf = sb("tbf", [128, NTB, 512], BF16)
        pre = sb("pre", [128, 2, 516], F32)
        LF = sb("LF", [128, 2, 4, 128], F32)
        gsm = sb("gsm", [128, 8, 16], F32)
        wsm = sb("wsm", [128, 8, 2], F32)
        ps = [es.enter_context(nc.psum_tensor("ps%d" % b, [128, 512], F32)) for b in range(8)]

        identF = cst[:, 0, :]
        onesMD = cst[:, 1, :]
        onesMH = cst[:, 2, :]
        Umat = cst[:, 3, :]
        NEGmat = cst[:, 4, :]
        onesMDb = cstB[:, 0, :]
        onesMHb = cstB[:, 1, :]
        NEGb = cstB[:, 2, :]

        o_g1 = 0
        o_g2 = o_g1 + L * 8
        o_gf = o_g2 + L * 8
        o_cw = o_gf + 8
        o_cb = o_cw + L * 32
        o_mh = o_cb + L * 8
        o_bi = o_mh + L * 4
        o_bf = o_bi + L * 16
        o_eps = o_bf + L * 16
        o_fb = o_eps + 1
        o_sm = o_fb + 1
        o_vd = o_sm + nstep

        def pcol(off):
            return par[:, off:off + 1]

        dma_sems = {}
        extra_sems = ([("snd", k) for k in range(NH)] + [("rcv", k) for k in range(NH)]
                      + [("cc", k) for k in range(NH * nblk)]) if pipe else []
        for k in ["setup", "setup_sw", "io_in", "io_out"] + extra_sems + [("w", s) for s in range(NWS)]:
            dma_sems[k] = es.enter_context(nc.semaphore("d_%s" % (str(k).replace(" ", ""))))

        sch = Sched()
        state = {"ps": 0, "tf": 0, "tb": 0, "w": 0, "alt": 0, "pre": 0, "lf": 0, "wsm": 0}

        def newps(exclude=()):
            b = state["ps"]
            while b in exclude:
                b = (b + 1) % 8
            state["ps"] = (b + 1) % 8
            return b

        def newtf():
            k = state["tf"]
            state["tf"] = (k + 1) % NTF
            return k

        def newtb():
            k = state["tb"]
            state["tb"] = (k + 1) % NTB
            return k

        def fsz(ap):
            n = 1
            for d in ap.shape[1:]:
                n *= int(d)
            return n

        def mm(out, lhsT, rhs, start, stop, r, w):
            c = max(fsz(rhs), 64) / 1.95 + 15.0
            if lhsT.dtype == F32:
                c *= 4.0
            sch.op("tensor", lambda e: e.matmul(out, lhsT, rhs, start=start, stop=stop), r, w, cost=c)

        def tr(out, in_, r, w):
            sch.op("tensor", lambda e: e.transpose(out, in_, identF), list(r) + ["cst"], w, cost=280.0)

        def act(out, in_, func, r, w, bias=None, scale=None):
            kw = {}
            if bias is not None:
                kw["bias"] = bias
            if scale is not None:
                kw["scale"] = scale
            c = fsz(out) / 1.4 + 220.0 + (90.0 if bias is not None and not isinstance(bias, float) else 0.0)
            sch.op("scalar", lambda e: e.activation(out, in_, func, **kw), r, w, cost=c)

        def dcost(ap, mult=1.0):
            return max(fsz(ap), 64) * mult / 0.96 + 90.0

        def tt(out, in0, in1, op, r, w, eng="vector"):
            sch.op(eng, lambda e: e.tensor_tensor(out, in0, in1, op), r, w, cost=dcost(out))

        def ts(out, in0, s1, s2, op0, op1, r, w, eng="vector"):
            if op1 is None:
                sch.op(eng, lambda e: e.tensor_scalar(out, in0, s1, None, op0), r, w, cost=dcost(out))
            else:
                sch.op(eng, lambda e: e.tensor_scalar(out, in0, s1, s2, op0, op1), r, w, cost=dcost(out))

        def stt(out, in0, scalar, in1, op0, op1, r, w):
            sch.op("vector", lambda e: e.scalar_tensor_tensor(out, in0, scalar, in1, op0, op1), r, w,
                   cost=dcost(out))

        def recip(out, in_, r, w):
            sch.op("vector", lambda e: e.reciprocal(out, in_), r, w, cost=dcost(out, 8.0))

        def copy_any(out, in_, r, w):
            state["alt"] ^= 1
            if state["alt"]:
                sch.op("scalar", lambda e: e.copy(out, in_), r, w, cost=fsz(out) / 1.4 + 220.0)
            else:
                sch.op("vector", lambda e: e.tensor_copy(out, in_), r, w, cost=dcost(out))

        def dsetup(eng, out, in_, w):
            sch.op(eng, lambda e: e.dma_start(out=out, in_=in_), [], w,
                   dma_sem="setup_sw" if eng == "gpsimd" else "setup")

        dsetup("sync", cst[:, :, :], cst_d.rearrange("p (k n) -> p k n", k=5), ["cst"])
        dsetup("sync", par[:, :], par_d, ["par"])
        dsetup("gpsimd", wif[:, :], wif_d, ["wif"])
        dsetup("gpsimd", negm[:, :], negm_d, ["negm"])
        for l in range(L):
            for h in range(8):
                dsetup("gpsimd", BM[:, l * 8 + h, :],
                       bias_d[:, (l * 8 + h) * 640:(l * 8 + h + 1) * 640], [("BM", l * 8 + h)])
        sch.op("vector", lambda e: e.memset(S[:, :, :], 0.0), [], ["S%d" % k for k in range(L * 4)])
        sch.op("vector", lambda e: e.memset(Sb[:, :, :], 0.0), [], ["Sb%d" % k for k in range(L * 4)])
        sch.op("vector", lambda e: e.memset(halo[:, :, :], 0.0), [], [("halo", k) for k in range(L * 8)])
        sch.op("vector", lambda e: e.memset(kh[:, :, :], 0.0), [], [("kh", k) for k in range(L * 8)])
        sch.op("vector", lambda e: e.memset(vh[:, :, :], 0.0), [], [("vh", k) for k in range(L * 8)])
        sch.op("vector", lambda e: e.memset(onesB[:, :], 1.0), [], ["onesB"])
        sch.op("vector", lambda e: e.memset(onesF[:, :], 1.0), [], ["onesF"])
        sch.op("vector", lambda e: e.tensor_copy(identB[:, :], identF), ["cst"], ["identB"])
        sch.op("vector", lambda e: e.tensor_copy(cstB[:, 0:2, :], cst[:, 1:3, :]), ["cst"], ["cstB"])
        sch.op("vector", lambda e: e.tensor_copy(cstB[:, 2, :], cst[:, 4, :]), ["cst", "cstB"], ["cstB"])
        for l in range(L):
            for h in range(8):
                k = l * 8 + h
                tt(BM[:, k, :], BM[:, k, :], negm[:, :], ALU.add, [("BM", k), "negm"], [("BM", k)])
                if ATT_MUL_ENG:
                    act(BM[:, k, :], BM[:, k, :], AF.Exp, [("BM", k)], [("BM", k)])

        def next_w(l, k_expected):
            n = state["w"]
            state["w"] = n + 1
            assert n % NW_TILES == k_expected and n // NW_TILES % L == l, (n, l, k_expected)
            slot = n % NWS
            src = w_d[(n % (L * NW_TILES)), :, :]
            gate = []
            if pipe and GATE_PREFETCH and n // NW_TILES >= 1 and n % NW_TILES < NWS:
                gate = [("recv", n // NW_TILES - 1, NH - 1)]
            sch.op("gpsimd", lambda e: e.dma_start(out=wb[:, slot, :], in_=src), gate, [("w", slot)],
                   dma_sem=("w", slot), cost=9000.0)
            return slot

        def wview(slot, kc, c0, n, kcn=8):
            width = 4096 // kcn
            return wb[:, slot, kc * width + c0: kc * width + c0 + n]

        def rstd_from(P_ap, kr, rd):
            act(tf[:, kr, :], P_ap, AF.Ln, rd + ["par"], [("tf", kr)], bias=pcol(o_eps))
            act(tf[:, kr, :], tf[:, kr, :], AF.Exp, [("tf", kr)], [("tf", kr)], scale=-0.5)

        def norm_stats():
            P = newps()
            for c in range(8):
                k = newtb()
                act(tbf[:, k, :], xT[:, c, :], AF.Square, [("xT", c)], [("tb", k)])
                mm(ps[P][:, :], onesMDb, tbf[:, k, :], c == 0, c == 7, [("tb", k), "cstB"], [("ps", P)])
            kr = newtf()
            rstd_from(ps[P][:, :], kr, [("ps", P)])
            return kr

        def rmsnorm(gcol0, dst_bf16):
            kr = norm_stats()
            for c in range(8):
                if dst_bf16:
                    stt(xnT[:, c, :], xT[:, c, :], pcol(gcol0 + c), tf[:, kr, :], ALU.mult, ALU.mult,
                        [("xT", c), ("tf", kr), "par"], [("xn", c)])
                else:
                    stt(xT[:, c, :], xT[:, c, :], pcol(gcol0 + c), tf[:, kr, :], ALU.mult, ALU.mult,
                        [("xT", c), ("tf", kr), "par"], [("xT", c)])

        def pg(n):
            return ("pg", n)

        SC_M = 1.0 / math.sqrt(128.0)

        def layer(l, j):
            parity = j % 2
            kcur = lambda p: (l * 2 + parity) * 4 + p
            kprev = lambda p: (l * 2 + 1 - parity) * 4 + p

            rmsnorm(o_g1 + l * 8, True)
            xn_all = [("xn", c) for c in range(8)]

            for g in range(2):
                slot = next_w(l, g)
                for m in range(4):
                    P = newps()
                    for kc in range(8):
                        mm(ps[P][:, :], wview(slot, kc, m * 128, 128), xnT[:, kc, :], kc == 0, kc == 7,
                           [("w", slot), ("xn", kc)], [("ps", P)])
                    if g == 0:
                        copy_any(arena[:, m, :], ps[P][:, :], [("ps", P)], [pg(m)])
                    else:
                        copy_any(kh[:, kcur(m), :], ps[P][:, :], [("ps", P)], [("kh", kcur(m))])
            slot = next_w(l, 2)
            for i in range(4):
                P = newps()
                for kc in range(8):
                    mm(ps[P][:, :], xnT[:, kc, i * 128:(i + 1) * 128], wview(slot, kc, 0, 512), kc == 0, kc == 7,
                       [("w", slot), ("xn", kc)], [("ps", P)])
                copy_any(vh[:, kcur(i), :], ps[P][:, :], [("ps", P)], [("vh", kcur(i))])

            for p in range(4):
                O = newps()
                Dn = newps()
                for e_ in range(2):
                    h = 2 * p + e_
                    rows = slice(e_ * 64, (e_ + 1) * 64)
                    rlist = [4] + [r for r in range(8) if (j > 0 or r >= 4 or pipe) and r != 4]
                    for idx, r in enumerate(rlist):
                        kidx = kprev(p) if r < 4 else kcur(p)
                        vidx = kprev(r % 4) if r < 4 else kcur(r % 4)
                        qlo = max(8, 2 * r) * 64 - 512
                        qhi = (min(15, 2 * r + 9) + 1) * 64 - 512
                        n = qhi - qlo
                        u0 = 512 + qlo - 128 * r
                        sc = newps(exclude=(O, Dn))
                        mm(ps[sc][:, 0:n], kh[rows, kidx, (r % 4) * 128:(r % 4 + 1) * 128], arena[rows, p, qlo:qhi],
                           True, True, [("kh", kidx), pg(p)], [("ps", sc)])
                        k1 = newtf()
                        k2 = newtb()
                        if ATT_MUL_ENG:
                            if pipe and r < 4:
                                act(tf[:, k1, 0:n], ps[sc][:, 0:n], AF.Exp, [("ps", sc), "par"], [("tf", k1)],
                                    bias=pcol(o_sm + j), scale=0.125)
                            else:
                                act(tf[:, k1, 0:n], ps[sc][:, 0:n], AF.Exp, [("ps", sc)], [("tf", k1)], scale=0.125)
                            sch.op(ATT_MUL_ENG, lambda e, k1=k1, k2=k2, n=n, u0=u0, lh=l * 8 + h: e.tensor_tensor(
                                tbf[:, k2, 0:n], tf[:, k1, 0:n], BM[:, lh, u0:u0 + n], ALU.mult),
                                [("tf", k1), ("BM", l * 8 + h)], [("tb", k2)],
                                cost=(n / 0.5 + 500.0) if ATT_MUL_ENG == "gpsimd" else (n / 0.96 + 90.0))
                        else:
                            stt(tf[:, k1, 0:n], ps[sc][:, 0:n], 0.125, BM[:, l * 8 + h, u0:u0 + n], ALU.mult, ALU.add,
                                [("ps", sc), ("BM", l * 8 + h)], [("tf", k1)])
                            if pipe and r < 4:
                                act(tbf[:, k2, 0:n], tf[:, k1, 0:n], AF.Exp, [("tf", k1), "par"], [("tb", k2)],
                                    bias=pcol(o_sm + j))
                            else:
                                act(tbf[:, k2, 0:n], tf[:, k1, 0:n], AF.Exp, [("tf", k1)], [("tb", k2)])
                        first = idx == 0
                        last = idx == len(rlist) - 1
                        mm(ps[O][rows, qlo:qhi], vh[:, vidx, h * 64:(h + 1) * 64], tbf[:, k2, 0:n], first, last,
                           [("vh", vidx), ("tb", k2)], [("ps", O)])
                        mm(ps[Dn][rows, qlo:qhi], onesB[:, 0:64], tbf[:, k2, 0:n], first, last,
                           ["onesB", ("tb", k2)], [("ps", Dn)])
                kr = newtf()
                act(tf[:, kr, :], ps[Dn][:, :], AF.Ln, [("ps", Dn)], [("tf", kr)])
                act(tf[:, kr, :], tf[:, kr, :], AF.Exp, [("tf", kr)], [("tf", kr)], scale=-1.0)
                tt(arena[:, 24 + p, :], ps[O][:, :], tf[:, kr, :], ALU.mult, [("ps", O), ("tf", kr)], [pg(24 + p)])

            for g in (3, 4):
                slot = next_w(l, g)
                for m in range(4):
                    ch = (g - 3) * 4 + m
                    P = newps()
                    for kc in range(8):
                        mm(ps[P][:, :], wview(slot, kc, m * 128, 128), xnT[:, kc, :], kc == 0, kc == 7,
                           [("w", slot), ("xn", kc)], [("ps", P)])
                    pk = state["pre"]
                    state["pre"] ^= 1
                    hk = l * 8 + ch
                    copy_any(pre[:, pk, 3:515], ps[P][:, :], [("ps", P)], [("pre", pk)])
                    sch.op("vector", lambda e, pk=pk, hk=hk: e.tensor_copy(pre[:, pk, 0:3], halo[:, hk, 0:3]),
                           [("halo", hk)], [("pre", pk)])
                    sch.op("vector", lambda e, pk=pk, hk=hk: e.tensor_copy(halo[:, hk, 0:3], pre[:, pk, 512:515]),
                           [("pre", pk)], [("halo", hk)])
                    ka = newtf()
                    cw = lambda tap: pcol(o_cw + l * 32 + tap * 8 + ch)
                    ts(tf[:, ka, :], pre[:, pk, 0:512], cw(0), None, ALU.mult, None,
                       [("pre", pk), "par"], [("tf", ka)])
                    for tap in (1, 2, 3):
                        stt(tf[:, ka, :], pre[:, pk, tap:tap + 512], cw(tap), tf[:, ka, :], ALU.mult, ALU.add,
                            [("pre", pk), "par", ("tf", ka)], [("tf", ka)])
                    act(arena[:, 4 + ch, :], tf[:, ka, :], AF.Silu, [("tf", ka), "par"], [pg(4 + ch)],
                        bias=pcol(o_cb + l * 8 + ch))
            slot = next_w(l, 5)
            for i in range(4):
                P = newps()
                for kc in range(8):
                    mm(ps[P][:, :], xnT[:, kc, i * 128:(i + 1) * 128], wview(slot, kc, 0, 512), kc == 0, kc == 7,
                       [("w", slot), ("xn", kc)], [("ps", P)])
                copy_any(arena[:, 12 + i, :], ps[P][:, :], [("ps", P)], [pg(12 + i)])
            slot = next_w(l, 6)
            for m in range(4):
                P = newps()
                for kc in range(8):
                    mm(ps[P][:, :], wview(slot, kc, m * 128, 128), xnT[:, kc, :], kc == 0, kc == 7,
                       [("w", slot), ("xn", kc)], [("ps", P)])
                act(arena[:, 20 + m, :], ps[P][:, :], AF.Sigmoid, [("ps", P)], [pg(20 + m)])
            G = newps()
            for i in range(4):
                for kc in range(8):
                    mm(ps[G][:, i * 8:(i + 1) * 8], xnT[:, kc, i * 128:(i + 1) * 128],
                       wif[:, l * 64 + kc * 8: l * 64 + kc * 8 + 8], kc == 0, kc == 7,
                       ["wif", ("xn", kc)], [("ps", G)])
            G3 = ps[G][:, 0:32].rearrange("p (i c) -> p i c", c=8)
            ig = gsm[:, 0, :]
            zf = gsm[:, 1, :]
            ef = gsm[:, 2, :]
            spv = gsm[:, 3, :]
            logf = gsm[:, 4, :]
            biasv = gsm[:, 5, :]
            v3 = lambda ap: ap.rearrange("p (i c) -> p i c", c=4)
            tt(v3(ig), G3[:, :, 0:4], v3(par[:, o_bi + l * 16:o_bi + l * 16 + 16]), ALU.add,
               [("ps", G), "par"], ["g_ig"])
            tt(v3(zf), G3[:, :, 4:8], v3(par[:, o_bf + l * 16:o_bf + l * 16 + 16]), ALU.add,
               [("ps", G), "par"], ["g_zf"])
            act(ef, zf, AF.Exp, ["g_zf"], ["g_ef"], scale=-1.0)
            act(spv, ef, AF.Ln, ["g_ef"], ["g_sp"], bias=1.0)
            ts(logf, spv, -1.0, None, ALU.mult, None, ["g_sp"], ["g_lf"])
            Cm = newps()
            mm(ps[Cm][:, 0:16], Umat, logf, True, True, ["cst", "g_lf"], [("ps", Cm)])
            tt(biasv, ig, ps[Cm][:, 0:16], ALU.subtract, ["g_ig", ("ps", Cm)], ["g_bv"])

            for i in range(4):
                lfk = state["lf"]
                state["lf"] ^= 1
                for h in range(4):
                    col = i * 4 + h
                    ts(LF[:, lfk, h, :], onesF[:, :], logf[:, col:col + 1], None, ALU.mult, None,
                       ["onesF", "g_lf"], [("LF", lfk, h)])
                tsl = slice(i * 128, (i + 1) * 128)
                for h in range(4):
                    col = i * 4 + h
                    sk = l * 4 + h
                    kT = arena[:, 8 + h, tsl]
                    qT_ = arena[:, 4 + h, tsl]
                    vt = arena[:, 12 + i, h * 128:(h + 1) * 128]
                    A = newps()
                    mm(ps[A][:, 0:128], kT, qT_, True, True, [pg(8 + h), pg(4 + h)], [("ps", A)])
                    mm(ps[A][:, 128:256], LF[:, lfk, h, :], Umat, True, True, [("LF", lfk, h), "cst"], [("ps", A)])
                    mm(ps[A][:, 256:384], LF[:, lfk, h, :], Umat, True, False, [("LF", lfk, h), "cst"], [("ps", A)])
                    mm(ps[A][:, 256:384], identB[:, :], NEGb, False, True, ["identB", "cstB"], [("ps", A)])
                    ka = newtf()
                    act(tf[:, ka, 0:128], ps[A][:, 256:384], AF.Exp, [("ps", A), "g_bv"], [("tf", ka)],
                        bias=biasv[:, col:col + 1])
                    act(tf[:, ka, 128:256], ps[A][:, 128:256], AF.Exp, [("ps", A)], [("tf", ka)])
                    wk = state["wsm"]
                    state["wsm"] = (wk + 1) % 8
                    if pipe:
                        ts(wsm[:, wk, 0:1], tf[:, ka, 127:128], pcol(o_vd + j), None, ALU.mult, None,
                           [("tf", ka), "par"], [("wsm", wk)])
                    else:
                        sch.op("vector", lambda e, wk=wk, ka=ka: e.tensor_copy(wsm[:, wk, 0:1], tf[:, ka, 127:128]),
                               [("tf", ka)], [("wsm", wk)])
                    sch.op("vector", lambda e, wk=wk, ka=ka: e.tensor_copy(wsm[:, wk, 1:2], tf[:, ka, 255:256]),
                           [("tf", ka)], [("wsm", wk)])
                    kb = newtb()
                    stt(tbf[:, kb, 0:128], ps[A][:, 0:128], SC_M, tf[:, ka, 0:128], ALU.mult, ALU.mult,
                        [("ps", A), ("tf", ka)], [("tb", kb)])
                    stt(tbf[:, kb, 128:256], qT_, SC_M, tf[:, ka, 128:256], ALU.mult, ALU.mult,
                        [pg(4 + h), ("tf", ka)], [("tb", kb)])
                    AT = tbf[:, kb, 0:128]
                    qsT = tbf[:, kb, 128:256]
                    B = newps()
                    mm(ps[B][:, 0:128], vt, AT, True, False, [pg(12 + i), ("tb", kb)], [("ps", B)])
                    mm(ps[B][:, 0:128], Sb[:, sk, 0:128], qsT, False, True, ["Sb%d" % sk, ("tb", kb)], [("ps", B)])
                    mm(ps[B][:, 128:256], onesB[:, :], AT, True, False, ["onesB", ("tb", kb)], [("ps", B)])
                    mm(ps[B][:, 128:256], Sb[:, sk, 128:256], qsT, False, True, ["Sb%d" % sk, ("tb", kb)],
                       [("ps", B)])
                    kc_ = newtf()
                    act(tf[:, kc_, 0:128], ps[B][:, 128:256], AF.Abs, [("ps", B)], [("tf", kc_)])
                    ts(tf[:, kc_, 0:128], tf[:, kc_, 0:128], 1.0, None, ALU.max, None, [("tf", kc_)], [("tf", kc_)])
                    recip(tf[:, kc_, 0:128], tf[:, kc_, 0:128], [("tf", kc_)], [("tf", kc_)])
                    tt(tf[:, kc_, 128:256], ps[B][:, 0:128], tf[:, kc_, 0:128], ALU.mult,
                       [("ps", B), ("tf", kc_)], [("tf", kc_)])
                    act(tbf[:, kb, 384:512], tf[:, kc_, 128:256], AF.Square, [("tf", kc_)], [("tb", kb)])
                    mm(ps[B][:, 256:384], onesMHb, tbf[:, kb, 384:512], True, True, ["cstB", ("tb", kb)], [("ps", B)])
                    act(tf[:, kc_, 384:512], ps[B][:, 256:384], AF.Ln, [("ps", B), "par"], [("tf", kc_)],
                        bias=pcol(o_eps))
                    act(tf[:, kc_, 384:512], tf[:, kc_, 384:512], AF.Exp, [("tf", kc_)], [("tf", kc_)], scale=-0.5)
                    stt(tf[:, kc_, 0:128], tf[:, kc_, 128:256], pcol(o_mh + l * 4 + h), tf[:, kc_, 384:512],
                        ALU.mult, ALU.mult, [("tf", kc_), "par"], [("tf", kc_)])
                    tt(arena[:, 28 + h, tsl], tf[:, kc_, 0:128], arena[:, 20 + h, tsl], ALU.mult,
                       [("tf", kc_), pg(20 + h)], [pg(28 + h)])
                    C = newps()
                    mm(ps[C][:, 0:128], kT, identB[:, :], True, True, [pg(8 + h), "identB"], [("ps", C)])
                    act(tbf[:, kb, 256:384], ps[C][:, 0:128], AF.Copy, [("ps", C), ("wsm", wk)], [("tb", kb)],
                        scale=wsm[:, wk, 0:1])
                    kw_ = tbf[:, kb, 256:384]
                    mm(ps[C][:, 128:256], kw_, vt, True, True, [("tb", kb), pg(12 + i)], [("ps", C)])
                    mm(ps[C][:, 256:384], kw_, onesB[:, :], True, True, [("tb", kb), "onesB"], [("ps", C)])
                    stt(S[:, sk, :], S[:, sk, :], wsm[:, wk, 1:2], ps[C][:, 128:384], ALU.mult, ALU.add,
                        ["S%d" % sk, ("wsm", wk), ("ps", C)], ["S%d" % sk])
                    sch.op("scalar", lambda e, sk=sk: e.copy(Sb[:, sk, :], S[:, sk, :]), ["S%d" % sk], ["Sb%d" % sk])

            gidx = {0: 8, 1: 9, 2: 11, 3: 12}
            spp = None
            for t in range(4):
                if t % 2 == 0:
                    spp = next_w(l, 7 if t == 0 else 10)
                sg = next_w(l, gidx[t])
                for cc in range(2):
                    c = 2 * t + cc
                    GA = newps()
                    for kc in range(8):
                        mm(ps[GA][:, :], wview(sg, kc, cc * 128, 128), xnT[:, kc, :], kc == 0, kc == 7,
                           [("w", sg), ("xn", kc)], [("ps", GA)])
                    GM = newps()
                    for kc in range(8):
                        mm(ps[GM][:, :], wview(sg, kc, 256 + cc * 128, 128), xnT[:, kc, :], kc == 0, kc == 7,
                           [("w", sg), ("xn", kc)], [("ps", GM)])
                    pbase = (t % 2) * 2048 + cc * 128
                    PA = newps()
                    for kc in range(4):
                        o_ = pbase + kc * 256
                        mm(ps[PA][:, :], wb[:, spp, o_:o_ + 128], arena[:, 24 + kc, :], kc == 0, kc == 3,
                           [("w", spp), pg(24 + kc)], [("ps", PA)])
                    PM = newps()
                    for kc in range(4):
                        o_ = pbase + 1024 + kc * 256
                        mm(ps[PM][:, :], wb[:, spp, o_:o_ + 128], arena[:, 28 + kc, :], kc == 0, kc == 3,
                           [("w", spp), pg(28 + kc)], [("ps", PM)])
                    k1 = newtf()
                    k2 = newtf()
                    act(tf[:, k1, :], ps[GA][:, :], AF.Sigmoid, [("ps", GA)], [("tf", k1)])
                    act(tf[:, k2, :], ps[GM][:, :], AF.Sigmoid, [("ps", GM)], [("tf", k2)])
                    tt(tf[:, k1, :], ps[PA][:, :], tf[:, k1, :], ALU.mult, [("ps", PA), ("tf", k1)], [("tf", k1)])
                    tt(tf[:, k2, :], ps[PM][:, :], tf[:, k2, :], ALU.mult, [("ps", PM), ("tf", k2)], [("tf", k2)])
                    tt(arena[:, 4 + c, :], tf[:, k1, :], tf[:, k2, :], ALU.add, [("tf", k1), ("tf", k2)], [pg(4 + c)])
            for t in range(2):
                so = next_w(l, 13 + t)
                for m in range(4):
                    c = 4 * t + m
                    P = newps()
                    for kc in range(8):
                        mm(ps[P][:, :], wview(so, kc, m * 128, 128), arena[:, 4 + kc, :], kc == 0, kc == 7,
                           [("w", so), pg(4 + kc)], [("ps", P)])
                    tt(xT[:, c, :], xT[:, c, :], ps[P][:, :], ALU.add, [("xT", c), ("ps", P)], [("xT", c)])
            rmsnorm(o_g2 + l * 8, True)
            for t in range(8):
                su = next_w(l, 15 + t)
                for m in range(4):
                    f = 4 * t + m
                    P = newps()
                    for kc in range(8):
                        mm(ps[P][:, :], wview(su, kc, m * 128, 128), xnT[:, kc, :], kc == 0, kc == 7,
                           [("w", su), ("xn", kc)], [("ps", P)])
                    k1 = newtf()
                    act(tf[:, k1, :], ps[P][:, :], AF.Relu, [("ps", P)], [("tf", k1)])
                    tt(arena[:, f, :], ps[P][:, :], tf[:, k1, :], ALU.mult, [("ps", P), ("tf", k1)], [pg(f)])
            for c in range(8):
                sd = next_w(l, 23 + c)
                P = newps()
                for f in range(32):
                    mm(ps[P][:, :], wview(sd, f, 0, 128, kcn=32), arena[:, f, :], f == 0, f == 31,
                       [("w", sd), pg(f)], [("ps", P)])
                tt(xT[:, c, :], xT[:, c, :], ps[P][:, :], ALU.add, [("xT", c), ("ps", P)], [("xT", c)])

        def load_x(jb):
            src = x_d[jb * TB:(jb + 1) * TB, :].rearrange("(i p) d -> p i d", p=128)
            sch.op("sync", lambda e, src=src: e.dma_start(out=xin[:, :, :], in_=src), [],
                   [("xin", i) for i in range(4)], dma_sem="io_in", cost=8000.0)
            for c in range(8):
                P = newps()
                for i in range(4):
                    tr(ps[P][:, i * 128:(i + 1) * 128], xin[:, i, c * 128:(c + 1) * 128], [("xin", i)], [("ps", P)])
                copy_any(xT[:, c, :], ps[P][:, :], [("ps", P)], [("xT", c)])

        def store_out(jb, srcT, srcname):
            for i in range(4):
                for half in range(2):
                    P = newps()
                    for cc in range(4):
                        c = half * 4 + cc
                        tr(ps[P][:, cc * 128:(cc + 1) * 128], srcT[:, c, i * 128:(i + 1) * 128], [(srcname, c)],
                           [("ps", P)])
                    copy_any(io[:, i, half * 512:(half + 1) * 512], ps[P][:, :], [("ps", P)], [("io", i)])
            dst = out_d[jb * TB:(jb + 1) * TB, :].rearrange("(i p) d -> p i d", p=128)
            sch.op("sync", lambda e, dst=dst: e.dma_start(out=dst, in_=io[:, :, :]),
                   [("io", i) for i in range(4)], [], dma_sem="io_out", cost=8000.0)

        if not pipe:
            for j in range(nblk):
                load_x(j)
                for l in range(L):
                    layer(l, j)
                rmsnorm(o_gf, False)
                store_out(j, xT, "xT")
        else:
            for st in range(nstep):
                jb = min(st, nblk - 1)
                xsrc = x_d[:, jb * TB:(jb + 1) * TB].rearrange("(c p) t -> p c t", p=128)
                sch.op("sync", lambda e, xsrc=xsrc: e.dma_start(out=xl[:, :, :], in_=xsrc), [],
                       [("xl", c) for c in range(8)], dma_sem="io_in", cost=8000.0)
                if st == 0:
                    for c in range(8):
                        copy_any(xT[:, c, :], xl[:, c, :], [("xl", c)], [("xT", c)])
                else:
                    for hf in range(NH):
                        rsrc = recv_d[st - 1][hf][0:128, :].rearrange("p (c t) -> p c t", c=CPP)
                        sch.op("sync", lambda e, rsrc=rsrc, hf=hf: e.dma_start(out=rbuf[:, CPP * hf:CPP * hf + CPP, :], in_=rsrc),
                               [("recv", st - 1, hf)], [("rb", c) for c in range(CPP * hf, CPP * hf + CPP)], dma_sem=("rcv", hf),
                               cost=2000.0 + 750.0 * CPP)
                        for c in range(CPP * hf, CPP * hf + CPP):
                            stt(xT[:, c, :], rbuf[:, c, :], pcol(o_fb), xl[:, c, :], ALU.mult, ALU.add,
                                [("rb", c), ("xl", c), "par"], [("xT", c)])
                layer(0, st)
                if st < nblk:
                    for hf in range(NH):
                        sdst = send_d[st][hf].rearrange("p (c t) -> p c t", c=CPP)
                        sch.op("sync", lambda e, sdst=sdst, hf=hf: e.dma_start(out=sdst, in_=xT[:, CPP * hf:CPP * hf + CPP, :]),
                               [("xT", c) for c in range(CPP * hf, CPP * hf + CPP)], [("send", st, hf)], dma_sem=("snd", hf),
                               cost=2000.0 + 750.0 * CPP)
                        sch.op("gpsimd", lambda e, st=st, hf=hf: e.collective_compute(
                            "AllGather", ALU.bypass, replica_groups=[[0, 1], [2, 3], [4, 5], [6, 7]],
                            ins=[send_d[st][hf]], outs=[recv_d[st][hf]]),
                            [("send", st, hf)], [("recv", st, hf), "cc_serial"], dma_sem=("cc", NH * st + hf),
                            cost=3000.0)
                if st >= 1:
                    kr = norm_stats()
                    for c in range(8):
                        stt(rbuf[:, c, :], xT[:, c, :], pcol(o_gf + c), tf[:, kr, :], ALU.mult, ALU.mult,
                            [("xT", c), ("tf", kr), "par"], [("rb", c)])
                    odst = out_d[:, (st - 1) * TB:st * TB].rearrange("(c p) t -> p c t", p=128)
                    sch.op("sync", lambda e, odst=odst: e.dma_start(out=odst, in_=rbuf[:, :, :]),
                           [("rb", c) for c in range(8)], [], dma_sem="io_out", cost=8000.0)

        sch.finalize()
        sch.emit(nc, es, dma_sems, ["io_out"])
    return nc


def _fm(W, cols):
    sub = W[:, cols]
    n = sub.shape[1]
    return np.ascontiguousarray(sub.reshape(8, 128, n).transpose(1, 0, 2).reshape(128, 8 * n))


def host_constants():
    s = np.arange(128)
    ident = np.eye(128, dtype=np.float32)
    onesMD = np.full((128, 128), 1.0 / 1024.0, np.float32)
    onesMH = np.full((128, 128), 1.0 / 128.0, np.float32)
    U = (s[:, None] <= s[None, :]).astype(np.float32)
    NEGm = np.where(s[:, None] <= s[None, :], 0.0, NEG).astype(np.float32)
    cst = np.concatenate([ident, onesMD, onesMH, U, NEGm], axis=1)
    kp = np.arange(128)[:, None]
    u = np.arange(640)[None, :]
    dq = u // 64 - (kp >= 64)
    negm = np.where((dq >= 0) & (dq <= 8), 0.0, NEG).astype(np.float32)
    return np.ascontiguousarray(cst), np.ascontiguousarray(negm)


def host_prep(inp, depth=DEPTH, layers=None, pipe_role=None, nblk=SEQ // TB):
    if layers is None:
        layers = list(range(depth))
    L = len(layers)
    sel = np.asarray(layers)
    inp = dict(inp)
    for k_ in ("mix_norm_g", "w_in", "conv_w", "conv_b", "b_igate", "b_fgate", "rel_bias", "mh_norm_g",
               "w_att_proj", "w_mlstm_proj", "w_out", "ffn_norm_g", "w_up", "w_down"):
        inp[k_] = np.asarray(inp[k_], np.float32)[sel]
    w_in = inp["w_in"]
    tiles = []
    wifs = []
    for l in range(L):
        for g in range(7):
            tiles.append(_fm(w_in[l], np.arange(512 * g, 512 * g + 512)))
        w_att = np.asarray(inp["w_att_proj"][l], np.float32)
        w_ml = np.asarray(inp["w_mlstm_proj"][l], np.float32)

        def pp_tile(u):
            parts = []
            for tt_ in range(2):
                t = 2 * u + tt_
                for w in (w_att, w_ml):
                    sub = w[:, t * 256:(t + 1) * 256]
                    parts.append(sub.reshape(4, 128, 256).transpose(1, 0, 2).reshape(128, 1024))
            return np.ascontiguousarray(np.concatenate(parts, axis=1))

        def gagm_tile(t):
            cols = np.concatenate([GA0 + 256 * t + np.arange(256), GM0 + 256 * t + np.arange(256)])
            return _fm(w_in[l], cols)

        tiles.append(pp_tile(0))
        tiles.append(gagm_tile(0))
        tiles.append(gagm_tile(1))
        tiles.append(pp_tile(1))
        tiles.append(gagm_tile(2))
        tiles.append(gagm_tile(3))
        w_out = np.asarray(inp["w_out"][l], np.float32)
        for t in range(2):
            tiles.append(_fm(w_out, np.arange(512 * t, 512 * t + 512)))
        w_up = np.asarray(inp["w_up"][l], np.float32)
        for t in range(8):
            tiles.append(_fm(w_up, np.arange(512 * t, 512 * t + 512)))
        w_down = np.asarray(inp["w_down"][l], np.float32)
        for c in range(8):
            sub = w_down[:, c * 128:(c + 1) * 128]
            tiles.append(np.ascontiguousarray(sub.reshape(32, 128, 128).transpose(1, 0, 2).reshape(128, 4096)))
        wifs.append(_fm(w_in[l], np.arange(3584, 3592)))
    wstream = np.stack(tiles, axis=0)
    assert wstream.shape == (L * NW_TILES, 128, 4096)
    wif = np.concatenate(wifs, axis=1)

    kp = np.arange(128)[:, None]
    u = np.arange(640)[None, :]
    rel_idx = np.clip(u - kp, -256, 256) + 256
    rb = np.asarray(inp["rel_bias"], np.float32)[:L]
    bg = rb[:, :, rel_idx]
    biasg = np.ascontiguousarray(bg.transpose(2, 0, 1, 3).reshape(128, L * 8 * 640))

    def fmv(v):
        v = np.asarray(v, np.float32)
        k = v.shape[-1] // 128
        return np.moveaxis(v.reshape(v.shape[:-1] + (k, 128)), -1, 0)

    g1 = fmv(inp["mix_norm_g"][:L]).reshape(128, L * 8)
    g2 = fmv(inp["ffn_norm_g"][:L]).reshape(128, L * 8)
    gf = fmv(inp["final_norm_g"]).reshape(128, 8)
    cw = fmv(inp["conv_w"][:L]).reshape(128, L * 32)
    cb = fmv(inp["conv_b"][:L]).reshape(128, L * 8)
    mh = fmv(inp["mh_norm_g"][:L]).reshape(128, L * 4)
    bi = np.broadcast_to(np.asarray(inp["b_igate"], np.float32)[:L][None, :, None, :], (128, L, 4, 4)).reshape(128, L * 16)
    bf = np.broadcast_to(np.asarray(inp["b_fgate"], np.float32)[:L][None, :, None, :], (128, L, 4, 4)).reshape(128, L * 16)
    epsc = np.full((128, 1), EPS, np.float32)
    plist = [g1, g2, gf, cw, cb, mh, bi, bf, epsc]
    if pipe_role is not None:
        nstep = nblk + 1
        fb = np.full((128, 1), float(pipe_role), np.float32)
        smask = np.zeros((128, nstep), np.float32)
        smask[:, :pipe_role + 1] = NEG
        valid = np.ones((128, nstep), np.float32)
        valid[:, :pipe_role] = 0.0
        plist += [fb, smask, valid]
    params = np.ascontiguousarray(np.concatenate(plist, axis=1).astype(np.float32))
    cst, negm = host_constants()
    return {"wstream": wstream, "wif": np.ascontiguousarray(wif), "biasg": biasg, "cst": cst, "negm": negm,
            "params": params}


_NC_CACHE = {}


PIPE = True


def kernel(**inputs):
    x = np.asarray(inputs["x"], np.float32)
    B, S_, D = x.shape
    nblk = S_ // TB
    n_cores = 8
    key = (nblk, DEPTH, PIPE)
    if key not in _NC_CACHE:
        _NC_CACHE[key] = build_program(nblk, DEPTH, pipe=PIPE)
    nc = _NC_CACHE[key]
    in_maps = []
    if PIPE:
        assert DEPTH == 2 and B * 2 == n_cores
        roles = [host_prep(inputs, DEPTH, layers=[r], pipe_role=r, nblk=nblk) for r in range(2)]
        zeros = np.zeros((D, S_), np.float32)
        for c in range(n_cores):
            m = dict(roles[c % 2])
            m["x"] = np.ascontiguousarray(x[c // 2].T) if c % 2 == 0 else zeros
            in_maps.append(m)
        res = run_bass_kernel_spmd(nc, in_maps, core_ids=list(range(n_cores)))
        out = np.stack([np.ascontiguousarray(np.asarray(res.results[2 * b + 1]["out"], np.float32).T)
                        for b in range(B)], axis=0)
        return out
    shared = host_prep(inputs, DEPTH)
    for c in range(n_cores):
        m = dict(shared)
        m["x"] = np.ascontiguousarray(x[c % B])
        in_maps.append(m)
    res = run_bass_kernel_spmd(nc, in_maps, core_ids=list(range(n_cores)))
    out = np.stack([np.asarray(res.results[b]["out"], np.float32) for b in range(B)], axis=0)
    return out
```

```python
import math
from contextlib import ExitStack

import numpy as np
import concourse.bass as bass
import concourse.mybir as mybir
from concourse.bass_utils import run_bass_kernel_spmd

F32 = mybir.dt.float32
BF16 = mybir.dt.bfloat16
AF = mybir.ActivationFunctionType
ALU = mybir.AluOpType

D_MODEL = 1024
BATCH = 4
SEQ = 4096
DEPTH = 2
TB = 512
NW_TILES = 31
NWS = 6
NTF = 8
NTB = 4
EPS = 1e-6
NEG = -30000.0
GA0 = 3592
GM0 = 4616
EPOCH = 8000
SAME_ENGINE_SYNC = "raw"
LIST_SCHED = True
GATE_PREFETCH = True
ATT_MUL_ENG = "vector"
NH = 4
XENG_LAT = 200.0
import os as _os
KNOP = set(int(v) for v in _os.environ.get('KNOP', '').split(',') if v)


def _is_cc(key):
    return isinstance(key, tuple) and key[0] == "cc"


class _Op:
    __slots__ = ("eng", "fn", "deps", "dma_sem", "token", "needs_inc", "idx", "raw", "cost", "alldeps")


class Sched:
    def __init__(self):
        self.ops = []
        self.lastw = {}
        self.readers = {}

    def op(self, eng, fn, reads=(), writes=(), dma_sem=None, cost=300.0):
        o = _Op()
        o.cost = cost
        o.eng = eng
        o.fn = fn
        o.dma_sem = dma_sem
        o.idx = len(self.ops)
        o.needs_inc = dma_sem is not None
        o.token = None
        deps = set()
        raw = set()
        for r in reads:
            if r in self.lastw:
                deps.add(self.lastw[r])
                raw.add(self.lastw[r])
        o.raw = raw
        for w in writes:
            if w in self.lastw:
                deps.add(self.lastw[w])
            for rd in self.readers.get(w, ()):
                deps.add(rd)
        deps.discard(o.idx)
        o.deps = deps
        for r in reads:
            self.readers.setdefault(r, set()).add(o.idx)
        for w in writes:
            self.lastw[w] = o.idx
            self.readers[w] = set()
        self.ops.append(o)
        return o

    def finalize(self):
        import os
        if os.environ.get("KTRUNC"):
            self.ops = self.ops[:int(os.environ["KTRUNC"])]
        if os.environ.get("KSKIP"):
            a, b = [int(v) for v in os.environ["KSKIP"].split(",")]
            kept = [o for o in self.ops if not (a <= o.idx < b)]
            remap = {o.idx: i for i, o in enumerate(kept)}
            for o in kept:
                o.deps = set(remap[d] for d in o.deps if d in remap)
                o.raw = set(remap[d] for d in o.raw if d in remap)
                o.idx = remap[o.idx]
            self.ops = kept
        ops = self.ops
        for o in ops:
            o.alldeps = set(o.deps)
        self.order = self.list_schedule() if LIST_SCHED else None
        for o in ops:
            keep = set()
            for d in o.deps:
                do = ops[d]
                if do.eng == o.eng and do.dma_sem is None and o.dma_sem is None:
                    if o.eng == "tensor" or not SAME_ENGINE_SYNC:
                        continue
                    if SAME_ENGINE_SYNC == "raw" and d not in o.raw:
                        continue
                keep.add(d)
            o.deps = keep
            for d in keep:
                ops[d].needs_inc = True

    def list_schedule(self):
        import heapq
        ops = self.ops
        n = len(ops)
        ndep = [len(o.alldeps) for o in ops]
        users = [[] for _ in range(n)]
        for o in ops:
            for d in o.alldeps:
                users[d].append(o.idx)
        finish = [0.0] * n
        ready_t = [0.0] * n
        engs = ["tensor", "vector", "scalar", "gpsimd", "sync"]
        pend = {e: [] for e in engs}
        avail = {e: [] for e in engs}
        t_free = {e: 0.0 for e in engs}
        for o in ops:
            if ndep[o.idx] == 0:
                heapq.heappush(pend[o.eng], (0.0, o.idx))
        order = []
        done = 0
        while done < n:
            best = None
            for e in engs:
                while pend[e] and pend[e][0][0] <= t_free[e]:
                    rt, i = heapq.heappop(pend[e])
                    heapq.heappush(avail[e], i)
                if avail[e]:
                    cand = (t_free[e], avail[e][0], e, True)
                elif pend[e]:
                    cand = (pend[e][0][0], pend[e][0][1], e, False)
                else:
                    continue
                if best is None or cand[:2] < best[:2]:
                    best = cand
            st, i, e, from_avail = best
            if from_avail:
                heapq.heappop(avail[e])
            else:
                heapq.heappop(pend[e])
            o = ops[i]
            fin = st + o.cost
            t_free[e] = fin
            finish[i] = fin
            order.append(i)
            done += 1
            for u in users[i]:
                lat = 0.0 if ops[u].eng == e else XENG_LAT
                if fin + lat > ready_t[u]:
                    ready_t[u] = fin + lat
                ndep[u] -= 1
                if ndep[u] == 0:
                    heapq.heappush(pend[ops[u].eng], (ready_t[u], u))
        self.est_total = max(finish) if finish else 0.0
        return order

    def emit(self, nc, es, dma_sems, final_waits):
        ops = self.ops
        seq = [ops[i] for i in self.order] if self.order is not None else list(ops)
        eng_sems = {}
        counters = {}
        dma_counts = {}
        for o in seq:
            if o.dma_sem is not None:
                c = dma_counts.get(o.dma_sem, 0) + (1 if _is_cc(o.dma_sem) else 16)
                dma_counts[o.dma_sem] = c
                o.token = (dma_sems[o.dma_sem], c)
            elif o.needs_inc:
                ep, c = counters.get(o.eng, (0, 0))
                if c >= EPOCH:
                    ep, c = ep + 1, 0
                c += 1
                counters[o.eng] = (ep, c)
                key = (o.eng, ep)
                if key not in eng_sems:
                    eng_sems[key] = es.enter_context(nc.semaphore("p_%s_%d" % (o.eng, ep)))
                o.token = (eng_sems[key], c)
        for o in ops:
            if o.dma_sem in ("setup", "setup_sw"):
                o.token = (dma_sems[o.dma_sem], dma_counts[o.dma_sem])
        by_eng = {}
        for o in seq:
            by_eng.setdefault(o.eng, []).append(o)

        def run(eng_name, e):
            waited = {}
            for o in by_eng.get(eng_name, []):
                need = {}
                for d in o.deps:
                    s, v = ops[d].token
                    k = id(s)
                    if waited.get(k, 0) >= v:
                        continue
                    if k not in need or need[k][1] < v:
                        need[k] = (s, v)
                for k, (s, v) in need.items():
                    e.wait_ge(s, v)
                    waited[k] = v
                if o.idx in KNOP:
                    ins = e.nop()
                else:
                    ins = o.fn(e)
                if o.dma_sem is not None:
                    if _is_cc(o.dma_sem):
                        ins.then_inc(o.token[0])
                    else:
                        ins.then_inc(o.token[0], 16)
                elif o.needs_inc:
                    ins.then_inc(o.token[0], 1)
            mine = []
            for o in by_eng.get(eng_name, []):
                if o.dma_sem is not None and o.dma_sem not in mine:
                    mine.append(o.dma_sem)
            for key in mine:
                e.wait_ge(dma_sems[key], dma_counts[key])

        with nc.Block() as block:
            @block.sync
            def _(e):
                run("sync", e)

            @block.gpsimd
            def _(e):
                run("gpsimd", e)

            @block.tensor
            def _(e):
                run("tensor", e)

            @block.vector
            def _(e):
                run("vector", e)

            @block.scalar
            def _(e):
                run("scalar", e)


def build_program(nblk, depth=DEPTH, pipe=False):
    T = nblk * TB
    nc = bass.Bass("TRN2", target_bir_lowering=False)
    L = 1 if pipe else depth
    nstep = nblk + 1 if pipe else nblk
    if pipe:
        x_d = nc.dram_tensor("x", [D_MODEL, T], F32, kind="ExternalInput").ap()
    else:
        x_d = nc.dram_tensor("x", [T, D_MODEL], F32, kind="ExternalInput").ap()
    w_d = nc.dram_tensor("wstream", [L * NW_TILES, 128, 4096], F32, kind="ExternalInput").ap()
    wif_d = nc.dram_tensor("wif", [128, L * 64], F32, kind="ExternalInput").ap()
    bias_d = nc.dram_tensor("biasg", [128, L * 8 * 640], F32, kind="ExternalInput").ap()
    cst_d = nc.dram_tensor("cst", [128, 5 * 128], F32, kind="ExternalInput").ap()
    negm_d = nc.dram_tensor("negm", [128, 640], F32, kind="ExternalInput").ap()
    NPAR = L * 8 + L * 8 + 8 + L * 32 + L * 8 + L * 4 + L * 16 + L * 16 + 1 + (1 + 2 * (nblk + 1) if pipe else 0)
    par_d = nc.dram_tensor("params", [128, NPAR], F32, kind="ExternalInput").ap()
    if pipe:
        out_d = nc.dram_tensor("out", [D_MODEL, T], F32, kind="ExternalOutput").ap()
    else:
        out_d = nc.dram_tensor("out", [T, D_MODEL], F32, kind="ExternalOutput").ap()

    es = ExitStack()
    with es:
        def sb(name, shape, dt):
            return es.enter_context(nc.sbuf_tensor(name, shape, dt))

        if not pipe:
            io = sb("io", [128, 4, 1024], F32)
        if pipe:
            xl = sb("xl", [128, 8, TB], F32)
            rbuf = sb("rbuf", [128, 8, TB], F32)
            CPP = 8 // NH
            send_d = [[nc.dram_tensor("send%d_%d" % (k, hf), [128, CPP * TB], F32, kind="Internal").ap()
                       for hf in range(NH)] for k in range(nblk)]
            recv_d = [[nc.dram_tensor("recv%d_%d" % (k, hf), [256, CPP * TB], F32, kind="Internal").ap()
                       for hf in range(NH)] for k in range(nblk)]
        else:
            xin = io
        xT = sb("xT", [128, 8, TB], F32)
        xnT = sb("xnT", [128, 8, TB], BF16)
        arena = sb("arena", [128, 32, TB], BF16)
        kh = sb("kh", [128, L * 2 * 4, TB], BF16)
        vh = sb("vh", [128, L * 2 * 4, TB], BF16)
        wb = sb("wb", [128, NWS, 4096], BF16)
        wif = sb("wifs", [128, L * 64], BF16)
        BM = sb("BM", [128, L * 8, 640], BF16)
        negm = sb("negms", [128, 640], BF16)
        S = sb("S", [128, L * 4, 256], F32)
        Sb = sb("Sb", [128, L * 4, 256], BF16)
        halo = sb("halo", [128, L * 8, 4], F32)
        cst = sb("csts", [128, 5, 128], F32)
        identB = sb("identB", [128, 128], BF16)
        cstB = sb("cstB", [128, 3, 128], BF16)
        onesB = sb("onesB", [128, 128], BF16)
        onesF = sb("onesF", [128, 128], F32)
        par = sb("pars", [128, NPAR], F32)
        tf = sb("tf", [128, NTF, 512], F32)
        tbf = sb("tbf", [128, NTB, 512], BF16)
        pre = sb("pre", [128, 2, 516], F32)
        LF = sb("LF", [128, 2, 4, 128], F32)
        gsm = sb("gsm", [128, 8, 16], F32)
        wsm = sb("wsm", [128, 8, 2], F32)
        ps = [es.enter_context(nc.psum_tensor("ps%d" % b, [128, 512], F32)) for b in range(8)]

        identF = cst[:, 0, :]
        onesMD = cst[:, 1, :]
        onesMH = cst[:, 2, :]
        Umat = cst[:, 3, :]
        NEGmat = cst[:, 4, :]
        onesMDb = cstB[:, 0, :]
        onesMHb = cstB[:, 1, :]
        NEGb = cstB[:, 2, :]

        o_g1 = 0
        o_g2 = o_g1 + L * 8
        o_gf = o_g2 + L * 8
        o_cw = o_gf + 8
        o_cb = o_cw + L * 32
        o_mh = o_cb + L * 8
        o_bi = o_mh + L * 4
        o_bf = o_bi + L * 16
        o_eps = o_bf + L * 16
        o_fb = o_eps + 1
        o_sm = o_fb + 1
        o_vd = o_sm + nstep

        def pcol(off):
            return par[:, off:off + 1]

        dma_sems = {}
        extra_sems = ([("snd", k) for k in range(NH)] + [("rcv", k) for k in range(NH)]
                      + [("cc", k) for k in range(NH * nblk)]) if pipe else []
        for k in ["setup", "setup_sw", "io_in", "io_out"] + extra_sems + [("w", s) for s in range(NWS)]:
            dma_sems[k] = es.enter_context(nc.semaphore("d_%s" % (str(k).replace(" ", ""))))

        sch = Sched()
        state = {"ps": 0, "tf": 0, "tb": 0, "w": 0, "alt": 0, "pre": 0, "lf": 0, "wsm": 0}

        def newps(exclude=()):
            b = state["ps"]
            while b in exclude:
                b = (b + 1) % 8
            state["ps"] = (b + 1) % 8
            return b

        def newtf():
            k = state["tf"]
            state["tf"] = (k + 1) % NTF
            return k

        def newtb():
            k = state["tb"]
            state["tb"] = (k + 1) % NTB
            return k

        def fsz(ap):
            n = 1
            for d in ap.shape[1:]:
                n *= int(d)
            return n

        def mm(out, lhsT, rhs, start, stop, r, w):
            c = max(fsz(rhs), 64) / 1.95 + 15.0
            if lhsT.dtype == F32:
                c *= 4.0
            sch.op("tensor", lambda e: e.matmul(out, lhsT, rhs, start=start, stop=stop), r, w, cost=c)

        def tr(out, in_, r, w):
            sch.op("tensor", lambda e: e.transpose(out, in_, identF), list(r) + ["cst"], w, cost=280.0)

        def act(out, in_, func, r, w, bias=None, scale=None):
            kw = {}
            if bias is not None:
                kw["bias"] = bias
            if scale is not None:
                kw["scale"] = scale
            c = fsz(out) / 1.4 + 220.0 + (90.0 if bias is not None and not isinstance(bias, float) else 0.0)
            sch.op("scalar", lambda e: e.activation(out, in_, func, **kw), r, w, cost=c)

        def dcost(ap, mult=1.0):
            return max(fsz(ap), 64) * mult / 0.96 + 90.0

        def tt(out, in0, in1, op, r, w, eng="vector"):
            sch.op(eng, lambda e: e.tensor_tensor(out, in0, in1, op), r, w, cost=dcost(out))

        def ts(out, in0, s1, s2, op0, op1, r, w, eng="vector"):
            if op1 is None:
                sch.op(eng, lambda e: e.tensor_scalar(out, in0, s1, None, op0), r, w, cost=dcost(out))
            else:
                sch.op(eng, lambda e: e.tensor_scalar(out, in0, s1, s2, op0, op1), r, w, cost=dcost(out))

        def stt(out, in0, scalar, in1, op0, op1, r, w):
            sch.op("vector", lambda e: e.scalar_tensor_tensor(out, in0, scalar, in1, op0, op1), r, w,
                   cost=dcost(out))

        def recip(out, in_, r, w):
            sch.op("vector", lambda e: e.reciprocal(out, in_), r, w, cost=dcost(out, 8.0))

        def copy_any(out, in_, r, w):
            state["alt"] ^= 1
            if state["alt"]:
                sch.op("scalar", lambda e: e.copy(out, in_), r, w, cost=fsz(out) / 1.4 + 220.0)
            else:
                sch.op("vector", lambda e: e.tensor_copy(out, in_), r, w, cost=dcost(out))

        def dsetup(eng, out, in_, w):
            sch.op(eng, lambda e: e.dma_start(out=out, in_=in_), [], w,
                   dma_sem="setup_sw" if eng == "gpsimd" else "setup")

        dsetup("sync", cst[:, :, :], cst_d.rearrange("p (k n) -> p k n", k=5), ["cst"])
        dsetup("sync", par[:, :], par_d, ["par"])
        dsetup("gpsimd", wif[:, :], wif_d, ["wif"])
        dsetup("gpsimd", negm[:, :], negm_d, ["negm"])
        for l in range(L):
            for h in range(8):
                dsetup("gpsimd", BM[:, l * 8 + h, :],
                       bias_d[:, (l * 8 + h) * 640:(l * 8 + h + 1) * 640], [("BM", l * 8 + h)])
        sch.op("vector", lambda e: e.memset(S[:, :, :], 0.0), [], ["S%d" % k for k in range(L * 4)])
        sch.op("vector", lambda e: e.memset(Sb[:, :, :], 0.0), [], ["Sb%d" % k for k in range(L * 4)])
        sch.op("vector", lambda e: e.memset(halo[:, :, :], 0.0), [], [("halo", k) for k in range(L * 8)])
        sch.op("vector", lambda e: e.memset(kh[:, :, :], 0.0), [], [("kh", k) for k in range(L * 8)])
        sch.op("vector", lambda e: e.memset(vh[:, :, :], 0.0), [], [("vh", k) for k in range(L * 8)])
        sch.op("vector", lambda e: e.memset(onesB[:, :], 1.0), [], ["onesB"])
        sch.op("vector", lambda e: e.memset(onesF[:, :], 1.0), [], ["onesF"])
        sch.op("vector", lambda e: e.tensor_copy(identB[:, :], identF), ["cst"], ["identB"])
        sch.op("vector", lambda e: e.tensor_copy(cstB[:, 0:2, :], cst[:, 1:3, :]), ["cst"], ["cstB"])
        sch.op("vector", lambda e: e.tensor_copy(cstB[:, 2, :], cst[:, 4, :]), ["cst", "cstB"], ["cstB"])
        for l in range(L):
            for h in range(8):
                k = l * 8 + h
                tt(BM[:, k, :], BM[:, k, :], negm[:, :], ALU.add, [("BM", k), "negm"], [("BM", k)])
                if ATT_MUL_ENG:
                    act(BM[:, k, :], BM[:, k, :], AF.Exp, [("BM", k)], [("BM", k)])

        def next_w(l, k_expected):
            n = state["w"]
            state["w"] = n + 1
            assert n % NW_TILES == k_expected and n // NW_TILES % L == l, (n, l, k_expected)
            slot = n % NWS
            src = w_d[(n % (L * NW_TILES)), :, :]
            gate = []
            if pipe and GATE_PREFETCH and n // NW_TILES >= 1 and n % NW_TILES < NWS:
                gate = [("recv", n // NW_TILES - 1, min(n % NW_TILES, NH - 1))]
            sch.op("gpsimd", lambda e: e.dma_start(out=wb[:, slot, :], in_=src), gate, [("w", slot)],
                   dma_sem=("w", slot), cost=9000.0)
            return slot

        def wview(slot, kc, c0, n, kcn=8):
            width = 4096 // kcn
            return wb[:, slot, kc * width + c0: kc * width + c0 + n]

        def rstd_from(P_ap, kr, rd):
            act(tf[:, kr, :], P_ap, AF.Ln, rd + ["par"], [("tf", kr)], bias=pcol(o_eps))
            act(tf[:, kr, :], tf[:, kr, :], AF.Exp, [("tf", kr)], [("tf", kr)], scale=-0.5)

        def norm_stats():
            P = newps()
            for c in range(8):
                k = newtb()
                act(tbf[:, k, :], xT[:, c, :], AF.Square, [("xT", c)], [("tb", k)])
                mm(ps[P][:, :], onesMDb, tbf[:, k, :], c == 0, c == 7, [("tb", k), "cstB"], [("ps", P)])
            kr = newtf()
            rstd_from(ps[P][:, :], kr, [("ps", P)])
            return kr

        def rmsnorm(gcol0, dst_bf16):
            kr = norm_stats()
            for c in range(8):
                if dst_bf16:
                    stt(xnT[:, c, :], xT[:, c, :], pcol(gcol0 + c), tf[:, kr, :], ALU.mult, ALU.mult,
                        [("xT", c), ("tf", kr), "par"], [("xn", c)])
                else:
                    stt(xT[:, c, :], xT[:, c, :], pcol(gcol0 + c), tf[:, kr, :], ALU.mult, ALU.mult,
                        [("xT", c), ("tf", kr), "par"], [("xT", c)])

        def pg(n):
            return ("pg", n)

        SC_M = 1.0 / math.sqrt(128.0)

        def layer(l, j):
            parity = j % 2
            kcur = lambda p: (l * 2 + parity) * 4 + p
            kprev = lambda p: (l * 2 + 1 - parity) * 4 + p

            rmsnorm(o_g1 + l * 8, True)
            xn_all = [("xn", c) for c in range(8)]

            for g in range(2):
                slot = next_w(l, g)
                for m in range(4):
                    P = newps()
                    for kc in range(8):
                        mm(ps[P][:, :], wview(slot, kc, m * 128, 128), xnT[:, kc, :], kc == 0, kc == 7,
                           [("w", slot), ("xn", kc)], [("ps", P)])
                    if g == 0:
                        copy_any(arena[:, m, :], ps[P][:, :], [("ps", P)], [pg(m)])
                    else:
                        copy_any(kh[:, kcur(m), :], ps[P][:, :], [("ps", P)], [("kh", kcur(m))])
            slot = next_w(l, 2)
            for i in range(4):
                P = newps()
                for kc in range(8):
                    mm(ps[P][:, :], xnT[:, kc, i * 128:(i + 1) * 128], wview(slot, kc, 0, 512), kc == 0, kc == 7,
                       [("w", slot), ("xn", kc)], [("ps", P)])
                copy_any(vh[:, kcur(i), :], ps[P][:, :], [("ps", P)], [("vh", kcur(i))])

            for p in range(4):
                O = newps()
                Dn = newps()
                for e_ in range(2):
                    h = 2 * p + e_
                    rows = slice(e_ * 64, (e_ + 1) * 64)
                    rlist = [4] + [r for r in range(8) if (j > 0 or r >= 4 or pipe) and r != 4]
                    for idx, r in enumerate(rlist):
                        kidx = kprev(p) if r < 4 else kcur(p)
                        vidx = kprev(r % 4) if r < 4 else kcur(r % 4)
                        qlo = max(8, 2 * r) * 64 - 512
                        qhi = (min(15, 2 * r + 9) + 1) * 64 - 512
                        n = qhi - qlo
                        u0 = 512 + qlo - 128 * r
                        sc = newps(exclude=(O, Dn))
                        mm(ps[sc][:, 0:n], kh[rows, kidx, (r % 4) * 128:(r % 4 + 1) * 128], arena[rows, p, qlo:qhi],
                           True, True, [("kh", kidx), pg(p)], [("ps", sc)])
                        k1 = newtf()
                        k2 = newtb()
                        if ATT_MUL_ENG:
                            if pipe and r < 4:
                                act(tf[:, k1, 0:n], ps[sc][:, 0:n], AF.Exp, [("ps", sc), "par"], [("tf", k1)],
                                    bias=pcol(o_sm + j), scale=0.125)
                            else:
                                act(tf[:, k1, 0:n], ps[sc][:, 0:n], AF.Exp, [("ps", sc)], [("tf", k1)], scale=0.125)
                            sch.op(ATT_MUL_ENG, lambda e, k1=k1, k2=k2, n=n, u0=u0, lh=l * 8 + h: e.tensor_tensor(
                                tbf[:, k2, 0:n], tf[:, k1, 0:n], BM[:, lh, u0:u0 + n], ALU.mult),
                                [("tf", k1), ("BM", l * 8 + h)], [("tb", k2)],
                                cost=(n / 0.5 + 500.0) if ATT_MUL_ENG == "gpsimd" else (n / 0.96 + 90.0))
                        else:
                            stt(tf[:, k1, 0:n], ps[sc][:, 0:n], 0.125, BM[:, l * 8 + h, u0:u0 + n], ALU.mult, ALU.add,
                                [("ps", sc), ("BM", l * 8 + h)], [("tf", k1)])
                            if pipe and r < 4:
                                act(tbf[:, k2, 0:n], tf[:, k1, 0:n], AF.Exp, [("tf", k1), "par"], [("tb", k2)],
                                    bias=pcol(o_sm + j))
                            else:
                                act(tbf[:, k2, 0:n], tf[:, k1, 0:n], AF.Exp, [("tf", k1)], [("tb", k2)])
                        first = idx == 0
                        last = idx == len(rlist) - 1
                        mm(ps[O][rows, qlo:qhi], vh[:, vidx, h * 64:(h + 1) * 64], tbf[:, k2, 0:n], first, last,
                           [("vh", vidx), ("tb", k2)], [("ps", O)])
                        mm(ps[Dn][rows, qlo:qhi], onesB[:, 0:64], tbf[:, k2, 0:n], first, last,
                           ["onesB", ("tb", k2)], [("ps", Dn)])
                kr = newtf()
                act(tf[:, kr, :], ps[Dn][:, :], AF.Ln, [("ps", Dn)], [("tf", kr)])
                act(tf[:, kr, :], tf[:, kr, :], AF.Exp, [("tf", kr)], [("tf", kr)], scale=-1.0)
                tt(arena[:, 24 + p, :], ps[O][:, :], tf[:, kr, :], ALU.mult, [("ps", O), ("tf", kr)], [pg(24 + p)])

            for g in (3, 4):
                slot = next_w(l, g)
                for m in range(4):
                    ch = (g - 3) * 4 + m
                    P = newps()
                    for kc in range(8):
                        mm(ps[P][:, :], wview(slot, kc, m * 128, 128), xnT[:, kc, :], kc == 0, kc == 7,
                           [("w", slot), ("xn", kc)], [("ps", P)])
                    pk = state["pre"]
                    state["pre"] ^= 1
                    hk = l * 8 + ch
                    copy_any(pre[:, pk, 3:515], ps[P][:, :], [("ps", P)], [("pre", pk)])
                    sch.op("vector", lambda e, pk=pk, hk=hk: e.tensor_copy(pre[:, pk, 0:3], halo[:, hk, 0:3]),
                           [("halo", hk)], [("pre", pk)])
                    sch.op("vector", lambda e, pk=pk, hk=hk: e.tensor_copy(halo[:, hk, 0:3], pre[:, pk, 512:515]),
                           [("pre", pk)], [("halo", hk)])
                    ka = newtf()
                    cw = lambda tap: pcol(o_cw + l * 32 + tap * 8 + ch)
                    ts(tf[:, ka, :], pre[:, pk, 0:512], cw(0), None, ALU.mult, None,
                       [("pre", pk), "par"], [("tf", ka)])
                    for tap in (1, 2, 3):
                        stt(tf[:, ka, :], pre[:, pk, tap:tap + 512], cw(tap), tf[:, ka, :], ALU.mult, ALU.add,
                            [("pre", pk), "par", ("tf", ka)], [("tf", ka)])
                    act(arena[:, 4 + ch, :], tf[:, ka, :], AF.Silu, [("tf", ka), "par"], [pg(4 + ch)],
                        bias=pcol(o_cb + l * 8 + ch))
            slot = next_w(l, 5)
            for i in range(4):
                P = newps()
                for kc in range(8):
                    mm(ps[P][:, :], xnT[:, kc, i * 128:(i + 1) * 128], wview(slot, kc, 0, 512), kc == 0, kc == 7,
                       [("w", slot), ("xn", kc)], [("ps", P)])
                copy_any(arena[:, 12 + i, :], ps[P][:, :], [("ps", P)], [pg(12 + i)])
            slot = next_w(l, 6)
            for m in range(4):
                P = newps()
                for kc in range(8):
                    mm(ps[P][:, :], wview(slot, kc, m * 128, 128), xnT[:, kc, :], kc == 0, kc == 7,
                       [("w", slot), ("xn", kc)], [("ps", P)])
                act(arena[:, 20 + m, :], ps[P][:, :], AF.Sigmoid, [("ps", P)], [pg(20 + m)])
            G = newps()
            for i in range(4):
                for kc in range(8):
                    mm(ps[G][:, i * 8:(i + 1) * 8], xnT[:, kc, i * 128:(i + 1) * 128],
                       wif[:, l * 64 + kc * 8: l * 64 + kc * 8 + 8], kc == 0, kc == 7,
                       ["wif", ("xn", kc)], [("ps", G)])
            G3 = ps[G][:, 0:32].rearrange("p (i c) -> p i c", c=8)
            ig = gsm[:, 0, :]
            zf = gsm[:, 1, :]
            ef = gsm[:, 2, :]
            spv = gsm[:, 3, :]
            logf = gsm[:, 4, :]
            biasv = gsm[:, 5, :]
            v3 = lambda ap: ap.rearrange("p (i c) -> p i c", c=4)
            tt(v3(ig), G3[:, :, 0:4], v3(par[:, o_bi + l * 16:o_bi + l * 16 + 16]), ALU.add,
               [("ps", G), "par"], ["g_ig"])
            tt(v3(zf), G3[:, :, 4:8], v3(par[:, o_bf + l * 16:o_bf + l * 16 + 16]), ALU.add,
               [("ps", G), "par"], ["g_zf"])
            act(ef, zf, AF.Exp, ["g_zf"], ["g_ef"], scale=-1.0)
            act(spv, ef, AF.Ln, ["g_ef"], ["g_sp"], bias=1.0)
            ts(logf, spv, -1.0, None, ALU.mult, None, ["g_sp"], ["g_lf"])
            Cm = newps()
            mm(ps[Cm][:, 0:16], Umat, logf, True, True, ["cst", "g_lf"], [("ps", Cm)])
            tt(biasv, ig, ps[Cm][:, 0:16], ALU.subtract, ["g_ig", ("ps", Cm)], ["g_bv"])

            for i in range(4):
                lfk = state["lf"]
                state["lf"] ^= 1
                for h in range(4):
                    col = i * 4 + h
                    ts(LF[:, lfk, h, :], onesF[:, :], logf[:, col:col + 1], None, ALU.mult, None,
                       ["onesF", "g_lf"], [("LF", lfk, h)])
                tsl = slice(i * 128, (i + 1) * 128)
                for h in range(4):
                    col = i * 4 + h
                    sk = l * 4 + h
                    kT = arena[:, 8 + h, tsl]
                    qT_ = arena[:, 4 + h, tsl]
                    vt = arena[:, 12 + i, h * 128:(h + 1) * 128]
                    A = newps()
                    mm(ps[A][:, 0:128], kT, qT_, True, True, [pg(8 + h), pg(4 + h)], [("ps", A)])
                    mm(ps[A][:, 128:256], LF[:, lfk, h, :], Umat, True, True, [("LF", lfk, h), "cst"], [("ps", A)])
                    mm(ps[A][:, 256:384], LF[:, lfk, h, :], Umat, True, False, [("LF", lfk, h), "cst"], [("ps", A)])
                    mm(ps[A][:, 256:384], identB[:, :], NEGb, False, True, ["identB", "cstB"], [("ps", A)])
                    ka = newtf()
                    act(tf[:, ka, 0:128], ps[A][:, 256:384], AF.Exp, [("ps", A), "g_bv"], [("tf", ka)],
                        bias=biasv[:, col:col + 1])
                    act(tf[:, ka, 128:256], ps[A][:, 128:256], AF.Exp, [("ps", A)], [("tf", ka)])
                    wk = state["wsm"]
                    state["wsm"] = (wk + 1) % 8
                    if pipe:
                        ts(wsm[:, wk, 0:1], tf[:, ka, 127:128], pcol(o_vd + j), None, ALU.mult, None,
                           [("tf", ka), "par"], [("wsm", wk)])
                    else:
                        sch.op("vector", lambda e, wk=wk, ka=ka: e.tensor_copy(wsm[:, wk, 0:1], tf[:, ka, 127:128]),
                               [("tf", ka)], [("wsm", wk)])
                    sch.op("vector", lambda e, wk=wk, ka=ka: e.tensor_copy(wsm[:, wk, 1:2], tf[:, ka, 255:256]),
                           [("tf", ka)], [("wsm", wk)])
                    kb = newtb()
                    stt(tbf[:, kb, 0:128], ps[A][:, 0:128], SC_M, tf[:, ka, 0:128], ALU.mult, ALU.mult,
                        [("ps", A), ("tf", ka)], [("tb", kb)])
                    stt(tbf[:, kb, 128:256], qT_, SC_M, tf[:, ka, 128:256], ALU.mult, ALU.mult,
                        [pg(4 + h), ("tf", ka)], [("tb", kb)])
                    AT = tbf[:, kb, 0:128]
                    qsT = tbf[:, kb, 128:256]
                    B = newps()
                    mm(ps[B][:, 0:128], vt, AT, True, False, [pg(12 + i), ("tb", kb)], [("ps", B)])
                    mm(ps[B][:, 0:128], Sb[:, sk, 0:128], qsT, False, True, ["Sb%d" % sk, ("tb", kb)], [("ps", B)])
                    mm(ps[B][:, 128:256], onesB[:, :], AT, True, False, ["onesB", ("tb", kb)], [("ps", B)])
                    mm(ps[B][:, 128:256], Sb[:, sk, 128:256], qsT, False, True, ["Sb%d" % sk, ("tb", kb)],
                       [("ps", B)])
                    kc_ = newtf()
                    act(tf[:, kc_, 0:128], ps[B][:, 128:256], AF.Abs, [("ps", B)], [("tf", kc_)])
                    ts(tf[:, kc_, 0:128], tf[:, kc_, 0:128], 1.0, None, ALU.max, None, [("tf", kc_)], [("tf", kc_)])
                    recip(tf[:, kc_, 0:128], tf[:, kc_, 0:128], [("tf", kc_)], [("tf", kc_)])
                    tt(tf[:, kc_, 128:256], ps[B][:, 0:128], tf[:, kc_, 0:128], ALU.mult,
                       [("ps", B), ("tf", kc_)], [("tf", kc_)])
                    act(tbf[:, kb, 384:512], tf[:, kc_, 128:256], AF.Square, [("tf", kc_)], [("tb", kb)])
                    mm(ps[B][:, 256:384], onesMHb, tbf[:, kb, 384:512], True, True, ["cstB", ("tb", kb)], [("ps", B)])
                    act(tf[:, kc_, 384:512], ps[B][:, 256:384], AF.Ln, [("ps", B), "par"], [("tf", kc_)],
                        bias=pcol(o_eps))
                    act(tf[:, kc_, 384:512], tf[:, kc_, 384:512], AF.Exp, [("tf", kc_)], [("tf", kc_)], scale=-0.5)
                    stt(tf[:, kc_, 0:128], tf[:, kc_, 128:256], pcol(o_mh + l * 4 + h), tf[:, kc_, 384:512],
                        ALU.mult, ALU.mult, [("tf", kc_), "par"], [("tf", kc_)])
                    tt(arena[:, 28 + h, tsl], tf[:, kc_, 0:128], arena[:, 20 + h, tsl], ALU.mult,
                       [("tf", kc_), pg(20 + h)], [pg(28 + h)])
                    C = newps()
                    mm(ps[C][:, 0:128], kT, identB[:, :], True, True, [pg(8 + h), "identB"], [("ps", C)])
                    act(tbf[:, kb, 256:384], ps[C][:, 0:128], AF.Copy, [("ps", C), ("wsm", wk)], [("tb", kb)],
                        scale=wsm[:, wk, 0:1])
                    kw_ = tbf[:, kb, 256:384]
                    mm(ps[C][:, 128:256], kw_, vt, True, True, [("tb", kb), pg(12 + i)], [("ps", C)])
                    mm(ps[C][:, 256:384], kw_, onesB[:, :], True, True, [("tb", kb), "onesB"], [("ps", C)])
                    stt(S[:, sk, :], S[:, sk, :], wsm[:, wk, 1:2], ps[C][:, 128:384], ALU.mult, ALU.add,
                        ["S%d" % sk, ("wsm", wk), ("ps", C)], ["S%d" % sk])
                    sch.op("scalar", lambda e, sk=sk: e.copy(Sb[:, sk, :], S[:, sk, :]), ["S%d" % sk], ["Sb%d" % sk])

            gidx = {0: 8, 1: 9, 2: 11, 3: 12}
            spp = None
            for t in range(4):
                if t % 2 == 0:
                    spp = next_w(l, 7 if t == 0 else 10)
                sg = next_w(l, gidx[t])
                for cc in range(2):
                    c = 2 * t + cc
                    GA = newps()
                    for kc in range(8):
                        mm(ps[GA][:, :], wview(sg, kc, cc * 128, 128), xnT[:, kc, :], kc == 0, kc == 7,
                           [("w", sg), ("xn", kc)], [("ps", GA)])
                    GM = newps()
                    for kc in range(8):
                        mm(ps[GM][:, :], wview(sg, kc, 256 + cc * 128, 128), xnT[:, kc, :], kc == 0, kc == 7,
                           [("w", sg), ("xn", kc)], [("ps", GM)])
                    pbase = (t % 2) * 2048 + cc * 128
                    PA = newps()
                    for kc in range(4):
                        o_ = pbase + kc * 256
                        mm(ps[PA][:, :], wb[:, spp, o_:o_ + 128], arena[:, 24 + kc, :], kc == 0, kc == 3,
                           [("w", spp), pg(24 + kc)], [("ps", PA)])
                    PM = newps()
                    for kc in range(4):
                        o_ = pbase + 1024 + kc * 256
                        mm(ps[PM][:, :], wb[:, spp, o_:o_ + 128], arena[:, 28 + kc, :], kc == 0, kc == 3,
                           [("w", spp), pg(28 + kc)], [("ps", PM)])
                    k1 = newtf()
                    k2 = newtf()
                    act(tf[:, k1, :], ps[GA][:, :], AF.Sigmoid, [("ps", GA)], [("tf", k1)])
                    act(tf[:, k2, :], ps[GM][:, :], AF.Sigmoid, [("ps", GM)], [("tf", k2)])
                    tt(tf[:, k1, :], ps[PA][:, :], tf[:, k1, :], ALU.mult, [("ps", PA), ("tf", k1)], [("tf", k1)])
                    tt(tf[:, k2, :], ps[PM][:, :], tf[:, k2, :], ALU.mult, [("ps", PM), ("tf", k2)], [("tf", k2)])
                    tt(arena[:, 4 + c, :], tf[:, k1, :], tf[:, k2, :], ALU.add, [("tf", k1), ("tf", k2)], [pg(4 + c)])
            for t in range(2):
                so = next_w(l, 13 + t)
                for m in range(4):
                    c = 4 * t + m
                    P = newps()
                    for kc in range(8):
                        mm(ps[P][:, :], wview(so, kc, m * 128, 128), arena[:, 4 + kc, :], kc == 0, kc == 7,
                           [("w", so), pg(4 + kc)], [("ps", P)])
                    tt(xT[:, c, :], xT[:, c, :], ps[P][:, :], ALU.add, [("xT", c), ("ps", P)], [("xT", c)])
            rmsnorm(o_g2 + l * 8, True)
            for t in range(8):
                su = next_w(l, 15 + t)
                for m in range(4):
                    f = 4 * t + m
                    P = newps()
                    for kc in range(8):
                        mm(ps[P][:, :], wview(su, kc, m * 128, 128), xnT[:, kc, :], kc == 0, kc == 7,
                           [("w", su), ("xn", kc)], [("ps", P)])
                    k1 = newtf()
                    act(tf[:, k1, :], ps[P][:, :], AF.Relu, [("ps", P)], [("tf", k1)])
                    tt(arena[:, f, :], ps[P][:, :], tf[:, k1, :], ALU.mult, [("ps", P), ("tf", k1)], [pg(f)])
            for c in range(8):
                sd = next_w(l, 23 + c)
                P = newps()
                for f in range(32):
                    mm(ps[P][:, :], wview(sd, f, 0, 128, kcn=32), arena[:, f, :], f == 0, f == 31,
                       [("w", sd), pg(f)], [("ps", P)])
                tt(xT[:, c, :], xT[:, c, :], ps[P][:, :], ALU.add, [("xT", c), ("ps", P)], [("xT", c)])

        def load_x(jb):
            src = x_d[jb * TB:(jb + 1) * TB, :].rearrange("(i p) d -> p i d", p=128)
            sch.op("sync", lambda e, src=src: e.dma_start(out=xin[:, :, :], in_=src), [],
                   [("xin", i) for i in range(4)], dma_sem="io_in", cost=8000.0)
            for c in range(8):
                P = newps()
                for i in range(4):
                    tr(ps[P][:, i * 128:(i + 1) * 128], xin[:, i, c * 128:(c + 1) * 128], [("xin", i)], [("ps", P)])
                copy_any(xT[:, c, :], ps[P][:, :], [("ps", P)], [("xT", c)])

        def store_out(jb, srcT, srcname):
            for i in range(4):
                for half in range(2):
                    P = newps()
                    for cc in range(4):
                        c = half * 4 + cc
                        tr(ps[P][:, cc * 128:(cc + 1) * 128], srcT[:, c, i * 128:(i + 1) * 128], [(srcname, c)],
                           [("ps", P)])
                    copy_any(io[:, i, half * 512:(half + 1) * 512], ps[P][:, :], [("ps", P)], [("io", i)])
            dst = out_d[jb * TB:(jb + 1) * TB, :].rearrange("(i p) d -> p i d", p=128)
            sch.op("sync", lambda e, dst=dst: e.dma_start(out=dst, in_=io[:, :, :]),
                   [("io", i) for i in range(4)], [], dma_sem="io_out", cost=8000.0)

        if not pipe:
            for j in range(nblk):
                load_x(j)
                for l in range(L):
                    layer(l, j)
                rmsnorm(o_gf, False)
                store_out(j, xT, "xT")
        else:
            for st in range(nstep):
                jb = min(st, nblk - 1)
                xsrc = x_d[:, jb * TB:(jb + 1) * TB].rearrange("(c p) t -> p c t", p=128)
                sch.op("sync", lambda e, xsrc=xsrc: e.dma_start(out=xl[:, :, :], in_=xsrc), [],
                       [("xl", c) for c in range(8)], dma_sem="io_in", cost=8000.0)
                if st == 0:
                    for c in range(8):
                        copy_any(xT[:, c, :], xl[:, c, :], [("xl", c)], [("xT", c)])
                else:
                    for hf in range(NH):
                        rsrc = recv_d[st - 1][hf][0:128, :].rearrange("p (c t) -> p c t", c=CPP)
                        sch.op("sync", lambda e, rsrc=rsrc, hf=hf: e.dma_start(out=rbuf[:, CPP * hf:CPP * hf + CPP, :], in_=rsrc),
                               [("recv", st - 1, hf)], [("rb", c) for c in range(CPP * hf, CPP * hf + CPP)], dma_sem=("rcv", hf),
                               cost=2000.0 + 750.0 * CPP)
                        for c in range(CPP * hf, CPP * hf + CPP):
                            stt(xT[:, c, :], rbuf[:, c, :], pcol(o_fb), xl[:, c, :], ALU.mult, ALU.add,
                                [("rb", c), ("xl", c), "par"], [("xT", c)])
                layer(0, st)
                if st < nblk:
                    for hf in range(NH):
                        sdst = send_d[st][hf].rearrange("p (c t) -> p c t", c=CPP)
                        sch.op("sync", lambda e, sdst=sdst, hf=hf: e.dma_start(out=sdst, in_=xT[:, CPP * hf:CPP * hf + CPP, :]),
                               [("xT", c) for c in range(CPP * hf, CPP * hf + CPP)], [("send", st, hf)], dma_sem=("snd", hf),
                               cost=2000.0 + 750.0 * CPP)
                        sch.op("gpsimd", lambda e, st=st, hf=hf: e.collective_compute(
                            "AllGather", ALU.bypass, replica_groups=[[0, 1], [2, 3], [4, 5], [6, 7]],
                            ins=[send_d[st][hf]], outs=[recv_d[st][hf]]),
                            [("send", st, hf)], [("recv", st, hf), "cc_serial"], dma_sem=("cc", NH * st + hf),
                            cost=3000.0)
                if st >= 1:
                    kr = norm_stats()
                    for c in range(8):
                        stt(rbuf[:, c, :], xT[:, c, :], pcol(o_gf + c), tf[:, kr, :], ALU.mult, ALU.mult,
                            [("xT", c), ("tf", kr), "par"], [("rb", c)])
                    odst = out_d[:, (st - 1) * TB:st * TB].rearrange("(c p) t -> p c t", p=128)
                    sch.op("sync", lambda e, odst=odst: e.dma_start(out=odst, in_=rbuf[:, :, :]),
                           [("rb", c) for c in range(8)], [], dma_sem="io_out", cost=8000.0)

        sch.finalize()
        sch.emit(nc, es, dma_sems, ["io_out"])
    return nc


def _fm(W, cols):
    sub = W[:, cols]
    n = sub.shape[1]
    return np.ascontiguousarray(sub.reshape(8, 128, n).transpose(1, 0, 2).reshape(128, 8 * n))


def host_constants():
    s = np.arange(128)
    ident = np.eye(128, dtype=np.float32)
    onesMD = np.full((128, 128), 1.0 / 1024.0, np.float32)
    onesMH = np.full((128, 128), 1.0 / 128.0, np.float32)
    U = (s[:, None] <= s[None, :]).astype(np.float32)
    NEGm = np.where(s[:, None] <= s[None, :], 0.0, NEG).astype(np.float32)
    cst = np.concatenate([ident, onesMD, onesMH, U, NEGm], axis=1)
    kp = np.arange(128)[:, None]
    u = np.arange(640)[None, :]
    dq = u // 64 - (kp >= 64)
    negm = np.where((dq >= 0) & (dq <= 8), 0.0, NEG).astype(np.float32)
    return np.ascontiguousarray(cst), np.ascontiguousarray(negm)


def host_prep(inp, depth=DEPTH, layers=None, pipe_role=None, nblk=SEQ // TB):
    if layers is None:
        layers = list(range(depth))
    L = len(layers)
    sel = np.asarray(layers)
    inp = dict(inp)
    for k_ in ("mix_norm_g", "w_in", "conv_w", "conv_b", "b_igate", "b_fgate", "rel_bias", "mh_norm_g",
               "w_att_proj", "w_mlstm_proj", "w_out", "ffn_norm_g", "w_up", "w_down"):
        inp[k_] = np.asarray(inp[k_], np.float32)[sel]
    w_in = inp["w_in"]
    tiles = []
    wifs = []
    for l in range(L):
        for g in range(7):
            tiles.append(_fm(w_in[l], np.arange(512 * g, 512 * g + 512)))
        w_att = np.asarray(inp["w_att_proj"][l], np.float32)
        w_ml = np.asarray(inp["w_mlstm_proj"][l], np.float32)

        def pp_tile(u):
            parts = []
            for tt_ in range(2):
                t = 2 * u + tt_
                for w in (w_att, w_ml):
                    sub = w[:, t * 256:(t + 1) * 256]
                    parts.append(sub.reshape(4, 128, 256).transpose(1, 0, 2).reshape(128, 1024))
            return np.ascontiguousarray(np.concatenate(parts, axis=1))

        def gagm_tile(t):
            cols = np.concatenate([GA0 + 256 * t + np.arange(256), GM0 + 256 * t + np.arange(256)])
            return _fm(w_in[l], cols)

        tiles.append(pp_tile(0))
        tiles.append(gagm_tile(0))
        tiles.append(gagm_tile(1))
        tiles.append(pp_tile(1))
        tiles.append(gagm_tile(2))
        tiles.append(gagm_tile(3))
        w_out = np.asarray(inp["w_out"][l], np.float32)
        for t in range(2):
            tiles.append(_fm(w_out, np.arange(512 * t, 512 * t + 512)))
        w_up = np.asarray(inp["w_up"][l], np.float32)
        for t in range(8):
            tiles.append(_fm(w_up, np.arange(512 * t, 512 * t + 512)))
        w_down = np.asarray(inp["w_down"][l], np.float32)
        for c in range(8):
            sub = w_down[:, c * 128:(c + 1) * 128]
            tiles.append(np.ascontiguousarray(sub.reshape(32, 128, 128).transpose(1, 0, 2).reshape(128, 4096)))
        wifs.append(_fm(w_in[l], np.arange(3584, 3592)))
    wstream = np.stack(tiles, axis=0)
    assert wstream.shape == (L * NW_TILES, 128, 4096)
    wif = np.concatenate(wifs, axis=1)

    kp = np.arange(128)[:, None]
    u = np.arange(640)[None, :]
    rel_idx = np.clip(u - kp, -256, 256) + 256
    rb = np.asarray(inp["rel_bias"], np.float32)[:L]
    bg = rb[:, :, rel_idx]
    biasg = np.ascontiguousarray(bg.transpose(2, 0, 1, 3).reshape(128, L * 8 * 640))

    def fmv(v):
        v = np.asarray(v, np.float32)
        k = v.shape[-1] // 128
        return np.moveaxis(v.reshape(v.shape[:-1] + (k, 128)), -1, 0)

    g1 = fmv(inp["mix_norm_g"][:L]).reshape(128, L * 8)
    g2 = fmv(inp["ffn_norm_g"][:L]).reshape(128, L * 8)
    gf = fmv(inp["final_norm_g"]).reshape(128, 8)
    cw = fmv(inp["conv_w"][:L]).reshape(128, L * 32)
    cb = fmv(inp["conv_b"][:L]).reshape(128, L * 8)
    mh = fmv(inp["mh_norm_g"][:L]).reshape(128, L * 4)
    bi = np.broadcast_to(np.asarray(inp["b_igate"], np.float32)[:L][None, :, None, :], (128, L, 4, 4)).reshape(128, L * 16)
    bf = np.broadcast_to(np.asarray(inp["b_fgate"], np.float32)[:L][None, :, None, :], (128, L, 4, 4)).reshape(128, L * 16)
    epsc = np.full((128, 1), EPS, np.float32)
    plist = [g1, g2, gf, cw, cb, mh, bi, bf, epsc]
    if pipe_role is not None:
        nstep = nblk + 1
        fb = np.full((128, 1), float(pipe_role), np.float32)
        smask = np.zeros((128, nstep), np.float32)
        smask[:, :pipe_role + 1] = NEG
        valid = np.ones((128, nstep), np.float32)
        valid[:, :pipe_role] = 0.0
        plist += [fb, smask, valid]
    params = np.ascontiguousarray(np.concatenate(plist, axis=1).astype(np.float32))
    cst, negm = host_constants()
    return {"wstream": wstream, "wif": np.ascontiguousarray(wif), "biasg": biasg, "cst": cst, "negm": negm,
            "params": params}


_NC_CACHE = {}


PIPE = True


def kernel(**inputs):
    x = np.asarray(inputs["x"], np.float32)
    B, S_, D = x.shape
    nblk = S_ // TB
    n_cores = 8
    key = (nblk, DEPTH, PIPE)
    if key not in _NC_CACHE:
        _NC_CACHE[key] = build_program(nblk, DEPTH, pipe=PIPE)
    nc = _NC_CACHE[key]
    in_maps = []
    if PIPE:
        assert DEPTH == 2 and B * 2 == n_cores
        roles = [host_prep(inputs, DEPTH, layers=[r], pipe_role=r, nblk=nblk) for r in range(2)]
        zeros = np.zeros((D, S_), np.float32)
        for c in range(n_cores):
            m = dict(roles[c % 2])
            m["x"] = np.ascontiguousarray(x[c // 2].T) if c % 2 == 0 else zeros
            in_maps.append(m)
        res = run_bass_kernel_spmd(nc, in_maps, core_ids=list(range(n_cores)))
        out = np.stack([np.ascontiguousarray(np.asarray(res.results[2 * b + 1]["out"], np.float32).T)
                        for b in range(B)], axis=0)
        return out
    shared = host_prep(inputs, DEPTH)
    for c in range(n_cores):
        m = dict(shared)
        m["x"] = np.ascontiguousarray(x[c % B])
        in_maps.append(m)
    res = run_bass_kernel_spmd(nc, in_maps, core_ids=list(range(n_cores)))
    out = np.stack([np.asarray(res.results[b]["out"], np.float32) for b in range(B)], axis=0)
    return out
```

```python
import math
from contextlib import ExitStack

import numpy as np
import concourse.bass as bass
import concourse.mybir as mybir
from concourse.bass_utils import run_bass_kernel_spmd

F32 = mybir.dt.float32
BF16 = mybir.dt.bfloat16
AF = mybir.ActivationFunctionType
ALU = mybir.AluOpType

D_MODEL = 1024
BATCH = 4
SEQ = 4096
DEPTH = 2
TB = 512
NW_TILES = 31
NWS = 5
NTF = 8
NTB = 4
EPS = 1e-6
NEG = -30000.0
GA0 = 3592
GM0 = 4616
EPOCH = 8000
SAME_ENGINE_SYNC = "raw"
LIST_SCHED = True
GATE_PREFETCH = True
GATE_DIV = 2
ATT_MUL_ENG = "vector"
NH = 2
XENG_LAT = 1000.0
import os as _os
KNOP = set(int(v) for v in _os.environ.get('KNOP', '').split(',') if v)


def _is_cc(key):
    return isinstance(key, tuple) and key[0] == "cc"


class _Op:
    __slots__ = ("eng", "fn", "deps", "dma_sem", "token", "needs_inc", "idx", "raw", "cost", "alldeps")


class Sched:
    def __init__(self):
        self.ops = []
        self.lastw = {}
        self.readers = {}

    def op(self, eng, fn, reads=(), writes=(), dma_sem=None, cost=300.0):
        o = _Op()
        o.cost = cost
        o.eng = eng
        o.fn = fn
        o.dma_sem = dma_sem
        o.idx = len(self.ops)
        o.needs_inc = dma_sem is not None
        o.token = None
        deps = set()
        raw = set()
        for r in reads:
            if r in self.lastw:
                deps.add(self.lastw[r])
                raw.add(self.lastw[r])
        o.raw = raw
        for w in writes:
            if w in self.lastw:
                deps.add(self.lastw[w])
            for rd in self.readers.get(w, ()):
                deps.add(rd)
        deps.discard(o.idx)
        o.deps = deps
        for r in reads:
            self.readers.setdefault(r, set()).add(o.idx)
        for w in writes:
            self.lastw[w] = o.idx
            self.readers[w] = set()
        self.ops.append(o)
        return o

    def finalize(self):
        import os
        if os.environ.get("KTRUNC"):
            self.ops = self.ops[:int(os.environ["KTRUNC"])]
        if os.environ.get("KSKIP"):
            a, b = [int(v) for v in os.environ["KSKIP"].split(",")]
            kept = [o for o in self.ops if not (a <= o.idx < b)]
            remap = {o.idx: i for i, o in enumerate(kept)}
            for o in kept:
                o.deps = set(remap[d] for d in o.deps if d in remap)
                o.raw = set(remap[d] for d in o.raw if d in remap)
                o.idx = remap[o.idx]
            self.ops = kept
        ops = self.ops
        for o in ops:
            o.alldeps = set(o.deps)
        self.order = self.list_schedule() if LIST_SCHED else None
        for o in ops:
            keep = set()
            for d in o.deps:
                do = ops[d]
                if do.eng == o.eng and do.dma_sem is None and o.dma_sem is None:
                    if o.eng == "tensor" or not SAME_ENGINE_SYNC:
                        continue
                    if SAME_ENGINE_SYNC == "raw" and d not in o.raw:
                        continue
                keep.add(d)
            o.deps = keep
            for d in keep:
                ops[d].needs_inc = True

    def list_schedule(self):
        import heapq
        ops = self.ops
        n = len(ops)
        ndep = [len(o.alldeps) for o in ops]
        users = [[] for _ in range(n)]
        for o in ops:
            for d in o.alldeps:
                users[d].append(o.idx)
        finish = [0.0] * n
        ready_t = [0.0] * n
        engs = ["tensor", "vector", "scalar", "gpsimd", "sync"]
        pend = {e: [] for e in engs}
        avail = {e: [] for e in engs}
        t_free = {e: 0.0 for e in engs}
        for o in ops:
            if ndep[o.idx] == 0:
                heapq.heappush(pend[o.eng], (0.0, o.idx))
        order = []
        done = 0
        while done < n:
            best = None
            for e in engs:
                while pend[e] and pend[e][0][0] <= t_free[e]:
                    rt, i = heapq.heappop(pend[e])
                    heapq.heappush(avail[e], i)
                if avail[e]:
                    cand = (t_free[e], avail[e][0], e, True)
                elif pend[e]:
                    cand = (pend[e][0][0], pend[e][0][1], e, False)
                else:
                    continue
                if best is None or cand[:2] < best[:2]:
                    best = cand
            st, i, e, from_avail = best
            if from_avail:
                heapq.heappop(avail[e])
            else:
                heapq.heappop(pend[e])
            o = ops[i]
            fin = st + o.cost
            t_free[e] = fin
            finish[i] = fin
            order.append(i)
            done += 1
            for u in users[i]:
                lat = 0.0 if ops[u].eng == e else XENG_LAT
                if fin + lat > ready_t[u]:
                    ready_t[u] = fin + lat
                ndep[u] -= 1
                if ndep[u] == 0:
                    heapq.heappush(pend[ops[u].eng], (ready_t[u], u))
        self.est_total = max(finish) if finish else 0.0
        return order

    def emit(self, nc, es, dma_sems, final_waits):
        ops = self.ops
        seq = [ops[i] for i in self.order] if self.order is not None else list(ops)
        eng_sems = {}
        counters = {}
        dma_counts = {}
        for o in seq:
            if o.dma_sem is not None:
                c = dma_counts.get(o.dma_sem, 0) + (1 if _is_cc(o.dma_sem) else 16)
                dma_counts[o.dma_sem] = c
                o.token = (dma_sems[o.dma_sem], c)
            elif o.needs_inc:
                ep, c = counters.get(o.eng, (0, 0))
                if c >= EPOCH:
                    ep, c = ep + 1, 0
                c += 1
                counters[o.eng] = (ep, c)
                key = (o.eng, ep)
                if key not in eng_sems:
                    eng_sems[key] = es.enter_context(nc.semaphore("p_%s_%d" % (o.eng, ep)))
                o.token = (eng_sems[key], c)
        for o in ops:
            if o.dma_sem in ("setup", "setup_sw"):
                o.token = (dma_sems[o.dma_sem], dma_counts[o.dma_sem])
        by_eng = {}
        for o in seq:
            by_eng.setdefault(o.eng, []).append(o)

        def run(eng_name, e):
            waited = {}
            for o in by_eng.get(eng_name, []):
                need = {}
                for d in o.deps:
                    s, v = ops[d].token
                    k = id(s)
                    if waited.get(k, 0) >= v:
                        continue
                    if k not in need or need[k][1] < v:
                        need[k] = (s, v)
                for k, (s, v) in need.items():
                    e.wait_ge(s, v)
                    waited[k] = v
                if o.idx in KNOP:
                    ins = e.nop()
                else:
                    ins = o.fn(e)
                if o.dma_sem is not None:
                    if _is_cc(o.dma_sem):
                        ins.then_inc(o.token[0])
                    else:
                        ins.then_inc(o.token[0], 16)
                elif o.needs_inc:
                    ins.then_inc(o.token[0], 1)
            mine = []
            for o in by_eng.get(eng_name, []):
                if o.dma_sem is not None and o.dma_sem not in mine:
                    mine.append(o.dma_sem)
            for key in mine:
                e.wait_ge(dma_sems[key], dma_counts[key])

        with nc.Block() as block:
            @block.sync
            def _(e):
                run("sync", e)

            @block.gpsimd
            def _(e):
                run("gpsimd", e)

            @block.tensor
            def _(e):
                run("tensor", e)

            @block.vector
            def _(e):
                run("vector", e)

            @block.scalar
            def _(e):
                run("scalar", e)


def build_program(nblk, depth=DEPTH, pipe=False):
    T = nblk * TB
    nc = bass.Bass("TRN2", target_bir_lowering=False)
    L = 1 if pipe else depth
    nstep = nblk + 1 if pipe else nblk
    if pipe:
        x_d = nc.dram_tensor("x", [D_MODEL, T], F32, kind="ExternalInput").ap()
    else:
        x_d = nc.dram_tensor("x", [T, D_MODEL], F32, kind="ExternalInput").ap()
    w_d = nc.dram_tensor("wstream", [L * NW_TILES, 128, 4096], F32, kind="ExternalInput").ap()
    wif_d = nc.dram_tensor("wif", [128, L * 64], F32, kind="ExternalInput").ap()
    bias_d = nc.dram_tensor("biasg", [128, L * 8 * 640], F32, kind="ExternalInput").ap()
    cst_d = nc.dram_tensor("cst", [128, 5 * 128], F32, kind="ExternalInput").ap()
    negm_d = nc.dram_tensor("negm", [128, 640], F32, kind="ExternalInput").ap()
    NPAR = L * 8 + L * 8 + 8 + L * 32 + L * 8 + L * 4 + L * 16 + L * 16 + 1 + (1 + 2 * (nblk + 1) if pipe else 0)
    par_d = nc.dram_tensor("params", [128, NPAR], F32, kind="ExternalInput").ap()
    if pipe:
        out_d = nc.dram_tensor("out", [D_MODEL, T], F32, kind="ExternalOutput").ap()
    else:
        out_d = nc.dram_tensor("out", [T, D_MODEL], F32, kind="ExternalOutput").ap()

    es = ExitStack()
    with es:
        def sb(name, shape, dt):
            return es.enter_context(nc.sbuf_tensor(name, shape, dt))

        if not pipe:
            io = sb("io", [128, 4, 1024], F32)
        if pipe:
            xl = sb("xl", [128, 8, TB], F32)
            rbuf = sb("rbuf", [128, 8, TB], F32)
            CPP = 8 // NH
            send_d = [[nc.dram_tensor("send%d_%d" % (k, hf), [128, CPP * TB], F32, kind="Internal").ap()
                       for hf in range(NH)] for k in range(nblk)]
            recv_d = [[nc.dram_tensor("recv%d_%d" % (k, hf), [256, CPP * TB], F32, kind="Internal").ap()
                       for hf in range(NH)] for k in range(nblk)]
        else:
            xin = io
        xT = sb("xT", [128, 8, TB], F32)
        xnT = sb("xnT", [128, 8, TB], BF16)
        arena = sb("arena", [128, 32, TB], BF16)
        kh = sb("kh", [128, L * 2 * 4, TB], BF16)
        sgmb = sb("sgmb", [128, 8, TB], BF16)
        vh = sb("vh", [128, L * 2 * 4, TB], BF16)
        wb = sb("wb", [128, NWS, 4096], BF16)
        wif = sb("wifs", [128, L * 64], BF16)
        BM = sb("BM", [128, L * 8, 640], BF16)
        negm = sb("negms", [128, 640], BF16)
        S = sb("S", [128, L * 4, 256], F32)
        Sb = sb("Sb", [128, L * 4, 256], BF16)
        halo = sb("halo", [128, L * 8, 4], F32)
        cst = sb("csts", [128, 5, 128], F32)
        identB = sb("identB", [128, 128], BF16)
        cstB = sb("cstB", [128, 3, 128], BF16)
        onesB = sb("onesB", [128, 128], BF16)
        onesF = sb("onesF", [128, 128], F32)
        par = sb("pars", [128, NPAR], F32)
        tf = sb("tf", [128, NTF, 512], F32)
        tbf = sb("tbf", [128, NTB, 512], BF16)
        pre = sb("pre", [128, 2, 516], F32)
        LF = sb("LF", [128, 2, 4, 128], F32)
        gsm = sb("gsm", [128, 8, 16], F32)
        wsm = sb("wsm", [128, 8, 2], F32)
        ps = [es.enter_context(nc.psum_tensor("ps%d" % b, [128, 512], F32)) for b in range(8)]

        identF = cst[:, 0, :]
        onesMD = cst[:, 1, :]
        onesMH = cst[:, 2, :]
        Umat = cst[:, 3, :]
        NEGmat = cst[:, 4, :]
        onesMDb = cstB[:, 0, :]
        onesMHb = cstB[:, 1, :]
        NEGb = cstB[:, 2, :]

        o_g1 = 0
        o_g2 = o_g1 + L * 8
        o_gf = o_g2 + L * 8
        o_cw = o_gf + 8
        o_cb = o_cw + L * 32
        o_mh = o_cb + L * 8
        o_bi = o_mh + L * 4
        o_bf = o_bi + L * 16
        o_eps = o_bf + L * 16
        o_fb = o_eps + 1
        o_sm = o_fb + 1
        o_vd = o_sm + nstep

        def pcol(off):
            return par[:, off:off + 1]

        dma_sems = {}
        extra_sems = ([("snd", k) for k in range(NH)] + [("rcv", k) for k in range(NH)]
                      + [("cc", k) for k in range(NH * nblk)]) if pipe else []
        extra_sems = extra_sems + [("bm", k) for k in range(L * 8)]
        for k in ["setup", "setup_sw", "io_in", "io_out"] + extra_sems + [("w", s) for s in range(NWS)]:
            dma_sems[k] = es.enter_context(nc.semaphore("d_%s" % (str(k).replace(" ", ""))))

        sch = Sched()
        state = {"ps": 0, "tf": 0, "tb": 0, "w": 0, "alt": 0, "pre": 0, "lf": 0, "wsm": 0}

        def newps(exclude=()):
            b = state["ps"]
            while b in exclude:
                b = (b + 1) % 8
            state["ps"] = (b + 1) % 8
            return b

        def newtf():
            k = state["tf"]
            state["tf"] = (k + 1) % NTF
            return k

        def newtb():
            k = state["tb"]
            state["tb"] = (k + 1) % NTB
            return k

        def fsz(ap):
            n = 1
            for d in ap.shape[1:]:
                n *= int(d)
            return n

        def mm(out, lhsT, rhs, start, stop, r, w):
            c = max(fsz(rhs), 64) / 1.95 + 15.0
            if lhsT.dtype == F32:
                c *= 4.0
            sch.op("tensor", lambda e: e.matmul(out, lhsT, rhs, start=start, stop=stop), r, w, cost=c)

        def tr(out, in_, r, w):
            sch.op("tensor", lambda e: e.transpose(out, in_, identF), list(r) + ["cst"], w, cost=280.0)

        def act(out, in_, func, r, w, bias=None, scale=None):
            kw = {}
            if bias is not None:
                kw["bias"] = bias
            if scale is not None:
                kw["scale"] = scale
            c = fsz(out) / 1.4 + 220.0 + (90.0 if bias is not None and not isinstance(bias, float) else 0.0)
            sch.op("scalar", lambda e: e.activation(out, in_, func, **kw), r, w, cost=c)

        def dcost(ap, mult=1.0):
            return max(fsz(ap), 64) * mult / 0.96 + 90.0

        def tt(out, in0, in1, op, r, w, eng="vector"):
            sch.op(eng, lambda e: e.tensor_tensor(out, in0, in1, op), r, w, cost=dcost(out))

        def ts(out, in0, s1, s2, op0, op1, r, w, eng="vector"):
            if op1 is None:
                sch.op(eng, lambda e: e.tensor_scalar(out, in0, s1, None, op0), r, w, cost=dcost(out))
            else:
                sch.op(eng, lambda e: e.tensor_scalar(out, in0, s1, s2, op0, op1), r, w, cost=dcost(out))

        def stt(out, in0, scalar, in1, op0, op1, r, w):
            sch.op("vector", lambda e: e.scalar_tensor_tensor(out, in0, scalar, in1, op0, op1), r, w,
                   cost=dcost(out))

        def recip(out, in_, r, w):
            sch.op("vector", lambda e: e.reciprocal(out, in_), r, w, cost=dcost(out, 8.0))

        def copy_any(out, in_, r, w):
            state["alt"] ^= 1
            if state["alt"]:
                sch.op("scalar", lambda e: e.copy(out, in_), r, w, cost=fsz(out) / 1.4 + 220.0)
            else:
                sch.op("vector", lambda e: e.tensor_copy(out, in_), r, w, cost=dcost(out))

        def dsetup(eng, out, in_, w):
            sch.op(eng, lambda e: e.dma_start(out=out, in_=in_), [], w,
                   dma_sem="setup_sw" if eng == "gpsimd" else "setup")

        dsetup("sync", cst[:, :, :], cst_d.rearrange("p (k n) -> p k n", k=5), ["cst"])
        dsetup("sync", par[:, :], par_d, ["par"])
        dsetup("gpsimd", wif[:, :], wif_d, ["wif"])
        dsetup("gpsimd", negm[:, :], negm_d, ["negm"])
        def setup_bm():
            for l in range(L):
                for h in range(8):
                    k = l * 8 + h
                    sch.op("gpsimd", lambda e, k=k: e.dma_start(out=BM[:, k, :], in_=bias_d[:, k * 640:(k + 1) * 640]),
                           [], [("BM", k)], dma_sem=("bm", k), cost=2500.0)
                    tt(BM[:, k, :], BM[:, k, :], negm[:, :], ALU.add, [("BM", k), "negm"], [("BM", k)])
                    if ATT_MUL_ENG:
                        act(BM[:, k, :], BM[:, k, :], AF.Exp, [("BM", k)], [("BM", k)])

        sch.op("vector", lambda e: e.memset(S[:, :, :], 0.0), [], ["S%d" % k for k in range(L * 4)])
        sch.op("vector", lambda e: e.memset(Sb[:, :, :], 0.0), [], ["Sb%d" % k for k in range(L * 4)])
        sch.op("vector", lambda e: e.memset(halo[:, :, :], 0.0), [], [("halo", k) for k in range(L * 8)])
        sch.op("vector", lambda e: e.memset(kh[:, :, :], 0.0), [], [("kh", k) for k in range(L * 8)])
        sch.op("vector", lambda e: e.memset(vh[:, :, :], 0.0), [], [("vh", k) for k in range(L * 8)])
        sch.op("vector", lambda e: e.memset(onesB[:, :], 1.0), [], ["onesB"])
        sch.op("vector", lambda e: e.memset(onesF[:, :], 1.0), [], ["onesF"])
        sch.op("vector", lambda e: e.tensor_copy(identB[:, :], identF), ["cst"], ["identB"])
        sch.op("vector", lambda e: e.tensor_copy(cstB[:, 0:2, :], cst[:, 1:3, :]), ["cst"], ["cstB"])
        sch.op("vector", lambda e: e.tensor_copy(cstB[:, 2, :], cst[:, 4, :]), ["cst", "cstB"], ["cstB"])
        def next_w(l, k_expected):
            n = state["w"]
            state["w"] = n + 1
            assert n % NW_TILES == k_expected and n // NW_TILES % L == l, (n, l, k_expected)
            slot = n % NWS
            src = w_d[(n % (L * NW_TILES)), :, :]
            gate = []
            if pipe and GATE_PREFETCH and n // NW_TILES >= 1 and n % NW_TILES < NWS:
                gate = [("recv", n // NW_TILES - 1, min((n % NW_TILES) // GATE_DIV, NH - 1))]
            sch.op("gpsimd", lambda e: e.dma_start(out=wb[:, slot, :], in_=src), gate, [("w", slot)],
                   dma_sem=("w", slot), cost=9000.0)
            return slot

        def wview(slot, kc, c0, n, kcn=8):
            width = 4096 // kcn
            return wb[:, slot, kc * width + c0: kc * width + c0 + n]

        def rstd_from(P_ap, kr, rd):
            act(tf[:, kr, :], P_ap, AF.Ln, rd + ["par"], [("tf", kr)], bias=pcol(o_eps))
            act(tf[:, kr, :], tf[:, kr, :], AF.Exp, [("tf", kr)], [("tf", kr)], scale=-0.5)

        def norm_stats():
            P = newps()
            for c in range(8):
                k = newtb()
                act(tbf[:, k, :], xT[:, c, :], AF.Square, [("xT", c)], [("tb", k)])
                mm(ps[P][:, :], onesMDb, tbf[:, k, :], c == 0, c == 7, [("tb", k), "cstB"], [("ps", P)])
            kr = newtf()
            rstd_from(ps[P][:, :], kr, [("ps", P)])
            return kr

        def rmsnorm(gcol0, dst_bf16):
            kr = norm_stats()
            for c in range(8):
                if dst_bf16:
                    stt(xnT[:, c, :], xT[:, c, :], pcol(gcol0 + c), tf[:, kr, :], ALU.mult, ALU.mult,
                        [("xT", c), ("tf", kr), "par"], [("xn", c)])
                else:
                    stt(xT[:, c, :], xT[:, c, :], pcol(gcol0 + c), tf[:, kr, :], ALU.mult, ALU.mult,
                        [("xT", c), ("tf", kr), "par"], [("xT", c)])

        def pg(n):
            return ("pg", n)

        SC_M = 1.0 / math.sqrt(128.0)

        def layer(l, j):
            parity = j % 2
            kcur = lambda p: (l * 2 + parity) * 4 + p
            kprev = lambda p: (l * 2 + 1 - parity) * 4 + p

            rmsnorm(o_g1 + l * 8, True)
            xn_all = [("xn", c) for c in range(8)]

            for g in range(2):
                slot = next_w(l, g)
                for m in range(4):
                    P = newps()
                    for kc in range(8):
                        mm(ps[P][:, :], wview(slot, kc, m * 128, 128), xnT[:, kc, :], kc == 0, kc == 7,
                           [("w", slot), ("xn", kc)], [("ps", P)])
                    if g == 0:
                        copy_any(arena[:, m, :], ps[P][:, :], [("ps", P)], [pg(m)])
                    else:
                        copy_any(kh[:, kcur(m), :], ps[P][:, :], [("ps", P)], [("kh", kcur(m))])
            slot = next_w(l, 2)
            for i in range(4):
                P = newps()
                for kc in range(8):
                    mm(ps[P][:, :], xnT[:, kc, i * 128:(i + 1) * 128], wview(slot, kc, 0, 512), kc == 0, kc == 7,
                       [("w", slot), ("xn", kc)], [("ps", P)])
                copy_any(vh[:, kcur(i), :], ps[P][:, :], [("ps", P)], [("vh", kcur(i))])

            if not state.get("bm_done"):
                state["bm_done"] = True
                setup_bm()
            for p in range(4):
                O = newps()
                Dn = newps()
                for e_ in range(2):
                    h = 2 * p + e_
                    rows = slice(e_ * 64, (e_ + 1) * 64)
                    rlist = [4] + [r for r in range(8) if (j > 0 or r >= 4 or pipe) and r != 4]
                    for idx, r in enumerate(rlist):
                        kidx = kprev(p) if r < 4 else kcur(p)
                        vidx = kprev(r % 4) if r < 4 else kcur(r % 4)
                        qlo = max(8, 2 * r) * 64 - 512
                        qhi = (min(15, 2 * r + 9) + 1) * 64 - 512
                        n = qhi - qlo
                        u0 = 512 + qlo - 128 * r
                        sc = newps(exclude=(O, Dn))
                        mm(ps[sc][:, 0:n], kh[rows, kidx, (r % 4) * 128:(r % 4 + 1) * 128], arena[rows, p, qlo:qhi],
                           True, True, [("kh", kidx), pg(p)], [("ps", sc)])
                        k1 = newtf()
                        k2 = newtb()
                        if ATT_MUL_ENG:
                            if pipe and r < 4:
                                act(tf[:, k1, 0:n], ps[sc][:, 0:n], AF.Exp, [("ps", sc), "par"], [("tf", k1)],
                                    bias=pcol(o_sm + j), scale=0.125)
                            else:
                                act(tf[:, k1, 0:n], ps[sc][:, 0:n], AF.Exp, [("ps", sc)], [("tf", k1)], scale=0.125)
                            sch.op(ATT_MUL_ENG, lambda e, k1=k1, k2=k2, n=n, u0=u0, lh=l * 8 + h: e.tensor_tensor(
                                tbf[:, k2, 0:n], tf[:, k1, 0:n], BM[:, lh, u0:u0 + n], ALU.mult),
                                [("tf", k1), ("BM", l * 8 + h)], [("tb", k2)],
                                cost=(n / 0.5 + 500.0) if ATT_MUL_ENG == "gpsimd" else (n / 0.96 + 90.0))
                        else:
                            stt(tf[:, k1, 0:n], ps[sc][:, 0:n], 0.125, BM[:, l * 8 + h, u0:u0 + n], ALU.mult, ALU.add,
                                [("ps", sc), ("BM", l * 8 + h)], [("tf", k1)])
                            if pipe and r < 4:
                                act(tbf[:, k2, 0:n], tf[:, k1, 0:n], AF.Exp, [("tf", k1), "par"], [("tb", k2)],
                                    bias=pcol(o_sm + j))
                            else:
                                act(tbf[:, k2, 0:n], tf[:, k1, 0:n], AF.Exp, [("tf", k1)], [("tb", k2)])
                        first = idx == 0
                        last = idx == len(rlist) - 1
                        mm(ps[O][rows, qlo:qhi], vh[:, vidx, h * 64:(h + 1) * 64], tbf[:, k2, 0:n], first, last,
                           [("vh", vidx), ("tb", k2)], [("ps", O)])
                        mm(ps[Dn][rows, qlo:qhi], onesB[:, 0:64], tbf[:, k2, 0:n], first, last,
                           ["onesB", ("tb", k2)], [("ps", Dn)])
                kr = newtf()
                act(tf[:, kr, :], ps[Dn][:, :], AF.Ln, [("ps", Dn)], [("tf", kr)])
                act(tf[:, kr, :], tf[:, kr, :], AF.Exp, [("tf", kr)], [("tf", kr)], scale=-1.0)
                tt(arena[:, 24 + p, :], ps[O][:, :], tf[:, kr, :], ALU.mult, [("ps", O), ("tf", kr)], [pg(24 + p)])

            for g in (3, 4):
                slot = next_w(l, g)
                for m in range(4):
                    ch = (g - 3) * 4 + m
                    P = newps()
                    for kc in range(8):
                        mm(ps[P][:, :], wview(slot, kc, m * 128, 128), xnT[:, kc, :], kc == 0, kc == 7,
                           [("w", slot), ("xn", kc)], [("ps", P)])
                    pk = state["pre"]
                    state["pre"] ^= 1
                    hk = l * 8 + ch
                    copy_any(pre[:, pk, 3:515], ps[P][:, :], [("ps", P)], [("pre", pk)])
                    sch.op("vector", lambda e, pk=pk, hk=hk: e.tensor_copy(pre[:, pk, 0:3], halo[:, hk, 0:3]),
                           [("halo", hk)], [("pre", pk)])
                    sch.op("vector", lambda e, pk=pk, hk=hk: e.tensor_copy(halo[:, hk, 0:3], pre[:, pk, 512:515]),
                           [("pre", pk)], [("halo", hk)])
                    ka = newtf()
                    cw = lambda tap: pcol(o_cw + l * 32 + tap * 8 + ch)
                    ts(tf[:, ka, :], pre[:, pk, 0:512], cw(0), None, ALU.mult, None,
                       [("pre", pk), "par"], [("tf", ka)])
                    for tap in (1, 2, 3):
                        stt(tf[:, ka, :], pre[:, pk, tap:tap + 512], cw(tap), tf[:, ka, :], ALU.mult, ALU.add,
                            [("pre", pk), "par", ("tf", ka)], [("tf", ka)])
                    act(arena[:, 4 + ch, :], tf[:, ka, :], AF.Silu, [("tf", ka), "par"], [pg(4 + ch)],
                        bias=pcol(o_cb + l * 8 + ch))
            slot = next_w(l, 5)
            for i in range(4):
                P = newps()
                for kc in range(8):
                    mm(ps[P][:, :], xnT[:, kc, i * 128:(i + 1) * 128], wview(slot, kc, 0, 512), kc == 0, kc == 7,
                       [("w", slot), ("xn", kc)], [("ps", P)])
                copy_any(arena[:, 12 + i, :], ps[P][:, :], [("ps", P)], [pg(12 + i)])
            slot = next_w(l, 6)
            for m in range(4):
                P = newps()
                for kc in range(8):
                    mm(ps[P][:, :], wview(slot, kc, m * 128, 128), xnT[:, kc, :], kc == 0, kc == 7,
                       [("w", slot), ("xn", kc)], [("ps", P)])
                act(arena[:, 20 + m, :], ps[P][:, :], AF.Sigmoid, [("ps", P)], [pg(20 + m)])
            G = newps()
            for i in range(4):
                for kc in range(8):
                    mm(ps[G][:, i * 8:(i + 1) * 8], xnT[:, kc, i * 128:(i + 1) * 128],
                       wif[:, l * 64 + kc * 8: l * 64 + kc * 8 + 8], kc == 0, kc == 7,
                       ["wif", ("xn", kc)], [("ps", G)])
            G3 = ps[G][:, 0:32].rearrange("p (i c) -> p i c", c=8)
            ig = gsm[:, 0, :]
            zf = gsm[:, 1, :]
            ef = gsm[:, 2, :]
            spv = gsm[:, 3, :]
            logf = gsm[:, 4, :]
            biasv = gsm[:, 5, :]
            v3 = lambda ap: ap.rearrange("p (i c) -> p i c", c=4)
            tt(v3(ig), G3[:, :, 0:4], v3(par[:, o_bi + l * 16:o_bi + l * 16 + 16]), ALU.add,
               [("ps", G), "par"], ["g_ig"])
            tt(v3(zf), G3[:, :, 4:8], v3(par[:, o_bf + l * 16:o_bf + l * 16 + 16]), ALU.add,
               [("ps", G), "par"], ["g_zf"])
            act(ef, zf, AF.Exp, ["g_zf"], ["g_ef"], scale=-1.0)
            act(spv, ef, AF.Ln, ["g_ef"], ["g_sp"], bias=1.0)
            ts(logf, spv, -1.0, None, ALU.mult, None, ["g_sp"], ["g_lf"])
            Cm = newps()
            mm(ps[Cm][:, 0:16], Umat, logf, True, True, ["cst", "g_lf"], [("ps", Cm)])
            tt(biasv, ig, ps[Cm][:, 0:16], ALU.subtract, ["g_ig", ("ps", Cm)], ["g_bv"])

            for i in range(4):
                lfk = state["lf"]
                state["lf"] ^= 1
                for h in range(4):
                    col = i * 4 + h
                    ts(LF[:, lfk, h, :], onesF[:, :], logf[:, col:col + 1], None, ALU.mult, None,
                       ["onesF", "g_lf"], [("LF", lfk, h)])
                tsl = slice(i * 128, (i + 1) * 128)
                for h in range(4):
                    col = i * 4 + h
                    sk = l * 4 + h
                    kT = arena[:, 8 + h, tsl]
                    qT_ = arena[:, 4 + h, tsl]
                    vt = arena[:, 12 + i, h * 128:(h + 1) * 128]
                    A = newps()
                    mm(ps[A][:, 0:128], kT, qT_, True, True, [pg(8 + h), pg(4 + h)], [("ps", A)])
                    mm(ps[A][:, 128:256], LF[:, lfk, h, :], Umat, True, True, [("LF", lfk, h), "cst"], [("ps", A)])
                    mm(ps[A][:, 256:384], LF[:, lfk, h, :], Umat, True, False, [("LF", lfk, h), "cst"], [("ps", A)])
                    mm(ps[A][:, 256:384], identB[:, :], NEGb, False, True, ["identB", "cstB"], [("ps", A)])
                    ka = newtf()
                    act(tf[:, ka, 0:128], ps[A][:, 256:384], AF.Exp, [("ps", A), "g_bv"], [("tf", ka)],
                        bias=biasv[:, col:col + 1])
                    act(tf[:, ka, 128:256], ps[A][:, 128:256], AF.Exp, [("ps", A)], [("tf", ka)])
                    wk = state["wsm"]
                    state["wsm"] = (wk + 1) % 8
                    if pipe:
                        ts(wsm[:, wk, 0:1], tf[:, ka, 127:128], pcol(o_vd + j), None, ALU.mult, None,
                           [("tf", ka), "par"], [("wsm", wk)])
                    else:
                        sch.op("vector", lambda e, wk=wk, ka=ka: e.tensor_copy(wsm[:, wk, 0:1], tf[:, ka, 127:128]),
                               [("tf", ka)], [("wsm", wk)])
                    sch.op("vector", lambda e, wk=wk, ka=ka: e.tensor_copy(wsm[:, wk, 1:2], tf[:, ka, 255:256]),
                           [("tf", ka)], [("wsm", wk)])
                    kb = newtb()
                    stt(tbf[:, kb, 0:128], ps[A][:, 0:128], SC_M, tf[:, ka, 0:128], ALU.mult, ALU.mult,
                        [("ps", A), ("tf", ka)], [("tb", kb)])
                    stt(tbf[:, kb, 128:256], qT_, SC_M, tf[:, ka, 128:256], ALU.mult, ALU.mult,
                        [pg(4 + h), ("tf", ka)], [("tb", kb)])
                    AT = tbf[:, kb, 0:128]
                    qsT = tbf[:, kb, 128:256]
                    B = newps()
                    mm(ps[B][:, 0:128], vt, AT, True, False, [pg(12 + i), ("tb", kb)], [("ps", B)])
                    mm(ps[B][:, 0:128], Sb[:, sk, 0:128], qsT, False, True, ["Sb%d" % sk, ("tb", kb)], [("ps", B)])
                    mm(ps[B][:, 128:256], onesB[:, :], AT, True, False, ["onesB", ("tb", kb)], [("ps", B)])
                    mm(ps[B][:, 128:256], Sb[:, sk, 128:256], qsT, False, True, ["Sb%d" % sk, ("tb", kb)],
                       [("ps", B)])
                    kc_ = newtf()
                    act(tf[:, kc_, 0:128], ps[B][:, 128:256], AF.Abs, [("ps", B)], [("tf", kc_)])
                    ts(tf[:, kc_, 0:128], tf[:, kc_, 0:128], 1.0, None, ALU.max, None, [("tf", kc_)], [("tf", kc_)])
                    recip(tf[:, kc_, 0:128], tf[:, kc_, 0:128], [("tf", kc_)], [("tf", kc_)])
                    tt(tf[:, kc_, 128:256], ps[B][:, 0:128], tf[:, kc_, 0:128], ALU.mult,
                       [("ps", B), ("tf", kc_)], [("tf", kc_)])
                    act(tbf[:, kb, 384:512], tf[:, kc_, 128:256], AF.Square, [("tf", kc_)], [("tb", kb)])
                    mm(ps[B][:, 256:384], onesMHb, tbf[:, kb, 384:512], True, True, ["cstB", ("tb", kb)], [("ps", B)])
                    act(tf[:, kc_, 384:512], ps[B][:, 256:384], AF.Ln, [("ps", B), "par"], [("tf", kc_)],
                        bias=pcol(o_eps))
                    act(tf[:, kc_, 384:512], tf[:, kc_, 384:512], AF.Exp, [("tf", kc_)], [("tf", kc_)], scale=-0.5)
                    stt(tf[:, kc_, 0:128], tf[:, kc_, 128:256], pcol(o_mh + l * 4 + h), tf[:, kc_, 384:512],
                        ALU.mult, ALU.mult, [("tf", kc_), "par"], [("tf", kc_)])
                    tt(arena[:, 28 + h, tsl], tf[:, kc_, 0:128], arena[:, 20 + h, tsl], ALU.mult,
                       [("tf", kc_), pg(20 + h)], [pg(28 + h)])
                    C = newps()
                    mm(ps[C][:, 0:128], kT, identB[:, :], True, True, [pg(8 + h), "identB"], [("ps", C)])
                    act(tbf[:, kb, 256:384], ps[C][:, 0:128], AF.Copy, [("ps", C), ("wsm", wk)], [("tb", kb)],
                        scale=wsm[:, wk, 0:1])
                    kw_ = tbf[:, kb, 256:384]
                    mm(ps[C][:, 128:256], kw_, vt, True, True, [("tb", kb), pg(12 + i)], [("ps", C)])
                    mm(ps[C][:, 256:384], kw_, onesB[:, :], True, True, [("tb", kb), "onesB"], [("ps", C)])
                    stt(S[:, sk, :], S[:, sk, :], wsm[:, wk, 1:2], ps[C][:, 128:384], ALU.mult, ALU.add,
                        ["S%d" % sk, ("wsm", wk), ("ps", C)], ["S%d" % sk])
                    sch.op("scalar", lambda e, sk=sk: e.copy(Sb[:, sk, :], S[:, sk, :]), ["S%d" % sk], ["Sb%d" % sk])

            sga_pg = [16, 17, 18, 19, 0, 1, 2, 3]
            for t in range(4):
                sg = next_w(l, 7 + t)
                for cc in range(2):
                    c = 2 * t + cc
                    GA = newps()
                    for kc in range(8):
                        mm(ps[GA][:, :], wview(sg, kc, cc * 128, 128), xnT[:, kc, :], kc == 0, kc == 7,
                           [("w", sg), ("xn", kc)], [("ps", GA)])
                    act(arena[:, sga_pg[c], :], ps[GA][:, :], AF.Sigmoid, [("ps", GA)], [pg(sga_pg[c])])
                    GM = newps()
                    for kc in range(8):
                        mm(ps[GM][:, :], wview(sg, kc, 256 + cc * 128, 128), xnT[:, kc, :], kc == 0, kc == 7,
                           [("w", sg), ("xn", kc)], [("ps", GM)])
                    act(sgmb[:, c, :], ps[GM][:, :], AF.Sigmoid, [("ps", GM)], [("sgm", c)])
            for u in range(2):
                spp = next_w(l, 11 + u)
                for tt_ in range(2):
                    for cc in range(2):
                        c = 2 * (2 * u + tt_) + cc
                        pbase = tt_ * 2048 + cc * 128
                        PA = newps()
                        for kc in range(4):
                            o_ = pbase + kc * 256
                            mm(ps[PA][:, :], wb[:, spp, o_:o_ + 128], arena[:, 24 + kc, :], kc == 0, kc == 3,
                               [("w", spp), pg(24 + kc)], [("ps", PA)])
                        PM = newps()
                        for kc in range(4):
                            o_ = pbase + 1024 + kc * 256
                            mm(ps[PM][:, :], wb[:, spp, o_:o_ + 128], arena[:, 28 + kc, :], kc == 0, kc == 3,
                               [("w", spp), pg(28 + kc)], [("ps", PM)])
                        k1 = newtf()
                        k2 = newtf()
                        tt(tf[:, k1, :], ps[PA][:, :], arena[:, sga_pg[c], :], ALU.mult,
                           [("ps", PA), pg(sga_pg[c])], [("tf", k1)])
                        tt(tf[:, k2, :], ps[PM][:, :], sgmb[:, c, :], ALU.mult, [("ps", PM), ("sgm", c)], [("tf", k2)])
                        tt(arena[:, 4 + c, :], tf[:, k1, :], tf[:, k2, :], ALU.add, [("tf", k1), ("tf", k2)], [pg(4 + c)])
            for t in range(2):
                so = next_w(l, 13 + t)
                for m in range(4):
                    c = 4 * t + m
                    P = newps()
                    for kc in range(8):
                        mm(ps[P][:, :], wview(so, kc, m * 128, 128), arena[:, 4 + kc, :], kc == 0, kc == 7,
                           [("w", so), pg(4 + kc)], [("ps", P)])
                    tt(xT[:, c, :], xT[:, c, :], ps[P][:, :], ALU.add, [("xT", c), ("ps", P)], [("xT", c)])
            rmsnorm(o_g2 + l * 8, True)
            for t in range(8):
                su = next_w(l, 15 + t)
                for m in range(4):
                    f = 4 * t + m
                    P = newps()
                    for kc in range(8):
                        mm(ps[P][:, :], wview(su, kc, m * 128, 128), xnT[:, kc, :], kc == 0, kc == 7,
                           [("w", su), ("xn", kc)], [("ps", P)])
                    k1 = newtf()
                    act(tf[:, k1, :], ps[P][:, :], AF.Relu, [("ps", P)], [("tf", k1)])
                    tt(arena[:, f, :], ps[P][:, :], tf[:, k1, :], ALU.mult, [("ps", P), ("tf", k1)], [pg(f)])
            for c in range(8):
                sd = next_w(l, 23 + c)
                P = newps()
                for f in range(32):
                    mm(ps[P][:, :], wview(sd, f, 0, 128, kcn=32), arena[:, f, :], f == 0, f == 31,
                       [("w", sd), pg(f)], [("ps", P)])
                tt(xT[:, c, :], xT[:, c, :], ps[P][:, :], ALU.add, [("xT", c), ("ps", P)], [("xT", c)])

        def load_x(jb):
            src = x_d[jb * TB:(jb + 1) * TB, :].rearrange("(i p) d -> p i d", p=128)
            sch.op("sync", lambda e, src=src: e.dma_start(out=xin[:, :, :], in_=src), [],
                   [("xin", i) for i in range(4)], dma_sem="io_in", cost=8000.0)
            for c in range(8):
                P = newps()
                for i in range(4):
                    tr(ps[P][:, i * 128:(i + 1) * 128], xin[:, i, c * 128:(c + 1) * 128], [("xin", i)], [("ps", P)])
                copy_any(xT[:, c, :], ps[P][:, :], [("ps", P)], [("xT", c)])

        def store_out(jb, srcT, srcname):
            for i in range(4):
                for half in range(2):
                    P = newps()
                    for cc in range(4):
                        c = half * 4 + cc
                        tr(ps[P][:, cc * 128:(cc + 1) * 128], srcT[:, c, i * 128:(i + 1) * 128], [(srcname, c)],
                           [("ps", P)])
                    copy_any(io[:, i, half * 512:(half + 1) * 512], ps[P][:, :], [("ps", P)], [("io", i)])
            dst = out_d[jb * TB:(jb + 1) * TB, :].rearrange("(i p) d -> p i d", p=128)
            sch.op("sync", lambda e, dst=dst: e.dma_start(out=dst, in_=io[:, :, :]),
                   [("io", i) for i in range(4)], [], dma_sem="io_out", cost=8000.0)

        if not pipe:
            for j in range(nblk):
                load_x(j)
                for l in range(L):
                    layer(l, j)
                rmsnorm(o_gf, False)
                store_out(j, xT, "xT")
        else:
            for st in range(nstep):
                jb = min(st, nblk - 1)
                xsrc = x_d[:, jb * TB:(jb + 1) * TB].rearrange("(c p) t -> p c t", p=128)
                sch.op("sync", lambda e, xsrc=xsrc: e.dma_start(out=xl[:, :, :], in_=xsrc), [],
                       [("xl", c) for c in range(8)], dma_sem="io_in", cost=8000.0)
                if st == 0:
                    for c in range(8):
                        copy_any(xT[:, c, :], xl[:, c, :], [("xl", c)], [("xT", c)])
                else:
                    for hf in range(NH):
                        rsrc = recv_d[st - 1][hf][0:128, :].rearrange("p (c t) -> p c t", c=CPP)
                        sch.op("sync", lambda e, rsrc=rsrc, hf=hf: e.dma_start(out=rbuf[:, CPP * hf:CPP * hf + CPP, :], in_=rsrc),
                               [("recv", st - 1, hf)], [("rb", c) for c in range(CPP * hf, CPP * hf + CPP)], dma_sem=("rcv", hf),
                               cost=2000.0 + 750.0 * CPP)
                        for c in range(CPP * hf, CPP * hf + CPP):
                            stt(xT[:, c, :], rbuf[:, c, :], pcol(o_fb), xl[:, c, :], ALU.mult, ALU.add,
                                [("rb", c), ("xl", c), "par"], [("xT", c)])
                layer(0, st)
                if st < nblk:
                    for hf in range(NH):
                        sdst = send_d[st][hf].rearrange("p (c t) -> p c t", c=CPP)
                        sch.op("sync", lambda e, sdst=sdst, hf=hf: e.dma_start(out=sdst, in_=xT[:, CPP * hf:CPP * hf + CPP, :]),
                               [("xT", c) for c in range(CPP * hf, CPP * hf + CPP)], [("send", st, hf)], dma_sem=("snd", hf),
                               cost=2000.0 + 750.0 * CPP)
                        sch.op("gpsimd", lambda e, st=st, hf=hf: e.collective_compute(
                            "AllGather", ALU.bypass, replica_groups=[[0, 1], [2, 3], [4, 5], [6, 7]],
                            ins=[send_d[st][hf]], outs=[recv_d[st][hf]]),
                            [("send", st, hf)], [("recv", st, hf), "cc_serial"], dma_sem=("cc", NH * st + hf),
                            cost=3000.0)
                if st >= 1:
                    kr = norm_stats()
                    for c in range(8):
                        stt(rbuf[:, c, :], xT[:, c, :], pcol(o_gf + c), tf[:, kr, :], ALU.mult, ALU.mult,
                            [("xT", c), ("tf", kr), "par"], [("rb", c)])
                    odst = out_d[:, (st - 1) * TB:st * TB].rearrange("(c p) t -> p c t", p=128)
                    sch.op("sync", lambda e, odst=odst: e.dma_start(out=odst, in_=rbuf[:, :, :]),
                           [("rb", c) for c in range(8)], [], dma_sem="io_out", cost=8000.0)

        sch.finalize()
        sch.emit(nc, es, dma_sems, ["io_out"])
    return nc


def _fm(W, cols):
    sub = W[:, cols]
    n = sub.shape[1]
    return np.ascontiguousarray(sub.reshape(8, 128, n).transpose(1, 0, 2).reshape(128, 8 * n))


def host_constants():
    s = np.arange(128)
    ident = np.eye(128, dtype=np.float32)
    onesMD = np.full((128, 128), 1.0 / 1024.0, np.float32)
    onesMH = np.full((128, 128), 1.0 / 128.0, np.float32)
    U = (s[:, None] <= s[None, :]).astype(np.float32)
    NEGm = np.where(s[:, None] <= s[None, :], 0.0, NEG).astype(np.float32)
    cst = np.concatenate([ident, onesMD, onesMH, U, NEGm], axis=1)
    kp = np.arange(128)[:, None]
    u = np.arange(640)[None, :]
    dq = u // 64 - (kp >= 64)
    negm = np.where((dq >= 0) & (dq <= 8), 0.0, NEG).astype(np.float32)
    return np.ascontiguousarray(cst), np.ascontiguousarray(negm)


def host_prep(inp, depth=DEPTH, layers=None, pipe_role=None, nblk=SEQ // TB):
    if layers is None:
        layers = list(range(depth))
    L = len(layers)
    sel = np.asarray(layers)
    inp = dict(inp)
    for k_ in ("mix_norm_g", "w_in", "conv_w", "conv_b", "b_igate", "b_fgate", "rel_bias", "mh_norm_g",
               "w_att_proj", "w_mlstm_proj", "w_out", "ffn_norm_g", "w_up", "w_down"):
        inp[k_] = np.asarray(inp[k_], np.float32)[sel]
    w_in = inp["w_in"]
    tiles = []
    wifs = []
    for l in range(L):
        for g in range(7):
            tiles.append(_fm(w_in[l], np.arange(512 * g, 512 * g + 512)))
        w_att = np.asarray(inp["w_att_proj"][l], np.float32)
        w_ml = np.asarray(inp["w_mlstm_proj"][l], np.float32)

        def pp_tile(u):
            parts = []
            for tt_ in range(2):
                t = 2 * u + tt_
                for w in (w_att, w_ml):
                    sub = w[:, t * 256:(t + 1) * 256]
                    parts.append(sub.reshape(4, 128, 256).transpose(1, 0, 2).reshape(128, 1024))
            return np.ascontiguousarray(np.concatenate(parts, axis=1))

        def gagm_tile(t):
            cols = np.concatenate([GA0 + 256 * t + np.arange(256), GM0 + 256 * t + np.arange(256)])
            return _fm(w_in[l], cols)

        for t_ in range(4):
            tiles.append(gagm_tile(t_))
        tiles.append(pp_tile(0))
        tiles.append(pp_tile(1))
        w_out = np.asarray(inp["w_out"][l], np.float32)
        for t in range(2):
            tiles.append(_fm(w_out, np.arange(512 * t, 512 * t + 512)))
        w_up = np.asarray(inp["w_up"][l], np.float32)
        for t in range(8):
            tiles.append(_fm(w_up, np.arange(512 * t, 512 * t + 512)))
        w_down = np.asarray(inp["w_down"][l], np.float32)
        for c in range(8):
            sub = w_down[:, c * 128:(c + 1) * 128]
            tiles.append(np.ascontiguousarray(sub.reshape(32, 128, 128).transpose(1, 0, 2).reshape(128, 4096)))
        wifs.append(_fm(w_in[l], np.arange(3584, 3592)))
    wstream = np.stack(tiles, axis=0)
    assert wstream.shape == (L * NW_TILES, 128, 4096)
    wif = np.concatenate(wifs, axis=1)

    kp = np.arange(128)[:, None]
    u = np.arange(640)[None, :]
    rel_idx = np.clip(u - kp, -256, 256) + 256
    rb = np.asarray(inp["rel_bias"], np.float32)[:L]
    bg = rb[:, :, rel_idx]
    biasg = np.ascontiguousarray(bg.transpose(2, 0, 1, 3).reshape(128, L * 8 * 640))

    def fmv(v):
        v = np.asarray(v, np.float32)
        k = v.shape[-1] // 128
        return np.moveaxis(v.reshape(v.shape[:-1] + (k, 128)), -1, 0)

    g1 = fmv(inp["mix_norm_g"][:L]).reshape(128, L * 8)
    g2 = fmv(inp["ffn_norm_g"][:L]).reshape(128, L * 8)
    gf = fmv(inp["final_norm_g"]).reshape(128, 8)
    cw = fmv(inp["conv_w"][:L]).reshape(128, L * 32)
    cb = fmv(inp["conv_b"][:L]).reshape(128, L * 8)
    mh = fmv(inp["mh_norm_g"][:L]).reshape(128, L * 4)
    bi = np.broadcast_to(np.asarray(inp["b_igate"], np.float32)[:L][None, :, None, :], (128, L, 4, 4)).reshape(128, L * 16)
    bf = np.broadcast_to(np.asarray(inp["b_fgate"], np.float32)[:L][None, :, None, :], (128, L, 4, 4)).reshape(128, L * 16)
    epsc = np.full((128, 1), EPS, np.float32)
    plist = [g1, g2, gf, cw, cb, mh, bi, bf, epsc]
    if pipe_role is not None:
        nstep = nblk + 1
        fb = np.full((128, 1), float(pipe_role), np.float32)
        smask = np.zeros((128, nstep), np.float32)
        smask[:, :pipe_role + 1] = NEG
        valid = np.ones((128, nstep), np.float32)
        valid[:, :pipe_role] = 0.0
        plist += [fb, smask, valid]
    params = np.ascontiguousarray(np.concatenate(plist, axis=1).astype(np.float32))
    cst, negm = host_constants()
    return {"wstream": wstream, "wif": np.ascontiguousarray(wif), "biasg": biasg, "cst": cst, "negm": negm,
            "params": params}


_NC_CACHE = {}


PIPE = True


def kernel(**inputs):
    x = np.asarray(inputs["x"], np.float32)
    B, S_, D = x.shape
    nblk = S_ // TB
    n_cores = 8
    key = (nblk, DEPTH, PIPE)
    if key not in _NC_CACHE:
        _NC_CACHE[key] = build_program(nblk, DEPTH, pipe=PIPE)
    nc = _NC_CACHE[key]
    in_maps = []
    if PIPE:
        assert DEPTH == 2 and B * 2 == n_cores
        roles = [host_prep(inputs, DEPTH, layers=[r], pipe_role=r, nblk=nblk) for r in range(2)]
        zeros = np.zeros((D, S_), np.float32)
        for c in range(n_cores):
            m = dict(roles[c % 2])
            m["x"] = np.ascontiguousarray(x[c // 2].T) if c % 2 == 0 else zeros
            in_maps.append(m)
        res = run_bass_kernel_spmd(nc, in_maps, core_ids=list(range(n_cores)))
        out = np.stack([np.ascontiguousarray(np.asarray(res.results[2 * b + 1]["out"], np.float32).T)
                        for b in range(B)], axis=0)
        return out
    shared = host_prep(inputs, DEPTH)
    for c in range(n_cores):
        m = dict(shared)
        m["x"] = np.ascontiguousarray(x[c % B])
        in_maps.append(m)
    res = run_bass_kernel_spmd(nc, in_maps, core_ids=list(range(n_cores)))
    out = np.stack([np.asarray(res.results[b]["out"], np.float32) for b in range(B)], axis=0)
    return out
```

```python
import math
from contextlib import ExitStack

import numpy as np
import concourse.bass as bass
import concourse.mybir as mybir
from concourse.bass_utils import run_bass_kernel_spmd

F32 = mybir.dt.float32
BF16 = mybir.dt.bfloat16
AF = mybir.ActivationFunctionType
ALU = mybir.AluOpType

D_MODEL = 1024
BATCH = 4
SEQ = 4096
DEPTH = 2
TB = 512
NW_TILES = 31
NWS = 5
NTF = 8
NTB = 4
EPS = 1e-6
NEG = -30000.0
GA0 = 3592
GM0 = 4616
EPOCH = 8000
SAME_ENGINE_SYNC = "raw"
LIST_SCHED = True
GATE_PREFETCH = True
GATE_DIV = 2
ATT_MUL_ENG = "vector"
NH = 2
XENG_LAT = 1000.0
import os as _os
KNOP = set(int(v) for v in _os.environ.get('KNOP', '').split(',') if v)


def _is_cc(key):
    return isinstance(key, tuple) and key[0] == "cc"


class _Op:
    __slots__ = ("eng", "fn", "deps", "dma_sem", "token", "needs_inc", "idx", "raw", "cost", "alldeps")


class Sched:
    def __init__(self):
        self.ops = []
        self.lastw = {}
        self.readers = {}

    def op(self, eng, fn, reads=(), writes=(), dma_sem=None, cost=300.0):
        o = _Op()
        o.cost = cost
        o.eng = eng
        o.fn = fn
        o.dma_sem = dma_sem
        o.idx = len(self.ops)
        o.needs_inc = dma_sem is not None
        o.token = None
        deps = set()
        raw = set()
        for r in reads:
            if r in self.lastw:
                deps.add(self.lastw[r])
                raw.add(self.lastw[r])
        o.raw = raw
        for w in writes:
            if w in self.lastw:
                deps.add(self.lastw[w])
            for rd in self.readers.get(w, ()):
                deps.add(rd)
        deps.discard(o.idx)
        o.deps = deps
        for r in reads:
            self.readers.setdefault(r, set()).add(o.idx)
        for w in writes:
            self.lastw[w] = o.idx
            self.readers[w] = set()
        self.ops.append(o)
        return o

    def finalize(self):
        import os
        if os.environ.get("KTRUNC"):
            self.ops = self.ops[:int(os.environ["KTRUNC"])]
        if os.environ.get("KSKIP"):
            a, b = [int(v) for v in os.environ["KSKIP"].split(",")]
            kept = [o for o in self.ops if not (a <= o.idx < b)]
            remap = {o.idx: i for i, o in enumerate(kept)}
            for o in kept:
                o.deps = set(remap[d] for d in o.deps if d in remap)
                o.raw = set(remap[d] for d in o.raw if d in remap)
                o.idx = remap[o.idx]
            self.ops = kept
        ops = self.ops
        for o in ops:
            o.alldeps = set(o.deps)
        self.order = self.list_schedule() if LIST_SCHED else None
        for o in ops:
            keep = set()
            for d in o.deps:
                do = ops[d]
                if do.eng == o.eng and do.dma_sem is None and o.dma_sem is None:
                    if o.eng == "tensor" or not SAME_ENGINE_SYNC:
                        continue
                    if SAME_ENGINE_SYNC == "raw" and d not in o.raw:
                        continue
                keep.add(d)
            o.deps = keep
            for d in keep:
                ops[d].needs_inc = True

    def list_schedule(self):
        import heapq
        ops = self.ops
        n = len(ops)
        ndep = [len(o.alldeps) for o in ops]
        users = [[] for _ in range(n)]
        for o in ops:
            for d in o.alldeps:
                users[d].append(o.idx)
        finish = [0.0] * n
        ready_t = [0.0] * n
        engs = ["tensor", "vector", "scalar", "gpsimd", "sync"]
        pend = {e: [] for e in engs}
        avail = {e: [] for e in engs}
        t_free = {e: 0.0 for e in engs}
        for o in ops:
            if ndep[o.idx] == 0:
                heapq.heappush(pend[o.eng], (0.0, o.idx))
        order = []
        done = 0
        while done < n:
            best = None
            for e in engs:
                while pend[e] and pend[e][0][0] <= t_free[e]:
                    rt, i = heapq.heappop(pend[e])
                    heapq.heappush(avail[e], i)
                if avail[e]:
                    cand = (t_free[e], avail[e][0], e, True)
                elif pend[e]:
                    cand = (pend[e][0][0], pend[e][0][1], e, False)
                else:
                    continue
                if best is None or cand[:2] < best[:2]:
                    best = cand
            st, i, e, from_avail = best
            if from_avail:
                heapq.heappop(avail[e])
            else:
                heapq.heappop(pend[e])
            o = ops[i]
            fin = st + o.cost
            t_free[e] = fin
            finish[i] = fin
            order.append(i)
            done += 1
            for u in users[i]:
                lat = 0.0 if ops[u].eng == e else XENG_LAT
                if fin + lat > ready_t[u]:
                    ready_t[u] = fin + lat
                ndep[u] -= 1
                if ndep[u] == 0:
                    heapq.heappush(pend[ops[u].eng], (ready_t[u], u))
        self.est_total = max(finish) if finish else 0.0
        return order

    def emit(self, nc, es, dma_sems, final_waits):
        ops = self.ops
        seq = [ops[i] for i in self.order] if self.order is not None else list(ops)
        eng_sems = {}
        counters = {}
        dma_counts = {}
        for o in seq:
            if o.dma_sem is not None:
                c = dma_counts.get(o.dma_sem, 0) + (1 if _is_cc(o.dma_sem) else 16)
                dma_counts[o.dma_sem] = c
                o.token = (dma_sems[o.dma_sem], c)
            elif o.needs_inc:
                ep, c = counters.get(o.eng, (0, 0))
                if c >= EPOCH:
                    ep, c = ep + 1, 0
                c += 1
                counters[o.eng] = (ep, c)
                key = (o.eng, ep)
                if key not in eng_sems:
                    eng_sems[key] = es.enter_context(nc.semaphore("p_%s_%d" % (o.eng, ep)))
                o.token = (eng_sems[key], c)
        for o in ops:
            if o.dma_sem in ("setup", "setup_sw"):
                o.token = (dma_sems[o.dma_sem], dma_counts[o.dma_sem])
        by_eng = {}
        for o in seq:
            by_eng.setdefault(o.eng, []).append(o)

        def run(eng_name, e):
            waited = {}
            for o in by_eng.get(eng_name, []):
                need = {}
                for d in o.deps:
                    s, v = ops[d].token
                    k = id(s)
                    if waited.get(k, 0) >= v:
                        continue
                    if k not in need or need[k][1] < v:
                        need[k] = (s, v)
                for k, (s, v) in need.items():
                    e.wait_ge(s, v)
                    waited[k] = v
                if o.idx in KNOP:
                    ins = e.nop()
                else:
                    ins = o.fn(e)
                if o.dma_sem is not None:
                    if _is_cc(o.dma_sem):
                        ins.then_inc(o.token[0])
                    else:
                        ins.then_inc(o.token[0], 16)
                elif o.needs_inc:
                    ins.then_inc(o.token[0], 1)
            mine = []
            for o in by_eng.get(eng_name, []):
                if o.dma_sem is not None and o.dma_sem not in mine:
                    mine.append(o.dma_sem)
            for key in mine:
                e.wait_ge(dma_sems[key], dma_counts[key])

        with nc.Block() as block:
            @block.sync
            def _(e):
                run("sync", e)

            @block.gpsimd
            def _(e):
                run("gpsimd", e)

            @block.tensor
            def _(e):
                run("tensor", e)

            @block.vector
            def _(e):
                run("vector", e)

            @block.scalar
            def _(e):
                run("scalar", e)


def build_program(nblk, depth=DEPTH, pipe=False):
    T = nblk * TB
    nc = bass.Bass("TRN2", target_bir_lowering=False)
    L = 1 if pipe else depth
    nstep = nblk + 1 if pipe else nblk
    if pipe:
        x_d = nc.dram_tensor("x", [D_MODEL, T], F32, kind="ExternalInput").ap()
    else:
        x_d = nc.dram_tensor("x", [T, D_MODEL], F32, kind="ExternalInput").ap()
    w_d = nc.dram_tensor("wstream", [L * NW_TILES, 128, 4096], F32, kind="ExternalInput").ap()
    wif_d = nc.dram_tensor("wif", [128, L * 64], F32, kind="ExternalInput").ap()
    bias_d = nc.dram_tensor("biasg", [128, L * 8 * 640], F32, kind="ExternalInput").ap()
    cst_d = nc.dram_tensor("cst", [128, 5 * 128], F32, kind="ExternalInput").ap()
    negm_d = nc.dram_tensor("negm", [128, 640], F32, kind="ExternalInput").ap()
    NPAR = L * 8 + L * 8 + 8 + L * 32 + L * 8 + L * 4 + L * 16 + L * 16 + 1 + (1 + 2 * (nblk + 1) if pipe else 0)
    par_d = nc.dram_tensor("params", [128, NPAR], F32, kind="ExternalInput").ap()
    if pipe:
        out_d = nc.dram_tensor("out", [D_MODEL, T], F32, kind="ExternalOutput").ap()
    else:
        out_d = nc.dram_tensor("out", [T, D_MODEL], F32, kind="ExternalOutput").ap()

    es = ExitStack()
    with es:
        def sb(name, shape, dt):
            return es.enter_context(nc.sbuf_tensor(name, shape, dt))

        if not pipe:
            io = sb("io", [128, 4, 1024], F32)
        if pipe:
            xl = sb("xl", [128, 8, TB], F32)
            rbuf = sb("rbuf", [128, 8, TB], F32)
            CPP = 8 // NH
            send_d = [[nc.dram_tensor("send%d_%d" % (k, hf), [128, CPP * TB], F32, kind="Internal").ap()
                       for hf in range(NH)] for k in range(nblk)]
            recv_d = [[nc.dram_tensor("recv%d_%d" % (k, hf), [256, CPP * TB], F32, kind="Internal").ap()
                       for hf in range(NH)] for k in range(nblk)]
        else:
            xin = io
        xT = sb("xT", [128, 8, TB], F32)
        xnT = sb("xnT", [128, 8, TB], BF16)
        arena = sb("arena", [128, 32, TB], BF16)
        kh = sb("kh", [128, L * 2 * 4, TB], BF16)
        sgmb = sb("sgmb", [128, 8, TB], BF16)
        vh = sb("vh", [128, L * 2 * 4, TB], BF16)
        wb = sb("wb", [128, NWS, 4096], BF16)
        wif = sb("wifs", [128, L * 64], BF16)
        BM = sb("BM", [128, L * 8, 640], BF16)
        negm = sb("negms", [128, 640], BF16)
        S = sb("S", [128, L * 4, 256], F32)
        Sb = sb("Sb", [128, L * 4, 256], BF16)
        halo = sb("halo", [128, L * 8, 4], F32)
        cst = sb("csts", [128, 5, 128], F32)
        identB = sb("identB", [128, 128], BF16)
        cstB = sb("cstB", [128, 3, 128], BF16)
        onesB = sb("onesB", [128, 128], BF16)
        onesF = sb("onesF", [128, 128], F32)
        par = sb("pars", [128, NPAR], F32)
        tf = sb("tf", [128, NTF, 512], F32)
        tbf = sb("tbf", [128, NTB, 512], BF16)
        pre = sb("pre", [128, 2, 516], F32)
        LF = sb("LF", [128, 2, 4, 128], F32)
        gsm = sb("gsm", [128, 8, 16], F32)
        wsm = sb("wsm", [128, 8, 2], F32)
        ps = [es.enter_context(nc.psum_tensor("ps%d" % b, [128, 512], F32)) for b in range(8)]

        identF = cst[:, 0, :]
        onesMD = cst[:, 1, :]
        onesMH = cst[:, 2, :]
        Umat = cst[:, 3, :]
        NEGmat = cst[:, 4, :]
        onesMDb = cstB[:, 0, :]
        onesMHb = cstB[:, 1, :]
        NEGb = cstB[:, 2, :]

        o_g1 = 0
        o_g2 = o_g1 + L * 8
        o_gf = o_g2 + L * 8
        o_cw = o_gf + 8
        o_cb = o_cw + L * 32
        o_mh = o_cb + L * 8
        o_bi = o_mh + L * 4
        o_bf = o_bi + L * 16
        o_eps = o_bf + L * 16
        o_fb = o_eps + 1
        o_sm = o_fb + 1
        o_vd = o_sm + nstep

        def pcol(off):
            return par[:, off:off + 1]

        dma_sems = {}
        extra_sems = ([("snd", k) for k in range(NH)] + [("rcv", k) for k in range(NH)]
                      + [("cc", k) for k in range(NH * nblk)]) if pipe else []
        extra_sems = extra_sems + [("bm", k) for k in range(L * 8)]
        for k in ["setup", "setup_sw", "io_in", "io_out"] + extra_sems + [("w", s) for s in range(NWS)]:
            dma_sems[k] = es.enter_context(nc.semaphore("d_%s" % (str(k).replace(" ", ""))))

        sch = Sched()
        state = {"ps": 0, "tf": 0, "tb": 0, "w": 0, "alt": 0, "pre": 0, "lf": 0, "wsm": 0}

        def newps(exclude=()):
            b = state["ps"]
            while b in exclude:
                b = (b + 1) % 8
            state["ps"] = (b + 1) % 8
            return b

        def newtf():
            k = state["tf"]
            state["tf"] = (k + 1) % NTF
            return k

        def newtb():
            k = state["tb"]
            state["tb"] = (k + 1) % NTB
            return k

        def fsz(ap):
            n = 1
            for d in ap.shape[1:]:
                n *= int(d)
            return n

        def mm(out, lhsT, rhs, start, stop, r, w):
            c = max(fsz(rhs), 64) / 1.95 + 15.0
            if lhsT.dtype == F32:
                c *= 4.0
            sch.op("tensor", lambda e: e.matmul(out, lhsT, rhs, start=start, stop=stop), r, w, cost=c)

        def tr(out, in_, r, w):
            sch.op("tensor", lambda e: e.transpose(out, in_, identF), list(r) + ["cst"], w, cost=280.0)

        def act(out, in_, func, r, w, bias=None, scale=None):
            kw = {}
            if bias is not None:
                kw["bias"] = bias
            if scale is not None:
                kw["scale"] = scale
            c = fsz(out) / 1.4 + 220.0 + (90.0 if bias is not None and not isinstance(bias, float) else 0.0)
            sch.op("scalar", lambda e: e.activation(out, in_, func, **kw), r, w, cost=c)

        def dcost(ap, mult=1.0):
            return max(fsz(ap), 64) * mult / 0.96 + 90.0

        def tt(out, in0, in1, op, r, w, eng="vector"):
            sch.op(eng, lambda e: e.tensor_tensor(out, in0, in1, op), r, w, cost=dcost(out))

        def ts(out, in0, s1, s2, op0, op1, r, w, eng="vector"):
            if op1 is None:
                sch.op(eng, lambda e: e.tensor_scalar(out, in0, s1, None, op0), r, w, cost=dcost(out))
            else:
                sch.op(eng, lambda e: e.tensor_scalar(out, in0, s1, s2, op0, op1), r, w, cost=dcost(out))

        def stt(out, in0, scalar, in1, op0, op1, r, w):
            sch.op("vector", lambda e: e.scalar_tensor_tensor(out, in0, scalar, in1, op0, op1), r, w,
                   cost=dcost(out))

        def recip(out, in_, r, w):
            sch.op("vector", lambda e: e.reciprocal(out, in_), r, w, cost=dcost(out, 8.0))

        def copy_any(out, in_, r, w):
            state["alt"] ^= 1
            if state["alt"]:
                sch.op("scalar", lambda e: e.copy(out, in_), r, w, cost=fsz(out) / 1.4 + 220.0)
            else:
                sch.op("vector", lambda e: e.tensor_copy(out, in_), r, w, cost=dcost(out))

        def dsetup(eng, out, in_, w):
            sch.op(eng, lambda e: e.dma_start(out=out, in_=in_), [], w,
                   dma_sem="setup_sw" if eng == "gpsimd" else "setup")

        dsetup("sync", cst[:, :, :], cst_d.rearrange("p (k n) -> p k n", k=5), ["cst"])
        dsetup("sync", par[:, :], par_d, ["par"])
        dsetup("gpsimd", wif[:, :], wif_d, ["wif"])
        dsetup("gpsimd", negm[:, :], negm_d, ["negm"])
        def setup_bm():
            for l in range(L):
                for h in range(8):
                    k = l * 8 + h
                    sch.op("gpsimd", lambda e, k=k: e.dma_start(out=BM[:, k, :], in_=bias_d[:, k * 640:(k + 1) * 640]),
                           [], [("BM", k)], dma_sem=("bm", k), cost=2500.0)
                    tt(BM[:, k, :], BM[:, k, :], negm[:, :], ALU.add, [("BM", k), "negm"], [("BM", k)])
                    if ATT_MUL_ENG:
                        act(BM[:, k, :], BM[:, k, :], AF.Exp, [("BM", k)], [("BM", k)])

        sch.op("vector", lambda e: e.memset(S[:, :, :], 0.0), [], ["S%d" % k for k in range(L * 4)])
        sch.op("vector", lambda e: e.memset(Sb[:, :, :], 0.0), [], ["Sb%d" % k for k in range(L * 4)])
        sch.op("vector", lambda e: e.memset(halo[:, :, :], 0.0), [], [("halo", k) for k in range(L * 8)])
        sch.op("vector", lambda e: e.memset(kh[:, :, :], 0.0), [], [("kh", k) for k in range(L * 8)])
        sch.op("vector", lambda e: e.memset(vh[:, :, :], 0.0), [], [("vh", k) for k in range(L * 8)])
        sch.op("vector", lambda e: e.memset(onesB[:, :], 1.0), [], ["onesB"])
        sch.op("vector", lambda e: e.memset(onesF[:, :], 1.0), [], ["onesF"])
        sch.op("vector", lambda e: e.tensor_copy(identB[:, :], identF), ["cst"], ["identB"])
        sch.op("vector", lambda e: e.tensor_copy(cstB[:, 0:2, :], cst[:, 1:3, :]), ["cst"], ["cstB"])
        sch.op("vector", lambda e: e.tensor_copy(cstB[:, 2, :], cst[:, 4, :]), ["cst", "cstB"], ["cstB"])
        def next_w(l, k_expected):
            n = state["w"]
            state["w"] = n + 1
            assert n % NW_TILES == k_expected and n // NW_TILES % L == l, (n, l, k_expected)
            slot = n % NWS
            src = w_d[(n % (L * NW_TILES)), :, :]
            gate = []
            if pipe and GATE_PREFETCH and n // NW_TILES >= 1 and n % NW_TILES < NWS:
                gate = [("recv", n // NW_TILES - 1, min((n % NW_TILES) // GATE_DIV, NH - 1))]
            sch.op("gpsimd", lambda e: e.dma_start(out=wb[:, slot, :], in_=src), gate, [("w", slot)],
                   dma_sem=("w", slot), cost=9000.0)
            return slot

        def wview(slot, kc, c0, n, kcn=8):
            width = 4096 // kcn
            return wb[:, slot, kc * width + c0: kc * width + c0 + n]

        def rstd_from(P_ap, kr, rd):
            act(tf[:, kr, :], P_ap, AF.Ln, rd + ["par"], [("tf", kr)], bias=pcol(o_eps))
            act(tf[:, kr, :], tf[:, kr, :], AF.Exp, [("tf", kr)], [("tf", kr)], scale=-0.5)

        def norm_stats():
            P = newps()
            for c in range(8):
                k = newtb()
                act(tbf[:, k, :], xT[:, c, :], AF.Square, [("xT", c)], [("tb", k)])
                mm(ps[P][:, :], onesMDb, tbf[:, k, :], c == 0, c == 7, [("tb", k), "cstB"], [("ps", P)])
            kr = newtf()
            rstd_from(ps[P][:, :], kr, [("ps", P)])
            return kr

        def rmsnorm(gcol0, dst_bf16):
            kr = norm_stats()
            for c in range(8):
                if dst_bf16:
                    stt(xnT[:, c, :], xT[:, c, :], pcol(gcol0 + c), tf[:, kr, :], ALU.mult, ALU.mult,
                        [("xT", c), ("tf", kr), "par"], [("xn", c)])
                else:
                    stt(xT[:, c, :], xT[:, c, :], pcol(gcol0 + c), tf[:, kr, :], ALU.mult, ALU.mult,
                        [("xT", c), ("tf", kr), "par"], [("xT", c)])

        def pg(n):
            return ("pg", n)

        SC_M = 1.0 / math.sqrt(128.0)

        def layer(l, j):
            parity = j % 2
            kcur = lambda p: (l * 2 + parity) * 4 + p
            kprev = lambda p: (l * 2 + 1 - parity) * 4 + p

            rmsnorm(o_g1 + l * 8, True)
            xn_all = [("xn", c) for c in range(8)]

            for g in range(2):
                slot = next_w(l, g)
                for m in range(4):
                    P = newps()
                    for kc in range(8):
                        mm(ps[P][:, :], wview(slot, kc, m * 128, 128), xnT[:, kc, :], kc == 0, kc == 7,
                           [("w", slot), ("xn", kc)], [("ps", P)])
                    if g == 0:
                        copy_any(arena[:, m, :], ps[P][:, :], [("ps", P)], [pg(m)])
                    else:
                        copy_any(kh[:, kcur(m), :], ps[P][:, :], [("ps", P)], [("kh", kcur(m))])
            slot = next_w(l, 2)
            for i in range(4):
                P = newps()
                for kc in range(8):
                    mm(ps[P][:, :], xnT[:, kc, i * 128:(i + 1) * 128], wview(slot, kc, 0, 512), kc == 0, kc == 7,
                       [("w", slot), ("xn", kc)], [("ps", P)])
                copy_any(vh[:, kcur(i), :], ps[P][:, :], [("ps", P)], [("vh", kcur(i))])

            if not state.get("bm_done"):
                state["bm_done"] = True
                setup_bm()
            for p in range(4):
                O = newps()
                Dn = newps()
                rlist = [4] + [r for r in range(8) if (j > 0 or r >= 4 or pipe) and r != 4]
                for idx, r in enumerate(rlist):
                    for e_ in range(2):
                        h = 2 * p + e_
                        rows = slice(e_ * 64, (e_ + 1) * 64)
                        if True:
                            kidx = kprev(p) if r < 4 else kcur(p)
                            vidx = kprev(r % 4) if r < 4 else kcur(r % 4)
                            qlo = max(8, 2 * r) * 64 - 512
                            qhi = (min(15, 2 * r + 9) + 1) * 64 - 512
                            n = qhi - qlo
                            u0 = 512 + qlo - 128 * r
                            sc = newps(exclude=(O, Dn))
                            mm(ps[sc][:, 0:n], kh[rows, kidx, (r % 4) * 128:(r % 4 + 1) * 128], arena[rows, p, qlo:qhi],
                               True, True, [("kh", kidx), pg(p)], [("ps", sc)])
                            k1 = newtf()
                            k2 = newtb()
                            if ATT_MUL_ENG:
                                if pipe and r < 4:
                                    act(tf[:, k1, 0:n], ps[sc][:, 0:n], AF.Exp, [("ps", sc), "par"], [("tf", k1)],
                                        bias=pcol(o_sm + j), scale=0.125)
                                else:
                                    act(tf[:, k1, 0:n], ps[sc][:, 0:n], AF.Exp, [("ps", sc)], [("tf", k1)], scale=0.125)
                                sch.op(ATT_MUL_ENG, lambda e, k1=k1, k2=k2, n=n, u0=u0, lh=l * 8 + h: e.tensor_tensor(
                                    tbf[:, k2, 0:n], tf[:, k1, 0:n], BM[:, lh, u0:u0 + n], ALU.mult),
                                    [("tf", k1), ("BM", l * 8 + h)], [("tb", k2)],
                                    cost=(n / 0.5 + 500.0) if ATT_MUL_ENG == "gpsimd" else (n / 0.96 + 90.0))
                            else:
                                stt(tf[:, k1, 0:n], ps[sc][:, 0:n], 0.125, BM[:, l * 8 + h, u0:u0 + n], ALU.mult, ALU.add,
                                    [("ps", sc), ("BM", l * 8 + h)], [("tf", k1)])
                                if pipe and r < 4:
                                    act(tbf[:, k2, 0:n], tf[:, k1, 0:n], AF.Exp, [("tf", k1), "par"], [("tb", k2)],
                                        bias=pcol(o_sm + j))
                                else:
                                    act(tbf[:, k2, 0:n], tf[:, k1, 0:n], AF.Exp, [("tf", k1)], [("tb", k2)])
                            first = idx == 0
                            last = idx == len(rlist) - 1
                            mm(ps[O][rows, qlo:qhi], vh[:, vidx, h * 64:(h + 1) * 64], tbf[:, k2, 0:n], first, last,
                               [("vh", vidx), ("tb", k2)], [("ps", O)])
                            mm(ps[Dn][rows, qlo:qhi], onesB[:, 0:64], tbf[:, k2, 0:n], first, last,
                               ["onesB", ("tb", k2)], [("ps", Dn)])
                kr = newtf()
                act(tf[:, kr, :], ps[Dn][:, :], AF.Ln, [("ps", Dn)], [("tf", kr)])
                act(tf[:, kr, :], tf[:, kr, :], AF.Exp, [("tf", kr)], [("tf", kr)], scale=-1.0)
                tt(arena[:, 24 + p, :], ps[O][:, :], tf[:, kr, :], ALU.mult, [("ps", O), ("tf", kr)], [pg(24 + p)])

            for g in (3, 4):
                slot = next_w(l, g)
                for m in range(4):
                    ch = (g - 3) * 4 + m
                    P = newps()
                    for kc in range(8):
                        mm(ps[P][:, :], wview(slot, kc, m * 128, 128), xnT[:, kc, :], kc == 0, kc == 7,
                           [("w", slot), ("xn", kc)], [("ps", P)])
                    pk = state["pre"]
                    state["pre"] ^= 1
                    hk = l * 8 + ch
                    copy_any(pre[:, pk, 3:515], ps[P][:, :], [("ps", P)], [("pre", pk)])
                    sch.op("vector", lambda e, pk=pk, hk=hk: e.tensor_copy(pre[:, pk, 0:3], halo[:, hk, 0:3]),
                           [("halo", hk)], [("pre", pk)])
                    sch.op("vector", lambda e, pk=pk, hk=hk: e.tensor_copy(halo[:, hk, 0:3], pre[:, pk, 512:515]),
                           [("pre", pk)], [("halo", hk)])
                    ka = newtf()
                    cw = lambda tap: pcol(o_cw + l * 32 + tap * 8 + ch)
                    ts(tf[:, ka, :], pre[:, pk, 0:512], cw(0), None, ALU.mult, None,
                       [("pre", pk), "par"], [("tf", ka)])
                    for tap in (1, 2, 3):
                        stt(tf[:, ka, :], pre[:, pk, tap:tap + 512], cw(tap), tf[:, ka, :], ALU.mult, ALU.add,
                            [("pre", pk), "par", ("tf", ka)], [("tf", ka)])
                    act(arena[:, 4 + ch, :], tf[:, ka, :], AF.Silu, [("tf", ka), "par"], [pg(4 + ch)],
                        bias=pcol(o_cb + l * 8 + ch))
            slot = next_w(l, 5)
            for i in range(4):
                P = newps()
                for kc in range(8):
                    mm(ps[P][:, :], xnT[:, kc, i * 128:(i + 1) * 128], wview(slot, kc, 0, 512), kc == 0, kc == 7,
                       [("w", slot), ("xn", kc)], [("ps", P)])
                copy_any(arena[:, 12 + i, :], ps[P][:, :], [("ps", P)], [pg(12 + i)])
            slot = next_w(l, 6)
            for m in range(4):
                P = newps()
                for kc in range(8):
                    mm(ps[P][:, :], wview(slot, kc, m * 128, 128), xnT[:, kc, :], kc == 0, kc == 7,
                       [("w", slot), ("xn", kc)], [("ps", P)])
                act(arena[:, 20 + m, :], ps[P][:, :], AF.Sigmoid, [("ps", P)], [pg(20 + m)])
            G = newps()
            for i in range(4):
                for kc in range(8):
                    mm(ps[G][:, i * 8:(i + 1) * 8], xnT[:, kc, i * 128:(i + 1) * 128],
                       wif[:, l * 64 + kc * 8: l * 64 + kc * 8 + 8], kc == 0, kc == 7,
                       ["wif", ("xn", kc)], [("ps", G)])
            G3 = ps[G][:, 0:32].rearrange("p (i c) -> p i c", c=8)
            ig = gsm[:, 0, :]
            zf = gsm[:, 1, :]
            ef = gsm[:, 2, :]
            spv = gsm[:, 3, :]
            logf = gsm[:, 4, :]
            biasv = gsm[:, 5, :]
            v3 = lambda ap: ap.rearrange("p (i c) -> p i c", c=4)
            tt(v3(ig), G3[:, :, 0:4], v3(par[:, o_bi + l * 16:o_bi + l * 16 + 16]), ALU.add,
               [("ps", G), "par"], ["g_ig"])
            tt(v3(zf), G3[:, :, 4:8], v3(par[:, o_bf + l * 16:o_bf + l * 16 + 16]), ALU.add,
               [("ps", G), "par"], ["g_zf"])
            act(ef, zf, AF.Exp, ["g_zf"], ["g_ef"], scale=-1.0)
            act(spv, ef, AF.Ln, ["g_ef"], ["g_sp"], bias=1.0)
            ts(logf, spv, -1.0, None, ALU.mult, None, ["g_sp"], ["g_lf"])
            Cm = newps()
            mm(ps[Cm][:, 0:16], Umat, logf, True, True, ["cst", "g_lf"], [("ps", Cm)])
            tt(biasv, ig, ps[Cm][:, 0:16], ALU.subtract, ["g_ig", ("ps", Cm)], ["g_bv"])

            for i in range(4):
                lfk = state["lf"]
                state["lf"] ^= 1
                for h in range(4):
                    col = i * 4 + h
                    ts(LF[:, lfk, h, :], onesF[:, :], logf[:, col:col + 1], None, ALU.mult, None,
                       ["onesF", "g_lf"], [("LF", lfk, h)])
                tsl = slice(i * 128, (i + 1) * 128)
                for h in range(4):
                    col = i * 4 + h
                    sk = l * 4 + h
                    kT = arena[:, 8 + h, tsl]
                    qT_ = arena[:, 4 + h, tsl]
                    vt = arena[:, 12 + i, h * 128:(h + 1) * 128]
                    A = newps()
                    mm(ps[A][:, 0:128], kT, qT_, True, True, [pg(8 + h), pg(4 + h)], [("ps", A)])
                    mm(ps[A][:, 128:256], LF[:, lfk, h, :], Umat, True, True, [("LF", lfk, h), "cst"], [("ps", A)])
                    mm(ps[A][:, 256:384], LF[:, lfk, h, :], Umat, True, False, [("LF", lfk, h), "cst"], [("ps", A)])
                    mm(ps[A][:, 256:384], identB[:, :], NEGb, False, True, ["identB", "cstB"], [("ps", A)])
                    ka = newtf()
                    act(tf[:, ka, 0:128], ps[A][:, 256:384], AF.Exp, [("ps", A), "g_bv"], [("tf", ka)],
                        bias=biasv[:, col:col + 1])
                    act(tf[:, ka, 128:256], ps[A][:, 128:256], AF.Exp, [("ps", A)], [("tf", ka)])
                    wk = state["wsm"]
                    state["wsm"] = (wk + 1) % 8
                    if pipe:
                        ts(wsm[:, wk, 0:1], tf[:, ka, 127:128], pcol(o_vd + j), None, ALU.mult, None,
                           [("tf", ka), "par"], [("wsm", wk)])
                    else:
                        sch.op("vector", lambda e, wk=wk, ka=ka: e.tensor_copy(wsm[:, wk, 0:1], tf[:, ka, 127:128]),
                               [("tf", ka)], [("wsm", wk)])
                    sch.op("vector", lambda e, wk=wk, ka=ka: e.tensor_copy(wsm[:, wk, 1:2], tf[:, ka, 255:256]),
                           [("tf", ka)], [("wsm", wk)])
                    kb = newtb()
                    stt(tbf[:, kb, 0:128], ps[A][:, 0:128], SC_M, tf[:, ka, 0:128], ALU.mult, ALU.mult,
                        [("ps", A), ("tf", ka)], [("tb", kb)])
                    stt(tbf[:, kb, 128:256], qT_, SC_M, tf[:, ka, 128:256], ALU.mult, ALU.mult,
                        [pg(4 + h), ("tf", ka)], [("tb", kb)])
                    AT = tbf[:, kb, 0:128]
                    qsT = tbf[:, kb, 128:256]
                    B = newps()
                    mm(ps[B][:, 0:128], vt, AT, True, False, [pg(12 + i), ("tb", kb)], [("ps", B)])
                    mm(ps[B][:, 0:128], Sb[:, sk, 0:128], qsT, False, True, ["Sb%d" % sk, ("tb", kb)], [("ps", B)])
                    mm(ps[B][:, 128:256], onesB[:, :], AT, True, False, ["onesB", ("tb", kb)], [("ps", B)])
                    mm(ps[B][:, 128:256], Sb[:, sk, 128:256], qsT, False, True, ["Sb%d" % sk, ("tb", kb)],
                       [("ps", B)])
                    kc_ = newtf()
                    act(tf[:, kc_, 0:128], ps[B][:, 128:256], AF.Abs, [("ps", B)], [("tf", kc_)])
                    ts(tf[:, kc_, 0:128], tf[:, kc_, 0:128], 1.0, None, ALU.max, None, [("tf", kc_)], [("tf", kc_)])
                    recip(tf[:, kc_, 0:128], tf[:, kc_, 0:128], [("tf", kc_)], [("tf", kc_)])
                    tt(tf[:, kc_, 128:256], ps[B][:, 0:128], tf[:, kc_, 0:128], ALU.mult,
                       [("ps", B), ("tf", kc_)], [("tf", kc_)])
                    act(tbf[:, kb, 384:512], tf[:, kc_, 128:256], AF.Square, [("tf", kc_)], [("tb", kb)])
                    mm(ps[B][:, 256:384], onesMHb, tbf[:, kb, 384:512], True, True, ["cstB", ("tb", kb)], [("ps", B)])
                    act(tf[:, kc_, 384:512], ps[B][:, 256:384], AF.Ln, [("ps", B), "par"], [("tf", kc_)],
                        bias=pcol(o_eps))
                    act(tf[:, kc_, 384:512], tf[:, kc_, 384:512], AF.Exp, [("tf", kc_)], [("tf", kc_)], scale=-0.5)
                    stt(tf[:, kc_, 0:128], tf[:, kc_, 128:256], pcol(o_mh + l * 4 + h), tf[:, kc_, 384:512],
                        ALU.mult, ALU.mult, [("tf", kc_), "par"], [("tf", kc_)])
                    tt(arena[:, 28 + h, tsl], tf[:, kc_, 0:128], arena[:, 20 + h, tsl], ALU.mult,
                       [("tf", kc_), pg(20 + h)], [pg(28 + h)])
                    C = newps()
                    mm(ps[C][:, 0:128], kT, identB[:, :], True, True, [pg(8 + h), "identB"], [("ps", C)])
                    act(tbf[:, kb, 256:384], ps[C][:, 0:128], AF.Copy, [("ps", C), ("wsm", wk)], [("tb", kb)],
                        scale=wsm[:, wk, 0:1])
                    kw_ = tbf[:, kb, 256:384]
                    mm(ps[C][:, 128:256], kw_, vt, True, True, [("tb", kb), pg(12 + i)], [("ps", C)])
                    mm(ps[C][:, 256:384], kw_, onesB[:, :], True, True, [("tb", kb), "onesB"], [("ps", C)])
                    stt(S[:, sk, :], S[:, sk, :], wsm[:, wk, 1:2], ps[C][:, 128:384], ALU.mult, ALU.add,
                        ["S%d" % sk, ("wsm", wk), ("ps", C)], ["S%d" % sk])
                    sch.op("scalar", lambda e, sk=sk: e.copy(Sb[:, sk, :], S[:, sk, :]), ["S%d" % sk], ["Sb%d" % sk])

            sga_pg = [16, 17, 18, 19, 0, 1, 2, 3]
            for t in range(4):
                sg = next_w(l, 7 + t)
                for cc in range(2):
                    c = 2 * t + cc
                    GA = newps()
                    for kc in range(8):
                        mm(ps[GA][:, :], wview(sg, kc, cc * 128, 128), xnT[:, kc, :], kc == 0, kc == 7,
                           [("w", sg), ("xn", kc)], [("ps", GA)])
                    act(arena[:, sga_pg[c], :], ps[GA][:, :], AF.Sigmoid, [("ps", GA)], [pg(sga_pg[c])])
                    GM = newps()
                    for kc in range(8):
                        mm(ps[GM][:, :], wview(sg, kc, 256 + cc * 128, 128), xnT[:, kc, :], kc == 0, kc == 7,
                           [("w", sg), ("xn", kc)], [("ps", GM)])
                    act(sgmb[:, c, :], ps[GM][:, :], AF.Sigmoid, [("ps", GM)], [("sgm", c)])
            for u in range(2):
                spp = next_w(l, 11 + u)
                for tt_ in range(2):
                    for cc in range(2):
                        c = 2 * (2 * u + tt_) + cc
                        pbase = tt_ * 2048 + cc * 128
                        PA = newps()
                        for kc in range(4):
                            o_ = pbase + kc * 256
                            mm(ps[PA][:, :], wb[:, spp, o_:o_ + 128], arena[:, 24 + kc, :], kc == 0, kc == 3,
                               [("w", spp), pg(24 + kc)], [("ps", PA)])
                        PM = newps()
                        for kc in range(4):
                            o_ = pbase + 1024 + kc * 256
                            mm(ps[PM][:, :], wb[:, spp, o_:o_ + 128], arena[:, 28 + kc, :], kc == 0, kc == 3,
                               [("w", spp), pg(28 + kc)], [("ps", PM)])
                        k1 = newtf()
                        k2 = newtf()
                        tt(tf[:, k1, :], ps[PA][:, :], arena[:, sga_pg[c], :], ALU.mult,
                           [("ps", PA), pg(sga_pg[c])], [("tf", k1)])
                        tt(tf[:, k2, :], ps[PM][:, :], sgmb[:, c, :], ALU.mult, [("ps", PM), ("sgm", c)], [("tf", k2)])
                        tt(arena[:, 4 + c, :], tf[:, k1, :], tf[:, k2, :], ALU.add, [("tf", k1), ("tf", k2)], [pg(4 + c)])
            for t in range(2):
                so = next_w(l, 13 + t)
                for m in range(4):
                    c = 4 * t + m
                    P = newps()
                    for kc in range(8):
                        mm(ps[P][:, :], wview(so, kc, m * 128, 128), arena[:, 4 + kc, :], kc == 0, kc == 7,
                           [("w", so), pg(4 + kc)], [("ps", P)])
                    tt(xT[:, c, :], xT[:, c, :], ps[P][:, :], ALU.add, [("xT", c), ("ps", P)], [("xT", c)])
            rmsnorm(o_g2 + l * 8, True)
            for t in range(8):
                su = next_w(l, 15 + t)
                for m in range(4):
                    f = 4 * t + m
                    P = newps()
                    for kc in range(8):
                        mm(ps[P][:, :], wview(su, kc, m * 128, 128), xnT[:, kc, :], kc == 0, kc == 7,
                           [("w", su), ("xn", kc)], [("ps", P)])
                    k1 = newtf()
                    act(tf[:, k1, :], ps[P][:, :], AF.Relu, [("ps", P)], [("tf", k1)])
                    tt(arena[:, f, :], ps[P][:, :], tf[:, k1, :], ALU.mult, [("ps", P), ("tf", k1)], [pg(f)])
            for c in range(8):
                sd = next_w(l, 23 + c)
                P = newps()
                for f in range(32):
                    mm(ps[P][:, :], wview(sd, f, 0, 128, kcn=32), arena[:, f, :], f == 0, f == 31,
                       [("w", sd), pg(f)], [("ps", P)])
                tt(xT[:, c, :], xT[:, c, :], ps[P][:, :], ALU.add, [("xT", c), ("ps", P)], [("xT", c)])

        def load_x(jb):
            src = x_d[jb * TB:(jb + 1) * TB, :].rearrange("(i p) d -> p i d", p=128)
            sch.op("sync", lambda e, src=src: e.dma_start(out=xin[:, :, :], in_=src), [],
                   [("xin", i) for i in range(4)], dma_sem="io_in", cost=8000.0)
            for c in range(8):
                P = newps()
                for i in range(4):
                    tr(ps[P][:, i * 128:(i + 1) * 128], xin[:, i, c * 128:(c + 1) * 128], [("xin", i)], [("ps", P)])
                copy_any(xT[:, c, :], ps[P][:, :], [("ps", P)], [("xT", c)])

        def store_out(jb, srcT, srcname):
            for i in range(4):
                for half in range(2):
                    P = newps()
                    for cc in range(4):
                        c = half * 4 + cc
                        tr(ps[P][:, cc * 128:(cc + 1) * 128], srcT[:, c, i * 128:(i + 1) * 128], [(srcname, c)],
                           [("ps", P)])
                    copy_any(io[:, i, half * 512:(half + 1) * 512], ps[P][:, :], [("ps", P)], [("io", i)])
            dst = out_d[jb * TB:(jb + 1) * TB, :].rearrange("(i p) d -> p i d", p=128)
            sch.op("sync", lambda e, dst=dst: e.dma_start(out=dst, in_=io[:, :, :]),
                   [("io", i) for i in range(4)], [], dma_sem="io_out", cost=8000.0)

        if not pipe:
            for j in range(nblk):
                load_x(j)
                for l in range(L):
                    layer(l, j)
                rmsnorm(o_gf, False)
                store_out(j, xT, "xT")
        else:
            for st in range(nstep):
                jb = min(st, nblk - 1)
                xsrc = x_d[:, jb * TB:(jb + 1) * TB].rearrange("(c p) t -> p c t", p=128)
                sch.op("sync", lambda e, xsrc=xsrc: e.dma_start(out=xl[:, :, :], in_=xsrc), [],
                       [("xl", c) for c in range(8)], dma_sem="io_in", cost=8000.0)
                if st == 0:
                    for c in range(8):
                        copy_any(xT[:, c, :], xl[:, c, :], [("xl", c)], [("xT", c)])
                else:
                    for hf in range(NH):
                        rsrc = recv_d[st - 1][hf][0:128, :].rearrange("p (c t) -> p c t", c=CPP)
                        sch.op("sync", lambda e, rsrc=rsrc, hf=hf: e.dma_start(out=rbuf[:, CPP * hf:CPP * hf + CPP, :], in_=rsrc),
                               [("recv", st - 1, hf)], [("rb", c) for c in range(CPP * hf, CPP * hf + CPP)], dma_sem=("rcv", hf),
                               cost=2000.0 + 750.0 * CPP)
                        for c in range(CPP * hf, CPP * hf + CPP):
                            stt(xT[:, c, :], rbuf[:, c, :], pcol(o_fb), xl[:, c, :], ALU.mult, ALU.add,
                                [("rb", c), ("xl", c), "par"], [("xT", c)])
                layer(0, st)
                if st < nblk:
                    for hf in range(NH):
                        sdst = send_d[st][hf].rearrange("p (c t) -> p c t", c=CPP)
                        sch.op("sync", lambda e, sdst=sdst, hf=hf: e.dma_start(out=sdst, in_=xT[:, CPP * hf:CPP * hf + CPP, :]),
                               [("xT", c) for c in range(CPP * hf, CPP * hf + CPP)], [("send", st, hf)], dma_sem=("snd", hf),
                               cost=2000.0 + 750.0 * CPP)
                        sch.op("gpsimd", lambda e, st=st, hf=hf: e.collective_compute(
                            "AllGather", ALU.bypass, replica_groups=[[0, 1], [2, 3], [4, 5], [6, 7]],
                            ins=[send_d[st][hf]], outs=[recv_d[st][hf]]),
                            [("send", st, hf)], [("recv", st, hf), "cc_serial"], dma_sem=("cc", NH * st + hf),
                            cost=3000.0)
                if st >= 1:
                    kr = norm_stats()
                    for c in range(8):
                        stt(rbuf[:, c, :], xT[:, c, :], pcol(o_gf + c), tf[:, kr, :], ALU.mult, ALU.mult,
                            [("xT", c), ("tf", kr), "par"], [("rb", c)])
                    odst = out_d[:, (st - 1) * TB:st * TB].rearrange("(c p) t -> p c t", p=128)
                    sch.op("sync", lambda e, odst=odst: e.dma_start(out=odst, in_=rbuf[:, :, :]),
                           [("rb", c) for c in range(8)], [], dma_sem="io_out", cost=8000.0)

        sch.finalize()
        sch.emit(nc, es, dma_sems, ["io_out"])
    return nc


def _fm(W, cols):
    sub = W[:, cols]
    n = sub.shape[1]
    return np.ascontiguousarray(sub.reshape(8, 128, n).transpose(1, 0, 2).reshape(128, 8 * n))


def host_constants():
    s = np.arange(128)
    ident = np.eye(128, dtype=np.float32)
    onesMD = np.full((128, 128), 1.0 / 1024.0, np.float32)
    onesMH = np.full((128, 128), 1.0 / 128.0, np.float32)
    U = (s[:, None] <= s[None, :]).astype(np.float32)
    NEGm = np.where(s[:, None] <= s[None, :], 0.0, NEG).astype(np.float32)
    cst = np.concatenate([ident, onesMD, onesMH, U, NEGm], axis=1)
    kp = np.arange(128)[:, None]
    u = np.arange(640)[None, :]
    dq = u // 64 - (kp >= 64)
    negm = np.where((dq >= 0) & (dq <= 8), 0.0, NEG).astype(np.float32)
    return np.ascontiguousarray(cst), np.ascontiguousarray(negm)


def host_prep(inp, depth=DEPTH, layers=None, pipe_role=None, nblk=SEQ // TB):
    if layers is None:
        layers = list(range(depth))
    L = len(layers)
    sel = np.asarray(layers)
    inp = dict(inp)
    for k_ in ("mix_norm_g", "w_in", "conv_w", "conv_b", "b_igate", "b_fgate", "rel_bias", "mh_norm_g",
               "w_att_proj", "w_mlstm_proj", "w_out", "ffn_norm_g", "w_up", "w_down"):
        inp[k_] = np.asarray(inp[k_], np.float32)[sel]
    w_in = inp["w_in"]
    tiles = []
    wifs = []
    for l in range(L):
        for g in range(7):
            tiles.append(_fm(w_in[l], np.arange(512 * g, 512 * g + 512)))
        w_att = np.asarray(inp["w_att_proj"][l], np.float32)
        w_ml = np.asarray(inp["w_mlstm_proj"][l], np.float32)

        def pp_tile(u):
            parts = []
            for tt_ in range(2):
                t = 2 * u + tt_
                for w in (w_att, w_ml):
                    sub = w[:, t * 256:(t + 1) * 256]
                    parts.append(sub.reshape(4, 128, 256).transpose(1, 0, 2).reshape(128, 1024))
            return np.ascontiguousarray(np.concatenate(parts, axis=1))

        def gagm_tile(t):
            cols = np.concatenate([GA0 + 256 * t + np.arange(256), GM0 + 256 * t + np.arange(256)])
            return _fm(w_in[l], cols)

        for t_ in range(4):
            tiles.append(gagm_tile(t_))
        tiles.append(pp_tile(0))
        tiles.append(pp_tile(1))
        w_out = np.asarray(inp["w_out"][l], np.float32)
        for t in range(2):
            tiles.append(_fm(w_out, np.arange(512 * t, 512 * t + 512)))
        w_up = np.asarray(inp["w_up"][l], np.float32)
        for t in range(8):
            tiles.append(_fm(w_up, np.arange(512 * t, 512 * t + 512)))
        w_down = np.asarray(inp["w_down"][l], np.float32)
        for c in range(8):
            sub = w_down[:, c * 128:(c + 1) * 128]
            tiles.append(np.ascontiguousarray(sub.reshape(32, 128, 128).transpose(1, 0, 2).reshape(128, 4096)))
        wifs.append(_fm(w_in[l], np.arange(3584, 3592)))
    wstream = np.stack(tiles, axis=0)
    assert wstream.shape == (L * NW_TILES, 128, 4096)
    wif = np.concatenate(wifs, axis=1)

    kp = np.arange(128)[:, None]
    u = np.arange(640)[None, :]
    rel_idx = np.clip(u - kp, -256, 256) + 256
    rb = np.asarray(inp["rel_bias"], np.float32)[:L]
    bg = rb[:, :, rel_idx]
    biasg = np.ascontiguousarray(bg.transpose(2, 0, 1, 3).reshape(128, L * 8 * 640))

    def fmv(v):
        v = np.asarray(v, np.float32)
        k = v.shape[-1] // 128
        return np.moveaxis(v.reshape(v.shape[:-1] + (k, 128)), -1, 0)

    g1 = fmv(inp["mix_norm_g"][:L]).reshape(128, L * 8)
    g2 = fmv(inp["ffn_norm_g"][:L]).reshape(128, L * 8)
    gf = fmv(inp["final_norm_g"]).reshape(128, 8)
    cw = fmv(inp["conv_w"][:L]).reshape(128, L * 32)
    cb = fmv(inp["conv_b"][:L]).reshape(128, L * 8)
    mh = fmv(inp["mh_norm_g"][:L]).reshape(128, L * 4)
    bi = np.broadcast_to(np.asarray(inp["b_igate"], np.float32)[:L][None, :, None, :], (128, L, 4, 4)).reshape(128, L * 16)
    bf = np.broadcast_to(np.asarray(inp["b_fgate"], np.float32)[:L][None, :, None, :], (128, L, 4, 4)).reshape(128, L * 16)
    epsc = np.full((128, 1), EPS, np.float32)
    plist = [g1, g2, gf, cw, cb, mh, bi, bf, epsc]
    if pipe_role is not None:
        nstep = nblk + 1
        fb = np.full((128, 1), float(pipe_role), np.float32)
        smask = np.zeros((128, nstep), np.float32)
        smask[:, :pipe_role + 1] = NEG
        valid = np.ones((128, nstep), np.float32)
        valid[:, :pipe_role] = 0.0
        plist += [fb, smask, valid]
    params = np.ascontiguousarray(np.concatenate(plist, axis=1).astype(np.float32))
    cst, negm = host_constants()
    return {"wstream": wstream, "wif": np.ascontiguousarray(wif), "biasg": biasg, "cst": cst, "negm": negm,
            "params": params}


_NC_CACHE = {}


PIPE = True


def kernel(**inputs):
    x = np.asarray(inputs["x"], np.float32)
    B, S_, D = x.shape
    nblk = S_ // TB
    n_cores = 8
    key = (nblk, DEPTH, PIPE)
    if key not in _NC_CACHE:
        _NC_CACHE[key] = build_program(nblk, DEPTH, pipe=PIPE)
    nc = _NC_CACHE[key]
    in_maps = []
    if PIPE:
        assert DEPTH == 2 and B * 2 == n_cores
        roles = [host_prep(inputs, DEPTH, layers=[r], pipe_role=r, nblk=nblk) for r in range(2)]
        zeros = np.zeros((D, S_), np.float32)
        for c in range(n_cores):
            m = dict(roles[c % 2])
            m["x"] = np.ascontiguousarray(x[c // 2].T) if c % 2 == 0 else zeros
            in_maps.append(m)
        res = run_bass_kernel_spmd(nc, in_maps, core_ids=list(range(n_cores)))
        out = np.stack([np.ascontiguousarray(np.asarray(res.results[2 * b + 1]["out"], np.float32).T)
                        for b in range(B)], axis=0)
        return out
    shared = host_prep(inputs, DEPTH)
    for c in range(n_cores):
        m = dict(shared)
        m["x"] = np.ascontiguousarray(x[c % B])
        in_maps.append(m)
    res = run_bass_kernel_spmd(nc, in_maps, core_ids=list(range(n_cores)))
    out = np.stack([np.asarray(res.results[b]["out"], np.float32) for b in range(B)], axis=0)
    return out
```

```python
import math
from contextlib import ExitStack

import numpy as np
import concourse.bass as bass
import concourse.mybir as mybir
from concourse.bass_utils import run_bass_kernel_spmd

F32 = mybir.dt.float32
BF16 = mybir.dt.bfloat16
AF = mybir.ActivationFunctionType
ALU = mybir.AluOpType

D_MODEL = 1024
BATCH = 4
SEQ = 4096
DEPTH = 2
TB = 512
NW_TILES = 31
NWS = 5
NTF = 8
NTB = 4
EPS = 1e-6
NEG = -30000.0
GA0 = 3592
GM0 = 4616
EPOCH = 8000
SAME_ENGINE_SYNC = "raw"
LIST_SCHED = True
GATE_PREFETCH = True
GATE_DIV = 2
ATT_MUL_ENG = "vector"
NH = 2
XENG_LAT = 1000.0
SAME_LAT = 300.0
import os as _os
KNOP = set(int(v) for v in _os.environ.get('KNOP', '').split(',') if v)


def _is_cc(key):
    return isinstance(key, tuple) and key[0] == "cc"


class _Op:
    __slots__ = ("eng", "fn", "deps", "dma_sem", "token", "needs_inc", "idx", "raw", "cost", "alldeps")


class Sched:
    def __init__(self):
        self.ops = []
        self.lastw = {}
        self.readers = {}

    def op(self, eng, fn, reads=(), writes=(), dma_sem=None, cost=300.0):
        o = _Op()
        o.cost = cost
        o.eng = eng
        o.fn = fn
        o.dma_sem = dma_sem
        o.idx = len(self.ops)
        o.needs_inc = dma_sem is not None
        o.token = None
        deps = set()
        raw = set()
        for r in reads:
            if r in self.lastw:
                deps.add(self.lastw[r])
                raw.add(self.lastw[r])
        o.raw = raw
        for w in writes:
            if w in self.lastw:
                deps.add(self.lastw[w])
            for rd in self.readers.get(w, ()):
                deps.add(rd)
        deps.discard(o.idx)
        o.deps = deps
        for r in reads:
            self.readers.setdefault(r, set()).add(o.idx)
        for w in writes:
            self.lastw[w] = o.idx
            self.readers[w] = set()
        self.ops.append(o)
        return o

    def finalize(self):
        import os
        if os.environ.get("KTRUNC"):
            self.ops = self.ops[:int(os.environ["KTRUNC"])]
        if os.environ.get("KSKIP"):
            a, b = [int(v) for v in os.environ["KSKIP"].split(",")]
            kept = [o for o in self.ops if not (a <= o.idx < b)]
            remap = {o.idx: i for i, o in enumerate(kept)}
            for o in kept:
                o.deps = set(remap[d] for d in o.deps if d in remap)
                o.raw = set(remap[d] for d in o.raw if d in remap)
                o.idx = remap[o.idx]
            self.ops = kept
        ops = self.ops
        for o in ops:
            o.alldeps = set(o.deps)
        self.order = self.list_schedule() if LIST_SCHED else None
        for o in ops:
            keep = set()
            for d in o.deps:
                do = ops[d]
                if do.eng == o.eng and do.dma_sem is None and o.dma_sem is None:
                    if o.eng == "tensor" or not SAME_ENGINE_SYNC:
                        continue
                    if SAME_ENGINE_SYNC == "raw" and d not in o.raw:
                        continue
                keep.add(d)
            o.deps = keep
            for d in keep:
                ops[d].needs_inc = True

    def list_schedule(self):
        import heapq
        ops = self.ops
        n = len(ops)
        ndep = [len(o.alldeps) for o in ops]
        users = [[] for _ in range(n)]
        for o in ops:
            for d in o.alldeps:
                users[d].append(o.idx)
        finish = [0.0] * n
        ready_t = [0.0] * n
        engs = ["tensor", "vector", "scalar", "gpsimd", "sync"]
        pend = {e: [] for e in engs}
        avail = {e: [] for e in engs}
        t_free = {e: 0.0 for e in engs}
        for o in ops:
            if ndep[o.idx] == 0:
                heapq.heappush(pend[o.eng], (0.0, o.idx))
        order = []
        done = 0
        while done < n:
            best = None
            for e in engs:
                while pend[e] and pend[e][0][0] <= t_free[e]:
                    rt, i = heapq.heappop(pend[e])
                    heapq.heappush(avail[e], i)
                if avail[e]:
                    cand = (t_free[e], avail[e][0], e, True)
                elif pend[e]:
                    cand = (pend[e][0][0], pend[e][0][1], e, False)
                else:
                    continue
                if best is None or cand[:2] < best[:2]:
                    best = cand
            st, i, e, from_avail = best
            if from_avail:
                heapq.heappop(avail[e])
            else:
                heapq.heappop(pend[e])
            o = ops[i]
            fin = st + o.cost
            t_free[e] = fin
            finish[i] = fin
            order.append(i)
            done += 1
            for u in users[i]:
                if ops[u].eng == e:
                    lat = SAME_LAT if (e in ("vector", "scalar") and i in ops[u].raw) else 0.0
                else:
                    lat = XENG_LAT
                if fin + lat > ready_t[u]:
                    ready_t[u] = fin + lat
                ndep[u] -= 1
                if ndep[u] == 0:
                    heapq.heappush(pend[ops[u].eng], (ready_t[u], u))
        self.est_total = max(finish) if finish else 0.0
        return order

    def emit(self, nc, es, dma_sems, final_waits):
        ops = self.ops
        seq = [ops[i] for i in self.order] if self.order is not None else list(ops)
        eng_sems = {}
        counters = {}
        dma_counts = {}
        for o in seq:
            if o.dma_sem is not None:
                c = dma_counts.get(o.dma_sem, 0) + (1 if _is_cc(o.dma_sem) else 16)
                dma_counts[o.dma_sem] = c
                o.token = (dma_sems[o.dma_sem], c)
            elif o.needs_inc:
                ep, c = counters.get(o.eng, (0, 0))
                if c >= EPOCH:
                    ep, c = ep + 1, 0
                c += 1
                counters[o.eng] = (ep, c)
                key = (o.eng, ep)
                if key not in eng_sems:
                    eng_sems[key] = es.enter_context(nc.semaphore("p_%s_%d" % (o.eng, ep)))
                o.token = (eng_sems[key], c)
        for o in ops:
            if o.dma_sem in ("setup", "setup_sw"):
                o.token = (dma_sems[o.dma_sem], dma_counts[o.dma_sem])
        by_eng = {}
        for o in seq:
            by_eng.setdefault(o.eng, []).append(o)

        def run(eng_name, e):
            waited = {}
            for o in by_eng.get(eng_name, []):
                need = {}
                for d in o.deps:
                    s, v = ops[d].token
                    k = id(s)
                    if waited.get(k, 0) >= v:
                        continue
                    if k not in need or need[k][1] < v:
                        need[k] = (s, v)
                for k, (s, v) in need.items():
                    e.wait_ge(s, v)
                    waited[k] = v
                if o.idx in KNOP:
                    ins = e.nop()
                else:
                    ins = o.fn(e)
                if o.dma_sem is not None:
                    if _is_cc(o.dma_sem):
                        ins.then_inc(o.token[0])
                    else:
                        ins.then_inc(o.token[0], 16)
                elif o.needs_inc:
                    ins.then_inc(o.token[0], 1)
            mine = []
            for o in by_eng.get(eng_name, []):
                if o.dma_sem is not None and o.dma_sem not in mine:
                    mine.append(o.dma_sem)
            for key in mine:
                e.wait_ge(dma_sems[key], dma_counts[key])

        with nc.Block() as block:
            @block.sync
            def _(e):
                run("sync", e)

            @block.gpsimd
            def _(e):
                run("gpsimd", e)

            @block.tensor
            def _(e):
                run("tensor", e)

            @block.vector
            def _(e):
                run("vector", e)

            @block.scalar
            def _(e):
                run("scalar", e)


def build_program(nblk, depth=DEPTH, pipe=False):
    T = nblk * TB
    nc = bass.Bass("TRN2", target_bir_lowering=False)
    L = 1 if pipe else depth
    nstep = nblk + 1 if pipe else nblk
    if pipe:
        x_d = nc.dram_tensor("x", [D_MODEL, T], F32, kind="ExternalInput").ap()
    else:
        x_d = nc.dram_tensor("x", [T, D_MODEL], F32, kind="ExternalInput").ap()
    w_d = nc.dram_tensor("wstream", [L * NW_TILES, 128, 4096], F32, kind="ExternalInput").ap()
    wif_d = nc.dram_tensor("wif", [128, L * 64], F32, kind="ExternalInput").ap()
    bias_d = nc.dram_tensor("biasg", [128, L * 8 * 640], F32, kind="ExternalInput").ap()
    cst_d = nc.dram_tensor("cst", [128, 5 * 128], F32, kind="ExternalInput").ap()
    negm_d = nc.dram_tensor("negm", [128, 640], F32, kind="ExternalInput").ap()
    NPAR = L * 8 + L * 8 + 8 + L * 32 + L * 8 + L * 4 + L * 16 + L * 16 + 1 + (1 + 2 * (nblk + 1) if pipe else 0)
    par_d = nc.dram_tensor("params", [128, NPAR], F32, kind="ExternalInput").ap()
    if pipe:
        out_d = nc.dram_tensor("out", [D_MODEL, T], F32, kind="ExternalOutput").ap()
    else:
        out_d = nc.dram_tensor("out", [T, D_MODEL], F32, kind="ExternalOutput").ap()

    es = ExitStack()
    with es:
        def sb(name, shape, dt):
            return es.enter_context(nc.sbuf_tensor(name, shape, dt))

        if not pipe:
            io = sb("io", [128, 4, 1024], F32)
        if pipe:
            xl = sb("xl", [128, 8, TB], F32)
            rbuf = sb("rbuf", [128, 8, TB], F32)
            CPP = 8 // NH
            send_d = [[nc.dram_tensor("send%d_%d" % (k, hf), [128, CPP * TB], F32, kind="Internal").ap()
                       for hf in range(NH)] for k in range(nblk)]
            recv_d = [[nc.dram_tensor("recv%d_%d" % (k, hf), [256, CPP * TB], F32, kind="Internal").ap()
                       for hf in range(NH)] for k in range(nblk)]
        else:
            xin = io
        xT = sb("xT", [128, 8, TB], F32)
        xnT = sb("xnT", [128, 8, TB], BF16)
        arena = sb("arena", [128, 32, TB], BF16)
        kh = sb("kh", [128, L * 2 * 4, TB], BF16)
        sgmb = sb("sgmb", [128, 8, TB], BF16)
        vh = sb("vh", [128, L * 2 * 4, TB], BF16)
        wb = sb("wb", [128, NWS, 4096], BF16)
        wif = sb("wifs", [128, L * 64], BF16)
        BM = sb("BM", [128, L * 8, 640], BF16)
        negm = sb("negms", [128, 640], BF16)
        S = sb("S", [128, L * 4, 256], F32)
        Sb = sb("Sb", [128, L * 4, 256], BF16)
        halo = sb("halo", [128, L * 8, 4], F32)
        cst = sb("csts", [128, 5, 128], F32)
        identB = sb("identB", [128, 128], BF16)
        cstB = sb("cstB", [128, 3, 128], BF16)
        onesB = sb("onesB", [128, 128], BF16)
        onesF = sb("onesF", [128, 128], F32)
        par = sb("pars", [128, NPAR], F32)
        tf = sb("tf", [128, NTF, 512], F32)
        tbf = sb("tbf", [128, NTB, 512], BF16)
        pre = sb("pre", [128, 2, 516], F32)
        LF = sb("LF", [128, 2, 4, 128], F32)
        gsm = sb("gsm", [128, 8, 16], F32)
        wsm = sb("wsm", [128, 8, 2], F32)
        ps = [es.enter_context(nc.psum_tensor("ps%d" % b, [128, 512], F32)) for b in range(8)]

        identF = cst[:, 0, :]
        onesMD = cst[:, 1, :]
        onesMH = cst[:, 2, :]
        Umat = cst[:, 3, :]
        NEGmat = cst[:, 4, :]
        onesMDb = cstB[:, 0, :]
        onesMHb = cstB[:, 1, :]
        NEGb = cstB[:, 2, :]

        o_g1 = 0
        o_g2 = o_g1 + L * 8
        o_gf = o_g2 + L * 8
        o_cw = o_gf + 8
        o_cb = o_cw + L * 32
        o_mh = o_cb + L * 8
        o_bi = o_mh + L * 4
        o_bf = o_bi + L * 16
        o_eps = o_bf + L * 16
        o_fb = o_eps + 1
        o_sm = o_fb + 1
        o_vd = o_sm + nstep

        def pcol(off):
            return par[:, off:off + 1]

        dma_sems = {}
        extra_sems = ([("snd", k) for k in range(NH)] + [("rcv", k) for k in range(NH)]
                      + [("cc", k) for k in range(NH * nblk)]) if pipe else []
        extra_sems = extra_sems + [("bm", k) for k in range(L * 8)]
        for k in ["setup", "setup_sw", "io_in", "io_out"] + extra_sems + [("w", s) for s in range(NWS)]:
            dma_sems[k] = es.enter_context(nc.semaphore("d_%s" % (str(k).replace(" ", ""))))

        sch = Sched()
        state = {"ps": 0, "tf": 0, "tb": 0, "w": 0, "alt": 0, "pre": 0, "lf": 0, "wsm": 0}

        def newps(exclude=()):
            b = state["ps"]
            while b in exclude:
                b = (b + 1) % 8
            state["ps"] = (b + 1) % 8
            return b

        def newtf():
            k = state["tf"]
            state["tf"] = (k + 1) % NTF
            return k

        def newtb():
            k = state["tb"]
            state["tb"] = (k + 1) % NTB
            return k

        def fsz(ap):
            n = 1
            for d in ap.shape[1:]:
                n *= int(d)
            return n

        def mm(out, lhsT, rhs, start, stop, r, w):
            c = max(fsz(rhs), 64) / 1.95 + 15.0
            if lhsT.dtype == F32:
                c *= 4.0
            sch.op("tensor", lambda e: e.matmul(out, lhsT, rhs, start=start, stop=stop), r, w, cost=c)

        def tr(out, in_, r, w):
            sch.op("tensor", lambda e: e.transpose(out, in_, identF), list(r) + ["cst"], w, cost=280.0)

        def act(out, in_, func, r, w, bias=None, scale=None):
            kw = {}
            if bias is not None:
                kw["bias"] = bias
            if scale is not None:
                kw["scale"] = scale
            c = fsz(out) / 1.4 + 220.0 + (90.0 if bias is not None and not isinstance(bias, float) else 0.0)
            sch.op("scalar", lambda e: e.activation(out, in_, func, **kw), r, w, cost=c)

        def dcost(ap, mult=1.0):
            return max(fsz(ap), 64) * mult / 0.96 + 90.0

        def tt(out, in0, in1, op, r, w, eng="vector"):
            sch.op(eng, lambda e: e.tensor_tensor(out, in0, in1, op), r, w, cost=dcost(out))

        def ts(out, in0, s1, s2, op0, op1, r, w, eng="vector"):
            if op1 is None:
                sch.op(eng, lambda e: e.tensor_scalar(out, in0, s1, None, op0), r, w, cost=dcost(out))
            else:
                sch.op(eng, lambda e: e.tensor_scalar(out, in0, s1, s2, op0, op1), r, w, cost=dcost(out))

        def stt(out, in0, scalar, in1, op0, op1, r, w):
            sch.op("vector", lambda e: e.scalar_tensor_tensor(out, in0, scalar, in1, op0, op1), r, w,
                   cost=dcost(out))

        def recip(out, in_, r, w):
            sch.op("vector", lambda e: e.reciprocal(out, in_), r, w, cost=dcost(out, 8.0))

        def copy_any(out, in_, r, w):
            state["alt"] ^= 1
            if state["alt"]:
                sch.op("scalar", lambda e: e.copy(out, in_), r, w, cost=fsz(out) / 1.4 + 220.0)
            else:
                sch.op("vector", lambda e: e.tensor_copy(out, in_), r, w, cost=dcost(out))

        def dsetup(eng, out, in_, w):
            sch.op(eng, lambda e: e.dma_start(out=out, in_=in_), [], w,
                   dma_sem="setup_sw" if eng == "gpsimd" else "setup")

        dsetup("sync", cst[:, :, :], cst_d.rearrange("p (k n) -> p k n", k=5), ["cst"])
        dsetup("sync", par[:, :], par_d, ["par"])
        dsetup("gpsimd", wif[:, :], wif_d, ["wif"])
        dsetup("gpsimd", negm[:, :], negm_d, ["negm"])
        def setup_bm():
            for l in range(L):
                for h in range(8):
                    k = l * 8 + h
                    sch.op("gpsimd", lambda e, k=k: e.dma_start(out=BM[:, k, :], in_=bias_d[:, k * 640:(k + 1) * 640]),
                           [], [("BM", k)], dma_sem=("bm", k), cost=2500.0)
                    tt(BM[:, k, :], BM[:, k, :], negm[:, :], ALU.add, [("BM", k), "negm"], [("BM", k)])
                    if ATT_MUL_ENG:
                        act(BM[:, k, :], BM[:, k, :], AF.Exp, [("BM", k)], [("BM", k)])

        sch.op("vector", lambda e: e.memset(S[:, :, :], 0.0), [], ["S%d" % k for k in range(L * 4)])
        sch.op("vector", lambda e: e.memset(Sb[:, :, :], 0.0), [], ["Sb%d" % k for k in range(L * 4)])
        sch.op("vector", lambda e: e.memset(halo[:, :, :], 0.0), [], [("halo", k) for k in range(L * 8)])
        sch.op("vector", lambda e: e.memset(kh[:, :, :], 0.0), [], [("kh", k) for k in range(L * 8)])
        sch.op("vector", lambda e: e.memset(vh[:, :, :], 0.0), [], [("vh", k) for k in range(L * 8)])
        sch.op("vector", lambda e: e.memset(onesB[:, :], 1.0), [], ["onesB"])
        sch.op("vector", lambda e: e.memset(onesF[:, :], 1.0), [], ["onesF"])
        sch.op("vector", lambda e: e.tensor_copy(identB[:, :], identF), ["cst"], ["identB"])
        sch.op("vector", lambda e: e.tensor_copy(cstB[:, 0:2, :], cst[:, 1:3, :]), ["cst"], ["cstB"])
        sch.op("vector", lambda e: e.tensor_copy(cstB[:, 2, :], cst[:, 4, :]), ["cst", "cstB"], ["cstB"])
        def next_w(l, k_expected):
            n = state["w"]
            state["w"] = n + 1
            assert n % NW_TILES == k_expected and n // NW_TILES % L == l, (n, l, k_expected)
            slot = n % NWS
            src = w_d[(n % (L * NW_TILES)), :, :]
            gate = []
            if pipe and GATE_PREFETCH and n // NW_TILES >= 1 and n % NW_TILES < NWS:
                gate = [("recv", n // NW_TILES - 1, min((n % NW_TILES) // GATE_DIV, NH - 1))]
            sch.op("gpsimd", lambda e: e.dma_start(out=wb[:, slot, :], in_=src), gate, [("w", slot)],
                   dma_sem=("w", slot), cost=9000.0)
            return slot

        def wview(slot, kc, c0, n, kcn=8):
            width = 4096 // kcn
            return wb[:, slot, kc * width + c0: kc * width + c0 + n]

        def rstd_from(P_ap, kr, rd):
            act(tf[:, kr, :], P_ap, AF.Ln, rd + ["par"], [("tf", kr)], bias=pcol(o_eps))
            act(tf[:, kr, :], tf[:, kr, :], AF.Exp, [("tf", kr)], [("tf", kr)], scale=-0.5)

        def norm_stats():
            P = newps()
            for c in range(8):
                k = newtb()
                act(tbf[:, k, :], xT[:, c, :], AF.Square, [("xT", c)], [("tb", k)])
                mm(ps[P][:, :], onesMDb, tbf[:, k, :], c == 0, c == 7, [("tb", k), "cstB"], [("ps", P)])
            kr = newtf()
            rstd_from(ps[P][:, :], kr, [("ps", P)])
            return kr

        def rmsnorm(gcol0, dst_bf16):
            kr = norm_stats()
            for c in range(8):
                if dst_bf16:
                    stt(xnT[:, c, :], xT[:, c, :], pcol(gcol0 + c), tf[:, kr, :], ALU.mult, ALU.mult,
                        [("xT", c), ("tf", kr), "par"], [("xn", c)])
                else:
                    stt(xT[:, c, :], xT[:, c, :], pcol(gcol0 + c), tf[:, kr, :], ALU.mult, ALU.mult,
                        [("xT", c), ("tf", kr), "par"], [("xT", c)])

        def pg(n):
            return ("pg", n)

        SC_M = 1.0 / math.sqrt(128.0)

        def layer(l, j):
            parity = j % 2
            kcur = lambda p: (l * 2 + parity) * 4 + p
            kprev = lambda p: (l * 2 + 1 - parity) * 4 + p

            rmsnorm(o_g1 + l * 8, True)
            xn_all = [("xn", c) for c in range(8)]

            for g in range(2):
                slot = next_w(l, g)
                for m in range(4):
                    P = newps()
                    for kc in range(8):
                        mm(ps[P][:, :], wview(slot, kc, m * 128, 128), xnT[:, kc, :], kc == 0, kc == 7,
                           [("w", slot), ("xn", kc)], [("ps", P)])
                    if g == 0:
                        copy_any(arena[:, m, :], ps[P][:, :], [("ps", P)], [pg(m)])
                    else:
                        copy_any(kh[:, kcur(m), :], ps[P][:, :], [("ps", P)], [("kh", kcur(m))])
            slot = next_w(l, 2)
            for i in range(4):
                P = newps()
                for kc in range(8):
                    mm(ps[P][:, :], xnT[:, kc, i * 128:(i + 1) * 128], wview(slot, kc, 0, 512), kc == 0, kc == 7,
                       [("w", slot), ("xn", kc)], [("ps", P)])
                copy_any(vh[:, kcur(i), :], ps[P][:, :], [("ps", P)], [("vh", kcur(i))])

            if not state.get("bm_done"):
                state["bm_done"] = True
                setup_bm()
            for p in range(4):
                O = newps()
                Dn = newps()
                for e_ in range(2):
                    h = 2 * p + e_
                    rows = slice(e_ * 64, (e_ + 1) * 64)
                    rlist = [4] + [r for r in range(8) if (j > 0 or r >= 4 or pipe) and r != 4]
                    for idx, r in enumerate(rlist):
                        kidx = kprev(p) if r < 4 else kcur(p)
                        vidx = kprev(r % 4) if r < 4 else kcur(r % 4)
                        qlo = max(8, 2 * r) * 64 - 512
                        qhi = (min(15, 2 * r + 9) + 1) * 64 - 512
                        n = qhi - qlo
                        u0 = 512 + qlo - 128 * r
                        sc = newps(exclude=(O, Dn))
                        mm(ps[sc][:, 0:n], kh[rows, kidx, (r % 4) * 128:(r % 4 + 1) * 128], arena[rows, p, qlo:qhi],
                           True, True, [("kh", kidx), pg(p)], [("ps", sc)])
                        k1 = newtf()
                        k2 = newtb()
                        if ATT_MUL_ENG:
                            if pipe and r < 4:
                                act(tf[:, k1, 0:n], ps[sc][:, 0:n], AF.Exp, [("ps", sc), "par"], [("tf", k1)],
                                    bias=pcol(o_sm + j), scale=0.125)
                            else:
                                act(tf[:, k1, 0:n], ps[sc][:, 0:n], AF.Exp, [("ps", sc)], [("tf", k1)], scale=0.125)
                            sch.op(ATT_MUL_ENG, lambda e, k1=k1, k2=k2, n=n, u0=u0, lh=l * 8 + h: e.tensor_tensor(
                                tbf[:, k2, 0:n], tf[:, k1, 0:n], BM[:, lh, u0:u0 + n], ALU.mult),
                                [("tf", k1), ("BM", l * 8 + h)], [("tb", k2)],
                                cost=(n / 0.5 + 500.0) if ATT_MUL_ENG == "gpsimd" else (n / 0.96 + 90.0))
                        else:
                            stt(tf[:, k1, 0:n], ps[sc][:, 0:n], 0.125, BM[:, l * 8 + h, u0:u0 + n], ALU.mult, ALU.add,
                                [("ps", sc), ("BM", l * 8 + h)], [("tf", k1)])
                            if pipe and r < 4:
                                act(tbf[:, k2, 0:n], tf[:, k1, 0:n], AF.Exp, [("tf", k1), "par"], [("tb", k2)],
                                    bias=pcol(o_sm + j))
                            else:
                                act(tbf[:, k2, 0:n], tf[:, k1, 0:n], AF.Exp, [("tf", k1)], [("tb", k2)])
                        first = idx == 0
                        last = idx == len(rlist) - 1
                        mm(ps[O][rows, qlo:qhi], vh[:, vidx, h * 64:(h + 1) * 64], tbf[:, k2, 0:n], first, last,
                           [("vh", vidx), ("tb", k2)], [("ps", O)])
                        mm(ps[Dn][rows, qlo:qhi], onesB[:, 0:64], tbf[:, k2, 0:n], first, last,
                           ["onesB", ("tb", k2)], [("ps", Dn)])
                kr = newtf()
                act(tf[:, kr, :], ps[Dn][:, :], AF.Ln, [("ps", Dn)], [("tf", kr)])
                act(tf[:, kr, :], tf[:, kr, :], AF.Exp, [("tf", kr)], [("tf", kr)], scale=-1.0)
                tt(arena[:, 24 + p, :], ps[O][:, :], tf[:, kr, :], ALU.mult, [("ps", O), ("tf", kr)], [pg(24 + p)])

            for g in (3, 4):
                slot = next_w(l, g)
                for m in range(4):
                    ch = (g - 3) * 4 + m
                    P = newps()
                    for kc in range(8):
                        mm(ps[P][:, :], wview(slot, kc, m * 128, 128), xnT[:, kc, :], kc == 0, kc == 7,
                           [("w", slot), ("xn", kc)], [("ps", P)])
                    pk = state["pre"]
                    state["pre"] ^= 1
                    hk = l * 8 + ch
                    copy_any(pre[:, pk, 3:515], ps[P][:, :], [("ps", P)], [("pre", pk)])
                    sch.op("vector", lambda e, pk=pk, hk=hk: e.tensor_copy(pre[:, pk, 0:3], halo[:, hk, 0:3]),
                           [("halo", hk)], [("pre", pk)])
                    sch.op("vector", lambda e, pk=pk, hk=hk: e.tensor_copy(halo[:, hk, 0:3], pre[:, pk, 512:515]),
                           [("pre", pk)], [("halo", hk)])
                    ka = newtf()
                    cw = lambda tap: pcol(o_cw + l * 32 + tap * 8 + ch)
                    ts(tf[:, ka, :], pre[:, pk, 0:512], cw(0), None, ALU.mult, None,
                       [("pre", pk), "par"], [("tf", ka)])
                    for tap in (1, 2, 3):
                        stt(tf[:, ka, :], pre[:, pk, tap:tap + 512], cw(tap), tf[:, ka, :], ALU.mult, ALU.add,
                            [("pre", pk), "par", ("tf", ka)], [("tf", ka)])
                    act(arena[:, 4 + ch, :], tf[:, ka, :], AF.Silu, [("tf", ka), "par"], [pg(4 + ch)],
                        bias=pcol(o_cb + l * 8 + ch))
            slot = next_w(l, 5)
            for i in range(4):
                P = newps()
                for kc in range(8):
                    mm(ps[P][:, :], xnT[:, kc, i * 128:(i + 1) * 128], wview(slot, kc, 0, 512), kc == 0, kc == 7,
                       [("w", slot), ("xn", kc)], [("ps", P)])
                copy_any(arena[:, 12 + i, :], ps[P][:, :], [("ps", P)], [pg(12 + i)])
            slot = next_w(l, 6)
            for m in range(4):
                P = newps()
                for kc in range(8):
                    mm(ps[P][:, :], wview(slot, kc, m * 128, 128), xnT[:, kc, :], kc == 0, kc == 7,
                       [("w", slot), ("xn", kc)], [("ps", P)])
                act(arena[:, 20 + m, :], ps[P][:, :], AF.Sigmoid, [("ps", P)], [pg(20 + m)])
            G = newps()
            for i in range(4):
                for kc in range(8):
                    mm(ps[G][:, i * 8:(i + 1) * 8], xnT[:, kc, i * 128:(i + 1) * 128],
                       wif[:, l * 64 + kc * 8: l * 64 + kc * 8 + 8], kc == 0, kc == 7,
                       ["wif", ("xn", kc)], [("ps", G)])
            G3 = ps[G][:, 0:32].rearrange("p (i c) -> p i c", c=8)
            ig = gsm[:, 0, :]
            zf = gsm[:, 1, :]
            ef = gsm[:, 2, :]
            spv = gsm[:, 3, :]
            logf = gsm[:, 4, :]
            biasv = gsm[:, 5, :]
            v3 = lambda ap: ap.rearrange("p (i c) -> p i c", c=4)
            tt(v3(ig), G3[:, :, 0:4], v3(par[:, o_bi + l * 16:o_bi + l * 16 + 16]), ALU.add,
               [("ps", G), "par"], ["g_ig"])
            tt(v3(zf), G3[:, :, 4:8], v3(par[:, o_bf + l * 16:o_bf + l * 16 + 16]), ALU.add,
               [("ps", G), "par"], ["g_zf"])
            act(ef, zf, AF.Exp, ["g_zf"], ["g_ef"], scale=-1.0)
            act(spv, ef, AF.Ln, ["g_ef"], ["g_sp"], bias=1.0)
            ts(logf, spv, -1.0, None, ALU.mult, None, ["g_sp"], ["g_lf"])
            Cm = newps()
            mm(ps[Cm][:, 0:16], Umat, logf, True, True, ["cst", "g_lf"], [("ps", Cm)])
            tt(biasv, ig, ps[Cm][:, 0:16], ALU.subtract, ["g_ig", ("ps", Cm)], ["g_bv"])

            for i in range(4):
                lfk = state["lf"]
                state["lf"] ^= 1
                for h in range(4):
                    col = i * 4 + h
                    ts(LF[:, lfk, h, :], onesF[:, :], logf[:, col:col + 1], None, ALU.mult, None,
                       ["onesF", "g_lf"], [("LF", lfk, h)])
                tsl = slice(i * 128, (i + 1) * 128)
                for h in range(4):
                    col = i * 4 + h
                    sk = l * 4 + h
                    kT = arena[:, 8 + h, tsl]
                    qT_ = arena[:, 4 + h, tsl]
                    vt = arena[:, 12 + i, h * 128:(h + 1) * 128]
                    A = newps()
                    mm(ps[A][:, 0:128], kT, qT_, True, True, [pg(8 + h), pg(4 + h)], [("ps", A)])
                    mm(ps[A][:, 128:256], LF[:, lfk, h, :], Umat, True, True, [("LF", lfk, h), "cst"], [("ps", A)])
                    mm(ps[A][:, 256:384], LF[:, lfk, h, :], Umat, True, False, [("LF", lfk, h), "cst"], [("ps", A)])
                    mm(ps[A][:, 256:384], identB[:, :], NEGb, False, True, ["identB", "cstB"], [("ps", A)])
                    ka = newtf()
                    act(tf[:, ka, 0:128], ps[A][:, 256:384], AF.Exp, [("ps", A), "g_bv"], [("tf", ka)],
                        bias=biasv[:, col:col + 1])
                    act(tf[:, ka, 128:256], ps[A][:, 128:256], AF.Exp, [("ps", A)], [("tf", ka)])
                    wk = state["wsm"]
                    state["wsm"] = (wk + 1) % 8
                    if pipe:
                        ts(wsm[:, wk, 0:1], tf[:, ka, 127:128], pcol(o_vd + j), None, ALU.mult, None,
                           [("tf", ka), "par"], [("wsm", wk)])
                    else:
                        sch.op("vector", lambda e, wk=wk, ka=ka: e.tensor_copy(wsm[:, wk, 0:1], tf[:, ka, 127:128]),
                               [("tf", ka)], [("wsm", wk)])
                    sch.op("vector", lambda e, wk=wk, ka=ka: e.tensor_copy(wsm[:, wk, 1:2], tf[:, ka, 255:256]),
                           [("tf", ka)], [("wsm", wk)])
                    kb = newtb()
                    stt(tbf[:, kb, 0:128], ps[A][:, 0:128], SC_M, tf[:, ka, 0:128], ALU.mult, ALU.mult,
                        [("ps", A), ("tf", ka)], [("tb", kb)])
                    stt(tbf[:, kb, 128:256], qT_, SC_M, tf[:, ka, 128:256], ALU.mult, ALU.mult,
                        [pg(4 + h), ("tf", ka)], [("tb", kb)])
                    AT = tbf[:, kb, 0:128]
                    qsT = tbf[:, kb, 128:256]
                    B = newps()
                    mm(ps[B][:, 0:128], vt, AT, True, False, [pg(12 + i), ("tb", kb)], [("ps", B)])
                    mm(ps[B][:, 0:128], Sb[:, sk, 0:128], qsT, False, True, ["Sb%d" % sk, ("tb", kb)], [("ps", B)])
                    mm(ps[B][:, 128:256], onesB[:, :], AT, True, False, ["onesB", ("tb", kb)], [("ps", B)])
                    mm(ps[B][:, 128:256], Sb[:, sk, 128:256], qsT, False, True, ["Sb%d" % sk, ("tb", kb)],
                       [("ps", B)])
                    kc_ = newtf()
                    act(tf[:, kc_, 0:128], ps[B][:, 128:256], AF.Abs, [("ps", B)], [("tf", kc_)])
                    ts(tf[:, kc_, 0:128], tf[:, kc_, 0:128], 1.0, None, ALU.max, None, [("tf", kc_)], [("tf", kc_)])
                    recip(tf[:, kc_, 0:128], tf[:, kc_, 0:128], [("tf", kc_)], [("tf", kc_)])
                    tt(tf[:, kc_, 128:256], ps[B][:, 0:128], tf[:, kc_, 0:128], ALU.mult,
                       [("ps", B), ("tf", kc_)], [("tf", kc_)])
                    act(tbf[:, kb, 384:512], tf[:, kc_, 128:256], AF.Square, [("tf", kc_)], [("tb", kb)])
                    mm(ps[B][:, 256:384], onesMHb, tbf[:, kb, 384:512], True, True, ["cstB", ("tb", kb)], [("ps", B)])
                    act(tf[:, kc_, 384:512], ps[B][:, 256:384], AF.Ln, [("ps", B), "par"], [("tf", kc_)],
                        bias=pcol(o_eps))
                    act(tf[:, kc_, 384:512], tf[:, kc_, 384:512], AF.Exp, [("tf", kc_)], [("tf", kc_)], scale=-0.5)
                    stt(tf[:, kc_, 0:128], tf[:, kc_, 128:256], pcol(o_mh + l * 4 + h), tf[:, kc_, 384:512],
                        ALU.mult, ALU.mult, [("tf", kc_), "par"], [("tf", kc_)])
                    tt(arena[:, 28 + h, tsl], tf[:, kc_, 0:128], arena[:, 20 + h, tsl], ALU.mult,
                       [("tf", kc_), pg(20 + h)], [pg(28 + h)])
                    C = newps()
                    mm(ps[C][:, 0:128], kT, identB[:, :], True, True, [pg(8 + h), "identB"], [("ps", C)])
                    act(tbf[:, kb, 256:384], ps[C][:, 0:128], AF.Copy, [("ps", C), ("wsm", wk)], [("tb", kb)],
                        scale=wsm[:, wk, 0:1])
                    kw_ = tbf[:, kb, 256:384]
                    mm(ps[C][:, 128:256], kw_, vt, True, True, [("tb", kb), pg(12 + i)], [("ps", C)])
                    mm(ps[C][:, 256:384], kw_, onesB[:, :], True, True, [("tb", kb), "onesB"], [("ps", C)])
                    stt(S[:, sk, :], S[:, sk, :], wsm[:, wk, 1:2], ps[C][:, 128:384], ALU.mult, ALU.add,
                        ["S%d" % sk, ("wsm", wk), ("ps", C)], ["S%d" % sk])
                    sch.op("scalar", lambda e, sk=sk: e.copy(Sb[:, sk, :], S[:, sk, :]), ["S%d" % sk], ["Sb%d" % sk])

            sga_pg = [16, 17, 18, 19, 0, 1, 2, 3]
            for t in range(4):
                sg = next_w(l, 7 + t)
                for cc in range(2):
                    c = 2 * t + cc
                    GA = newps()
                    for kc in range(8):
                        mm(ps[GA][:, :], wview(sg, kc, cc * 128, 128), xnT[:, kc, :], kc == 0, kc == 7,
                           [("w", sg), ("xn", kc)], [("ps", GA)])
                    act(arena[:, sga_pg[c], :], ps[GA][:, :], AF.Sigmoid, [("ps", GA)], [pg(sga_pg[c])])
                    GM = newps()
                    for kc in range(8):
                        mm(ps[GM][:, :], wview(sg, kc, 256 + cc * 128, 128), xnT[:, kc, :], kc == 0, kc == 7,
                           [("w", sg), ("xn", kc)], [("ps", GM)])
                    act(sgmb[:, c, :], ps[GM][:, :], AF.Sigmoid, [("ps", GM)], [("sgm", c)])
            for u in range(2):
                spp = next_w(l, 11 + u)
                for tt_ in range(2):
                    for cc in range(2):
                        c = 2 * (2 * u + tt_) + cc
                        pbase = tt_ * 2048 + cc * 128
                        PA = newps()
                        for kc in range(4):
                            o_ = pbase + kc * 256
                            mm(ps[PA][:, :], wb[:, spp, o_:o_ + 128], arena[:, 24 + kc, :], kc == 0, kc == 3,
                               [("w", spp), pg(24 + kc)], [("ps", PA)])
                        PM = newps()
                        for kc in range(4):
                            o_ = pbase + 1024 + kc * 256
                            mm(ps[PM][:, :], wb[:, spp, o_:o_ + 128], arena[:, 28 + kc, :], kc == 0, kc == 3,
                               [("w", spp), pg(28 + kc)], [("ps", PM)])
                        k1 = newtf()
                        k2 = newtf()
                        tt(tf[:, k1, :], ps[PA][:, :], arena[:, sga_pg[c], :], ALU.mult,
                           [("ps", PA), pg(sga_pg[c])], [("tf", k1)])
                        tt(tf[:, k2, :], ps[PM][:, :], sgmb[:, c, :], ALU.mult, [("ps", PM), ("sgm", c)], [("tf", k2)])
                        tt(arena[:, 4 + c, :], tf[:, k1, :], tf[:, k2, :], ALU.add, [("tf", k1), ("tf", k2)], [pg(4 + c)])
            for t in range(2):
                so = next_w(l, 13 + t)
                for m in range(4):
                    c = 4 * t + m
                    P = newps()
                    for kc in range(8):
                        mm(ps[P][:, :], wview(so, kc, m * 128, 128), arena[:, 4 + kc, :], kc == 0, kc == 7,
                           [("w", so), pg(4 + kc)], [("ps", P)])
                    tt(xT[:, c, :], xT[:, c, :], ps[P][:, :], ALU.add, [("xT", c), ("ps", P)], [("xT", c)])
            rmsnorm(o_g2 + l * 8, True)
            for t in range(8):
                su = next_w(l, 15 + t)
                for m in range(4):
                    f = 4 * t + m
                    P = newps()
                    for kc in range(8):
                        mm(ps[P][:, :], wview(su, kc, m * 128, 128), xnT[:, kc, :], kc == 0, kc == 7,
                           [("w", su), ("xn", kc)], [("ps", P)])
                    k1 = newtf()
                    act(tf[:, k1, :], ps[P][:, :], AF.Relu, [("ps", P)], [("tf", k1)])
                    tt(arena[:, f, :], ps[P][:, :], tf[:, k1, :], ALU.mult, [("ps", P), ("tf", k1)], [pg(f)])
            for c in range(8):
                sd = next_w(l, 23 + c)
                P = newps()
                for f in range(32):
                    mm(ps[P][:, :], wview(sd, f, 0, 128, kcn=32), arena[:, f, :], f == 0, f == 31,
                       [("w", sd), pg(f)], [("ps", P)])
                tt(xT[:, c, :], xT[:, c, :], ps[P][:, :], ALU.add, [("xT", c), ("ps", P)], [("xT", c)])

        def load_x(jb):
            src = x_d[jb * TB:(jb + 1) * TB, :].rearrange("(i p) d -> p i d", p=128)
            sch.op("sync", lambda e, src=src: e.dma_start(out=xin[:, :, :], in_=src), [],
                   [("xin", i) for i in range(4)], dma_sem="io_in", cost=8000.0)
            for c in range(8):
                P = newps()
                for i in range(4):
                    tr(ps[P][:, i * 128:(i + 1) * 128], xin[:, i, c * 128:(c + 1) * 128], [("xin", i)], [("ps", P)])
                copy_any(xT[:, c, :], ps[P][:, :], [("ps", P)], [("xT", c)])

        def store_out(jb, srcT, srcname):
            for i in range(4):
                for half in range(2):
                    P = newps()
                    for cc in range(4):
                        c = half * 4 + cc
                        tr(ps[P][:, cc * 128:(cc + 1) * 128], srcT[:, c, i * 128:(i + 1) * 128], [(srcname, c)],
                           [("ps", P)])
                    copy_any(io[:, i, half * 512:(half + 1) * 512], ps[P][:, :], [("ps", P)], [("io", i)])
            dst = out_d[jb * TB:(jb + 1) * TB, :].rearrange("(i p) d -> p i d", p=128)
            sch.op("sync", lambda e, dst=dst: e.dma_start(out=dst, in_=io[:, :, :]),
                   [("io", i) for i in range(4)], [], dma_sem="io_out", cost=8000.0)

        if not pipe:
            for j in range(nblk):
                load_x(j)
                for l in range(L):
                    layer(l, j)
                rmsnorm(o_gf, False)
                store_out(j, xT, "xT")
        else:
            for st in range(nstep):
                jb = min(st, nblk - 1)
                xsrc = x_d[:, jb * TB:(jb + 1) * TB].rearrange("(c p) t -> p c t", p=128)
                sch.op("sync", lambda e, xsrc=xsrc: e.dma_start(out=xl[:, :, :], in_=xsrc), [],
                       [("xl", c) for c in range(8)], dma_sem="io_in", cost=8000.0)
                if st == 0:
                    for c in range(8):
                        copy_any(xT[:, c, :], xl[:, c, :], [("xl", c)], [("xT", c)])
                else:
                    for hf in range(NH):
                        rsrc = recv_d[st - 1][hf][0:128, :].rearrange("p (c t) -> p c t", c=CPP)
                        sch.op("sync", lambda e, rsrc=rsrc, hf=hf: e.dma_start(out=rbuf[:, CPP * hf:CPP * hf + CPP, :], in_=rsrc),
                               [("recv", st - 1, hf)], [("rb", c) for c in range(CPP * hf, CPP * hf + CPP)], dma_sem=("rcv", hf),
                               cost=2000.0 + 750.0 * CPP)
                        for c in range(CPP * hf, CPP * hf + CPP):
                            stt(xT[:, c, :], rbuf[:, c, :], pcol(o_fb), xl[:, c, :], ALU.mult, ALU.add,
                                [("rb", c), ("xl", c), "par"], [("xT", c)])
                layer(0, st)
                if st < nblk:
                    for hf in range(NH):
                        sdst = send_d[st][hf].rearrange("p (c t) -> p c t", c=CPP)
                        sch.op("sync", lambda e, sdst=sdst, hf=hf: e.dma_start(out=sdst, in_=xT[:, CPP * hf:CPP * hf + CPP, :]),
                               [("xT", c) for c in range(CPP * hf, CPP * hf + CPP)], [("send", st, hf)], dma_sem=("snd", hf),
                               cost=2000.0 + 750.0 * CPP)
                        sch.op("gpsimd", lambda e, st=st, hf=hf: e.collective_compute(
                            "AllGather", ALU.bypass, replica_groups=[[0, 1], [2, 3], [4, 5], [6, 7]],
                            ins=[send_d[st][hf]], outs=[recv_d[st][hf]]),
                            [("send", st, hf)], [("recv", st, hf), "cc_serial"], dma_sem=("cc", NH * st + hf),
                            cost=3000.0)
                if st >= 1:
                    kr = norm_stats()
                    for c in range(8):
                        stt(rbuf[:, c, :], xT[:, c, :], pcol(o_gf + c), tf[:, kr, :], ALU.mult, ALU.mult,
                            [("xT", c), ("tf", kr), "par"], [("rb", c)])
                    odst = out_d[:, (st - 1) * TB:st * TB].rearrange("(c p) t -> p c t", p=128)
                    sch.op("sync", lambda e, odst=odst: e.dma_start(out=odst, in_=rbuf[:, :, :]),
                           [("rb", c) for c in range(8)], [], dma_sem="io_out", cost=8000.0)

        sch.finalize()
        sch.emit(nc, es, dma_sems, ["io_out"])
    return nc


def _fm(W, cols):
    sub = W[:, cols]
    n = sub.shape[1]
    return np.ascontiguousarray(sub.reshape(8, 128, n).transpose(1, 0, 2).reshape(128, 8 * n))


def host_constants():
    s = np.arange(128)
    ident = np.eye(128, dtype=np.float32)
    onesMD = np.full((128, 128), 1.0 / 1024.0, np.float32)
    onesMH = np.full((128, 128), 1.0 / 128.0, np.float32)
    U = (s[:, None] <= s[None, :]).astype(np.float32)
    NEGm = np.where(s[:, None] <= s[None, :], 0.0, NEG).astype(np.float32)
    cst = np.concatenate([ident, onesMD, onesMH, U, NEGm], axis=1)
    kp = np.arange(128)[:, None]
    u = np.arange(640)[None, :]
    dq = u // 64 - (kp >= 64)
    negm = np.where((dq >= 0) & (dq <= 8), 0.0, NEG).astype(np.float32)
    return np.ascontiguousarray(cst), np.ascontiguousarray(negm)


def host_prep(inp, depth=DEPTH, layers=None, pipe_role=None, nblk=SEQ // TB):
    if layers is None:
        layers = list(range(depth))
    L = len(layers)
    sel = np.asarray(layers)
    inp = dict(inp)
    for k_ in ("mix_norm_g", "w_in", "conv_w", "conv_b", "b_igate", "b_fgate", "rel_bias", "mh_norm_g",
               "w_att_proj", "w_mlstm_proj", "w_out", "ffn_norm_g", "w_up", "w_down"):
        inp[k_] = np.asarray(inp[k_], np.float32)[sel]
    w_in = inp["w_in"]
    tiles = []
    wifs = []
    for l in range(L):
        for g in range(7):
            tiles.append(_fm(w_in[l], np.arange(512 * g, 512 * g + 512)))
        w_att = np.asarray(inp["w_att_proj"][l], np.float32)
        w_ml = np.asarray(inp["w_mlstm_proj"][l], np.float32)

        def pp_tile(u):
            parts = []
            for tt_ in range(2):
                t = 2 * u + tt_
                for w in (w_att, w_ml):
                    sub = w[:, t * 256:(t + 1) * 256]
                    parts.append(sub.reshape(4, 128, 256).transpose(1, 0, 2).reshape(128, 1024))
            return np.ascontiguousarray(np.concatenate(parts, axis=1))

        def gagm_tile(t):
            cols = np.concatenate([GA0 + 256 * t + np.arange(256), GM0 + 256 * t + np.arange(256)])
            return _fm(w_in[l], cols)

        for t_ in range(4):
            tiles.append(gagm_tile(t_))
        tiles.append(pp_tile(0))
        tiles.append(pp_tile(1))
        w_out = np.asarray(inp["w_out"][l], np.float32)
        for t in range(2):
            tiles.append(_fm(w_out, np.arange(512 * t, 512 * t + 512)))
        w_up = np.asarray(inp["w_up"][l], np.float32)
        for t in range(8):
            tiles.append(_fm(w_up, np.arange(512 * t, 512 * t + 512)))
        w_down = np.asarray(inp["w_down"][l], np.float32)
        for c in range(8):
            sub = w_down[:, c * 128:(c + 1) * 128]
            tiles.append(np.ascontiguousarray(sub.reshape(32, 128, 128).transpose(1, 0, 2).reshape(128, 4096)))
        wifs.append(_fm(w_in[l], np.arange(3584, 3592)))
    wstream = np.stack(tiles, axis=0)
    assert wstream.shape == (L * NW_TILES, 128, 4096)
    wif = np.concatenate(wifs, axis=1)

    kp = np.arange(128)[:, None]
    u = np.arange(640)[None, :]
    rel_idx = np.clip(u - kp, -256, 256) + 256
    rb = np.asarray(inp["rel_bias"], np.float32)[:L]
    bg = rb[:, :, rel_idx]
    biasg = np.ascontiguousarray(bg.transpose(2, 0, 1, 3).reshape(128, L * 8 * 640))

    def fmv(v):
        v = np.asarray(v, np.float32)
        k = v.shape[-1] // 128
        return np.moveaxis(v.reshape(v.shape[:-1] + (k, 128)), -1, 0)

    g1 = fmv(inp["mix_norm_g"][:L]).reshape(128, L * 8)
    g2 = fmv(inp["ffn_norm_g"][:L]).reshape(128, L * 8)
    gf = fmv(inp["final_norm_g"]).reshape(128, 8)
    cw = fmv(inp["conv_w"][:L]).reshape(128, L * 32)
    cb = fmv(inp["conv_b"][:L]).reshape(128, L * 8)
    mh = fmv(inp["mh_norm_g"][:L]).reshape(128, L * 4)
    bi = np.broadcast_to(np.asarray(inp["b_igate"], np.float32)[:L][None, :, None, :], (128, L, 4, 4)).reshape(128, L * 16)
    bf = np.broadcast_to(np.asarray(inp["b_fgate"], np.float32)[:L][None, :, None, :], (128, L, 4, 4)).reshape(128, L * 16)
    epsc = np.full((128, 1), EPS, np.float32)
    plist = [g1, g2, gf, cw, cb, mh, bi, bf, epsc]
    if pipe_role is not None:
        nstep = nblk + 1
        fb = np.full((128, 1), float(pipe_role), np.float32)
        smask = np.zeros((128, nstep), np.float32)
        smask[:, :pipe_role + 1] = NEG
        valid = np.ones((128, nstep), np.float32)
        valid[:, :pipe_role] = 0.0
        plist += [fb, smask, valid]
    params = np.ascontiguousarray(np.concatenate(plist, axis=1).astype(np.float32))
    cst, negm = host_constants()
    return {"wstream": wstream, "wif": np.ascontiguousarray(wif), "biasg": biasg, "cst": cst, "negm": negm,
            "params": params}


_NC_CACHE = {}


PIPE = True


def kernel(**inputs):
    x = np.asarray(inputs["x"], np.float32)
    B, S_, D = x.shape
    nblk = S_ // TB
    n_cores = 8
    key = (nblk, DEPTH, PIPE)
    if key not in _NC_CACHE:
        _NC_CACHE[key] = build_program(nblk, DEPTH, pipe=PIPE)
    nc = _NC_CACHE[key]
    in_maps = []
    if PIPE:
        assert DEPTH == 2 and B * 2 == n_cores
        roles = [host_prep(inputs, DEPTH, layers=[r], pipe_role=r, nblk=nblk) for r in range(2)]
        zeros = np.zeros((D, S_), np.float32)
        for c in range(n_cores):
            m = dict(roles[c % 2])
            m["x"] = np.ascontiguousarray(x[c // 2].T) if c % 2 == 0 else zeros
            in_maps.append(m)
        res = run_bass_kernel_spmd(nc, in_maps, core_ids=list(range(n_cores)))
        out = np.stack([np.ascontiguousarray(np.asarray(res.results[2 * b + 1]["out"], np.float32).T)
                        for b in range(B)], axis=0)
        return out
    shared = host_prep(inputs, DEPTH)
    for c in range(n_cores):
        m = dict(shared)
        m["x"] = np.ascontiguousarray(x[c % B])
        in_maps.append(m)
    res = run_bass_kernel_spmd(nc, in_maps, core_ids=list(range(n_cores)))
    out = np.stack([np.asarray(res.results[b]["out"], np.float32) for b in range(B)], axis=0)
    return out
```

```python
import math
from contextlib import ExitStack

import numpy as np
import concourse.bass as bass
import concourse.mybir as mybir
from concourse.bass_utils import run_bass_kernel_spmd

F32 = mybir.dt.float32
BF16 = mybir.dt.bfloat16
AF = mybir.ActivationFunctionType
ALU = mybir.AluOpType

D_MODEL = 1024
BATCH = 4
SEQ = 4096
DEPTH = 2
TB = 512
NW_TILES = 31
NWS = 5
NTF = 8
NTB = 4
EPS = 1e-6
NEG = -30000.0
GA0 = 3592
GM0 = 4616
EPOCH = 8000
SAME_ENGINE_SYNC = "raw"
LIST_SCHED = True
GATE_PREFETCH = True
GATE_DIV = 2
ATT_MUL_ENG = "vector"
NH = 2
XENG_LAT = 1000.0
SAME_LAT = 200.0
import os as _os
KNOP = set(int(v) for v in _os.environ.get('KNOP', '').split(',') if v)


def _is_cc(key):
    return isinstance(key, tuple) and key[0] == "cc"


class _Op:
    __slots__ = ("eng", "fn", "deps", "dma_sem", "token", "needs_inc", "idx", "raw", "cost", "alldeps")


class Sched:
    def __init__(self):
        self.ops = []
        self.lastw = {}
        self.readers = {}

    def op(self, eng, fn, reads=(), writes=(), dma_sem=None, cost=300.0):
        o = _Op()
        o.cost = cost
        o.eng = eng
        o.fn = fn
        o.dma_sem = dma_sem
        o.idx = len(self.ops)
        o.needs_inc = dma_sem is not None
        o.token = None
        deps = set()
        raw = set()
        for r in reads:
            if r in self.lastw:
                deps.add(self.lastw[r])
                raw.add(self.lastw[r])
        o.raw = raw
        for w in writes:
            if w in self.lastw:
                deps.add(self.lastw[w])
            for rd in self.readers.get(w, ()):
                deps.add(rd)
        deps.discard(o.idx)
        o.deps = deps
        for r in reads:
            self.readers.setdefault(r, set()).add(o.idx)
        for w in writes:
            self.lastw[w] = o.idx
            self.readers[w] = set()
        self.ops.append(o)
        return o

    def finalize(self):
        import os
        if os.environ.get("KTRUNC"):
            self.ops = self.ops[:int(os.environ["KTRUNC"])]
        if os.environ.get("KSKIP"):
            a, b = [int(v) for v in os.environ["KSKIP"].split(",")]
            kept = [o for o in self.ops if not (a <= o.idx < b)]
            remap = {o.idx: i for i, o in enumerate(kept)}
            for o in kept:
                o.deps = set(remap[d] for d in o.deps if d in remap)
                o.raw = set(remap[d] for d in o.raw if d in remap)
                o.idx = remap[o.idx]
            self.ops = kept
        ops = self.ops
        for o in ops:
            o.alldeps = set(o.deps)
        self.order = self.list_schedule() if LIST_SCHED else None
        for o in ops:
            keep = set()
            for d in o.deps:
                do = ops[d]
                if do.eng == o.eng and do.dma_sem is None and o.dma_sem is None:
                    if o.eng == "tensor" or not SAME_ENGINE_SYNC:
                        continue
                    if SAME_ENGINE_SYNC == "raw" and d not in o.raw:
                        continue
                keep.add(d)
            o.deps = keep
            for d in keep:
                ops[d].needs_inc = True

    def list_schedule(self):
        import heapq
        ops = self.ops
        n = len(ops)
        ndep = [len(o.alldeps) for o in ops]
        users = [[] for _ in range(n)]
        for o in ops:
            for d in o.alldeps:
                users[d].append(o.idx)
        finish = [0.0] * n
        ready_t = [0.0] * n
        engs = ["tensor", "vector", "scalar", "gpsimd", "sync"]
        pend = {e: [] for e in engs}
        avail = {e: [] for e in engs}
        t_free = {e: 0.0 for e in engs}
        for o in ops:
            if ndep[o.idx] == 0:
                heapq.heappush(pend[o.eng], (0.0, o.idx))
        order = []
        done = 0
        while done < n:
            best = None
            for e in engs:
                while pend[e] and pend[e][0][0] <= t_free[e]:
                    rt, i = heapq.heappop(pend[e])
                    heapq.heappush(avail[e], i)
                if avail[e]:
                    cand = (t_free[e], avail[e][0], e, True)
                elif pend[e]:
                    cand = (pend[e][0][0], pend[e][0][1], e, False)
                else:
                    continue
                if best is None or cand[:2] < best[:2]:
                    best = cand
            st, i, e, from_avail = best
            if from_avail:
                heapq.heappop(avail[e])
            else:
                heapq.heappop(pend[e])
            o = ops[i]
            fin = st + o.cost
            t_free[e] = fin
            finish[i] = fin
            order.append(i)
            done += 1
            for u in users[i]:
                if ops[u].eng == e:
                    lat = SAME_LAT if (e in ("vector", "scalar") and i in ops[u].raw) else 0.0
                else:
                    lat = XENG_LAT
                if fin + lat > ready_t[u]:
                    ready_t[u] = fin + lat
                ndep[u] -= 1
                if ndep[u] == 0:
                    heapq.heappush(pend[ops[u].eng], (ready_t[u], u))
        self.est_total = max(finish) if finish else 0.0
        return order

    def emit(self, nc, es, dma_sems, final_waits):
        ops = self.ops
        seq = [ops[i] for i in self.order] if self.order is not None else list(ops)
        eng_sems = {}
        counters = {}
        dma_counts = {}
        for o in seq:
            if o.dma_sem is not None:
                c = dma_counts.get(o.dma_sem, 0) + (1 if _is_cc(o.dma_sem) else 16)
                dma_counts[o.dma_sem] = c
                o.token = (dma_sems[o.dma_sem], c)
            elif o.needs_inc:
                ep, c = counters.get(o.eng, (0, 0))
                if c >= EPOCH:
                    ep, c = ep + 1, 0
                c += 1
                counters[o.eng] = (ep, c)
                key = (o.eng, ep)
                if key not in eng_sems:
                    eng_sems[key] = es.enter_context(nc.semaphore("p_%s_%d" % (o.eng, ep)))
                o.token = (eng_sems[key], c)
        for o in ops:
            if o.dma_sem in ("setup", "setup_sw"):
                o.token = (dma_sems[o.dma_sem], dma_counts[o.dma_sem])
        by_eng = {}
        for o in seq:
            by_eng.setdefault(o.eng, []).append(o)

        def run(eng_name, e):
            waited = {}
            for o in by_eng.get(eng_name, []):
                need = {}
                for d in o.deps:
                    s, v = ops[d].token
                    k = id(s)
                    if waited.get(k, 0) >= v:
                        continue
                    if k not in need or need[k][1] < v:
                        need[k] = (s, v)
                for k, (s, v) in need.items():
                    e.wait_ge(s, v)
                    waited[k] = v
                if o.idx in KNOP:
                    ins = e.nop()
                else:
                    ins = o.fn(e)
                if o.dma_sem is not None:
                    if _is_cc(o.dma_sem):
                        ins.then_inc(o.token[0])
                    else:
                        ins.then_inc(o.token[0], 16)
                elif o.needs_inc:
                    ins.then_inc(o.token[0], 1)
            mine = []
            for o in by_eng.get(eng_name, []):
                if o.dma_sem is not None and o.dma_sem not in mine:
                    mine.append(o.dma_sem)
            for key in mine:
                e.wait_ge(dma_sems[key], dma_counts[key])

        with nc.Block() as block:
            @block.sync
            def _(e):
                run("sync", e)

            @block.gpsimd
            def _(e):
                run("gpsimd", e)

            @block.tensor
            def _(e):
                run("tensor", e)

            @block.vector
            def _(e):
                run("vector", e)

            @block.scalar
            def _(e):
                run("scalar", e)


def build_program(nblk, depth=DEPTH, pipe=False):
    T = nblk * TB
    nc = bass.Bass("TRN2", target_bir_lowering=False)
    L = 1 if pipe else depth
    nstep = nblk + 1 if pipe else nblk
    if pipe:
        x_d = nc.dram_tensor("x", [D_MODEL, T], F32, kind="ExternalInput").ap()
    else:
        x_d = nc.dram_tensor("x", [T, D_MODEL], F32, kind="ExternalInput").ap()
    w_d = nc.dram_tensor("wstream", [L * NW_TILES, 128, 4096], F32, kind="ExternalInput").ap()
    wif_d = nc.dram_tensor("wif", [128, L * 64], F32, kind="ExternalInput").ap()
    bias_d = nc.dram_tensor("biasg", [128, L * 8 * 640], F32, kind="ExternalInput").ap()
    cst_d = nc.dram_tensor("cst", [128, 5 * 128], F32, kind="ExternalInput").ap()
    negm_d = nc.dram_tensor("negm", [128, 640], F32, kind="ExternalInput").ap()
    NPAR = L * 8 + L * 8 + 8 + L * 32 + L * 8 + L * 4 + L * 16 + L * 16 + 1 + (1 + 2 * (nblk + 1) if pipe else 0)
    par_d = nc.dram_tensor("params", [128, NPAR], F32, kind="ExternalInput").ap()
    if pipe:
        out_d = nc.dram_tensor("out", [D_MODEL, T], F32, kind="ExternalOutput").ap()
    else:
        out_d = nc.dram_tensor("out", [T, D_MODEL], F32, kind="ExternalOutput").ap()

    es = ExitStack()
    with es:
        def sb(name, shape, dt):
            return es.enter_context(nc.sbuf_tensor(name, shape, dt))

        if not pipe:
            io = sb("io", [128, 4, 1024], F32)
        if pipe:
            xl = sb("xl", [128, 8, TB], F32)
            rbuf = sb("rbuf", [128, 8, TB], F32)
            CPP = 8 // NH
            send_d = [[nc.dram_tensor("send%d_%d" % (k, hf), [128, CPP * TB], F32, kind="Internal").ap()
                       for hf in range(NH)] for k in range(nblk)]
            recv_d = [[nc.dram_tensor("recv%d_%d" % (k, hf), [256, CPP * TB], F32, kind="Internal").ap()
                       for hf in range(NH)] for k in range(nblk)]
        else:
            xin = io
        xT = sb("xT", [128, 8, TB], F32)
        xnT = sb("xnT", [128, 8, TB], BF16)
        arena = sb("arena", [128, 32, TB], BF16)
        kh = sb("kh", [128, L * 2 * 4, TB], BF16)
        sgmb = sb("sgmb", [128, 8, TB], BF16)
        vh = sb("vh", [128, L * 2 * 4, TB], BF16)
        wb = sb("wb", [128, NWS, 4096], BF16)
        wif = sb("wifs", [128, L * 64], BF16)
        BM = sb("BM", [128, L * 8, 640], BF16)
        negm = sb("negms", [128, 640], BF16)
        S = sb("S", [128, L * 4, 256], F32)
        Sb = sb("Sb", [128, L * 4, 256], BF16)
        halo = sb("halo", [128, L * 8, 4], F32)
        cst = sb("csts", [128, 5, 128], F32)
        identB = sb("identB", [128, 128], BF16)
        cstB = sb("cstB", [128, 3, 128], BF16)
        onesB = sb("onesB", [128, 128], BF16)
        onesF = sb("onesF", [128, 128], F32)
        par = sb("pars", [128, NPAR], F32)
        tf = sb("tf", [128, NTF, 512], F32)
        tbf = sb("tbf", [128, NTB, 512], BF16)
        pre = sb("pre", [128, 2, 516], F32)
        LF = sb("LF", [128, 2, 4, 128], F32)
        gsm = sb("gsm", [128, 8, 16], F32)
        wsm = sb("wsm", [128, 8, 2], F32)
        ps = [es.enter_context(nc.psum_tensor("ps%d" % b, [128, 512], F32)) for b in range(8)]

        identF = cst[:, 0, :]
        onesMD = cst[:, 1, :]
        onesMH = cst[:, 2, :]
        Umat = cst[:, 3, :]
        NEGmat = cst[:, 4, :]
        onesMDb = cstB[:, 0, :]
        onesMHb = cstB[:, 1, :]
        NEGb = cstB[:, 2, :]

        o_g1 = 0
        o_g2 = o_g1 + L * 8
        o_gf = o_g2 + L * 8
        o_cw = o_gf + 8
        o_cb = o_cw + L * 32
        o_mh = o_cb + L * 8
        o_bi = o_mh + L * 4
        o_bf = o_bi + L * 16
        o_eps = o_bf + L * 16
        o_fb = o_eps + 1
        o_sm = o_fb + 1
        o_vd = o_sm + nstep

        def pcol(off):
            return par[:, off:off + 1]

        dma_sems = {}
        extra_sems = ([("snd", k) for k in range(NH)] + [("rcv", k) for k in range(NH)]
                      + [("cc", k) for k in range(NH * nblk)]) if pipe else []
        extra_sems = extra_sems + [("bm", k) for k in range(L * 8)]
        for k in ["setup", "setup_sw", "io_in", "io_out"] + extra_sems + [("w", s) for s in range(NWS)]:
            dma_sems[k] = es.enter_context(nc.semaphore("d_%s" % (str(k).replace(" ", ""))))

        sch = Sched()
        state = {"ps": 0, "tf": 0, "tb": 0, "w": 0, "alt": 0, "pre": 0, "lf": 0, "wsm": 0}

        def newps(exclude=()):
            b = state["ps"]
            while b in exclude:
                b = (b + 1) % 8
            state["ps"] = (b + 1) % 8
            return b

        def newtf():
            k = state["tf"]
            state["tf"] = (k + 1) % NTF
            return k

        def newtb():
            k = state["tb"]
            state["tb"] = (k + 1) % NTB
            return k

        def fsz(ap):
            n = 1
            for d in ap.shape[1:]:
                n *= int(d)
            return n

        def mm(out, lhsT, rhs, start, stop, r, w):
            c = max(fsz(rhs), 64) / 1.95 + 15.0
            if lhsT.dtype == F32:
                c *= 4.0
            sch.op("tensor", lambda e: e.matmul(out, lhsT, rhs, start=start, stop=stop), r, w, cost=c)

        def tr(out, in_, r, w):
            sch.op("tensor", lambda e: e.transpose(out, in_, identF), list(r) + ["cst"], w, cost=280.0)

        def act(out, in_, func, r, w, bias=None, scale=None):
            kw = {}
            if bias is not None:
                kw["bias"] = bias
            if scale is not None:
                kw["scale"] = scale
            c = fsz(out) / 1.4 + 220.0 + (90.0 if bias is not None and not isinstance(bias, float) else 0.0)
            sch.op("scalar", lambda e: e.activation(out, in_, func, **kw), r, w, cost=c)

        def dcost(ap, mult=1.0):
            return max(fsz(ap), 64) * mult / 0.96 + 90.0

        def tt(out, in0, in1, op, r, w, eng="vector"):
            sch.op(eng, lambda e: e.tensor_tensor(out, in0, in1, op), r, w, cost=dcost(out))

        def ts(out, in0, s1, s2, op0, op1, r, w, eng="vector"):
            if op1 is None:
                sch.op(eng, lambda e: e.tensor_scalar(out, in0, s1, None, op0), r, w, cost=dcost(out))
            else:
                sch.op(eng, lambda e: e.tensor_scalar(out, in0, s1, s2, op0, op1), r, w, cost=dcost(out))

        def stt(out, in0, scalar, in1, op0, op1, r, w):
            sch.op("vector", lambda e: e.scalar_tensor_tensor(out, in0, scalar, in1, op0, op1), r, w,
                   cost=dcost(out))

        def recip(out, in_, r, w):
            sch.op("vector", lambda e: e.reciprocal(out, in_), r, w, cost=dcost(out, 8.0))

        def copy_any(out, in_, r, w):
            state["alt"] ^= 1
            if state["alt"]:
                sch.op("scalar", lambda e: e.copy(out, in_), r, w, cost=fsz(out) / 1.4 + 220.0)
            else:
                sch.op("vector", lambda e: e.tensor_copy(out, in_), r, w, cost=dcost(out))

        def dsetup(eng, out, in_, w):
            sch.op(eng, lambda e: e.dma_start(out=out, in_=in_), [], w,
                   dma_sem="setup_sw" if eng == "gpsimd" else "setup")

        dsetup("sync", cst[:, :, :], cst_d.rearrange("p (k n) -> p k n", k=5), ["cst"])
        dsetup("sync", par[:, :], par_d, ["par"])
        dsetup("gpsimd", wif[:, :], wif_d, ["wif"])
        dsetup("gpsimd", negm[:, :], negm_d, ["negm"])
        def setup_bm():
            for l in range(L):
                for h in range(8):
                    k = l * 8 + h
                    sch.op("gpsimd", lambda e, k=k: e.dma_start(out=BM[:, k, :], in_=bias_d[:, k * 640:(k + 1) * 640]),
                           [], [("BM", k)], dma_sem=("bm", k), cost=2500.0)
                    tt(BM[:, k, :], BM[:, k, :], negm[:, :], ALU.add, [("BM", k), "negm"], [("BM", k)])
                    if ATT_MUL_ENG:
                        act(BM[:, k, :], BM[:, k, :], AF.Exp, [("BM", k)], [("BM", k)])

        sch.op("vector", lambda e: e.memset(S[:, :, :], 0.0), [], ["S%d" % k for k in range(L * 4)])
        sch.op("vector", lambda e: e.memset(Sb[:, :, :], 0.0), [], ["Sb%d" % k for k in range(L * 4)])
        sch.op("vector", lambda e: e.memset(halo[:, :, :], 0.0), [], [("halo", k) for k in range(L * 8)])
        sch.op("vector", lambda e: e.memset(kh[:, :, :], 0.0), [], [("kh", k) for k in range(L * 8)])
        sch.op("vector", lambda e: e.memset(vh[:, :, :], 0.0), [], [("vh", k) for k in range(L * 8)])
        sch.op("vector", lambda e: e.memset(onesB[:, :], 1.0), [], ["onesB"])
        sch.op("vector", lambda e: e.memset(onesF[:, :], 1.0), [], ["onesF"])
        sch.op("vector", lambda e: e.tensor_copy(identB[:, :], identF), ["cst"], ["identB"])
        sch.op("vector", lambda e: e.tensor_copy(cstB[:, 0:2, :], cst[:, 1:3, :]), ["cst"], ["cstB"])
        sch.op("vector", lambda e: e.tensor_copy(cstB[:, 2, :], cst[:, 4, :]), ["cst", "cstB"], ["cstB"])
        def next_w(l, k_expected):
            n = state["w"]
            state["w"] = n + 1
            assert n % NW_TILES == k_expected and n // NW_TILES % L == l, (n, l, k_expected)
            slot = n % NWS
            src = w_d[(n % (L * NW_TILES)), :, :]
            gate = []
            if pipe and GATE_PREFETCH and n // NW_TILES >= 1 and n % NW_TILES < NWS:
                gate = [("recv", n // NW_TILES - 1, min((n % NW_TILES) // GATE_DIV, NH - 1))]
            sch.op("gpsimd", lambda e: e.dma_start(out=wb[:, slot, :], in_=src), gate, [("w", slot)],
                   dma_sem=("w", slot), cost=9000.0)
            return slot

        def wview(slot, kc, c0, n, kcn=8):
            width = 4096 // kcn
            return wb[:, slot, kc * width + c0: kc * width + c0 + n]

        def rstd_from(P_ap, kr, rd):
            act(tf[:, kr, :], P_ap, AF.Ln, rd + ["par"], [("tf", kr)], bias=pcol(o_eps))
            act(tf[:, kr, :], tf[:, kr, :], AF.Exp, [("tf", kr)], [("tf", kr)], scale=-0.5)

        def norm_stats():
            P = newps()
            for c in range(8):
                k = newtb()
                act(tbf[:, k, :], xT[:, c, :], AF.Square, [("xT", c)], [("tb", k)])
                mm(ps[P][:, :], onesMDb, tbf[:, k, :], c == 0, c == 7, [("tb", k), "cstB"], [("ps", P)])
            kr = newtf()
            rstd_from(ps[P][:, :], kr, [("ps", P)])
            return kr

        def rmsnorm(gcol0, dst_bf16):
            kr = norm_stats()
            for c in range(8):
                if dst_bf16:
                    stt(xnT[:, c, :], xT[:, c, :], pcol(gcol0 + c), tf[:, kr, :], ALU.mult, ALU.mult,
                        [("xT", c), ("tf", kr), "par"], [("xn", c)])
                else:
                    stt(xT[:, c, :], xT[:, c, :], pcol(gcol0 + c), tf[:, kr, :], ALU.mult, ALU.mult,
                        [("xT", c), ("tf", kr), "par"], [("xT", c)])

        def pg(n):
            return ("pg", n)

        SC_M = 1.0 / math.sqrt(128.0)

        def layer(l, j):
            parity = j % 2
            kcur = lambda p: (l * 2 + parity) * 4 + p
            kprev = lambda p: (l * 2 + 1 - parity) * 4 + p

            rmsnorm(o_g1 + l * 8, True)
            xn_all = [("xn", c) for c in range(8)]

            for g in range(2):
                slot = next_w(l, g)
                for m in range(4):
                    P = newps()
                    for kc in range(8):
                        mm(ps[P][:, :], wview(slot, kc, m * 128, 128), xnT[:, kc, :], kc == 0, kc == 7,
                           [("w", slot), ("xn", kc)], [("ps", P)])
                    if g == 0:
                        copy_any(arena[:, m, :], ps[P][:, :], [("ps", P)], [pg(m)])
                    else:
                        copy_any(kh[:, kcur(m), :], ps[P][:, :], [("ps", P)], [("kh", kcur(m))])
            slot = next_w(l, 2)
            for i in range(4):
                P = newps()
                for kc in range(8):
                    mm(ps[P][:, :], xnT[:, kc, i * 128:(i + 1) * 128], wview(slot, kc, 0, 512), kc == 0, kc == 7,
                       [("w", slot), ("xn", kc)], [("ps", P)])
                copy_any(vh[:, kcur(i), :], ps[P][:, :], [("ps", P)], [("vh", kcur(i))])

            if not state.get("bm_done"):
                state["bm_done"] = True
                setup_bm()
            for p in range(4):
                O = newps()
                Dn = newps()
                for e_ in range(2):
                    h = 2 * p + e_
                    rows = slice(e_ * 64, (e_ + 1) * 64)
                    rlist = [4] + [r for r in range(8) if (j > 0 or r >= 4 or pipe) and r != 4]
                    for idx, r in enumerate(rlist):
                        kidx = kprev(p) if r < 4 else kcur(p)
                        vidx = kprev(r % 4) if r < 4 else kcur(r % 4)
                        qlo = max(8, 2 * r) * 64 - 512
                        qhi = (min(15, 2 * r + 9) + 1) * 64 - 512
                        n = qhi - qlo
                        u0 = 512 + qlo - 128 * r
                        sc = newps(exclude=(O, Dn))
                        mm(ps[sc][:, 0:n], kh[rows, kidx, (r % 4) * 128:(r % 4 + 1) * 128], arena[rows, p, qlo:qhi],
                           True, True, [("kh", kidx), pg(p)], [("ps", sc)])
                        k1 = newtf()
                        k2 = newtb()
                        if ATT_MUL_ENG:
                            if pipe and r < 4:
                                act(tf[:, k1, 0:n], ps[sc][:, 0:n], AF.Exp, [("ps", sc), "par"], [("tf", k1)],
                                    bias=pcol(o_sm + j), scale=0.125)
                            else:
                                act(tf[:, k1, 0:n], ps[sc][:, 0:n], AF.Exp, [("ps", sc)], [("tf", k1)], scale=0.125)
                            sch.op(ATT_MUL_ENG, lambda e, k1=k1, k2=k2, n=n, u0=u0, lh=l * 8 + h: e.tensor_tensor(
                                tbf[:, k2, 0:n], tf[:, k1, 0:n], BM[:, lh, u0:u0 + n], ALU.mult),
                                [("tf", k1), ("BM", l * 8 + h)], [("tb", k2)],
                                cost=(n / 0.5 + 500.0) if ATT_MUL_ENG == "gpsimd" else (n / 0.96 + 90.0))
                        else:
                            stt(tf[:, k1, 0:n], ps[sc][:, 0:n], 0.125, BM[:, l * 8 + h, u0:u0 + n], ALU.mult, ALU.add,
                                [("ps", sc), ("BM", l * 8 + h)], [("tf", k1)])
                            if pipe and r < 4:
                                act(tbf[:, k2, 0:n], tf[:, k1, 0:n], AF.Exp, [("tf", k1), "par"], [("tb", k2)],
                                    bias=pcol(o_sm + j))
                            else:
                                act(tbf[:, k2, 0:n], tf[:, k1, 0:n], AF.Exp, [("tf", k1)], [("tb", k2)])
                        first = idx == 0
                        last = idx == len(rlist) - 1
                        mm(ps[O][rows, qlo:qhi], vh[:, vidx, h * 64:(h + 1) * 64], tbf[:, k2, 0:n], first, last,
                           [("vh", vidx), ("tb", k2)], [("ps", O)])
                        mm(ps[Dn][rows, qlo:qhi], onesB[:, 0:64], tbf[:, k2, 0:n], first, last,
                           ["onesB", ("tb", k2)], [("ps", Dn)])
                kr = newtf()
                act(tf[:, kr, :], ps[Dn][:, :], AF.Ln, [("ps", Dn)], [("tf", kr)])
                act(tf[:, kr, :], tf[:, kr, :], AF.Exp, [("tf", kr)], [("tf", kr)], scale=-1.0)
                tt(arena[:, 24 + p, :], ps[O][:, :], tf[:, kr, :], ALU.mult, [("ps", O), ("tf", kr)], [pg(24 + p)])

            for g in (3, 4):
                slot = next_w(l, g)
                for m in range(4):
                    ch = (g - 3) * 4 + m
                    P = newps()
                    for kc in range(8):
                        mm(ps[P][:, :], wview(slot, kc, m * 128, 128), xnT[:, kc, :], kc == 0, kc == 7,
                           [("w", slot), ("xn", kc)], [("ps", P)])
                    pk = state["pre"]
                    state["pre"] ^= 1
                    hk = l * 8 + ch
                    copy_any(pre[:, pk, 3:515], ps[P][:, :], [("ps", P)], [("pre", pk)])
                    sch.op("vector", lambda e, pk=pk, hk=hk: e.tensor_copy(pre[:, pk, 0:3], halo[:, hk, 0:3]),
                           [("halo", hk)], [("pre", pk)])
                    sch.op("vector", lambda e, pk=pk, hk=hk: e.tensor_copy(halo[:, hk, 0:3], pre[:, pk, 512:515]),
                           [("pre", pk)], [("halo", hk)])
                    ka = newtf()
                    cw = lambda tap: pcol(o_cw + l * 32 + tap * 8 + ch)
                    ts(tf[:, ka, :], pre[:, pk, 0:512], cw(0), None, ALU.mult, None,
                       [("pre", pk), "par"], [("tf", ka)])
                    for tap in (1, 2, 3):
                        stt(tf[:, ka, :], pre[:, pk, tap:tap + 512], cw(tap), tf[:, ka, :], ALU.mult, ALU.add,
                            [("pre", pk), "par", ("tf", ka)], [("tf", ka)])
                    act(arena[:, 4 + ch, :], tf[:, ka, :], AF.Silu, [("tf", ka), "par"], [pg(4 + ch)],
                        bias=pcol(o_cb + l * 8 + ch))
            slot = next_w(l, 5)
            for i in range(4):
                P = newps()
                for kc in range(8):
                    mm(ps[P][:, :], xnT[:, kc, i * 128:(i + 1) * 128], wview(slot, kc, 0, 512), kc == 0, kc == 7,
                       [("w", slot), ("xn", kc)], [("ps", P)])
                copy_any(arena[:, 12 + i, :], ps[P][:, :], [("ps", P)], [pg(12 + i)])
            slot = next_w(l, 6)
            for m in range(4):
                P = newps()
                for kc in range(8):
                    mm(ps[P][:, :], wview(slot, kc, m * 128, 128), xnT[:, kc, :], kc == 0, kc == 7,
                       [("w", slot), ("xn", kc)], [("ps", P)])
                act(arena[:, 20 + m, :], ps[P][:, :], AF.Sigmoid, [("ps", P)], [pg(20 + m)])
            G = newps()
            for i in range(4):
                for kc in range(8):
                    mm(ps[G][:, i * 8:(i + 1) * 8], xnT[:, kc, i * 128:(i + 1) * 128],
                       wif[:, l * 64 + kc * 8: l * 64 + kc * 8 + 8], kc == 0, kc == 7,
                       ["wif", ("xn", kc)], [("ps", G)])
            G3 = ps[G][:, 0:32].rearrange("p (i c) -> p i c", c=8)
            ig = gsm[:, 0, :]
            zf = gsm[:, 1, :]
            ef = gsm[:, 2, :]
            spv = gsm[:, 3, :]
            logf = gsm[:, 4, :]
            biasv = gsm[:, 5, :]
            v3 = lambda ap: ap.rearrange("p (i c) -> p i c", c=4)
            tt(v3(ig), G3[:, :, 0:4], v3(par[:, o_bi + l * 16:o_bi + l * 16 + 16]), ALU.add,
               [("ps", G), "par"], ["g_ig"])
            tt(v3(zf), G3[:, :, 4:8], v3(par[:, o_bf + l * 16:o_bf + l * 16 + 16]), ALU.add,
               [("ps", G), "par"], ["g_zf"])
            act(ef, zf, AF.Exp, ["g_zf"], ["g_ef"], scale=-1.0)
            act(spv, ef, AF.Ln, ["g_ef"], ["g_sp"], bias=1.0)
            ts(logf, spv, -1.0, None, ALU.mult, None, ["g_sp"], ["g_lf"])
            Cm = newps()
            mm(ps[Cm][:, 0:16], Umat, logf, True, True, ["cst", "g_lf"], [("ps", Cm)])
            tt(biasv, ig, ps[Cm][:, 0:16], ALU.subtract, ["g_ig", ("ps", Cm)], ["g_bv"])

            for i in range(4):
                lfk = state["lf"]
                state["lf"] ^= 1
                for h in range(4):
                    col = i * 4 + h
                    ts(LF[:, lfk, h, :], onesF[:, :], logf[:, col:col + 1], None, ALU.mult, None,
                       ["onesF", "g_lf"], [("LF", lfk, h)])
                tsl = slice(i * 128, (i + 1) * 128)
                for h in range(4):
                    col = i * 4 + h
                    sk = l * 4 + h
                    kT = arena[:, 8 + h, tsl]
                    qT_ = arena[:, 4 + h, tsl]
                    vt = arena[:, 12 + i, h * 128:(h + 1) * 128]
                    A = newps()
                    mm(ps[A][:, 0:128], kT, qT_, True, True, [pg(8 + h), pg(4 + h)], [("ps", A)])
                    mm(ps[A][:, 128:256], LF[:, lfk, h, :], Umat, True, True, [("LF", lfk, h), "cst"], [("ps", A)])
                    mm(ps[A][:, 256:384], LF[:, lfk, h, :], Umat, True, False, [("LF", lfk, h), "cst"], [("ps", A)])
                    mm(ps[A][:, 256:384], identB[:, :], NEGb, False, True, ["identB", "cstB"], [("ps", A)])
                    ka = newtf()
                    act(tf[:, ka, 0:128], ps[A][:, 256:384], AF.Exp, [("ps", A), "g_bv"], [("tf", ka)],
                        bias=biasv[:, col:col + 1])
                    act(tf[:, ka, 128:256], ps[A][:, 128:256], AF.Exp, [("ps", A)], [("tf", ka)])
                    wk = state["wsm"]
                    state["wsm"] = (wk + 1) % 8
                    if pipe:
                        ts(wsm[:, wk, 0:1], tf[:, ka, 127:128], pcol(o_vd + j), None, ALU.mult, None,
                           [("tf", ka), "par"], [("wsm", wk)])
                    else:
                        sch.op("vector", lambda e, wk=wk, ka=ka: e.tensor_copy(wsm[:, wk, 0:1], tf[:, ka, 127:128]),
                               [("tf", ka)], [("wsm", wk)])
                    sch.op("vector", lambda e, wk=wk, ka=ka: e.tensor_copy(wsm[:, wk, 1:2], tf[:, ka, 255:256]),
                           [("tf", ka)], [("wsm", wk)])
                    kb = newtb()
                    stt(tbf[:, kb, 0:128], ps[A][:, 0:128], SC_M, tf[:, ka, 0:128], ALU.mult, ALU.mult,
                        [("ps", A), ("tf", ka)], [("tb", kb)])
                    stt(tbf[:, kb, 128:256], qT_, SC_M, tf[:, ka, 128:256], ALU.mult, ALU.mult,
                        [pg(4 + h), ("tf", ka)], [("tb", kb)])
                    AT = tbf[:, kb, 0:128]
                    qsT = tbf[:, kb, 128:256]
                    B = newps()
                    mm(ps[B][:, 0:128], vt, AT, True, False, [pg(12 + i), ("tb", kb)], [("ps", B)])
                    mm(ps[B][:, 0:128], Sb[:, sk, 0:128], qsT, False, True, ["Sb%d" % sk, ("tb", kb)], [("ps", B)])
                    mm(ps[B][:, 128:256], onesB[:, :], AT, True, False, ["onesB", ("tb", kb)], [("ps", B)])
                    mm(ps[B][:, 128:256], Sb[:, sk, 128:256], qsT, False, True, ["Sb%d" % sk, ("tb", kb)],
                       [("ps", B)])
                    kc_ = newtf()
                    act(tf[:, kc_, 0:128], ps[B][:, 128:256], AF.Abs, [("ps", B)], [("tf", kc_)])
                    ts(tf[:, kc_, 0:128], tf[:, kc_, 0:128], 1.0, None, ALU.max, None, [("tf", kc_)], [("tf", kc_)])
                    recip(tf[:, kc_, 0:128], tf[:, kc_, 0:128], [("tf", kc_)], [("tf", kc_)])
                    tt(tf[:, kc_, 128:256], ps[B][:, 0:128], tf[:, kc_, 0:128], ALU.mult,
                       [("ps", B), ("tf", kc_)], [("tf", kc_)])
                    act(tbf[:, kb, 384:512], tf[:, kc_, 128:256], AF.Square, [("tf", kc_)], [("tb", kb)])
                    mm(ps[B][:, 256:384], onesMHb, tbf[:, kb, 384:512], True, True, ["cstB", ("tb", kb)], [("ps", B)])
                    act(tf[:, kc_, 384:512], ps[B][:, 256:384], AF.Ln, [("ps", B), "par"], [("tf", kc_)],
                        bias=pcol(o_eps))
                    act(tf[:, kc_, 384:512], tf[:, kc_, 384:512], AF.Exp, [("tf", kc_)], [("tf", kc_)], scale=-0.5)
                    stt(tf[:, kc_, 0:128], tf[:, kc_, 128:256], pcol(o_mh + l * 4 + h), tf[:, kc_, 384:512],
                        ALU.mult, ALU.mult, [("tf", kc_), "par"], [("tf", kc_)])
                    tt(arena[:, 28 + h, tsl], tf[:, kc_, 0:128], arena[:, 20 + h, tsl], ALU.mult,
                       [("tf", kc_), pg(20 + h)], [pg(28 + h)])
                    C = newps()
                    mm(ps[C][:, 0:128], kT, identB[:, :], True, True, [pg(8 + h), "identB"], [("ps", C)])
                    act(tbf[:, kb, 256:384], ps[C][:, 0:128], AF.Copy, [("ps", C), ("wsm", wk)], [("tb", kb)],
                        scale=wsm[:, wk, 0:1])
                    kw_ = tbf[:, kb, 256:384]
                    mm(ps[C][:, 128:256], kw_, vt, True, True, [("tb", kb), pg(12 + i)], [("ps", C)])
                    mm(ps[C][:, 256:384], kw_, onesB[:, :], True, True, [("tb", kb), "onesB"], [("ps", C)])
                    stt(S[:, sk, :], S[:, sk, :], wsm[:, wk, 1:2], ps[C][:, 128:384], ALU.mult, ALU.add,
                        ["S%d" % sk, ("wsm", wk), ("ps", C)], ["S%d" % sk])
                    sch.op("scalar", lambda e, sk=sk: e.copy(Sb[:, sk, :], S[:, sk, :]), ["S%d" % sk], ["Sb%d" % sk])

            sga_pg = [16, 17, 18, 19, 0, 1, 2, 3]
            for t in range(4):
                sg = next_w(l, 7 + t)
                for cc in range(2):
                    c = 2 * t + cc
                    GA = newps()
                    for kc in range(8):
                        mm(ps[GA][:, :], wview(sg, kc, cc * 128, 128), xnT[:, kc, :], kc == 0, kc == 7,
                           [("w", sg), ("xn", kc)], [("ps", GA)])
                    act(arena[:, sga_pg[c], :], ps[GA][:, :], AF.Sigmoid, [("ps", GA)], [pg(sga_pg[c])])
                    GM = newps()
                    for kc in range(8):
                        mm(ps[GM][:, :], wview(sg, kc, 256 + cc * 128, 128), xnT[:, kc, :], kc == 0, kc == 7,
                           [("w", sg), ("xn", kc)], [("ps", GM)])
                    act(sgmb[:, c, :], ps[GM][:, :], AF.Sigmoid, [("ps", GM)], [("sgm", c)])
            for u in range(2):
                spp = next_w(l, 11 + u)
                for tt_ in range(2):
                    for cc in range(2):
                        c = 2 * (2 * u + tt_) + cc
                        pbase = tt_ * 2048 + cc * 128
                        PA = newps()
                        for kc in range(4):
                            o_ = pbase + kc * 256
                            mm(ps[PA][:, :], wb[:, spp, o_:o_ + 128], arena[:, 24 + kc, :], kc == 0, kc == 3,
                               [("w", spp), pg(24 + kc)], [("ps", PA)])
                        PM = newps()
                        for kc in range(4):
                            o_ = pbase + 1024 + kc * 256
                            mm(ps[PM][:, :], wb[:, spp, o_:o_ + 128], arena[:, 28 + kc, :], kc == 0, kc == 3,
                               [("w", spp), pg(28 + kc)], [("ps", PM)])
                        k1 = newtf()
                        k2 = newtf()
                        tt(tf[:, k1, :], ps[PA][:, :], arena[:, sga_pg[c], :], ALU.mult,
                           [("ps", PA), pg(sga_pg[c])], [("tf", k1)])
                        tt(tf[:, k2, :], ps[PM][:, :], sgmb[:, c, :], ALU.mult, [("ps", PM), ("sgm", c)], [("tf", k2)])
                        tt(arena[:, 4 + c, :], tf[:, k1, :], tf[:, k2, :], ALU.add, [("tf", k1), ("tf", k2)], [pg(4 + c)])
            for t in range(2):
                so = next_w(l, 13 + t)
                for m in range(4):
                    c = 4 * t + m
                    P = newps()
                    for kc in range(8):
                        mm(ps[P][:, :], wview(so, kc, m * 128, 128), arena[:, 4 + kc, :], kc == 0, kc == 7,
                           [("w", so), pg(4 + kc)], [("ps", P)])
                    tt(xT[:, c, :], xT[:, c, :], ps[P][:, :], ALU.add, [("xT", c), ("ps", P)], [("xT", c)])
            rmsnorm(o_g2 + l * 8, True)
            for t in range(8):
                su = next_w(l, 15 + t)
                for m in range(4):
                    f = 4 * t + m
                    P = newps()
                    for kc in range(8):
                        mm(ps[P][:, :], wview(su, kc, m * 128, 128), xnT[:, kc, :], kc == 0, kc == 7,
                           [("w", su), ("xn", kc)], [("ps", P)])
                    k1 = newtf()
                    act(tf[:, k1, :], ps[P][:, :], AF.Relu, [("ps", P)], [("tf", k1)])
                    tt(arena[:, f, :], ps[P][:, :], tf[:, k1, :], ALU.mult, [("ps", P), ("tf", k1)], [pg(f)])
            for c in range(8):
                sd = next_w(l, 23 + c)
                P = newps()
                for f in range(32):
                    mm(ps[P][:, :], wview(sd, f, 0, 128, kcn=32), arena[:, f, :], f == 0, f == 31,
                       [("w", sd), pg(f)], [("ps", P)])
                tt(xT[:, c, :], xT[:, c, :], ps[P][:, :], ALU.add, [("xT", c), ("ps", P)], [("xT", c)])

        def load_x(jb):
            src = x_d[jb * TB:(jb + 1) * TB, :].rearrange("(i p) d -> p i d", p=128)
            sch.op("sync", lambda e, src=src: e.dma_start(out=xin[:, :, :], in_=src), [],
                   [("xin", i) for i in range(4)], dma_sem="io_in", cost=8000.0)
            for c in range(8):
                P = newps()
                for i in range(4):
                    tr(ps[P][:, i * 128:(i + 1) * 128], xin[:, i, c * 128:(c + 1) * 128], [("xin", i)], [("ps", P)])
                copy_any(xT[:, c, :], ps[P][:, :], [("ps", P)], [("xT", c)])

        def store_out(jb, srcT, srcname):
            for i in range(4):
                for half in range(2):
                    P = newps()
                    for cc in range(4):
                        c = half * 4 + cc
                        tr(ps[P][:, cc * 128:(cc + 1) * 128], srcT[:, c, i * 128:(i + 1) * 128], [(srcname, c)],
                           [("ps", P)])
                    copy_any(io[:, i, half * 512:(half + 1) * 512], ps[P][:, :], [("ps", P)], [("io", i)])
            dst = out_d[jb * TB:(jb + 1) * TB, :].rearrange("(i p) d -> p i d", p=128)
            sch.op("sync", lambda e, dst=dst: e.dma_start(out=dst, in_=io[:, :, :]),
                   [("io", i) for i in range(4)], [], dma_sem="io_out", cost=8000.0)

        if not pipe:
            for j in range(nblk):
                load_x(j)
                for l in range(L):
                    layer(l, j)
                rmsnorm(o_gf, False)
                store_out(j, xT, "xT")
        else:
            for st in range(nstep):
                jb = min(st, nblk - 1)
                xsrc = x_d[:, jb * TB:(jb + 1) * TB].rearrange("(c p) t -> p c t", p=128)
                sch.op("sync", lambda e, xsrc=xsrc: e.dma_start(out=xl[:, :, :], in_=xsrc), [],
                       [("xl", c) for c in range(8)], dma_sem="io_in", cost=8000.0)
                if st == 0:
                    for c in range(8):
                        copy_any(xT[:, c, :], xl[:, c, :], [("xl", c)], [("xT", c)])
                else:
                    for hf in range(NH):
                        rsrc = recv_d[st - 1][hf][0:128, :].rearrange("p (c t) -> p c t", c=CPP)
                        sch.op("sync", lambda e, rsrc=rsrc, hf=hf: e.dma_start(out=rbuf[:, CPP * hf:CPP * hf + CPP, :], in_=rsrc),
                               [("recv", st - 1, hf)], [("rb", c) for c in range(CPP * hf, CPP * hf + CPP)], dma_sem=("rcv", hf),
                               cost=2000.0 + 750.0 * CPP)
                        for c in range(CPP * hf, CPP * hf + CPP):
                            stt(xT[:, c, :], rbuf[:, c, :], pcol(o_fb), xl[:, c, :], ALU.mult, ALU.add,
                                [("rb", c), ("xl", c), "par"], [("xT", c)])
                layer(0, st)
                if st < nblk:
                    for hf in range(NH):
                        sdst = send_d[st][hf].rearrange("p (c t) -> p c t", c=CPP)
                        sch.op("sync", lambda e, sdst=sdst, hf=hf: e.dma_start(out=sdst, in_=xT[:, CPP * hf:CPP * hf + CPP, :]),
                               [("xT", c) for c in range(CPP * hf, CPP * hf + CPP)], [("send", st, hf)], dma_sem=("snd", hf),
                               cost=2000.0 + 750.0 * CPP)
                        sch.op("gpsimd", lambda e, st=st, hf=hf: e.collective_compute(
                            "AllGather", ALU.bypass, replica_groups=[[0, 1], [2, 3], [4, 5], [6, 7]],
                            ins=[send_d[st][hf]], outs=[recv_d[st][hf]]),
                            [("send", st, hf)], [("recv", st, hf), "cc_serial"], dma_sem=("cc", NH * st + hf),
                            cost=3000.0)
                if st >= 1:
                    kr = norm_stats()
                    for c in range(8):
                        stt(rbuf[:, c, :], xT[:, c, :], pcol(o_gf + c), tf[:, kr, :], ALU.mult, ALU.mult,
                            [("xT", c), ("tf", kr), "par"], [("rb", c)])
                    odst = out_d[:, (st - 1) * TB:st * TB].rearrange("(c p) t -> p c t", p=128)
                    sch.op("sync", lambda e, odst=odst: e.dma_start(out=odst, in_=rbuf[:, :, :]),
                           [("rb", c) for c in range(8)], [], dma_sem="io_out", cost=8000.0)

        sch.finalize()
        sch.emit(nc, es, dma_sems, ["io_out"])
    return nc


def _fm(W, cols):
    sub = W[:, cols]
    n = sub.shape[1]
    return np.ascontiguousarray(sub.reshape(8, 128, n).transpose(1, 0, 2).reshape(128, 8 * n))


def host_constants():
    s = np.arange(128)
    ident = np.eye(128, dtype=np.float32)
    onesMD = np.full((128, 128), 1.0 / 1024.0, np.float32)
    onesMH = np.full((128, 128), 1.0 / 128.0, np.float32)
    U = (s[:, None] <= s[None, :]).astype(np.float32)
    NEGm = np.where(s[:, None] <= s[None, :], 0.0, NEG).astype(np.float32)
    cst = np.concatenate([ident, onesMD, onesMH, U, NEGm], axis=1)
    kp = np.arange(128)[:, None]
    u = np.arange(640)[None, :]
    dq = u // 64 - (kp >= 64)
    negm = np.where((dq >= 0) & (dq <= 8), 0.0, NEG).astype(np.float32)
    return np.ascontiguousarray(cst), np.ascontiguousarray(negm)


def host_prep(inp, depth=DEPTH, layers=None, pipe_role=None, nblk=SEQ // TB):
    if layers is None:
        layers = list(range(depth))
    L = len(layers)
    sel = np.asarray(layers)
    inp = dict(inp)
    for k_ in ("mix_norm_g", "w_in", "conv_w", "conv_b", "b_igate", "b_fgate", "rel_bias", "mh_norm_g",
               "w_att_proj", "w_mlstm_proj", "w_out", "ffn_norm_g", "w_up", "w_down"):
        inp[k_] = np.asarray(inp[k_], np.float32)[sel]
    w_in = inp["w_in"]
    tiles = []
    wifs = []
    for l in range(L):
        for g in range(7):
            tiles.append(_fm(w_in[l], np.arange(512 * g, 512 * g + 512)))
        w_att = np.asarray(inp["w_att_proj"][l], np.float32)
        w_ml = np.asarray(inp["w_mlstm_proj"][l], np.float32)

        def pp_tile(u):
            parts = []
            for tt_ in range(2):
                t = 2 * u + tt_
                for w in (w_att, w_ml):
                    sub = w[:, t * 256:(t + 1) * 256]
                    parts.append(sub.reshape(4, 128, 256).transpose(1, 0, 2).reshape(128, 1024))
            return np.ascontiguousarray(np.concatenate(parts, axis=1))

        def gagm_tile(t):
            cols = np.concatenate([GA0 + 256 * t + np.arange(256), GM0 + 256 * t + np.arange(256)])
            return _fm(w_in[l], cols)

        for t_ in range(4):
            tiles.append(gagm_tile(t_))
        tiles.append(pp_tile(0))
        tiles.append(pp_tile(1))
        w_out = np.asarray(inp["w_out"][l], np.float32)
        for t in range(2):
            tiles.append(_fm(w_out, np.arange(512 * t, 512 * t + 512)))
        w_up = np.asarray(inp["w_up"][l], np.float32)
        for t in range(8):
            tiles.append(_fm(w_up, np.arange(512 * t, 512 * t + 512)))
        w_down = np.asarray(inp["w_down"][l], np.float32)
        for c in range(8):
            sub = w_down[:, c * 128:(c + 1) * 128]
            tiles.append(np.ascontiguousarray(sub.reshape(32, 128, 128).transpose(1, 0, 2).reshape(128, 4096)))
        wifs.append(_fm(w_in[l], np.arange(3584, 3592)))
    wstream = np.stack(tiles, axis=0)
    assert wstream.shape == (L * NW_TILES, 128, 4096)
    wif = np.concatenate(wifs, axis=1)

    kp = np.arange(128)[:, None]
    u = np.arange(640)[None, :]
    rel_idx = np.clip(u - kp, -256, 256) + 256
    rb = np.asarray(inp["rel_bias"], np.float32)[:L]
    bg = rb[:, :, rel_idx]
    biasg = np.ascontiguousarray(bg.transpose(2, 0, 1, 3).reshape(128, L * 8 * 640))

    def fmv(v):
        v = np.asarray(v, np.float32)
        k = v.shape[-1] // 128
        return np.moveaxis(v.reshape(v.shape[:-1] + (k, 128)), -1, 0)

    g1 = fmv(inp["mix_norm_g"][:L]).reshape(128, L * 8)
    g2 = fmv(inp["ffn_norm_g"][:L]).reshape(128, L * 8)
    gf = fmv(inp["final_norm_g"]).reshape(128, 8)
    cw = fmv(inp["conv_w"][:L]).reshape(128, L * 32)
    cb = fmv(inp["conv_b"][:L]).reshape(128, L * 8)
    mh = fmv(inp["mh_norm_g"][:L]).reshape(128, L * 4)
    bi = np.broadcast_to(np.asarray(inp["b_igate"], np.float32)[:L][None, :, None, :], (128, L, 4, 4)).reshape(128, L * 16)
    bf = np.broadcast_to(np.asarray(inp["b_fgate"], np.float32)[:L][None, :, None, :], (128, L, 4, 4)).reshape(128, L * 16)
    epsc = np.full((128, 1), EPS, np.float32)
    plist = [g1, g2, gf, cw, cb, mh, bi, bf, epsc]
    if pipe_role is not None:
        nstep = nblk + 1
        fb = np.full((128, 1), float(pipe_role), np.float32)
        smask = np.zeros((128, nstep), np.float32)
        smask[:, :pipe_role + 1] = NEG
        valid = np.ones((128, nstep), np.float32)
        valid[:, :pipe_role] = 0.0
        plist += [fb, smask, valid]
    params = np.ascontiguousarray(np.concatenate(plist, axis=1).astype(np.float32))
    cst, negm = host_constants()
    return {"wstream": wstream, "wif": np.ascontiguousarray(wif), "biasg": biasg, "cst": cst, "negm": negm,
            "params": params}


_NC_CACHE = {}


PIPE = True


def kernel(**inputs):
    x = np.asarray(inputs["x"], np.float32)
    B, S_, D = x.shape
    nblk = S_ // TB
    n_cores = 8
    key = (nblk, DEPTH, PIPE)
    if key not in _NC_CACHE:
        _NC_CACHE[key] = build_program(nblk, DEPTH, pipe=PIPE)
    nc = _NC_CACHE[key]
    in_maps = []
    if PIPE:
        assert DEPTH == 2 and B * 2 == n_cores
        roles = [host_prep(inputs, DEPTH, layers=[r], pipe_role=r, nblk=nblk) for r in range(2)]
        zeros = np.zeros((D, S_), np.float32)
        for c in range(n_cores):
            m = dict(roles[c % 2])
            m["x"] = np.ascontiguousarray(x[c // 2].T) if c % 2 == 0 else zeros
            in_maps.append(m)
        res = run_bass_kernel_spmd(nc, in_maps, core_ids=list(range(n_cores)))
        out = np.stack([np.ascontiguousarray(np.asarray(res.results[2 * b + 1]["out"], np.float32).T)
                        for b in range(B)], axis=0)
        return out
    shared = host_prep(inputs, DEPTH)
    for c in range(n_cores):
        m = dict(shared)
        m["x"] = np.ascontiguousarray(x[c % B])
        in_maps.append(m)
    res = run_bass_kernel_spmd(nc, in_maps, core_ids=list(range(n_cores)))
    out = np.stack([np.asarray(res.results[b]["out"], np.float32) for b in range(B)], axis=0)
    return out
```
